# Optimizing a Trainium2 kernel written in Bass

```python
import jax, jax.numpy as jnp
from jax import lax
import numpy as np

D_MODEL = 2048
BATCH = 2
SEQ = 4096
DEPTH = 2
DEC_BATCH = 32
DEC_SEQ = 1
PAST_LEN = 16384
PAGE_SIZE = 128

D_MIX = D_MODEL
M_WIDTH = D_MIX // 2
M_HEADS = 4
M_DK = M_WIDTH // M_HEADS
M_CHUNK = 128
C_WIDTH = D_MIX // 4
CONV_K = 31
CONV_BUF = CONV_K - 1
A_WIDTH = D_MIX - M_WIDTH - C_WIDTH
HEAD_DIM = 64
A_HEADS = A_WIDTH // HEAD_DIM
KV_HEADS = 2
GROUP = A_HEADS // KV_HEADS
WINDOW = 128
A_BLOCK = WINDOW
ALPHA = (2 * DEPTH) ** 0.25
BETA = (8 * DEPTH) ** -0.25
LN_EPS = 1e-5
IN_SPLITS = (M_WIDTH, M_WIDTH, M_WIDTH, M_WIDTH, M_HEADS, M_HEADS, M_WIDTH,
             C_WIDTH, C_WIDTH, C_WIDTH,
             A_WIDTH, KV_HEADS * HEAD_DIM, KV_HEADS * HEAD_DIM, A_WIDTH)
D_IN = sum(IN_SPLITS)

kernel_name = 'hymba_mlstm_conformer_swa_decode_step'


def layer_norm(x, g, b):
    xf = x.astype(jnp.float32)
    mu = jnp.mean(xf, axis=-1, keepdims=True)
    var = jnp.mean(jnp.square(xf - mu), axis=-1, keepdims=True)
    y = (xf - mu) * lax.rsqrt(var + LN_EPS) * g.astype(jnp.float32) + b.astype(jnp.float32)
    return y.astype(x.dtype)


def in_projection(x, w_in):
    h = jnp.einsum('btd,de->bte', x, w_in)
    cuts = np.cumsum(np.array(IN_SPLITS))[:-1].tolist()
    return jnp.split(h, cuts, axis=-1)


def mlstm_chunk(q, k, v, ig, lf, C0, n0, m0):
    L = q.shape[2]
    b = jnp.cumsum(lf, axis=-1)
    log_w = b[..., :, None] - b[..., None, :] + ig[..., None, :]
    causal = jnp.tril(jnp.ones((L, L), dtype=bool))
    log_w = jnp.where(causal, log_w, -jnp.inf)
    log_w0 = b + m0[..., None]
    m = jnp.maximum(log_w0, jnp.max(log_w, axis=-1))
    w = jnp.exp(log_w - m[..., None])
    w0 = jnp.exp(log_w0 - m)
    s = jnp.einsum('bhtd,bhsd->bhts', q, k) * w
    num = jnp.einsum('bhts,bhse->bhte', s, v) + w0[..., None] * jnp.einsum('bhtd,bhde->bhte', q, C0)
    den = jnp.sum(s, axis=-1) + w0 * jnp.einsum('bhtd,bhd->bht', q, n0)
    h = num / jnp.maximum(jnp.abs(den), jnp.exp(-m))[..., None]
    m_end = m[..., -1]
    g = jnp.exp(b[..., -1:] - b + ig - m_end[..., None])
    g0 = jnp.exp(b[..., -1] + m0 - m_end)
    C = g0[..., None, None] * C0 + jnp.einsum('bhs,bhsd,bhse->bhde', g, k, v)
    n = g0[..., None] * n0 + jnp.einsum('bhs,bhsd->bhd', g, k)
    return h, C, n, m_end


def mlstm_prompt(q, k, v, ig, lf):
    B, T, H, D = q.shape
    nc = T // M_CHUNK

    def to_chunks(a):
        a = a.reshape((B, nc, M_CHUNK) + a.shape[2:])
        return jnp.moveaxis(jnp.moveaxis(a, 1, 0), 2, 3)

    def step(carry, xs):
        h, C, n, m = mlstm_chunk(*xs, *carry)
        return (C, n, m), h

    init = (jnp.zeros((B, H, D, D), jnp.float32), jnp.zeros((B, H, D), jnp.float32),
            jnp.zeros((B, H), jnp.float32))
    (C, n, m), h = lax.scan(step, init, tuple(to_chunks(a) for a in (q, k, v, ig, lf)))
    h = jnp.swapaxes(jnp.moveaxis(h, 0, 1), 2, 3).reshape(B, T, H, D)
    return h, C, n, m


def mlstm_step(q, k, v, ig, lf, C0, n0, m0):
    f32 = jnp.float32
    h, C, n, m = mlstm_chunk(jnp.swapaxes(q, 1, 2), jnp.swapaxes(k, 1, 2), jnp.swapaxes(v, 1, 2),
                             jnp.swapaxes(ig, 1, 2), jnp.swapaxes(lf, 1, 2),
                             C0.astype(f32), n0.astype(f32), m0.astype(f32))
    return jnp.swapaxes(h, 1, 2), C, n, m


def conformer_conv(u, g, buf, conv_w, conv_b, ln_g, ln_b):
    a = u * jax.nn.sigmoid(g)
    a_full = jnp.concatenate([buf.astype(a.dtype), a], axis=1)
    y = lax.conv_general_dilated(a_full, conv_w.astype(a.dtype)[:, None, :], window_strides=(1,),
                                 padding='VALID', dimension_numbers=('NWC', 'WIO', 'NWC'),
                                 feature_group_count=C_WIDTH)
    y = layer_norm(y + conv_b, ln_g, ln_b)
    return jax.nn.silu(y), a_full[:, a_full.shape[1] - CONV_BUF:]


def sink_softmax(scores, mask, sinks):
    scores = jnp.where(mask, scores, -jnp.inf)
    sink = jnp.broadcast_to(sinks.astype(jnp.float32).reshape(KV_HEADS, GROUP, 1, 1),
                            scores.shape[:-1] + (1,))
    probs = jax.nn.softmax(jnp.concatenate([scores, sink], axis=-1), axis=-1)
    return probs[..., :-1]


def swa_prompt(q, k, v, sinks):
    B, T = q.shape[:2]
    nb = T // A_BLOCK
    qb = q.reshape(B, nb, A_BLOCK, KV_HEADS, GROUP, HEAD_DIM)

    def with_prev(a):
        a = a.reshape(B, nb, A_BLOCK, KV_HEADS, HEAD_DIM)
        prev = jnp.concatenate([jnp.zeros_like(a[:, :1]), a[:, :-1]], axis=1)
        return jnp.concatenate([prev, a], axis=2)

    kk, vv = with_prev(k), with_prev(v)
    s = jnp.einsum('bnikgd,bnjkd->bnkgij', qb, kk, preferred_element_type=jnp.float32) * (HEAD_DIM ** -0.5)
    i = jnp.arange(A_BLOCK)[:, None]
    j = jnp.arange(2 * A_BLOCK)[None, :]
    rel = A_BLOCK + i - j
    band = (rel >= 0) & (rel <= WINDOW)
    has_prev = jnp.arange(nb)[:, None, None] > 0
    mask = band[None] & (has_prev | (j >= A_BLOCK)[None])
    p = sink_softmax(s, mask[None, :, None, None], sinks)
    o = jnp.einsum('bnkgij,bnjkd->bnikgd', p.astype(v.dtype), vv).reshape(B, T, A_WIDTH)
    return o, k[:, T - WINDOW:], v[:, T - WINDOW:]


def swa_sample(q, k, v, ck, cv, sinks):
    B, S = q.shape[:2]
    n_buf = ck.shape[1]
    kk = jnp.concatenate([ck.astype(k.dtype), k], axis=1)
    vv = jnp.concatenate([cv.astype(v.dtype), v], axis=1)
    qg = q.reshape(B, S, KV_HEADS, GROUP, HEAD_DIM)
    s = jnp.einsum('bikgd,bjkd->bkgij', qg, kk, preferred_element_type=jnp.float32) * (HEAD_DIM ** -0.5)
    i = jnp.arange(S)[:, None]
    j = jnp.arange(n_buf + S)[None, :]
    rel = n_buf + i - j
    mask = (rel >= 0) & (rel <= WINDOW)
    p = sink_softmax(s, mask, sinks)
    o = jnp.einsum('bkgij,bjkd->bikgd', p.astype(v.dtype), vv).reshape(B, S, A_WIDTH)
    return o, kk[:, S:], vv[:, S:]


def mixer_layer(x, w_in, w_out, b_igate, b_fgate, m_norm_g, conv_w, conv_b, conv_ln_g, conv_ln_b,
                sinks, past):
    B, T, _ = x.shape
    f32 = jnp.float32
    mq, mk, mv, mo, mi, mf, mz, cu, cg, cz, aq, ak, av, az = in_projection(x, w_in)
    q = mq.reshape(B, T, M_HEADS, M_DK).astype(f32)
    k = mk.reshape(B, T, M_HEADS, M_DK).astype(f32) * (M_DK ** -0.5)
    v = mv.reshape(B, T, M_HEADS, M_DK).astype(f32)
    ig = mi.astype(f32) + b_igate.astype(f32)
    lf = jax.nn.log_sigmoid(mf.astype(f32) + b_fgate.astype(f32))
    qa = aq.reshape(B, T, A_HEADS, HEAD_DIM)
    ka = ak.reshape(B, T, KV_HEADS, HEAD_DIM)
    va = av.reshape(B, T, KV_HEADS, HEAD_DIM)
    if past is None:
        h, C, n, m = mlstm_prompt(q, k, v, ig, lf)
        conv_buf = jnp.zeros((B, CONV_BUF, C_WIDTH), x.dtype)
        a_out, new_k, new_v = swa_prompt(qa, ka, va, sinks)
    else:
        C0, n0, m0, conv_buf, ck, cv = past
        h, C, n, m = mlstm_step(q, k, v, ig, lf, C0, n0, m0)
        a_out, new_k, new_v = swa_sample(qa, ka, va, ck, cv, sinks)
    mu = jnp.mean(h, axis=-1, keepdims=True)
    var = jnp.mean(jnp.square(h - mu), axis=-1, keepdims=True)
    hn = ((h - mu) * lax.rsqrt(var + LN_EPS)).reshape(B, T, M_WIDTH) * m_norm_g.astype(f32)
    m_out = hn * jax.nn.sigmoid(mo.astype(f32)) * jax.nn.silu(mz.astype(f32))
    c_out, new_buf = conformer_conv(cu, cg, conv_buf, conv_w, conv_b, conv_ln_g, conv_ln_b)
    merged = jnp.concatenate([m_out.astype(x.dtype),
                              (c_out * jax.nn.silu(cz)).astype(x.dtype),
                              (a_out * jax.nn.silu(az)).astype(x.dtype)], axis=-1)
    y = jnp.einsum('bte,ed->btd', merged, w_out)
    return y, (C, n, m, new_buf, new_k, new_v)


def setup_inputs(seed: int = 0) -> dict:
    key = jax.random.key(seed)
    ks = jax.random.split(key, 24)
    nrm = jax.random.normal
    n_buf = min(WINDOW, PAST_LEN)
    return {
        'x_prompt': nrm(ks[0], (BATCH, SEQ, D_MODEL), jnp.float32),
        'x_sample': nrm(ks[1], (DEC_BATCH, DEC_SEQ, D_MODEL), jnp.float32),
        'state_C': 0.5 * nrm(ks[2], (DEPTH, DEC_BATCH, M_HEADS, M_DK, M_DK), jnp.float32),
        'state_n': 0.5 * nrm(ks[3], (DEPTH, DEC_BATCH, M_HEADS, M_DK), jnp.float32),
        'state_m': nrm(ks[4], (DEPTH, DEC_BATCH, M_HEADS), jnp.float32),
        'state_conv': 0.5 * nrm(ks[5], (DEPTH, DEC_BATCH, CONV_BUF, C_WIDTH), jnp.float32),
        'cache_k': nrm(ks[6], (DEPTH, DEC_BATCH, n_buf, KV_HEADS, HEAD_DIM), jnp.float32),
        'cache_v': nrm(ks[7], (DEPTH, DEC_BATCH, n_buf, KV_HEADS, HEAD_DIM), jnp.float32),
        'w_in': nrm(ks[8], (DEPTH, D_MODEL, D_IN), jnp.float32) * (D_MODEL ** -0.5),
        'w_out': nrm(ks[9], (DEPTH, D_MIX, D_MODEL), jnp.float32) * (D_MIX ** -0.5) * BETA,
        'b_igate': 0.1 * nrm(ks[10], (DEPTH, M_HEADS), jnp.float32),
        'b_fgate': 3.0 + 3.0 * jax.random.uniform(ks[11], (DEPTH, M_HEADS), jnp.float32),
        'm_norm_g': 1.0 + 0.1 * nrm(ks[12], (DEPTH, M_WIDTH), jnp.float32),
        'conv_w': nrm(ks[13], (DEPTH, CONV_K, C_WIDTH), jnp.float32) * (CONV_K ** -0.5),
        'conv_b': 0.02 * nrm(ks[14], (DEPTH, C_WIDTH), jnp.float32),
        'conv_ln_g': 1.0 + 0.1 * nrm(ks[15], (DEPTH, C_WIDTH), jnp.float32),
        'conv_ln_b': 0.02 * nrm(ks[16], (DEPTH, C_WIDTH), jnp.float32),
        'sinks': 0.5 * nrm(ks[17], (DEPTH, A_HEADS), jnp.float32),
        'ln_g': 1.0 + 0.1 * nrm(ks[18], (DEPTH, D_MODEL), jnp.float32),
        'ln_b': 0.02 * nrm(ks[19], (DEPTH, D_MODEL), jnp.float32),
    }


def reference(x_prompt, x_sample, state_C, state_n, state_m, state_conv, cache_k, cache_v,
              w_in, w_out, b_igate, b_fgate, m_norm_g, conv_w, conv_b, conv_ln_g, conv_ln_b,
              sinks, ln_g, ln_b):
    xp, xs = x_prompt, x_sample
    new_p = [[] for _ in range(6)]
    new_s = [[] for _ in range(6)]
    for l in range(DEPTH):
        params = (w_in[l], w_out[l], b_igate[l], b_fgate[l], m_norm_g[l], conv_w[l], conv_b[l],
                  conv_ln_g[l], conv_ln_b[l], sinks[l])
        yp, sp = mixer_layer(xp, *params, None)
        ys, ss = mixer_layer(xs, *params, (state_C[l], state_n[l], state_m[l], state_conv[l],
                                           cache_k[l], cache_v[l]))
        xp = layer_norm(ALPHA * xp + yp, ln_g[l], ln_b[l])
        xs = layer_norm(ALPHA * xs + ys, ln_g[l], ln_b[l])
        for lst, a in zip(new_p, sp):
            lst.append(a)
        for lst, a in zip(new_s, ss):
            lst.append(a)
    new_C_p, new_n_p, new_m_p, new_conv_p, new_k_p, new_v_p = [jnp.stack(a, axis=0) for a in new_p]
    new_C_s, new_n_s, new_m_s, new_conv_s, new_k_s, new_v_s = [jnp.stack(a, axis=0) for a in new_s]
    return (xp, xs, new_C_p, new_n_p, new_m_p, new_conv_p, new_k_p, new_v_p,
            new_C_s, new_n_s, new_m_s, new_conv_s, new_k_s, new_v_s)
```

```python
import numpy as np
import concourse.bass as bass
import concourse.mybir as mybir
from concourse.bass_utils import run_bass_kernel_spmd

F32 = mybir.dt.float32
BF16 = mybir.dt.bfloat16
ALU = mybir.AluOpType
AF = mybir.ActivationFunctionType
AX = mybir.AxisListType

D = 2048
SEQ = 4096
DEPTH = 2
NCORE = 8
NS = 4
TB = 256
NT = TB // 128
NBLK = SEQ // TB
NSB = 2
NTT = max(NT, NSB)
NCF = max(TB, NSB)
ALPHA = (2 * DEPTH) ** 0.25
EPS = 1e-5
BIG = 30000.0
NW_IN = 16
NW = 20
EPOCH = 3000

T_KVG, T_QK, T_V, T_O, T_Z, T_CU, T_CG, T_CZ, T_AQ, T_AZ = 'kvg', 'qk', 'v', 'o', 'z', 'cu', 'cg', 'cz', 'aq', 'az'
TILE_ORDER = [('cu', 0), ('cg', 0), ('kvg', 0), ('qk', 0), ('qk', 1), ('v', 0), ('o', 0), ('z', 0),
              ('cz', 0), ('aq', 0), ('az', 0),
              ('qk', 2), ('qk', 3), ('v', 1), ('o', 1), ('z', 1)]

C_ID, C_TRI, C_SEL, C_ONE, C_EPS = 0, 128, 256, 512, 513
NCST = 514
CB_ID, CB_BIGM, CB_NMP, CB_NMC, CB_ONE = 0, 128, 256, 768, 1280
NCSTB = 1281


def host_weight_tiles(w_in, w_out):
    mq, mk, mv, mo, mi, mf, mz = 0, 1024, 2048, 3072, 4096, 4100, 4104
    cu, cg, cz, aq, ak, av, az = 5128, 5640, 6152, 6664, 7176, 7304, 7432
    L = w_in.shape[0]
    out = np.zeros((L, NW, 2048, 512), np.float32)
    for l in range(L):
        W = w_in[l]
        for j, (kind, i) in enumerate(TILE_ORDER):
            t = out[l, j]
            if kind == 'kvg':
                t[:, 0:128] = W[:, ak:ak + 128]
                t[:, 128:256] = W[:, av:av + 128]
                t[:, 256:260] = W[:, mi:mi + 4]
                t[:, 260:264] = W[:, mf:mf + 4]
            elif kind == 'qk':
                t[:, 0:256] = W[:, mq + 256 * i: mq + 256 * (i + 1)]
                t[:, 256:512] = W[:, mk + 256 * i: mk + 256 * (i + 1)]
            elif kind == 'v':
                t[:] = W[:, mv + 512 * i: mv + 512 * (i + 1)]
            elif kind == 'o':
                t[:] = W[:, mo + 512 * i: mo + 512 * (i + 1)]
            elif kind == 'z':
                t[:] = W[:, mz + 512 * i: mz + 512 * (i + 1)]
            elif kind == 'cu':
                t[:] = W[:, cu:cu + 512]
            elif kind == 'cg':
                t[:] = W[:, cg:cg + 512]
            elif kind == 'cz':
                t[:] = W[:, cz:cz + 512]
            elif kind == 'aq':
                for c in range(4):
                    t[:, c * 128: c * 128 + 64] = W[:, aq + 64 * c: aq + 64 * (c + 1)]
                    t[:, c * 128 + 64: c * 128 + 128] = W[:, aq + 64 * (4 + c): aq + 64 * (5 + c)]
            elif kind == 'az':
                t[:] = W[:, az:az + 512]
        for j in range(4):
            out[l, NW_IN + j] = w_out[l][:, 512 * j: 512 * (j + 1)]
    out = out.reshape(L, NW, 16, 128, 512).transpose(0, 1, 3, 2, 4)
    return np.ascontiguousarray(out).reshape(L, NW, 128, 16 * 512)


def host_consts():
    c = np.zeros((128, NCST), np.float32)
    cb = np.zeros((128, NCSTB), np.float32)
    s = np.arange(128)[:, None]
    t = np.arange(128)[None, :]
    c[:, C_ID:C_ID + 128] = (s == t)
    c[:, C_TRI:C_TRI + 128] = (s <= t)
    c[0, C_SEL:C_SEL + 128] = 1.0
    c[1, C_SEL + 128:C_SEL + 256] = 1.0
    c[:, C_ONE] = 1.0
    c[:, C_EPS] = EPS
    cb[:, CB_ID:CB_ID + 128] = (s == t)
    cb[:, CB_BIGM:CB_BIGM + 128] = np.where(s > t, BIG, 0.0)
    nmp = np.where(s < t, -BIG, 0.0)
    nmc = np.where(s > t, -BIG, 0.0)
    for h in range(4):
        cb[:, CB_NMP + 128 * h: CB_NMP + 128 * (h + 1)] = nmp
        cb[:, CB_NMC + 128 * h: CB_NMC + 128 * (h + 1)] = nmc
    cb[:, CB_ONE] = 1.0
    return c, cb


class Buf:
    __slots__ = ('name', 'w', 'r', 'sem', 'cnt')

    def __init__(self, name):
        self.name = name
        self.w = None
        self.r = []
        self.sem = None
        self.cnt = 0


class Op:
    __slots__ = ('eng', 'fn', 'deps', 'dma', 'chan', 'val', 'sig', 'signo', 'idx', 'dmaw')


class Prog:
    ENGS = ('pe', 'act', 'dve', 'pool', 'sp')

    def __init__(self, nc):
        self.nc = nc
        self.ops = []
        self.nbuf = 0

    def buf(self, name=None):
        self.nbuf += 1
        return Buf(name or "b%d" % self.nbuf)

    def op(self, eng, fn, rd=(), wr=(), chan=None):
        o = Op()
        o.eng, o.fn, o.idx = eng, fn, len(self.ops)
        deps = set()
        for b in rd:
            if b.w is not None:
                deps.add(b.w)
        for b in wr:
            if b.w is not None:
                deps.add(b.w)
            deps.update(b.r)
        o.deps = deps
        o.dmaw = {}
        for d in deps:
            p = self.ops[d]
            if p.dma:
                o.dmaw[id(p.chan)] = (p.chan, 16 * p.chan.cnt)
        o.dma = chan is not None
        o.chan = chan
        o.sig = False
        o.signo = 0
        o.val = 0
        if chan is not None:
            chan.cnt += 1
            o.val = 16 * chan.cnt
        for b in rd:
            b.r.append(o.idx)
        for b in wr:
            b.w = o.idx
            b.r = []
        self.ops.append(o)
        return o

    def emit(self, final_chans):
        nc = self.nc
        ops = self.ops
        for o in ops:
            keep = {}
            for d in o.deps:
                p = ops[d]
                if p.dma:
                    continue
                if p.eng == 'pe' and o.eng == 'pe' and not o.dma:
                    continue
                k = ('c', p.eng)
                if k not in keep or keep[k] < d:
                    keep[k] = d
            o.deps = sorted(keep.values())
            for d in o.deps:
                ops[d].sig = True
        cnt = {e: 0 for e in self.ENGS}
        for o in ops:
            if o.sig and not o.dma:
                cnt[o.eng] += 1
                o.signo = cnt[o.eng]
        esems = {e: [nc.alloc_semaphore(name="s_%s_%d" % (e, i)) for i in range((cnt[e] + EPOCH - 1) // EPOCH + 1)]
                 for e in self.ENGS}
        chans = {}
        for o in ops:
            if o.dma and o.chan.sem is None:
                o.chan.sem = nc.alloc_semaphore(name="d_%s_%d" % (o.chan.name, len(chans)))
                chans[id(o.chan)] = o.chan

        def target(p):
            if p.dma:
                return p.chan.sem, p.val
            n = p.signo - 1
            return esems[p.eng][n // EPOCH], n % EPOCH + 1

        by_eng = {e: [o for o in ops if o.eng == e] for e in self.ENGS}

        def run(ename, eng):
            waited = {}
            for o in by_eng[ename]:
                for ch, val in o.dmaw.values():
                    k = id(ch.sem)
                    if waited.get(k, 0) >= val:
                        continue
                    eng.wait_ge(ch.sem, val)
                    waited[k] = val
                for d in o.deps:
                    sem, val = target(ops[d])
                    k = id(sem)
                    if waited.get(k, 0) >= val:
                        continue
                    eng.wait_ge(sem, val)
                    waited[k] = val
                ins = o.fn(eng)
                if o.dma:
                    ins.then_inc(o.chan.sem, 16)
                elif o.sig:
                    sem, _ = target(o)
                    ins.then_inc(sem, 1)
            if ename == 'sp':
                for ch in chans.values():
                    eng.wait_ge(ch.sem, 16 * ch.cnt)

        with nc.Block() as block:
            @block.tensor
            def _(e):
                run('pe', e)

            @block.scalar
            def _(e):
                run('act', e)

            @block.vector
            def _(e):
                run('dve', e)

            @block.gpsimd
            def _(e):
                run('pool', e)

            @block.sync
            def _(e):
                run('sp', e)


class Unit:
    def __init__(self, L, tt, c0, sidx=None):
        self.L, self.tt, self.c0, self.sidx = L, tt, c0, sidx


class K:
    def __init__(self, nblk=NBLK, with_sample=True, debug=False, stage=None):
        self.stage = stage
        self.stage_n = 0
        self.debug = debug
        self.bg = []
        self.nblk = nblk
        self.with_sample = with_sample
        self.debug = debug
        self.nc = nc = bass.Bass("TRN2", target_bir_lowering=False)
        self.P = Prog(nc)
        self.final = []
        self.dbg_outs = []
        di = lambda n, s: nc.dram_tensor(n, list(s), F32, kind="ExternalInput").ap()
        do = lambda n, s: nc.dram_tensor(n, list(s), F32, kind="ExternalOutput").ap()
        self.xp = di("xp", (SEQ, D))
        self.xs = di("xs", (NS, D))
        self.wt = di("wt", (DEPTH, NW, 128, 16 * 512))
        self.wb = nc.dram_tensor("wb", [DEPTH, NW, 128, 16 * 512], BF16, kind="Internal").ap()
        self.bwb = [[self.P.buf("wb%d_%d" % (l, j)) for j in range(NW)] for l in range(DEPTH)]
        self.sC = di("sC", (DEPTH, NS, 4, 256, 256))
        self.sn = di("sn", (DEPTH, NS, 4, 256))
        self.sm = di("sm", (DEPTH, NS, 4))
        self.scv = di("scv", (DEPTH, NS, 512, 30))
        self.ck = di("ck", (DEPTH, NS, 128, 128))
        self.cv = di("cv", (DEPTH, NS, 128, 128))
        self.cst = di("cst", (128, NCST))
        self.cstB = di("cstB", (128, NCSTB))
        self.p_bi = di("p_bi", (DEPTH, 4))
        self.p_bf = di("p_bf", (DEPTH, 4))
        self.p_mng = di("p_mng", (DEPTH, 1024))
        self.p_cw = di("p_cw", (DEPTH, 128, 4 * 31))
        self.p_cb = di("p_cb", (DEPTH, 128, 4))
        self.p_clg = di("p_clg", (DEPTH, 512))
        self.p_clb = di("p_clb", (DEPTH, 512))
        self.p_sk = di("p_sk", (DEPTH, 8))
        self.p_lng = di("p_lng", (DEPTH, D))
        self.p_lnb = di("p_lnb", (DEPTH, D))
        self.yp = do("yp", (SEQ, D))
        self.ys = do("ys", (NS, D))
        self.oCp = do("oCp", (DEPTH, 4, 256, 256))
        self.onp = do("onp", (DEPTH, 4, 256))
        self.omp = do("omp", (DEPTH, 4))
        self.ocvp = do("ocvp", (DEPTH, 30, 512))
        self.okp = do("okp", (DEPTH, 128, 128))
        self.ovp = do("ovp", (DEPTH, 128, 128))
        self.oCs = do("oCs", (DEPTH, NS, 4, 256, 256))
        self.ons = do("ons", (DEPTH, NS, 4, 256))
        self.oms = do("oms", (DEPTH, NS, 4))
        self.ocvs = do("ocvs", (DEPTH, NS, 30, 512))
        self.oks = do("oks", (DEPTH, NS, 128, 128))
        self.ovs = do("ovs", (DEPTH, NS, 128, 128))
        self.alloc()

    def chk(self):
        self.stage_n += 1
        if self.stage is not None and self.stage_n >= self.stage:
            raise StopIteration

    def dump(self, name, ap, buf, shape, dt=F32):
        o = self.nc.dram_tensor("dbg_" + name, list(shape), dt, kind="ExternalOutput").ap()
        bufs = buf if isinstance(buf, list) else [buf]
        self.dma('sp', o, ap, rd=bufs, chan=bufs[0], final=True)

    def sb(self, name, shape, dt=F32):
        return self.nc.alloc_sbuf_tensor(name, list(shape), dt)

    def B(self, name=None):
        return self.P.buf(name)

    def op(self, eng, fn, rd=(), wr=(), chan=None):
        return self.P.op(eng, fn, rd, wr, chan)

    def rstd(self, out, in_, L, b):
        epsc = self.cst_sb[0:L, C_EPS:C_EPS + 1]
        self.op('act', lambda e: e.activation(out=out, in_=in_, func=AF.Ln, bias=epsc), rd=[b, self.bcst], wr=[b])
        self.op('act', lambda e: e.activation(out=out, in_=out, func=AF.Exp, scale=-0.5), rd=[b], wr=[b])

    def pull(self, n=1):
        for _ in range(n):
            for g in list(self.bg):
                try:
                    next(g)
                except StopIteration:
                    self.bg.remove(g)

    def drain(self, gens=None):
        while True:
            act = [g for g in self.bg if gens is None or g in gens]
            if not act:
                return
            self.pull()

    def dma(self, q, out, in_, rd=(), wr=(), chan=None, final=False):
        if final and chan not in self.final:
            self.final.append(chan)
        return self.op(q, lambda e, o=out, i=in_: e.dma_start(out=o, in_=i), rd=rd, wr=wr, chan=chan)

    def alloc(self):
        nc = self.nc
        sb, B = self.sb, self.B
        self.PS = [nc.alloc_psum_tensor("ps%d" % i, [128, 512], F32) for i in range(7)]
        self.PB = nc.alloc_psum_tensor("psb", [128, 1024], BF16)
        self.bPS = [B("ps%d" % i) for i in range(7)]
        _b = B("psb")
        self.bPB = [_b, _b]
        self.bP6 = {k: self.bPS[6] for k in ('b', 'tm', 'den', 'dn', 'arow', 'brow')}
        self.cst_sb = sb("cst_sb", [128, NCST])
        self.cstb = sb("cstb", [128, NCSTB], BF16)
        self.bcst = B("cst")
        self.bcstb = B("cstb")
        self.NSLOT = 3
        self.W = [sb("w%d" % i, [128, 16, 512], BF16) for i in range(self.NSLOT)]
        self.bW = [B("w%d" % i) for i in range(self.NSLOT)]
        self.wcount = 0
        self.X = sb("X", [128, NTT, D])
        self.bX = [B("X%d" % i) for i in range(NTT)]
        self.xT = sb("xT", [128, 16, NCF], BF16)
        self.bxT = [B("xT%d" % i) for i in range(NTT)]
        self.mT = self.xT
        self.bmT = self.bxT
        self.mtok = sb("mtok", [128, NTT, D], BF16)
        self.bmtok = [[B("mtok%d_%d" % (i, j)) for j in range(4)] for i in range(NTT)]
        self.xb = self.mtok[:, 0, :]
        self.bxb = B("xb")
        self.qT_ = [sb("qT%d" % p, [128, 2, 2, NCF], BF16) for p in range(2)]
        self.kT_ = [sb("kT%d" % p, [128, 2, 2, NCF], BF16) for p in range(2)]
        self.bqT_ = [[[B() for _ in range(2)] for _ in range(2)] for p in range(2)]
        self.bkT_ = [[[B() for _ in range(2)] for _ in range(2)] for p in range(2)]
        self.vtok_ = [sb("vtok%d" % p, [128, NTT, 2, 256], BF16) for p in range(2)]
        self.bv_ = [[B() for _ in range(NTT)] for p in range(2)]
        self.G_ = [sb("G%d" % p, [128, NTT, 2, 256]) for p in range(2)]
        self.bG_ = [[B() for _ in range(NTT)] for p in range(2)]
        self.gtmp = sb("gtmp", [128, 512])
        self.bgtmp = B()
        self.graw = sb("graw", [128, NTT, 8])
        self.ig = sb("ig", [128, NTT, 4])
        self.sp = sb("spl", [128, NTT, 4])
        self.bgate = [B() for _ in range(NTT)]
        self.a_sb = sb("a_sb", [128, 2]); self.ba = B()
        self.arow = sb("arow", [2, 128]); self.barow = B()
        self.Mrow = sb("Mrow", [2, 128]); self.bMrow = B()
        self.mrow = sb("mrow", [2, 128]); self.bmrow = B()
        self.w0row = sb("w0row", [2, 128]); self.bw0row = B()
        self.emrow = sb("emrow", [2, 128]); self.bemrow = B()
        self.wT = sb("wT", [128, 2, 128]); self.bwT = B()
        self.swT = sb("swT", [128, 2, 128], BF16); self.bswT = B()
        self.qs = sb("qs", [128, 2, 2, 128], BF16); self.bqs = B()
        self.g0bc = sb("g0bc", [128, 2]); self.bg0bc = B()
        self.ktok = sb("ktok", [128, 2, 256], BF16); self.bktok = B()
        self.gv = sb("gv", [128, 2, 256], BF16); self.bgv = B()
        self.gb = sb("gb", [128, 2], BF16); self.bgb = B()
        self.sm6 = sb("sm6", [128, 2, 6]); self.bsm6 = B()
        self.mv = sb("mv", [128, 2, 2]); self.bmv = B()
        self.sml = sb("sml", [128, 16]); self.bsml = B()
        self.hn = sb("hn", [128, 2, 256]); self.bhn = B()
        self.C = [sb("C%d" % l, [128, 4, 2, 256]) for l in range(DEPTH)]
        self.n = [sb("n%d" % l, [128, 4, 2]) for l in range(DEPTH)]
        self.Cb = [sb("Cb%d" % l, [128, 4, 2, 256], BF16) for l in range(DEPTH)]
        self.nb = [sb("nb%d" % l, [128, 4, 2], BF16) for l in range(DEPTH)]
        self.m = [[sb("m%d_%d" % (l, hp), [2, 1]) for hp in range(2)] for l in range(DEPTH)]
        self.bC = [[B() for _ in range(2)] for _ in range(DEPTH)]
        self.Cs = sb("Cs", [128, 2, 2, 256]); self.ns = sb("ns", [128, 2, 2])
        self.Csb = sb("Csb", [128, 2, 2, 256], BF16); self.nsb = sb("nsb", [128, 2, 2], BF16)
        self.ms = sb("ms", [2, 1]); self.bCs = B("Cs")
        self.aT = sb("aT", [128, 4, 30 + TB])
        self.baT = B("aT")
        self.aTs = sb("aTs", [128, 4, NS, 31])
        self.baTs = [B() for _ in range(NS)]
        self.bcu = [B() for _ in range(4)]
        self.cuF = sb("cuF", [128, 4, NCF])
        self.sg = sb("sg", [128, NCF]); self.bsg = B()
        self.yT = sb("yT", [128, 4, NCF]); self.byT = B("yT")
        self.ctmp = sb("ctmp", [128, 31]); self.bctmp = B()
        self.sz = sb("sz", [128, NTT, 512], BF16); self.bsz = [B() for _ in range(NTT)]
        self.yn = sb("yn", [128, 512]); self.byn = B()
        self.cst6 = sb("cst6", [128, 6]); self.cmv = sb("cmv", [128, 2]); self.csml = sb("csml", [128, 4]); self.bcsm = B()
        self.cvo = sb("cvo", [30, 512]); self.bcvo = B("cvo")
        self.aqT = sb("aqT", [128, 4, NCF], BF16); self.baq = [B() for _ in range(4)]
        self.kvf = sb("kvf", [128, NTT, 256]); self.bkvf = [B() for _ in range(NTT)]
        self.kvb = sb("kvb", [128, 128], BF16); self.bkvb = B()
        self.akT = [sb("akT%d" % l, [128, 2, 128], BF16) for l in range(DEPTH)]
        self.bakT = [[B(), B()] for l in range(DEPTH)]
        self.vaug = [sb("vaug%d" % l, [128, 2, 2, 65], BF16) for l in range(DEPTH)]
        self.bvaug = [[B(), B()] for l in range(DEPTH)]
        self.hist = [sb("hist%d" % l, [128, 4, 30]) for l in range(DEPTH)]
        self.bhist = [B() for l in range(DEPTH)]
        self.akTs = sb("akTs", [128, 128], BF16); self.vaugs = sb("vaugs", [128, 2, 65], BF16)
        self.ckf = sb("ckf", [128, 128]); self.cvf = sb("cvf", [128, 128]); self.bcache = B("cache")
        self.bakTs = B(); self.bvaugs = B()
        self.saz = sb("saz", [128, NTT, 512], BF16); self.bsaz = [B() for _ in range(NTT)]
        self.PT = sb("PT", [128, 2, 4, 128], BF16); self.bPT = [B(), B()]
        self.asml = sb("asml", [128, 8]); self.basml = B()
        self.ao = sb("ao", [128, 4, 64]); self.bao = B()
        self.bi_bc = sb("bi_bc", [128, DEPTH, 4]); self.bf_bc = sb("bf_bc", [128, DEPTH, 4])
        self.esk = sb("esk", [128, DEPTH, 8])
        self.mng = sb("mng", [128, 1024])
        self.cw = sb("cw", [128, DEPTH, 4, 31]); self.cb = sb("cb", [128, DEPTH, 4])
        self.clg = sb("clg", [128, 512]); self.clb = sb("clb", [128, 512])
        self.lnp = [sb("lnp%d" % i, [128, 2, 512]) for i in range(2)]
        self.bpar = B("par"); self.bmng = B("mng"); self.bcl = B("cl"); self.bln = [B("ln0"), B("ln1")]
        self.lst = sb("lst", [128, 4, 6]); self.lmv = sb("lmv", [128, 2]); self.lsm = sb("lsm", [128, 4]); self.blsm = B()
        self.bout = {k: B(k) for k in ('oCp', 'onp', 'omp', 'okp', 'ovp', 'oms', 'oks', 'ovs')}

    def setup(self):
        op, dma = self.op, self.dma
        dma('sp', self.cst_sb[:], self.cst, wr=[self.bcst], chan=self.bcst)
        dma('pool', self.cstb[:], self.cstB, wr=[self.bcstb], chan=self.bcstb)
        op('dve', lambda e: e.tensor_copy(out=self.cst_sb[:, C_ONE:C_ONE + 1], in_=self.cst_sb[:, C_ONE:C_ONE + 1]),
           rd=[self.bcst, self.bcstb], wr=[self.bcst])
        bp = self.bpar

        def bc(dst, src):
            dma('sp', dst, src.partition_broadcast(128), wr=[bp], chan=bp)
        bc(self.bi_bc[:].rearrange("p l f -> p (l f)"), self.p_bi.rearrange("l f -> (l f)"))
        bc(self.bf_bc[:].rearrange("p l f -> p (l f)"), self.p_bf.rearrange("l f -> (l f)"))
        bc(self.esk[:].rearrange("p l f -> p (l f)"), self.p_sk.rearrange("l f -> (l f)"))
        for l in range(DEPTH):
            dma('sp', self.cw[:, l].rearrange("p a b -> p (a b)"), self.p_cw[l], wr=[bp], chan=bp)
            dma('sp', self.cb[:, l], self.p_cb[l], wr=[bp], chan=bp)
        op('act', lambda e: e.activation(out=self.esk[:], in_=self.esk[:], func=AF.Exp), rd=[bp], wr=[bp])
        for l in range(DEPTH):
            for hp in range(2):
                hs = slice(2 * hp, 2 * hp + 2)
                b = self.bC[l][hp]
                op('dve', lambda e, l=l, hs=hs: e.memset(self.C[l][:, hs], 0.0), wr=[b])
                op('dve', lambda e, l=l, hs=hs: e.memset(self.n[l][:, hs], 0.0), wr=[b])
                op('dve', lambda e, l=l, hs=hs: e.memset(self.Cb[l][:, hs], 0.0), wr=[b])
                op('dve', lambda e, l=l, hs=hs: e.memset(self.nb[l][:, hs], 0.0), wr=[b])
                op('dve', lambda e, l=l, hp=hp: e.memset(self.m[l][hp][:], 0.0), wr=[b])
        for l in range(DEPTH):
            for r in range(2):
                op('dve', lambda e, l=l, r=r: e.memset(self.vaug[l][:, r, :, 64:65], 1.0), wr=[self.bvaug[l][r]])
        op('dve', lambda e: e.memset(self.vaugs[:, :, 64:65], 1.0), wr=[self.bvaugs])

    def load_params(self, l):
        dma = self.dma
        dma('sp', self.mng[:, :], self.p_mng[l].partition_broadcast(128), wr=[self.bmng], chan=self.bmng)
        dma('sp', self.clg[:, :], self.p_clg[l].partition_broadcast(128), wr=[self.bcl], chan=self.bcl)
        dma('sp', self.clb[:, :], self.p_clb[l].partition_broadcast(128), wr=[self.bcl], chan=self.bcl)

    def w_plan(self, seq):
        seen = set()
        for ent in seq:
            l, j = ent[0], ent[1]
            if (l, j) not in seen:
                seen.add((l, j))
                self.dma('pool', self.wb[l, j], self.wt[l, j], wr=[self.bwb[l][j]], chan=self.bwb[l][j])
        self.wseq = seq
        self.w_issued = 0
        self.w_used = 0

    def get_w(self):
        while self.w_issued < len(self.wseq) and self.w_issued < self.w_used + self.NSLOT:
            ent = self.wseq[self.w_issued]
            l, j = ent[0], ent[1]
            s = self.w_issued % self.NSLOT
            if len(ent) > 2 and ent[2] == 'B':
                self.dma('pool', self.W[s][:, 4:8, :], self.wb[l, j].rearrange("p (a b) -> p a b", b=512)[:, 4:8, :],
                         rd=[self.bwb[l][j]], wr=[self.bW[s]], chan=self.bW[s])
            else:
                self.dma('pool', self.W[s][:].rearrange("p a b -> p (a b)"), self.wb[l, j], rd=[self.bwb[l][j]], wr=[self.bW[s]],
                         chan=self.bW[s])
            self.w_issued += 1
        s = self.w_used % self.NSLOT
        self.w_used += 1
        return s

    def units(self, blk):
        if blk < 0:
            s0 = NSB * (blk + NS // NSB)
            return [Unit(1, j, j, sidx=s0 + j) for j in range(NSB)]
        return [Unit(128, tt, tt * 128) for tt in range(NT)]

    def Xap(self, u, cols=slice(0, D)):
        return self.X[0:u.L, u.tt, cols], self.bX[u.tt]

    def load_x(self, blk, us):
        for u in us:
            xa, bx = self.Xap(u)
            if u.sidx is None:
                r0 = blk * TB + u.tt * 128
                self.dma('sp', xa, self.xp[r0:r0 + 128, :], wr=[bx], chan=bx)
            else:
                self.dma('sp', xa, self.xs[u.sidx:u.sidx + 1, :], wr=[bx], chan=bx)

    def transpose_rows(self, src_of, bsrc, dst, bdst, u, k_evac=0, rounds=(0, 1, 2, 3)):
        L, c0 = u.L, u.c0
        idb = self.cstb[:, CB_ID:CB_ID + 128]
        for r in rounds:
            h = r % 2
            pb = self.PB[:, h * 512:(h + 1) * 512]

            def fn(e, r=r, pb=pb):
                ins = None
                for j in range(4):
                    ins = e.transpose(pb[:, j * 128:j * 128 + L], src_of(4 * r + j), idb[0:L, 0:L])
                return ins
            bs_r = bsrc(r) if callable(bsrc) else bsrc
            self.op('pe', fn, rd=[bs_r, self.bcst] if not isinstance(bs_r, list) else bs_r + [self.bcst], wr=[self.bPB[h]])
            src = pb.rearrange("p (a b) -> p a b", b=128)[:, :, 0:L]
            out = dst[:, 4 * r:4 * r + 4, c0:c0 + L]
            if (r + k_evac) % 2 == 0:
                self.op('act', lambda e, o=out, s=src: e.activation(out=o, in_=s, func=AF.Copy), rd=[self.bPB[h]], wr=[bdst])
            else:
                self.op('dve', lambda e, o=out, s=src: e.tensor_copy(out=o, in_=s), rd=[self.bPB[h]], wr=[bdst])

    def make_xT(self, u):
        L = u.L
        xa, bx = self.Xap(u)
        bxb = list(self.bmtok[0])
        self.op('act', lambda e: e.activation(out=self.xb[0:L, :], in_=xa, func=AF.Copy), rd=[bx], wr=bxb)
        self.transpose_rows(lambda kc: self.xb[0:L, kc * 128:(kc + 1) * 128], bxb, self.xT, self.bxT[u.tt], u)

    def inproj_tile(self, l, kind, i, us):
        s = self.get_w()
        W, bW = self.W[s], self.bW[s]
        op = self.op
        pp_ = (i // 2) if kind == 'qk' else (i if kind in ('v', 'o', 'z') else 0)
        self.qT, self.kT, self.bqT, self.bkT = self.qT_[pp_], self.kT_[pp_], self.bqT_[pp_], self.bkT_[pp_]
        self.vtok, self.bv, self.G, self.bG = self.vtok_[pp_], self.bv_[pp_], self.G_[pp_], self.bG_[pp_]
        hs_samp = us[0].sidx is not None
        ncols = NSB if hs_samp else TB
        assert ncols <= 512
        bxT_all = [self.bxT[u.tt] for u in us]
        one = self.cst_sb[:, C_ONE:C_ONE + 1]
        if kind in ('qk', 'cu', 'cg', 'aq'):
            for ec in range(4):
                self.pull()
                k = self.pcount = getattr(self, 'pcount', 0) + 1
                ps, bps = self.PS[k % 2], self.bPS[k % 2]

                def fn(e, ec=ec, ps=ps):
                    ins = None
                    for kc in range(16):
                        ins = e.matmul(ps[:, 0:ncols], lhsT=W[:, kc, ec * 128:(ec + 1) * 128], rhs=self.xT[:, kc, 0:ncols],
                                       start=(kc == 0), stop=(kc == 15))
                    return ins
                op('pe', fn, rd=[bW] + bxT_all, wr=[bps])
                src = ps[:, 0:ncols]
                if kind == 'qk':
                    hl = i % 2
                    if ec < 2:
                        op('act', lambda e, s_=src, o=self.qT[:, hl, ec, 0:ncols]: e.activation(out=o, in_=s_, func=AF.Copy),
                           rd=[bps], wr=[self.bqT[hl][ec]])
                    else:
                        op('act', lambda e, s_=src, o=self.kT[:, hl, ec - 2, 0:ncols]: e.activation(out=o, in_=s_, func=AF.Copy, scale=1.0 / 16.0),
                           rd=[bps], wr=[self.bkT[hl][ec - 2]])
                elif kind == 'cu':
                    op('act', lambda e, s_=src, o=self.cuF[:, ec, 0:ncols]: e.activation(out=o, in_=s_, func=AF.Copy),
                       rd=[bps], wr=[self.bcu[ec]])
                elif kind == 'cg':
                    op('act', lambda e, s_=src: e.activation(out=self.sg[:, 0:ncols], in_=s_, func=AF.Sigmoid),
                       rd=[bps], wr=[self.bsg])
                    if not hs_samp:
                        op('dve', lambda e, ec=ec: e.tensor_tensor(out=self.aT[:, ec, 30:30 + TB], in0=self.cuF[:, ec, 0:TB],
                                                                   in1=self.sg[:, 0:TB], op=ALU.mult),
                           rd=[self.bsg, self.bcu[ec]], wr=[self.baT])
                    else:
                        s0 = us[0].sidx
                        op('dve', lambda e, ec=ec, s0=s0: e.tensor_tensor(out=self.aTs[:, ec, s0:s0 + NSB, 30], in0=self.cuF[:, ec, 0:NSB],
                                                                          in1=self.sg[:, 0:NSB], op=ALU.mult),
                           rd=[self.bsg, self.bcu[ec]], wr=[self.baTs[u_.sidx] for u_ in us])
                elif kind == 'aq':
                    op('act', lambda e, s_=src, o=self.aqT[:, ec, 0:ncols]: e.activation(out=o, in_=s_, func=AF.Copy, scale=0.125),
                       rd=[bps], wr=[self.baq[ec]])
            return
        for u in us:
            L, tt, c0 = u.L, u.tt, u.c0
            self.pull()
            k = self.pcount = getattr(self, 'pcount', 0) + 1
            ps, bps = self.PS[k % 2], self.bPS[k % 2]

            def fn(e, ps=ps, L=L, c0=c0):
                ins = None
                for kc in range(16):
                    ins = e.matmul(ps[0:L, :], lhsT=self.xT[:, kc, c0:c0 + L], rhs=W[:, kc, :], start=(kc == 0), stop=(kc == 15))
                return ins
            op('pe', fn, rd=[bW, self.bxT[tt]], wr=[bps])
            src = ps[0:L, :]
            if kind == 'v':
                op('act', lambda e, s_=src, o=self.vtok[0:L, tt].rearrange("p a b -> p (a b)"): e.activation(out=o, in_=s_, func=AF.Copy),
                   rd=[bps], wr=[self.bv[tt]])
            elif kind == 'o':
                g = self.G[0:L, tt].rearrange("p a b -> p (a b)")
                op('act', lambda e, s_=src, g=g: e.activation(out=g, in_=s_, func=AF.Sigmoid), rd=[bps], wr=[self.bG[tt]])
                op('dve', lambda e, g=g, L=L: e.tensor_tensor(out=g, in0=g, in1=self.mng[0:L, i * 512:(i + 1) * 512], op=ALU.mult),
                   rd=[self.bG[tt], self.bmng], wr=[self.bG[tt]])
            elif kind == 'z':
                g = self.G[0:L, tt].rearrange("p a b -> p (a b)")
                op('act', lambda e, s_=src, L=L: e.activation(out=self.gtmp[0:L, :], in_=s_, func=AF.Silu), rd=[bps], wr=[self.bgtmp])
                op('dve', lambda e, g=g, L=L: e.tensor_tensor(out=g, in0=g, in1=self.gtmp[0:L, :], op=ALU.mult),
                   rd=[self.bG[tt], self.bgtmp], wr=[self.bG[tt]])
            elif kind == 'cz':
                op('act', lambda e, s_=src, o=self.sz[0:L, tt, :]: e.activation(out=o, in_=s_, func=AF.Silu), rd=[bps], wr=[self.bsz[tt]])
            elif kind == 'az':
                op('act', lambda e, s_=src, o=self.saz[0:L, tt, :]: e.activation(out=o, in_=s_, func=AF.Silu), rd=[bps], wr=[self.bsaz[tt]])
            elif kind == 'kvg':
                bg = self.bgate[tt]
                op('act', lambda e, ps=ps, o=self.kvf[0:L, tt, :], L=L: e.activation(out=o, in_=ps[0:L, 0:256], func=AF.Copy),
                   rd=[bps], wr=[self.bkvf[tt]])
                op('act', lambda e, ps=ps, L=L, tt=tt: e.activation(out=self.graw[0:L, tt, :], in_=ps[0:L, 256:264], func=AF.Copy),
                   rd=[bps], wr=[bg])
                op('dve', lambda e, L=L, tt=tt: e.tensor_tensor(out=self.ig[0:L, tt, :], in0=self.graw[0:L, tt, 0:4],
                                                               in1=self.bi_bc[0:L, l, :], op=ALU.add),
                   rd=[bg, self.bpar], wr=[bg])
                op('dve', lambda e, L=L, tt=tt: e.tensor_tensor(out=self.sp[0:L, tt, :], in0=self.graw[0:L, tt, 4:8],
                                                               in1=self.bf_bc[0:L, l, :], op=ALU.add),
                   rd=[bg, self.bpar], wr=[bg])
                op('act', lambda e, L=L, tt=tt: e.activation(out=self.sp[0:L, tt, :], in_=self.sp[0:L, tt, :], func=AF.Exp, scale=-1.0),
                   rd=[bg], wr=[bg])
                op('act', lambda e, L=L, tt=tt: e.activation(out=self.sp[0:L, tt, :], in_=self.sp[0:L, tt, :], func=AF.Ln,
                                                             bias=one[0:L, :]), rd=[bg, self.bcst], wr=[bg])

    def mlstm_unit(self, l, hp, u, last_blk):
        op, dma = self.op, self.dma
        qT_l, kT_l, vtok_l, G_l = self.qT_[hp], self.kT_[hp], self.vtok_[hp], self.G_[hp]
        bqT_l, bkT_l, bv_l, bG_l = self.bqT_[hp], self.bkT_[hp], self.bv_[hp], self.bG_[hp]
        L, tt, c0 = u.L, u.tt, u.c0
        cols = slice(c0, c0 + L)
        hs = slice(2 * hp, 2 * hp + 2)
        idf = self.cst_sb[:, C_ID:C_ID + 128]
        tri = self.cst_sb[:, C_TRI:C_TRI + 128]
        idb = self.cstb[:, CB_ID:CB_ID + 128]
        bigb = self.cstb[:, CB_BIGM:CB_BIGM + 128]
        onesb = self.cstb[:, CB_ONE:CB_ONE + 1]
        sel = lambda hl: self.cst_sb[0:2, C_SEL + 128 * hl:C_SEL + 128 * (hl + 1)]
        bcst = self.bcst
        P2, P3, P4, P5, P6 = self.PS[2], self.PS[3], self.PS[4], self.PS[5], self.PS[6]
        b2, b3, b4, b5 = self.bPS[2], self.bPS[3], self.bPS[4], self.bPS[5]
        b6 = self.bP6
        samp = u.sidx is not None
        if not samp:
            C, n, Cb, nb, m, bC = self.C[l][:, hs], self.n[l][:, hs], self.Cb[l][:, hs], self.nb[l][:, hs], self.m[l][hp], self.bC[l][hp]
        else:
            si = u.sidx
            C, n, Cb, nb, m, bC = self.Cs[:], self.ns[:], self.Csb[:], self.nsb[:], self.ms, self.bCs
            dma('sp', C, self.sC[l, si, hs].rearrange("h (dc p) e -> p h dc e", p=128), wr=[bC], chan=bC)
            self.op('sp', lambda e: e.dma_start(out=n, in_=self.sn[l, si, hs].rearrange("h (dc p) -> p h dc", p=128),
                                                allow_slow_non_contiguous=True), wr=[bC], chan=bC)
            dma('sp', m[:], self.sm[l, si, hs].rearrange("(h o) -> h o", o=1), wr=[bC], chan=bC)
            op('act', lambda e: e.activation(out=Cb, in_=C, func=AF.Copy), rd=[bC], wr=[bC])
            op('act', lambda e: e.activation(out=nb, in_=n, func=AF.Copy), rd=[bC], wr=[bC])
        bg = self.bgate[tt]
        sp_ = self.sp[0:L, tt, hs]
        op('pe', lambda e: e.matmul(P6[0:L, 0:2], lhsT=tri[0:L, 0:L], rhs=sp_, start=True, stop=True), rd=[bg, bcst], wr=[b6['b']])
        yield
        op('dve', lambda e: e.tensor_tensor(out=self.a_sb[0:L, :], in0=self.ig[0:L, tt, hs], in1=P6[0:L, 0:2], op=ALU.add),
           rd=[bg, b6['b']], wr=[self.ba])
        op('pe', lambda e: e.matmul(P6[0:2, 16:16 + L], lhsT=self.a_sb[0:L, :], rhs=idf[0:L, 0:L], start=True, stop=True),
           rd=[self.ba, bcst], wr=[b6['arow']])
        op('pe', lambda e: e.matmul(P6[0:2, 144:144 + L], lhsT=sp_, rhs=tri[0:L, 0:L], start=True, stop=True),
           rd=[bg, bcst], wr=[b6['brow']])
        op('dve', lambda e: e.tensor_copy(out=self.arow[:, 0:L], in_=P6[0:2, 16:16 + L]), rd=[b6['arow']], wr=[self.barow])
        yield
        op('dve', lambda e: e.tensor_tensor_scan(out=self.Mrow[:, 0:L], data0=self.arow[:, 0:L], data1=self.arow[:, 0:L],
                                                 initial=m[:], op0=ALU.max, op1=ALU.max), rd=[self.barow, bC], wr=[self.bMrow])
        op('dve', lambda e: e.tensor_tensor(out=self.mrow[:, 0:L], in0=self.Mrow[:, 0:L], in1=P6[0:2, 144:144 + L], op=ALU.subtract),
           rd=[self.bMrow, b6['brow']], wr=[self.bmrow])
        yield
        op('act', lambda e: e.activation(out=self.w0row[:, 0:L], in_=self.Mrow[:, 0:L], func=AF.Exp, scale=-1.0, bias=m[:]),
           rd=[self.bMrow, bC], wr=[self.bw0row])
        op('act', lambda e: e.activation(out=self.emrow[:, 0:L], in_=self.mrow[:, 0:L], func=AF.Exp, scale=-1.0),
           rd=[self.bmrow], wr=[self.bemrow])
        op('dve', lambda e: e.tensor_copy(out=m[:], in_=self.mrow[:, L - 1:L]), rd=[self.bmrow], wr=[bC])
        yield
        op('pe', lambda e: e.matmul(P6[0:L, 2:4], lhsT=self.emrow[0:2, 0:L], rhs=idf[0:2, 0:2], start=True, stop=True),
           rd=[self.bemrow, bcst], wr=[b6['tm']])

        def fn_bc(e):
            ins = None
            for hl in range(2):
                e.matmul(P3[0:L, hl * 128:hl * 128 + L], lhsT=sel(hl)[:, 0:L], rhs=self.Mrow[0:2, 0:L], start=True, stop=False)
                e.matmul(P3[0:L, hl * 128:hl * 128 + L], lhsT=idb[0:L, 0:L], rhs=bigb[0:L, 0:L], start=False, stop=True)
                ins = e.matmul(P3[:, 256 + hl * 128:256 + hl * 128 + L], lhsT=sel(hl), rhs=self.w0row[0:2, 0:L], start=True, stop=True)
            return ins
        op('pe', fn_bc, rd=[self.bMrow, self.bw0row, bcst], wr=[b3])
        yield
        for hl in range(2):
            op('act', lambda e, hl=hl: e.activation(out=self.wT[0:L, hl, 0:L], in_=P3[0:L, hl * 128:hl * 128 + L], func=AF.Exp,
                                                    scale=-1.0, bias=self.a_sb[0:L, hl:hl + 1]), rd=[b3, self.ba], wr=[self.bwT])

        yield
        def fn_s(e):
            ins = None
            for hl in range(2):
                for dc in range(2):
                    ins = e.matmul(P4[0:L, hl * 128:hl * 128 + L], lhsT=kT_l[:, hl, dc, cols], rhs=qT_l[:, hl, dc, cols],
                                   start=(dc == 0), stop=(dc == 1))
            return ins
        op('pe', fn_s, rd=bkT_l[0] + bkT_l[1] + bqT_l[0] + bqT_l[1], wr=[b4])
        for hl in range(2):
            op('dve', lambda e, hl=hl: e.tensor_tensor(out=self.swT[0:L, hl, 0:L], in0=P4[0:L, hl * 128:hl * 128 + L],
                                                       in1=self.wT[0:L, hl, 0:L], op=ALU.mult), rd=[b4, self.bwT], wr=[self.bswT])
        yield
        for hl in range(2):
            for dc in range(2):
                op('dve', lambda e, hl=hl, dc=dc: e.tensor_tensor(out=self.qs[:, hl, dc, 0:L], in0=qT_l[:, hl, dc, cols],
                                                                  in1=P3[:, 256 + hl * 128:256 + hl * 128 + L], op=ALU.mult),
                   rd=[b3, bqT_l[hl][dc]], wr=[self.bqs])
        op('dve', lambda e: e.tensor_copy(out=self.g0bc[:, :], in_=P3[:, 256:512].rearrange("p (a b) -> p a b", b=128)[:, :, L - 1]),
           rd=[b3], wr=[self.bg0bc])

        yield
        def fn_kt(e):
            ins = None
            for hl in range(2):
                for dc in range(2):
                    j = hl * 2 + dc
                    ins = e.transpose(self.PB[0:L, j * 128:(j + 1) * 128], kT_l[:, hl, dc, cols], idb)
            return ins
        op('pe', fn_kt, rd=bkT_l[0] + bkT_l[1] + [bcst], wr=[self.bPB[0]])
        op('act', lambda e: e.activation(out=self.ktok[0:L].rearrange("p a b -> p (a b)"), in_=self.PB[0:L, 0:512], func=AF.Copy),
           rd=[self.bPB[0]], wr=[self.bktok])

        yield
        def fn_num(e):
            ins = None
            for hl in range(2):
                e.matmul(P4[0:L, hl * 256:(hl + 1) * 256], lhsT=self.swT[0:L, hl, 0:L], rhs=vtok_l[0:L, tt, hl, :], start=True, stop=False)
                for dc in range(2):
                    e.matmul(P4[0:L, hl * 256:(hl + 1) * 256], lhsT=self.qs[:, hl, dc, 0:L], rhs=Cb[:, hl, dc, :],
                             start=False, stop=(dc == 1))
                e.matmul(P6[0:L, 4 + hl:5 + hl], lhsT=self.swT[0:L, hl, 0:L], rhs=onesb[0:L, :], start=True, stop=False)
                for dc in range(2):
                    ins = e.matmul(P6[0:L, 4 + hl:5 + hl], lhsT=self.qs[:, hl, dc, 0:L], rhs=nb[:, hl, dc:dc + 1],
                                   start=False, stop=(dc == 1))
            return ins
        op('pe', fn_num, rd=[self.bswT, self.bqs, bv_l[tt], bC, bcst], wr=[b4, b6['den']])
        yield
        s = self.sml
        bs = self.bsml
        op('act', lambda e: e.activation(out=s[0:L, 0:2], in_=P6[0:L, 4:6], func=AF.Abs), rd=[b6['den']], wr=[bs])
        op('dve', lambda e: e.tensor_tensor(out=s[0:L, 0:2], in0=s[0:L, 0:2], in1=P6[0:L, 2:4], op=ALU.max), rd=[bs, b6['tm']], wr=[bs])
        op('dve', lambda e: e.reciprocal(out=s[0:L, 2:4], in_=s[0:L, 0:2]), rd=[bs], wr=[bs])
        for hl in range(2):
            op('dve', lambda e, hl=hl: e.bn_stats(out=self.sm6[0:L, hl, :], in_=P4[0:L, hl * 256:(hl + 1) * 256]), rd=[b4], wr=[self.bsm6])
            op('dve', lambda e, hl=hl: e.bn_aggr(out=self.mv[0:L, hl, :], in_=self.sm6[0:L, hl, :]), rd=[self.bsm6], wr=[self.bmv])
        op('dve', lambda e: e.tensor_tensor(out=s[0:L, 4:6], in0=s[0:L, 2:4], in1=s[0:L, 2:4], op=ALU.mult), rd=[bs], wr=[bs])
        op('dve', lambda e: e.tensor_tensor(out=s[0:L, 4:6], in0=s[0:L, 4:6], in1=self.mv[0:L, :, 1], op=ALU.mult),
           rd=[bs, self.bmv], wr=[bs])
        self.rstd(s[0:L, 4:6], s[0:L, 4:6], L, bs)
        op('dve', lambda e: e.tensor_tensor(out=s[0:L, 6:8], in0=s[0:L, 4:6], in1=s[0:L, 2:4], op=ALU.mult), rd=[bs], wr=[bs])
        op('dve', lambda e: e.scalar_tensor_tensor(out=s[0:L, 8:10], in0=self.mv[0:L, :, 0], scalar=-1.0, in1=s[0:L, 6:8],
                                                   op0=ALU.mult, op1=ALU.mult), rd=[bs, self.bmv], wr=[bs])
        yield
        for hl in range(2):
            op('act', lambda e, hl=hl: e.activation(out=self.hn[0:L, hl, :], in_=P4[0:L, hl * 256:(hl + 1) * 256], func=AF.Identity,
                                                    scale=s[0:L, 6 + hl:7 + hl], bias=s[0:L, 8 + hl:9 + hl]), rd=[b4, bs], wr=[self.bhn])
        op('dve', lambda e: e.tensor_tensor(out=self.mtok[0:L, tt, hp * 512:(hp + 1) * 512], in0=self.hn[0:L].rearrange("p a b -> p (a b)"),
                                            in1=G_l[0:L, tt].rearrange("p a b -> p (a b)"), op=ALU.mult),
           rd=[self.bhn, bG_l[tt]], wr=[self.bmtok[tt][hp]])
        yield
        op('act', lambda e: e.activation(out=self.gb[0:L, :], in_=self.wT[0:L, :, L - 1], func=AF.Copy), rd=[self.bwT], wr=[self.bgb])
        for hl in range(2):
            op('dve', lambda e, hl=hl: e.tensor_scalar(out=self.gv[0:L, hl, :], in0=vtok_l[0:L, tt, hl, :],
                                                       scalar1=self.wT[0:L, hl, L - 1:L], scalar2=None, op0=ALU.mult),
               rd=[self.bwT, bv_l[tt]], wr=[self.bgv])
        for hl in range(2):
            yield

            def fn_dc(e, hl=hl):
                ins = None
                for dc in range(2):
                    e.matmul(P3[:, dc * 256:(dc + 1) * 256], lhsT=self.ktok[0:L, hl, dc * 128:(dc + 1) * 128], rhs=self.gv[0:L, hl, :],
                             start=True, stop=True)
                    ins = e.matmul(P6[:, 6 + 2 * hl + dc:7 + 2 * hl + dc], lhsT=self.ktok[0:L, hl, dc * 128:(dc + 1) * 128],
                                   rhs=self.gb[0:L, hl:hl + 1], start=True, stop=True)
                return ins
            op('pe', fn_dc, rd=[self.bktok, self.bgv, self.bgb], wr=[b3, b6['dn']])
            for dc in range(2):
                op('dve', lambda e, hl=hl, dc=dc: e.scalar_tensor_tensor(out=C[:, hl, dc, :], in0=C[:, hl, dc, :],
                                                                         scalar=self.g0bc[:, hl:hl + 1], in1=P3[:, dc * 256:(dc + 1) * 256],
                                                                         op0=ALU.mult, op1=ALU.add), rd=[b3, self.bg0bc, bC], wr=[bC])
            op('dve', lambda e, hl=hl: e.scalar_tensor_tensor(out=n[:, hl, :], in0=n[:, hl, :], scalar=self.g0bc[:, hl:hl + 1],
                                                              in1=P6[:, 6 + 2 * hl:8 + 2 * hl], op0=ALU.mult, op1=ALU.add),
               rd=[b6['dn'], self.bg0bc, bC], wr=[bC])
        op('act', lambda e: e.activation(out=Cb, in_=C, func=AF.Copy), rd=[bC], wr=[bC])
        op('act', lambda e: e.activation(out=nb, in_=n, func=AF.Copy), rd=[bC], wr=[bC])
        yield
        if samp:
            si = u.sidx
            dma('sp', self.oCs[l, si, hs].rearrange("h (dc p) e -> p h dc e", p=128), C, rd=[bC], chan=bC, final=True)
            self.final.append(bC) if bC not in self.final else None
            self.op('sp', lambda e: e.dma_start(out=self.ons[l, si, hs].rearrange("h (dc p) -> p h dc", p=128), in_=n,
                                                allow_slow_non_contiguous=True), rd=[bC], chan=bC)
            dma('sp', self.oms[l, si, hs].rearrange("(h o) -> h o", o=1), m[:], rd=[bC], chan=bC)
        elif last_blk and tt == NT - 1:
            dma('sp', self.oCp[l, hs].rearrange("h (dc p) e -> p h dc e", p=128), C, rd=[bC], chan=bC, final=True)
            self.op('sp', lambda e: e.dma_start(out=self.onp[l, hs].rearrange("h (dc p) -> p h dc", p=128), in_=n,
                                                allow_slow_non_contiguous=True), rd=[bC], chan=bC)
            dma('sp', self.omp[l, hs].rearrange("(h o) -> h o", o=1), m[:], rd=[bC], chan=bC)

    def conv_block(self, l, us):
        op = self.op
        cw, cb = self.cw, self.cb
        if us[0].sidx is None:
          for cc in range(4):
            op('dve', lambda e, cc=cc: e.tensor_scalar(out=self.yT[:, cc, 0:TB], in0=self.aT[:, cc, 0:TB], scalar1=cw[:, l, cc, 0:1],
                                                       scalar2=cb[:, l, cc:cc + 1], op0=ALU.mult, op1=ALU.add),
               rd=[self.baT, self.bpar], wr=[self.byT])
        for j in range(1, 31 if us[0].sidx is None else 1):
            yield
            for cc in range(4):
                op('dve', lambda e, cc=cc, j=j: e.scalar_tensor_tensor(out=self.yT[:, cc, 0:TB], in0=self.aT[:, cc, j:j + TB],
                                                                       scalar=cw[:, l, cc, j:j + 1], in1=self.yT[:, cc, 0:TB],
                                                                       op0=ALU.mult, op1=ALU.add),
                   rd=[self.baT, self.byT], wr=[self.byT])
        for u in us:
            if u.sidx is None:
                continue
            si = u.sidx
            self.dma('sp', self.aTs[:, :, si, 0:30], self.scv[l, si].rearrange("(cc c) j -> c cc j", c=128), wr=[self.baTs[si]],
                     chan=self.baTs[si])
        for u in us:
            if u.sidx is None:
                continue
            si = u.sidx
            c0_ = u.c0
            yield
            for cc in range(4):
                op('dve', lambda e, cc=cc, si=si: e.tensor_tensor(out=self.ctmp[:, :], in0=self.aTs[:, cc, si, :], in1=cw[:, l, cc, :],
                                                                  op=ALU.mult), rd=[self.baTs[si], self.bpar], wr=[self.bctmp])
                op('dve', lambda e, cc=cc, si=si, c0_=c0_: e.tensor_reduce(out=self.yT[:, cc, c0_:c0_ + 1], in_=self.ctmp[:, :],
                                                                  axis=AX.X, op=ALU.add), rd=[self.bctmp], wr=[self.byT])
                op('dve', lambda e, cc=cc, si=si, c0_=c0_: e.tensor_tensor(out=self.yT[:, cc, c0_:c0_ + 1],
                                                                  in0=self.yT[:, cc, c0_:c0_ + 1], in1=cb[:, l, cc:cc + 1],
                                                                  op=ALU.add), rd=[self.byT, self.bpar], wr=[self.byT])

    def conv_unit(self, l, u):
        op = self.op
        L, tt, c0 = u.L, u.tt, u.c0
        P5, b5 = self.PS[5], self.bPS[5]
        idf = self.cst_sb[:, C_ID:C_ID + 128]

        def fn(e):
            ins = None
            for cc in range(4):
                ins = e.transpose(P5[0:L, cc * 128:(cc + 1) * 128], self.yT[:, cc, c0:c0 + L], idf)
            return ins
        op('pe', fn, rd=[self.byT, self.bcst], wr=[b5])
        yield
        bs = self.bcsm
        op('dve', lambda e: e.bn_stats(out=self.cst6[0:L, :], in_=P5[0:L, :]), rd=[b5], wr=[bs])
        op('dve', lambda e: e.bn_aggr(out=self.cmv[0:L, :], in_=self.cst6[0:L, :]), rd=[bs], wr=[bs])
        self.rstd(self.csml[0:L, 0:1], self.cmv[0:L, 1:2], L, bs)
        op('dve', lambda e: e.scalar_tensor_tensor(out=self.csml[0:L, 1:2], in0=self.cmv[0:L, 0:1], scalar=-1.0, in1=self.csml[0:L, 0:1],
                                                   op0=ALU.mult, op1=ALU.mult), rd=[bs], wr=[bs])
        yield
        op('act', lambda e: e.activation(out=self.yn[0:L, :], in_=P5[0:L, :], func=AF.Identity, scale=self.csml[0:L, 0:1],
                                         bias=self.csml[0:L, 1:2]), rd=[b5, bs], wr=[self.byn])
        op('dve', lambda e: e.tensor_tensor(out=self.yn[0:L, :], in0=self.yn[0:L, :], in1=self.clg[0:L, :], op=ALU.mult),
           rd=[self.byn, self.bcl], wr=[self.byn])
        op('dve', lambda e: e.tensor_tensor(out=self.yn[0:L, :], in0=self.yn[0:L, :], in1=self.clb[0:L, :], op=ALU.add),
           rd=[self.byn, self.bcl], wr=[self.byn])
        op('act', lambda e: e.activation(out=self.yn[0:L, :], in_=self.yn[0:L, :], func=AF.Silu), rd=[self.byn], wr=[self.byn])
        op('dve', lambda e: e.tensor_tensor(out=self.mtok[0:L, tt, 1024:1536], in0=self.yn[0:L, :], in1=self.sz[0:L, tt, :], op=ALU.mult),
           rd=[self.byn, self.bsz[tt]], wr=[self.bmtok[tt][2]])

    def conv_state_out(self, l, src_of, bsrc, dst):
        P5, b5 = self.PS[5], self.bPS[5]
        idf = self.cst_sb[:, C_ID:C_ID + 128]

        def fn(e):
            ins = None
            for cc in range(4):
                ins = e.transpose(P5[0:30, cc * 128:(cc + 1) * 128], src_of(cc), idf)
            return ins
        self.op('pe', fn, rd=[bsrc, self.bcst], wr=[b5])
        self.op('act', lambda e: e.activation(out=self.cvo[:, :], in_=P5[0:30, :], func=AF.Copy), rd=[b5], wr=[self.bcvo])
        self.dma('sp', dst, self.cvo[:, :], rd=[self.bcvo], chan=self.bcvo, final=True)

    def conv_finish(self, l, us, last_blk):
        for u in us:
            if u.sidx is not None:
                si = u.sidx
                self.conv_state_out(l, lambda cc, si=si: self.aTs[:, cc, si, 1:31], self.baTs[si], self.ocvs[l, si])
        if us[0].sidx is not None:
            return
        if last_blk:
            self.conv_state_out(l, lambda cc: self.aT[:, cc, TB:TB + 30], self.baT, self.ocvp[l])
        else:
            self.op('act', lambda e: e.activation(out=self.hist[l][:, :, :], in_=self.aT[:, :, TB:TB + 30], func=AF.Copy),
                    rd=[self.baT], wr=[self.bhist[l]])

    def conv_start(self, l, blk):
        if blk < 0:
            return
        if blk == 0:
            self.op('dve', lambda e: e.memset(self.aT[:, :, 0:30], 0.0), wr=[self.baT])
        else:
            self.op('act', lambda e: e.activation(out=self.aT[:, :, 0:30], in_=self.hist[l][:, :, :], func=AF.Copy),
                    rd=[self.bhist[l]], wr=[self.baT])

    def attn_unit(self, l, u, ci, last_blk):
        op, dma = self.op, self.dma
        L, tt, c0 = u.L, u.tt, u.c0
        cols = slice(c0, c0 + L)
        idb = self.cstb[:, CB_ID:CB_ID + 128]
        idf = self.cst_sb[:, C_ID:C_ID + 128]
        nmp = self.cstb[:, CB_NMP:CB_NMP + 512].rearrange("p (a b) -> p a b", b=128)
        nmc = self.cstb[:, CB_NMC:CB_NMC + 512].rearrange("p (a b) -> p a b", b=128)
        P2, P3, P4, P5 = self.PS[2], self.PS[3], self.PS[4], self.PS[5]
        b2, b3, b4, b5 = self.bPS[2], self.bPS[3], self.bPS[4], self.bPS[5]
        samp = u.sidx is not None
        if samp:
            cur_kT, bcur_kT, cur_v, bcur_v = self.akT[l][:, 0, :], self.bakT[l][0], self.vaug[l][:, 0], self.bvaug[l][0]
        else:
            sl = ci % 2
            cur_kT, bcur_kT, cur_v, bcur_v = self.akT[l][:, sl, :], self.bakT[l][sl], self.vaug[l][:, sl], self.bvaug[l][sl]
        op('act', lambda e: e.activation(out=self.kvb[0:L, :], in_=self.kvf[0:L, tt, 0:128], func=AF.Copy), rd=[self.bkvf[tt]], wr=[self.bkvb])
        op('pe', lambda e: e.transpose(self.PB[:, 512:512 + L], self.kvb[0:L, :], idb[0:L, 0:L]), rd=[self.bkvb, self.bcst], wr=[self.bPB[1]])
        op('dve', lambda e: e.tensor_copy(out=cur_kT[:, 0:L], in_=self.PB[:, 512:512 + L]), rd=[self.bPB[1]], wr=[bcur_kT])
        op('dve', lambda e: e.tensor_copy(out=cur_v[0:L, :, 0:64], in_=self.kvf[0:L, tt, 128:256].rearrange("p (a b) -> p a b", b=64)),
           rd=[self.bkvf[tt]], wr=[bcur_v])
        yield
        blocks = []
        if samp:
            si = u.sidx
            bc = self.bcache
            dma('sp', self.ckf[:, :], self.ck[l, si], wr=[bc], chan=bc)
            dma('sp', self.cvf[:, :], self.cv[l, si], wr=[bc], chan=bc)
            op('act', lambda e: e.activation(out=self.kvb[:, :], in_=self.ckf[:, :], func=AF.Copy), rd=[bc], wr=[self.bkvb])
            op('pe', lambda e: e.transpose(self.PB[:, 512:640], self.kvb[:, :], idb), rd=[self.bkvb, self.bcst], wr=[self.bPB[1]])
            op('dve', lambda e: e.tensor_copy(out=self.akTs[:, :], in_=self.PB[:, 512:640]), rd=[self.bPB[1]], wr=[self.bakTs])
            op('dve', lambda e: e.tensor_copy(out=self.vaugs[:, :, 0:64], in_=self.cvf[:, :].rearrange("p (a b) -> p a b", b=64)),
               rd=[bc], wr=[self.bvaugs])
            blocks.append((self.akTs[:, :], self.vaugs[:, :, :], 128, nmp, [self.bakTs, self.bvaugs]))
            blocks.append((cur_kT, cur_v, 1, None, [bcur_kT, bcur_v]))
            bo = self.bout['oks']
            dma('sp', self.oks[l, si, 0:127, :], self.ck[l, si, 1:128, :], wr=[bo], chan=bo, final=True)
            dma('sp', self.ovs[l, si, 0:127, :], self.cv[l, si, 1:128, :], wr=[bo], chan=bo, final=True)
            dma('sp', self.oks[l, si, 127:128, :], self.kvf[0:1, tt, 0:128], rd=[self.bkvf[tt]], chan=self.bkvf[tt], final=True)
            dma('sp', self.ovs[l, si, 127:128, :], self.kvf[0:1, tt, 128:256], rd=[self.bkvf[tt]], chan=self.bkvf[tt], final=True)
        else:
            if ci > 0:
                ps_ = 1 - sl
                blocks.append((self.akT[l][:, ps_, :], self.vaug[l][:, ps_], 128, nmp, [self.bakT[l][ps_], self.bvaug[l][ps_]]))
            blocks.append((cur_kT, cur_v, 128, nmc, [bcur_kT, bcur_v]))
            if last_blk and tt == NT - 1:
                dma('sp', self.okp[l], self.kvf[:, tt, 0:128], rd=[self.bkvf[tt]], chan=self.bkvf[tt], final=True)
                dma('sp', self.ovp[l], self.kvf[:, tt, 128:256], rd=[self.bkvf[tt]], chan=self.bkvf[tt], final=True)
        for g in range(2):
            gp = slice(64 * g, 64 * g + 64)
            yield
            for bi, (kTa, va, Lk, mask, bufs) in enumerate(blocks):
                PSs, bPSs = P2, b2

                def fn(e, kTa=kTa, Lk=Lk, mask=mask, PSs=PSs, gp=gp):
                    out = PSs[0:Lk, :].rearrange("p (a b) -> p a b", b=128)[:, :, 0:L]
                    ins = e.matmul(out, lhsT=kTa[gp, 0:Lk], rhs=self.aqT[gp, :, cols], start=True, stop=(mask is None))
                    if mask is not None:
                        ins = e.matmul(out, lhsT=idb[0:Lk, 0:Lk], rhs=mask[0:Lk, :, 0:L], start=False, stop=True)
                    return ins
                op('pe', fn, rd=self.baq + [bufs[0], self.bcst], wr=[bPSs])
                op('act', lambda e, Lk=Lk, PSs=PSs, bi=bi: e.activation(
                    out=self.PT[0:Lk, bi, :, 0:L], in_=PSs[0:Lk, :].rearrange("p (a b) -> p a b", b=128)[:, :, 0:L], func=AF.Exp),
                    rd=[bPSs], wr=[self.bPT[bi]])
            Po, bPo = P5, b5
            yield

            def fn_pv(e, Po=Po, g=g):
                ins = None
                for i in range(4):
                    for bi, (kTa, va, Lk, mask, bufs) in enumerate(blocks):
                        ins = e.matmul(Po[0:L, i * 65:(i + 1) * 65], lhsT=self.PT[0:Lk, bi, i, 0:L], rhs=va[0:Lk, g, :],
                                       start=(bi == 0), stop=(bi == len(blocks) - 1))
                return ins
            op('pe', fn_pv, rd=[self.bPT[bi] for bi in range(len(blocks))] + [b[1] for b in [blk_[4] for blk_ in blocks]], wr=[bPo])
            po3 = Po[0:L, 0:260].rearrange("p (a b) -> p a b", b=65)
            yield
            bs = self.basml
            op('dve', lambda e, po3=po3, g=g: e.tensor_tensor(out=self.asml[0:L, 0:4], in0=po3[:, :, 64], in1=self.esk[0:L, l, 4 * g:4 * g + 4],
                                                              op=ALU.add), rd=[bPo, self.bpar], wr=[bs])
            op('dve', lambda e: e.reciprocal(out=self.asml[0:L, 4:8], in_=self.asml[0:L, 0:4]), rd=[bs], wr=[bs])
            for i in range(4):
                op('dve', lambda e, po3=po3, i=i: e.tensor_scalar(out=self.ao[0:L, i, :], in0=po3[:, i, 0:64], scalar1=self.asml[0:L, 4 + i:5 + i],
                                                                  scalar2=None, op0=ALU.mult), rd=[bPo, bs], wr=[self.bao])
            h0 = 1536 + 256 * g
            op('dve', lambda e, g=g, h0=h0: e.tensor_tensor(out=self.mtok[0:L, tt, h0:h0 + 256], in0=self.ao[0:L].rearrange("p a b -> p (a b)"),
                                                            in1=self.saz[0:L, tt, 256 * g:256 * g + 256], op=ALU.mult),
               rd=[self.bao, self.bsaz[tt]], wr=[self.bmtok[tt][3]])

    def merge_T(self, u, rounds=(0, 1, 2, 3)):
        L, tt = u.L, u.tt
        self.transpose_rows(lambda kc: self.mtok[0:L, tt, kc * 128:(kc + 1) * 128], lambda r: self.bmtok[tt][r], self.mT, self.bmT[tt], u,
                            k_evac=1, rounds=rounds)

    def outproj(self, l, us, part):
        op = self.op
        kcs = [0, 1, 2, 3] + list(range(8, 16)) if part == 'A' else [4, 5, 6, 7]
        for j in range(4):
            s = self.get_w()
            W, bW = self.W[s], self.bW[s]
            for u in us:
                L, tt, c0 = u.L, u.tt, u.c0
                k = self.pcount = getattr(self, 'pcount', 0) + 1
                ps, bps = self.PS[k % 3], self.bPS[k % 3]
                for g0 in range(0, len(kcs), 4):
                    self.pull(3)

                    def fn(e, ps=ps, L=L, c0=c0, W=W, g0=g0):
                        ins = None
                        for i_ in range(g0, min(g0 + 4, len(kcs))):
                            kc = kcs[i_]
                            ins = e.matmul(ps[0:L, :], lhsT=self.mT[:, kc, c0:c0 + L], rhs=W[:, kc, :], start=(i_ == 0),
                                           stop=(i_ == len(kcs) - 1))
                        return ins
                    op('pe', fn, rd=[bW, self.bmT[tt]], wr=[bps])
                xa, bx = self.Xap(u, slice(512 * j, 512 * (j + 1)))
                if part == 'A':
                    op('dve', lambda e, xa=xa, ps=ps, L=L: e.scalar_tensor_tensor(out=xa, in0=xa, scalar=ALPHA, in1=ps[0:L, :],
                                                                                  op0=ALU.mult, op1=ALU.add), rd=[bps, bx], wr=[bx])
                else:
                    op('dve', lambda e, xa=xa, ps=ps, L=L: e.tensor_tensor(out=xa, in0=xa, in1=ps[0:L, :], op=ALU.add), rd=[bps, bx], wr=[bx])

    def final_ln(self, l, u, blk):
        op = self.op
        L, tt = u.L, u.tt
        xa, bx = self.Xap(u)
        bs = self.blsm
        for j in range(4):
            xj, _ = self.Xap(u, slice(512 * j, 512 * (j + 1)))
            op('dve', lambda e, xj=xj, j=j: e.bn_stats(out=self.lst[0:L, j, :], in_=xj), rd=[bx], wr=[bs])
        op('dve', lambda e: e.bn_aggr(out=self.lmv[0:L, :], in_=self.lst[0:L].rearrange("p a b -> p (a b)")), rd=[bs], wr=[bs])
        self.rstd(self.lsm[0:L, 0:1], self.lmv[0:L, 1:2], L, bs)
        op('dve', lambda e: e.scalar_tensor_tensor(out=self.lsm[0:L, 1:2], in0=self.lmv[0:L, 0:1], scalar=-1.0, in1=self.lsm[0:L, 0:1],
                                                   op0=ALU.mult, op1=ALU.mult), rd=[bs], wr=[bs])
        op('act', lambda e: e.activation(out=xa, in_=xa, func=AF.Identity, scale=self.lsm[0:L, 0:1], bias=self.lsm[0:L, 1:2]),
           rd=[bx, bs], wr=[bx])

    def final_gain(self, l, us, blk):
        op = self.op
        for j in range(4):
            k = self.lncount = getattr(self, 'lncount', 0) + 1
            lp, bl = self.lnp[k % 2], self.bln[k % 2]
            cs = slice(512 * j, 512 * (j + 1))
            self.dma('sp', lp[:, 0, :], self.p_lng[l, cs].partition_broadcast(128), wr=[bl], chan=bl)
            self.dma('sp', lp[:, 1, :], self.p_lnb[l, cs].partition_broadcast(128), wr=[bl], chan=bl)
            for u in us:
                xj, bx = self.Xap(u, cs)
                L = u.L
                op('dve', lambda e, xj=xj, lp=lp, L=L: e.tensor_tensor(out=xj, in0=xj, in1=lp[0:L, 0, :], op=ALU.mult), rd=[bx, bl], wr=[bx])
                op('dve', lambda e, xj=xj, lp=lp, L=L: e.tensor_tensor(out=xj, in0=xj, in1=lp[0:L, 1, :], op=ALU.add), rd=[bx, bl], wr=[bx])
        for u in us:
            self.final_out(l, u, blk)
            if l < DEPTH - 1:
                self.make_xT(u)

    def final_out(self, l, u, blk):
        L, tt = u.L, u.tt
        xa, bx = self.Xap(u)
        if l == DEPTH - 1:
            if u.sidx is None:
                r0 = blk * TB + tt * 128
                self.dma('sp', self.yp[r0:r0 + 128, :], xa, rd=[bx], chan=bx, final=True)
            else:
                self.dma('sp', self.ys[u.sidx:u.sidx + 1, :], xa, rd=[bx], chan=bx, final=True)

    def seq(self, *gens):
        for g in gens:
            yield from g

    def layer_block(self, blk, l):
        us = self.units(blk)
        last_blk = (blk == self.nblk - 1)
        self.load_params(l)
        if l == 0:
            self.load_x(blk, us)
        self.chk()
        if l == 0:
            for u in us:
                self.make_xT(u)
        self.chk()
        self.conv_start(l, blk)
        T = lambda kind, i: self.inproj_tile(l, kind, i, us)
        T('cu', 0)
        T('cg', 0)
        g_cb = self.conv_block(l, us)
        self.bg.append(g_cb)
        T('kvg', 0)
        for k_, i_ in (('qk', 0), ('qk', 1), ('v', 0), ('o', 0), ('z', 0)):
            T(k_, i_)
        g_m0 = self.seq(*[self.mlstm_unit(l, 0, u, last_blk) for u in us])
        self.bg.append(g_m0)
        for k_, i_ in (('cz', 0), ('aq', 0), ('az', 0)):
            T(k_, i_)
        g_at = self.seq(*[self.attn_unit(l, u, (blk * NT + u.tt if u.sidx is None else None), last_blk) for u in us])
        self.bg.append(g_at)
        for k_, i_ in (('qk', 2), ('qk', 3), ('v', 1), ('o', 1), ('z', 1)):
            T(k_, i_)
        self.drain([g_cb, g_at])
        self.bg.append(self.seq(*[self.conv_unit(l, u) for u in us]))
        self.drain([g_m0])
        g_m1 = self.seq(*[self.mlstm_unit(l, 1, u, last_blk) for u in us])
        self.bg.append(g_m1)
        self.drain([g for g in self.bg if g is not g_m1])
        for u in us:
            self.merge_T(u, rounds=(0, 2, 3))
        self.outproj(l, us, 'A')
        self.drain()
        self.conv_finish(l, us, last_blk)
        self.chk()
        if self.debug and blk == 0 and l == 0:
            for u in us:
                self.dump("mtok%d" % u.tt, self.mtok[0:u.L, u.tt, :], list(self.bmtok[u.tt]), (u.L, D), BF16)
        for u in us:
            self.merge_T(u, rounds=(1,))
        self.chk()
        self.outproj(l, us, 'B')
        self.chk()
        for u in us:
            self.final_ln(l, u, blk)
        self.final_gain(l, us, blk)
        if self.debug and blk == 0 and l == 0:
            for u in us:
                self.dump("x1_%d" % u.tt, self.X[0:u.L, u.tt, :], self.bX[u.tt], (u.L, D))
        self.chk()

    def build(self):
        self.setup()
        seq = []
        blks = (list(range(-(NS // NSB), 0)) if self.with_sample else []) + list(range(self.nblk))
        for blk in blks:
            for l in range(DEPTH):
                seq += [(l, j) for j in range(NW)] + [(l, NW_IN + j, 'B') for j in range(4)]
        self.w_plan(seq)
        try:
            self.chk()
            for blk in blks:
                for l in range(DEPTH):
                    self.layer_block(blk, l)
        except StopIteration:
            pass
        self.P.emit(self.final)
        return self.nc


_CACHE = {}


def kernel(x_prompt, x_sample, state_C, state_n, state_m, state_conv, cache_k, cache_v,
           w_in, w_out, b_igate, b_fgate, m_norm_g, conv_w, conv_b, conv_ln_g, conv_ln_b,
           sinks, ln_g, ln_b):
    f = lambda a: np.ascontiguousarray(np.asarray(a, dtype=np.float32))
    x_prompt, x_sample = f(x_prompt), f(x_sample)
    if 'nc' not in _CACHE:
        _CACHE['nc'] = K().build()
    nc = _CACHE['nc']
    wt = host_weight_tiles(f(w_in), f(w_out))
    cst, cstB = host_consts()
    scv = np.ascontiguousarray(f(state_conv).transpose(0, 1, 3, 2))
    p_cw = np.ascontiguousarray(f(conv_w).transpose(0, 2, 1).reshape(DEPTH, 4, 128, 31).transpose(0, 2, 1, 3)).reshape(DEPTH, 128, 124)
    p_cb = np.ascontiguousarray(f(conv_b).reshape(DEPTH, 4, 128).transpose(0, 2, 1))
    sC, sn, sm = f(state_C), f(state_n), f(state_m)
    ck = f(cache_k).reshape(DEPTH, 32, 128, 128)
    cv = f(cache_v).reshape(DEPTH, 32, 128, 128)
    in_maps = []
    xp_dummy = np.zeros_like(x_prompt[0])
    for c in range(NCORE):
        b = c % 2
        ss = slice(NS * c, NS * (c + 1))
        in_maps.append({
            "xp": x_prompt[b] if c < 2 else xp_dummy, "xs": np.ascontiguousarray(x_sample[ss, 0, :]), "wt": wt,
            "sC": np.ascontiguousarray(sC[:, ss]), "sn": np.ascontiguousarray(sn[:, ss]), "sm": np.ascontiguousarray(sm[:, ss]),
            "scv": np.ascontiguousarray(scv[:, ss]), "ck": np.ascontiguousarray(ck[:, ss]), "cv": np.ascontiguousarray(cv[:, ss]),
            "cst": cst, "cstB": cstB, "p_bi": f(b_igate), "p_bf": f(b_fgate), "p_mng": f(m_norm_g), "p_cw": p_cw, "p_cb": p_cb,
            "p_clg": f(conv_ln_g), "p_clb": f(conv_ln_b), "p_sk": f(sinks), "p_lng": f(ln_g), "p_lnb": f(ln_b),
        })
    res = run_bass_kernel_spmd(nc, in_maps, core_ids=list(range(NCORE))).results
    cat = lambda k, ax: np.concatenate([r[k] for r in res], axis=ax)
    stack2 = lambda k: np.stack([res[0][k], res[1][k]], axis=1)
    y_prompt = np.stack([res[0]["yp"], res[1]["yp"]], axis=0)
    y_sample = cat("ys", 0).reshape(32, 1, D)
    new_C_p = stack2("oCp")
    new_n_p = stack2("onp")
    new_m_p = stack2("omp")
    new_conv_p = stack2("ocvp")
    new_k_p = stack2("okp").reshape(DEPTH, 2, 128, 2, 64)
    new_v_p = stack2("ovp").reshape(DEPTH, 2, 128, 2, 64)
    new_C_s = cat("oCs", 1)
    new_n_s = cat("ons", 1)
    new_m_s = cat("oms", 1)
    new_conv_s = cat("ocvs", 1)
    new_k_s = cat("oks", 1).reshape(DEPTH, 32, 128, 2, 64)
    new_v_s = cat("ovs", 1).reshape(DEPTH, 32, 128, 2, 64)
    return (y_prompt, y_sample, new_C_p, new_n_p, new_m_p, new_conv_p, new_k_p, new_v_p,
            new_C_s, new_n_s, new_m_s, new_conv_s, new_k_s, new_v_s)
```

```python
import numpy as np
import concourse.bass as bass
import concourse.mybir as mybir
from concourse.bass_utils import run_bass_kernel_spmd

F32 = mybir.dt.float32
BF16 = mybir.dt.bfloat16
ALU = mybir.AluOpType
AF = mybir.ActivationFunctionType
AX = mybir.AxisListType

D = 2048
SEQ = 4096
DEPTH = 2
NCORE = 8
NS = 4
TB = 256
NT = TB // 128
NBLK = SEQ // TB
NSB = 2
NTT = max(NT, NSB)
NCF = max(TB, NSB)
ALPHA = (2 * DEPTH) ** 0.25
EPS = 1e-5
BIG = 30000.0
NW_IN = 16
NW = 20
EPOCH = 3000

T_KVG, T_QK, T_V, T_O, T_Z, T_CU, T_CG, T_CZ, T_AQ, T_AZ = 'kvg', 'qk', 'v', 'o', 'z', 'cu', 'cg', 'cz', 'aq', 'az'
TILE_ORDER = [('cu', 0), ('cg', 0), ('kvg', 0), ('qk', 0), ('qk', 1), ('v', 0), ('o', 0), ('z', 0),
              ('cz', 0), ('aq', 0), ('az', 0),
              ('qk', 2), ('qk', 3), ('v', 1), ('o', 1), ('z', 1)]

C_ID, C_TRI, C_SEL, C_ONE, C_EPS = 0, 128, 256, 512, 513
NCST = 514
CB_ID, CB_BIGM, CB_NMP, CB_NMC, CB_ONE = 0, 128, 256, 768, 1280
NCSTB = 1281


def host_weight_tiles(w_in, w_out):
    mq, mk, mv, mo, mi, mf, mz = 0, 1024, 2048, 3072, 4096, 4100, 4104
    cu, cg, cz, aq, ak, av, az = 5128, 5640, 6152, 6664, 7176, 7304, 7432
    L = w_in.shape[0]
    out = np.zeros((L, NW, 2048, 512), np.float32)
    for l in range(L):
        W = w_in[l]
        for j, (kind, i) in enumerate(TILE_ORDER):
            t = out[l, j]
            if kind == 'kvg':
                t[:, 0:128] = W[:, ak:ak + 128]
                t[:, 128:256] = W[:, av:av + 128]
                t[:, 256:260] = W[:, mi:mi + 4]
                t[:, 260:264] = W[:, mf:mf + 4]
            elif kind == 'qk':
                t[:, 0:256] = W[:, mq + 256 * i: mq + 256 * (i + 1)]
                t[:, 256:512] = W[:, mk + 256 * i: mk + 256 * (i + 1)]
            elif kind == 'v':
                t[:] = W[:, mv + 512 * i: mv + 512 * (i + 1)]
            elif kind == 'o':
                t[:] = W[:, mo + 512 * i: mo + 512 * (i + 1)]
            elif kind == 'z':
                t[:] = W[:, mz + 512 * i: mz + 512 * (i + 1)]
            elif kind == 'cu':
                t[:] = W[:, cu:cu + 512]
            elif kind == 'cg':
                t[:] = W[:, cg:cg + 512]
            elif kind == 'cz':
                t[:] = W[:, cz:cz + 512]
            elif kind == 'aq':
                for c in range(4):
                    t[:, c * 128: c * 128 + 64] = W[:, aq + 64 * c: aq + 64 * (c + 1)]
                    t[:, c * 128 + 64: c * 128 + 128] = W[:, aq + 64 * (4 + c): aq + 64 * (5 + c)]
            elif kind == 'az':
                t[:] = W[:, az:az + 512]
        for j in range(4):
            out[l, NW_IN + j] = w_out[l][:, 512 * j: 512 * (j + 1)]
    out = out.reshape(L, NW, 16, 128, 512).transpose(0, 1, 3, 2, 4)
    return np.ascontiguousarray(out).reshape(L, NW, 128, 16 * 512)


def host_consts():
    c = np.zeros((128, NCST), np.float32)
    cb = np.zeros((128, NCSTB), np.float32)
    s = np.arange(128)[:, None]
    t = np.arange(128)[None, :]
    c[:, C_ID:C_ID + 128] = (s == t)
    c[:, C_TRI:C_TRI + 128] = (s <= t)
    c[0, C_SEL:C_SEL + 128] = 1.0
    c[1, C_SEL + 128:C_SEL + 256] = 1.0
    c[:, C_ONE] = 1.0
    c[:, C_EPS] = EPS
    cb[:, CB_ID:CB_ID + 128] = (s == t)
    cb[:, CB_BIGM:CB_BIGM + 128] = np.where(s > t, BIG, 0.0)
    nmp = np.where(s < t, -BIG, 0.0)
    nmc = np.where(s > t, -BIG, 0.0)
    for h in range(4):
        cb[:, CB_NMP + 128 * h: CB_NMP + 128 * (h + 1)] = nmp
        cb[:, CB_NMC + 128 * h: CB_NMC + 128 * (h + 1)] = nmc
    cb[:, CB_ONE] = 1.0
    return c, cb


class Buf:
    __slots__ = ('name', 'w', 'r', 'sem', 'cnt')

    def __init__(self, name):
        self.name = name
        self.w = None
        self.r = []
        self.sem = None
        self.cnt = 0


class Op:
    __slots__ = ('eng', 'fn', 'deps', 'dma', 'chan', 'val', 'sig', 'signo', 'idx', 'dmaw')


class Prog:
    ENGS = ('pe', 'act', 'dve', 'pool', 'sp')

    def __init__(self, nc):
        self.nc = nc
        self.ops = []
        self.nbuf = 0

    def buf(self, name=None):
        self.nbuf += 1
        return Buf(name or "b%d" % self.nbuf)

    def op(self, eng, fn, rd=(), wr=(), chan=None):
        o = Op()
        o.eng, o.fn, o.idx = eng, fn, len(self.ops)
        deps = set()
        for b in rd:
            if b.w is not None:
                deps.add(b.w)
        for b in wr:
            if b.w is not None:
                deps.add(b.w)
            deps.update(b.r)
        o.deps = deps
        o.dmaw = {}
        for d in deps:
            p = self.ops[d]
            if p.dma:
                o.dmaw[id(p.chan)] = (p.chan, 16 * p.chan.cnt)
        o.dma = chan is not None
        o.chan = chan
        o.sig = False
        o.signo = 0
        o.val = 0
        if chan is not None:
            chan.cnt += 1
            o.val = 16 * chan.cnt
        for b in rd:
            b.r.append(o.idx)
        for b in wr:
            b.w = o.idx
            b.r = []
        self.ops.append(o)
        return o

    def emit(self, final_chans):
        nc = self.nc
        ops = self.ops
        for o in ops:
            keep = {}
            for d in o.deps:
                p = ops[d]
                if p.dma:
                    continue
                if p.eng == 'pe' and o.eng == 'pe' and not o.dma:
                    continue
                k = ('c', p.eng)
                if k not in keep or keep[k] < d:
                    keep[k] = d
            o.deps = sorted(keep.values())
            for d in o.deps:
                ops[d].sig = True
        cnt = {e: 0 for e in self.ENGS}
        for o in ops:
            if o.sig and not o.dma:
                cnt[o.eng] += 1
                o.signo = cnt[o.eng]
        esems = {e: [nc.alloc_semaphore(name="s_%s_%d" % (e, i)) for i in range((cnt[e] + EPOCH - 1) // EPOCH + 1)]
                 for e in self.ENGS}
        chans = {}
        for o in ops:
            if o.dma and o.chan.sem is None:
                o.chan.sem = nc.alloc_semaphore(name="d_%s_%d" % (o.chan.name, len(chans)))
                chans[id(o.chan)] = o.chan

        def target(p):
            if p.dma:
                return p.chan.sem, p.val
            n = p.signo - 1
            return esems[p.eng][n // EPOCH], n % EPOCH + 1

        by_eng = {e: [o for o in ops if o.eng == e] for e in self.ENGS}

        def run(ename, eng):
            waited = {}
            for o in by_eng[ename]:
                for ch, val in o.dmaw.values():
                    k = id(ch.sem)
                    if waited.get(k, 0) >= val:
                        continue
                    eng.wait_ge(ch.sem, val)
                    waited[k] = val
                for d in o.deps:
                    sem, val = target(ops[d])
                    k = id(sem)
                    if waited.get(k, 0) >= val:
                        continue
                    eng.wait_ge(sem, val)
                    waited[k] = val
                ins = o.fn(eng)
                if o.dma:
                    ins.then_inc(o.chan.sem, 16)
                elif o.sig:
                    sem, _ = target(o)
                    ins.then_inc(sem, 1)
            if ename == 'sp':
                for ch in chans.values():
                    eng.wait_ge(ch.sem, 16 * ch.cnt)

        with nc.Block() as block:
            @block.tensor
            def _(e):
                run('pe', e)

            @block.scalar
            def _(e):
                run('act', e)

            @block.vector
            def _(e):
                run('dve', e)

            @block.gpsimd
            def _(e):
                run('pool', e)

            @block.sync
            def _(e):
                run('sp', e)


class Unit:
    def __init__(self, L, tt, c0, sidx=None):
        self.L, self.tt, self.c0, self.sidx = L, tt, c0, sidx


class K:
    def __init__(self, nblk=NBLK, with_sample=True, debug=False, stage=None):
        self.stage = stage
        self.stage_n = 0
        self.debug = debug
        self.bg = []
        self.nblk = nblk
        self.with_sample = with_sample
        self.debug = debug
        self.nc = nc = bass.Bass("TRN2", target_bir_lowering=False)
        self.P = Prog(nc)
        self.final = []
        self.dbg_outs = []
        di = lambda n, s: nc.dram_tensor(n, list(s), F32, kind="ExternalInput").ap()
        do = lambda n, s: nc.dram_tensor(n, list(s), F32, kind="ExternalOutput").ap()
        self.xp = di("xp", (SEQ, D))
        self.xs = di("xs", (NS, D))
        self.wt = di("wt", (DEPTH, NW, 128, 16 * 512))
        self.wb = nc.dram_tensor("wb", [DEPTH, NW, 128, 16 * 512], BF16, kind="Internal").ap()
        self.bwb = [[self.P.buf("wb%d_%d" % (l, j)) for j in range(NW)] for l in range(DEPTH)]
        self.sC = di("sC", (DEPTH, NS, 4, 256, 256))
        self.sn = di("sn", (DEPTH, NS, 4, 256))
        self.sm = di("sm", (DEPTH, NS, 4))
        self.scv = di("scv", (DEPTH, NS, 512, 30))
        self.ck = di("ck", (DEPTH, NS, 128, 128))
        self.cv = di("cv", (DEPTH, NS, 128, 128))
        self.cst = di("cst", (128, NCST))
        self.cstB = di("cstB", (128, NCSTB))
        self.p_bi = di("p_bi", (DEPTH, 4))
        self.p_bf = di("p_bf", (DEPTH, 4))
        self.p_mng = di("p_mng", (DEPTH, 1024))
        self.p_cw = di("p_cw", (DEPTH, 128, 4 * 31))
        self.p_cb = di("p_cb", (DEPTH, 128, 4))
        self.p_clg = di("p_clg", (DEPTH, 512))
        self.p_clb = di("p_clb", (DEPTH, 512))
        self.p_sk = di("p_sk", (DEPTH, 8))
        self.p_lng = di("p_lng", (DEPTH, D))
        self.p_lnb = di("p_lnb", (DEPTH, D))
        self.yp = do("yp", (SEQ, D))
        self.ys = do("ys", (NS, D))
        self.oCp = do("oCp", (DEPTH, 4, 256, 256))
        self.onp = do("onp", (DEPTH, 4, 256))
        self.omp = do("omp", (DEPTH, 4))
        self.ocvp = do("ocvp", (DEPTH, 30, 512))
        self.okp = do("okp", (DEPTH, 128, 128))
        self.ovp = do("ovp", (DEPTH, 128, 128))
        self.oCs = do("oCs", (DEPTH, NS, 4, 256, 256))
        self.ons = do("ons", (DEPTH, NS, 4, 256))
        self.oms = do("oms", (DEPTH, NS, 4))
        self.ocvs = do("ocvs", (DEPTH, NS, 30, 512))
        self.oks = do("oks", (DEPTH, NS, 128, 128))
        self.ovs = do("ovs", (DEPTH, NS, 128, 128))
        self.alloc()

    def chk(self):
        self.stage_n += 1
        if self.stage is not None and self.stage_n >= self.stage:
            raise StopIteration

    def dump(self, name, ap, buf, shape, dt=F32):
        o = self.nc.dram_tensor("dbg_" + name, list(shape), dt, kind="ExternalOutput").ap()
        bufs = buf if isinstance(buf, list) else [buf]
        self.dma('sp', o, ap, rd=bufs, chan=bufs[0], final=True)

    def sb(self, name, shape, dt=F32):
        return self.nc.alloc_sbuf_tensor(name, list(shape), dt)

    def B(self, name=None):
        return self.P.buf(name)

    def op(self, eng, fn, rd=(), wr=(), chan=None):
        return self.P.op(eng, fn, rd, wr, chan)

    def rstd(self, out, in_, L, b):
        epsc = self.cst_sb[0:L, C_EPS:C_EPS + 1]
        self.op('act', lambda e: e.activation(out=out, in_=in_, func=AF.Ln, bias=epsc), rd=[b, self.bcst], wr=[b])
        self.op('act', lambda e: e.activation(out=out, in_=out, func=AF.Exp, scale=-0.5), rd=[b], wr=[b])

    def pull(self, n=1):
        for _ in range(n):
            for g in list(self.bg):
                try:
                    next(g)
                except StopIteration:
                    self.bg.remove(g)

    def drain(self, gens=None):
        while True:
            act = [g for g in self.bg if gens is None or g in gens]
            if not act:
                return
            self.pull()

    def dma(self, q, out, in_, rd=(), wr=(), chan=None, final=False):
        if final and chan not in self.final:
            self.final.append(chan)
        return self.op(q, lambda e, o=out, i=in_: e.dma_start(out=o, in_=i), rd=rd, wr=wr, chan=chan)

    def alloc(self):
        nc = self.nc
        sb, B = self.sb, self.B
        self.PS = [nc.alloc_psum_tensor("ps%d" % i, [128, 512], F32) for i in range(7)]
        self.PB = nc.alloc_psum_tensor("psb", [128, 1024], BF16)
        self.bPS = [B("ps%d" % i) for i in range(7)]
        _b = B("psb")
        self.bPB = [_b, _b]
        self.bP6 = {k: self.bPS[6] for k in ('b', 'tm', 'den', 'dn', 'arow', 'brow')}
        self.cst_sb = sb("cst_sb", [128, NCST])
        self.cstb = sb("cstb", [128, NCSTB], BF16)
        self.bcst = B("cst")
        self.bcstb = B("cstb")
        self.NSLOT = 3
        self.W = [sb("w%d" % i, [128, 16, 512], BF16) for i in range(self.NSLOT)]
        self.bW = [B("w%d" % i) for i in range(self.NSLOT)]
        self.wcount = 0
        self.X = sb("X", [128, NTT, D])
        self.bX = [B("X%d" % i) for i in range(NTT)]
        self.xT = sb("xT", [128, 16, NCF], BF16)
        self.bxT = [B("xT%d" % i) for i in range(NTT)]
        self.mT = self.xT
        self.bmT = self.bxT
        self.mtok = sb("mtok", [128, NTT, D], BF16)
        self.bmtok = [[B("mtok%d_%d" % (i, j)) for j in range(4)] for i in range(NTT)]
        self.xb = self.mtok[:, 0, :]
        self.bxb = B("xb")
        self.qT_ = [sb("qT%d" % p, [128, 2, 2, NCF], BF16) for p in range(2)]
        self.kT_ = [sb("kT%d" % p, [128, 2, 2, NCF], BF16) for p in range(2)]
        self.bqT_ = [[[B() for _ in range(2)] for _ in range(2)] for p in range(2)]
        self.bkT_ = [[[B() for _ in range(2)] for _ in range(2)] for p in range(2)]
        self.vtok_ = [sb("vtok%d" % p, [128, NTT, 2, 256], BF16) for p in range(2)]
        self.bv_ = [[B() for _ in range(NTT)] for p in range(2)]
        self.G_ = [sb("G%d" % p, [128, NTT, 2, 256]) for p in range(2)]
        self.bG_ = [[B() for _ in range(NTT)] for p in range(2)]
        self.gtmp = sb("gtmp", [128, 512])
        self.bgtmp = B()
        self.graw = sb("graw", [128, NTT, 8])
        self.ig = sb("ig", [128, NTT, 4])
        self.sp = sb("spl", [128, NTT, 4])
        self.bgate = [B() for _ in range(NTT)]
        self.a_sb = sb("a_sb", [128, 2]); self.ba = B()
        self.arow = sb("arow", [2, 128]); self.barow = B()
        self.Mrow = sb("Mrow", [2, 128]); self.bMrow = B()
        self.mrow = sb("mrow", [2, 128]); self.bmrow = B()
        self.w0row = sb("w0row", [2, 128]); self.bw0row = B()
        self.emrow = sb("emrow", [2, 128]); self.bemrow = B()
        self.wT = sb("wT", [128, 2, 128]); self.bwT = B()
        self.swT = sb("swT", [128, 2, 128], BF16); self.bswT = B()
        self.qs = sb("qs", [128, 2, 2, 128], BF16); self.bqs = B()
        self.g0bc = sb("g0bc", [128, 2]); self.bg0bc = B()
        self.ktok = sb("ktok", [128, 2, 256], BF16); self.bktok = B()
        self.gv = sb("gv", [128, 2, 256], BF16); self.bgv = B()
        self.gb = sb("gb", [128, 2], BF16); self.bgb = B()
        self.sm6 = sb("sm6", [128, 2, 6]); self.bsm6 = B()
        self.mv = sb("mv", [128, 2, 2]); self.bmv = B()
        self.sml = sb("sml", [128, 16]); self.bsml = B()
        self.hn = sb("hn", [128, 2, 256]); self.bhn = B()
        self.C = [sb("C%d" % l, [128, 4, 2, 256]) for l in range(DEPTH)]
        self.n = [sb("n%d" % l, [128, 4, 2]) for l in range(DEPTH)]
        self.Cb = [sb("Cb%d" % l, [128, 4, 2, 256], BF16) for l in range(DEPTH)]
        self.nb = [sb("nb%d" % l, [128, 4, 2], BF16) for l in range(DEPTH)]
        self.m = [[sb("m%d_%d" % (l, hp), [2, 1]) for hp in range(2)] for l in range(DEPTH)]
        self.bC = [[B() for _ in range(2)] for _ in range(DEPTH)]
        self.Cs = sb("Cs", [128, 2, 2, 256]); self.ns = sb("ns", [128, 2, 2])
        self.Csb = sb("Csb", [128, 2, 2, 256], BF16); self.nsb = sb("nsb", [128, 2, 2], BF16)
        self.ms = sb("ms", [2, 1]); self.bCs = B("Cs")
        self.aT = sb("aT", [128, 4, 30 + TB])
        self.baT = B("aT")
        self.aTs = sb("aTs", [128, 4, NS, 31])
        self.baTs = [B() for _ in range(NS)]
        self.bcu = [B() for _ in range(4)]
        self.cuF = sb("cuF", [128, 4, NCF])
        self.sg = sb("sg", [128, NCF]); self.bsg = B()
        self.yT = sb("yT", [128, 4, NCF]); self.byT = B("yT")
        self.ctmp = sb("ctmp", [128, 31]); self.bctmp = B()
        self.sz = sb("sz", [128, NTT, 512], BF16); self.bsz = [B() for _ in range(NTT)]
        self.yn = sb("yn", [128, 512]); self.byn = B()
        self.cst6 = sb("cst6", [128, 6]); self.cmv = sb("cmv", [128, 2]); self.csml = sb("csml", [128, 4]); self.bcsm = B()
        self.cvo = sb("cvo", [30, 512]); self.bcvo = B("cvo")
        self.aqT = sb("aqT", [128, 4, NCF], BF16); self.baq = [B() for _ in range(4)]
        self.kvf = sb("kvf", [128, NTT, 256]); self.bkvf = [B() for _ in range(NTT)]
        self.kvb = sb("kvb", [128, 128], BF16); self.bkvb = B()
        self.akT = [sb("akT%d" % l, [128, 2, 128], BF16) for l in range(DEPTH)]
        self.bakT = [[B(), B()] for l in range(DEPTH)]
        self.vaug = [sb("vaug%d" % l, [128, 2, 2, 65], BF16) for l in range(DEPTH)]
        self.bvaug = [[B(), B()] for l in range(DEPTH)]
        self.hist = [sb("hist%d" % l, [128, 4, 30]) for l in range(DEPTH)]
        self.bhist = [B() for l in range(DEPTH)]
        self.akTs = sb("akTs", [128, 128], BF16); self.vaugs = sb("vaugs", [128, 2, 65], BF16)
        self.ckf = sb("ckf", [128, 128]); self.cvf = sb("cvf", [128, 128]); self.bcache = B("cache")
        self.bakTs = B(); self.bvaugs = B()
        self.saz = sb("saz", [128, NTT, 512], BF16); self.bsaz = [B() for _ in range(NTT)]
        self.PT = sb("PT", [128, 2, 4, 128], BF16); self.bPT = [B(), B()]
        self.asml = sb("asml", [128, 8]); self.basml = B()
        self.ao = sb("ao", [128, 4, 64]); self.bao = B()
        self.bi_bc = sb("bi_bc", [128, DEPTH, 4]); self.bf_bc = sb("bf_bc", [128, DEPTH, 4])
        self.esk = sb("esk", [128, DEPTH, 8])
        self.mng = sb("mng", [128, 1024])
        self.cw = sb("cw", [128, DEPTH, 4, 31]); self.cb = sb("cb", [128, DEPTH, 4])
        self.clg = sb("clg", [128, 512]); self.clb = sb("clb", [128, 512])
        self.lnp = [sb("lnp%d" % i, [128, 2, 512]) for i in range(2)]
        self.bpar = B("par"); self.bmng = B("mng"); self.bcl = B("cl"); self.bln = [B("ln0"), B("ln1")]
        self.lst = sb("lst", [128, 4, 6]); self.lmv = sb("lmv", [128, 2]); self.lsm = sb("lsm", [128, 4]); self.blsm = B()
        self.bout = {k: B(k) for k in ('oCp', 'onp', 'omp', 'okp', 'ovp', 'oms', 'oks', 'ovs')}

    def setup(self):
        op, dma = self.op, self.dma
        dma('sp', self.cst_sb[:], self.cst, wr=[self.bcst], chan=self.bcst)
        dma('pool', self.cstb[:], self.cstB, wr=[self.bcstb], chan=self.bcstb)
        op('dve', lambda e: e.tensor_copy(out=self.cst_sb[:, C_ONE:C_ONE + 1], in_=self.cst_sb[:, C_ONE:C_ONE + 1]),
           rd=[self.bcst, self.bcstb], wr=[self.bcst])
        bp = self.bpar

        def bc(dst, src):
            dma('sp', dst, src.partition_broadcast(128), wr=[bp], chan=bp)
        bc(self.bi_bc[:].rearrange("p l f -> p (l f)"), self.p_bi.rearrange("l f -> (l f)"))
        bc(self.bf_bc[:].rearrange("p l f -> p (l f)"), self.p_bf.rearrange("l f -> (l f)"))
        bc(self.esk[:].rearrange("p l f -> p (l f)"), self.p_sk.rearrange("l f -> (l f)"))
        for l in range(DEPTH):
            dma('sp', self.cw[:, l].rearrange("p a b -> p (a b)"), self.p_cw[l], wr=[bp], chan=bp)
            dma('sp', self.cb[:, l], self.p_cb[l], wr=[bp], chan=bp)
        op('act', lambda e: e.activation(out=self.esk[:], in_=self.esk[:], func=AF.Exp), rd=[bp], wr=[bp])
        for l in range(DEPTH):
            for hp in range(2):
                hs = slice(2 * hp, 2 * hp + 2)
                b = self.bC[l][hp]
                op('dve', lambda e, l=l, hs=hs: e.memset(self.C[l][:, hs], 0.0), wr=[b])
                op('dve', lambda e, l=l, hs=hs: e.memset(self.n[l][:, hs], 0.0), wr=[b])
                op('dve', lambda e, l=l, hs=hs: e.memset(self.Cb[l][:, hs], 0.0), wr=[b])
                op('dve', lambda e, l=l, hs=hs: e.memset(self.nb[l][:, hs], 0.0), wr=[b])
                op('dve', lambda e, l=l, hp=hp: e.memset(self.m[l][hp][:], 0.0), wr=[b])
        for l in range(DEPTH):
            for r in range(2):
                op('dve', lambda e, l=l, r=r: e.memset(self.vaug[l][:, r, :, 64:65], 1.0), wr=[self.bvaug[l][r]])
        op('dve', lambda e: e.memset(self.vaugs[:, :, 64:65], 1.0), wr=[self.bvaugs])

    def load_params(self, l):
        dma = self.dma
        dma('sp', self.mng[:, :], self.p_mng[l].partition_broadcast(128), wr=[self.bmng], chan=self.bmng)
        dma('sp', self.clg[:, :], self.p_clg[l].partition_broadcast(128), wr=[self.bcl], chan=self.bcl)
        dma('sp', self.clb[:, :], self.p_clb[l].partition_broadcast(128), wr=[self.bcl], chan=self.bcl)

    def w_plan(self, seq):
        seen = set()
        for ent in seq:
            l, j = ent[0], ent[1]
            if (l, j) not in seen:
                seen.add((l, j))
                self.dma('pool', self.wb[l, j], self.wt[l, j], wr=[self.bwb[l][j]], chan=self.bwb[l][j])
        self.wseq = seq
        self.w_issued = 0
        self.w_used = 0

    def get_w(self):
        while self.w_issued < len(self.wseq) and self.w_issued < self.w_used + self.NSLOT:
            ent = self.wseq[self.w_issued]
            l, j = ent[0], ent[1]
            s = self.w_issued % self.NSLOT
            if len(ent) > 2 and ent[2] == 'B':
                self.dma('pool', self.W[s][:, 4:8, :], self.wb[l, j].rearrange("p (a b) -> p a b", b=512)[:, 4:8, :],
                         rd=[self.bwb[l][j]], wr=[self.bW[s]], chan=self.bW[s])
            else:
                self.dma('pool', self.W[s][:].rearrange("p a b -> p (a b)"), self.wb[l, j], rd=[self.bwb[l][j]], wr=[self.bW[s]],
                         chan=self.bW[s])
            self.w_issued += 1
        s = self.w_used % self.NSLOT
        self.w_used += 1
        return s

    def units(self, blk):
        if blk < 0:
            s0 = NSB * (blk + NS // NSB)
            return [Unit(1, j, j, sidx=s0 + j) for j in range(NSB)]
        return [Unit(128, tt, tt * 128) for tt in range(NT)]

    def Xap(self, u, cols=slice(0, D)):
        return self.X[0:u.L, u.tt, cols], self.bX[u.tt]

    def load_x(self, blk, us):
        for u in us:
            xa, bx = self.Xap(u)
            if u.sidx is None:
                r0 = blk * TB + u.tt * 128
                self.dma('sp', xa, self.xp[r0:r0 + 128, :], wr=[bx], chan=bx)
            else:
                self.dma('sp', xa, self.xs[u.sidx:u.sidx + 1, :], wr=[bx], chan=bx)

    def transpose_rows(self, src_of, bsrc, dst, bdst, u, k_evac=0, rounds=(0, 1, 2, 3)):
        L, c0 = u.L, u.c0
        idb = self.cstb[:, CB_ID:CB_ID + 128]
        for r in rounds:
            h = r % 2
            pb = self.PB[:, h * 512:(h + 1) * 512]

            def fn(e, r=r, pb=pb):
                ins = None
                for j in range(4):
                    ins = e.transpose(pb[:, j * 128:j * 128 + L], src_of(4 * r + j), idb[0:L, 0:L])
                return ins
            bs_r = bsrc(r) if callable(bsrc) else bsrc
            self.op('pe', fn, rd=[bs_r, self.bcst] if not isinstance(bs_r, list) else bs_r + [self.bcst], wr=[self.bPB[h]])
            src = pb.rearrange("p (a b) -> p a b", b=128)[:, :, 0:L]
            out = dst[:, 4 * r:4 * r + 4, c0:c0 + L]
            if (r + k_evac) % 2 == 0:
                self.op('act', lambda e, o=out, s=src: e.activation(out=o, in_=s, func=AF.Copy), rd=[self.bPB[h]], wr=[bdst])
            else:
                self.op('dve', lambda e, o=out, s=src: e.tensor_copy(out=o, in_=s), rd=[self.bPB[h]], wr=[bdst])

    def make_xT(self, u):
        L = u.L
        xa, bx = self.Xap(u)
        bxb = list(self.bmtok[0])
        self.op('act', lambda e: e.activation(out=self.xb[0:L, :], in_=xa, func=AF.Copy), rd=[bx], wr=bxb)
        self.transpose_rows(lambda kc: self.xb[0:L, kc * 128:(kc + 1) * 128], bxb, self.xT, self.bxT[u.tt], u)

    def inproj_tile(self, l, kind, i, us):
        s = self.get_w()
        W, bW = self.W[s], self.bW[s]
        op = self.op
        pp_ = (i // 2) if kind == 'qk' else (i if kind in ('v', 'o', 'z') else 0)
        self.qT, self.kT, self.bqT, self.bkT = self.qT_[pp_], self.kT_[pp_], self.bqT_[pp_], self.bkT_[pp_]
        self.vtok, self.bv, self.G, self.bG = self.vtok_[pp_], self.bv_[pp_], self.G_[pp_], self.bG_[pp_]
        hs_samp = us[0].sidx is not None
        ncols = NSB if hs_samp else TB
        assert ncols <= 512
        bxT_all = [self.bxT[u.tt] for u in us]
        one = self.cst_sb[:, C_ONE:C_ONE + 1]
        if kind in ('qk', 'cu', 'cg', 'aq'):
            for ec in range(4):
                self.pull()
                k = self.pcount = getattr(self, 'pcount', 0) + 1
                ps, bps = self.PS[k % 2], self.bPS[k % 2]

                def fn(e, ec=ec, ps=ps):
                    ins = None
                    for kc in range(16):
                        ins = e.matmul(ps[:, 0:ncols], lhsT=W[:, kc, ec * 128:(ec + 1) * 128], rhs=self.xT[:, kc, 0:ncols],
                                       start=(kc == 0), stop=(kc == 15))
                    return ins
                op('pe', fn, rd=[bW] + bxT_all, wr=[bps])
                src = ps[:, 0:ncols]
                if kind == 'qk':
                    hl = i % 2
                    if ec < 2:
                        op('act', lambda e, s_=src, o=self.qT[:, hl, ec, 0:ncols]: e.activation(out=o, in_=s_, func=AF.Copy),
                           rd=[bps], wr=[self.bqT[hl][ec]])
                    else:
                        op('act', lambda e, s_=src, o=self.kT[:, hl, ec - 2, 0:ncols]: e.activation(out=o, in_=s_, func=AF.Copy, scale=1.0 / 16.0),
                           rd=[bps], wr=[self.bkT[hl][ec - 2]])
                elif kind == 'cu':
                    op('act', lambda e, s_=src, o=self.cuF[:, ec, 0:ncols]: e.activation(out=o, in_=s_, func=AF.Copy),
                       rd=[bps], wr=[self.bcu[ec]])
                elif kind == 'cg':
                    op('act', lambda e, s_=src: e.activation(out=self.sg[:, 0:ncols], in_=s_, func=AF.Sigmoid),
                       rd=[bps], wr=[self.bsg])
                    if not hs_samp:
                        op('dve', lambda e, ec=ec: e.tensor_tensor(out=self.aT[:, ec, 30:30 + TB], in0=self.cuF[:, ec, 0:TB],
                                                                   in1=self.sg[:, 0:TB], op=ALU.mult),
                           rd=[self.bsg, self.bcu[ec]], wr=[self.baT])
                    else:
                        s0 = us[0].sidx
                        op('dve', lambda e, ec=ec, s0=s0: e.tensor_tensor(out=self.aTs[:, ec, s0:s0 + NSB, 30], in0=self.cuF[:, ec, 0:NSB],
                                                                          in1=self.sg[:, 0:NSB], op=ALU.mult),
                           rd=[self.bsg, self.bcu[ec]], wr=[self.baTs[u_.sidx] for u_ in us])
                elif kind == 'aq':
                    op('act', lambda e, s_=src, o=self.aqT[:, ec, 0:ncols]: e.activation(out=o, in_=s_, func=AF.Copy, scale=0.125),
                       rd=[bps], wr=[self.baq[ec]])
            return
        for u in us:
            L, tt, c0 = u.L, u.tt, u.c0
            self.pull()
            k = self.pcount = getattr(self, 'pcount', 0) + 1
            ps, bps = self.PS[k % 2], self.bPS[k % 2]

            def fn(e, ps=ps, L=L, c0=c0):
                ins = None
                for kc in range(16):
                    ins = e.matmul(ps[0:L, :], lhsT=self.xT[:, kc, c0:c0 + L], rhs=W[:, kc, :], start=(kc == 0), stop=(kc == 15))
                return ins
            op('pe', fn, rd=[bW, self.bxT[tt]], wr=[bps])
            src = ps[0:L, :]
            if kind == 'v':
                op('act', lambda e, s_=src, o=self.vtok[0:L, tt].rearrange("p a b -> p (a b)"): e.activation(out=o, in_=s_, func=AF.Copy),
                   rd=[bps], wr=[self.bv[tt]])
            elif kind == 'o':
                g = self.G[0:L, tt].rearrange("p a b -> p (a b)")
                op('act', lambda e, s_=src, g=g: e.activation(out=g, in_=s_, func=AF.Sigmoid), rd=[bps], wr=[self.bG[tt]])
                op('dve', lambda e, g=g, L=L: e.tensor_tensor(out=g, in0=g, in1=self.mng[0:L, i * 512:(i + 1) * 512], op=ALU.mult),
                   rd=[self.bG[tt], self.bmng], wr=[self.bG[tt]])
            elif kind == 'z':
                g = self.G[0:L, tt].rearrange("p a b -> p (a b)")
                op('act', lambda e, s_=src, L=L: e.activation(out=self.gtmp[0:L, :], in_=s_, func=AF.Silu), rd=[bps], wr=[self.bgtmp])
                op('dve', lambda e, g=g, L=L: e.tensor_tensor(out=g, in0=g, in1=self.gtmp[0:L, :], op=ALU.mult),
                   rd=[self.bG[tt], self.bgtmp], wr=[self.bG[tt]])
            elif kind == 'cz':
                op('act', lambda e, s_=src, o=self.sz[0:L, tt, :]: e.activation(out=o, in_=s_, func=AF.Silu), rd=[bps], wr=[self.bsz[tt]])
            elif kind == 'az':
                op('act', lambda e, s_=src, o=self.saz[0:L, tt, :]: e.activation(out=o, in_=s_, func=AF.Silu), rd=[bps], wr=[self.bsaz[tt]])
            elif kind == 'kvg':
                bg = self.bgate[tt]
                op('act', lambda e, ps=ps, o=self.kvf[0:L, tt, :], L=L: e.activation(out=o, in_=ps[0:L, 0:256], func=AF.Copy),
                   rd=[bps], wr=[self.bkvf[tt]])
                op('dve', lambda e, ps=ps, L=L, tt=tt: e.tensor_tensor(out=self.ig[0:L, tt, :], in0=ps[0:L, 256:260],
                                                                      in1=self.bi_bc[0:L, l, :], op=ALU.add),
                   rd=[bps, self.bpar], wr=[bg])
                op('dve', lambda e, ps=ps, L=L, tt=tt: e.tensor_tensor(out=self.sp[0:L, tt, :], in0=ps[0:L, 260:264],
                                                                      in1=self.bf_bc[0:L, l, :], op=ALU.add),
                   rd=[bps, self.bpar], wr=[bg])
                op('act', lambda e, L=L, tt=tt: e.activation(out=self.sp[0:L, tt, :], in_=self.sp[0:L, tt, :], func=AF.Exp, scale=-1.0),
                   rd=[bg], wr=[bg])
                op('act', lambda e, L=L, tt=tt: e.activation(out=self.sp[0:L, tt, :], in_=self.sp[0:L, tt, :], func=AF.Ln,
                                                             bias=one[0:L, :]), rd=[bg, self.bcst], wr=[bg])

    def mlstm_unit(self, l, hp, u, last_blk):
        op, dma = self.op, self.dma
        qT_l, kT_l, vtok_l, G_l = self.qT_[hp], self.kT_[hp], self.vtok_[hp], self.G_[hp]
        bqT_l, bkT_l, bv_l, bG_l = self.bqT_[hp], self.bkT_[hp], self.bv_[hp], self.bG_[hp]
        L, tt, c0 = u.L, u.tt, u.c0
        cols = slice(c0, c0 + L)
        hs = slice(2 * hp, 2 * hp + 2)
        idf = self.cst_sb[:, C_ID:C_ID + 128]
        tri = self.cst_sb[:, C_TRI:C_TRI + 128]
        idb = self.cstb[:, CB_ID:CB_ID + 128]
        bigb = self.cstb[:, CB_BIGM:CB_BIGM + 128]
        onesb = self.cstb[:, CB_ONE:CB_ONE + 1]
        sel = lambda hl: self.cst_sb[0:2, C_SEL + 128 * hl:C_SEL + 128 * (hl + 1)]
        bcst = self.bcst
        P2, P3, P4, P5, P6 = self.PS[2], self.PS[3], self.PS[4], self.PS[5], self.PS[6]
        b2, b3, b4, b5 = self.bPS[2], self.bPS[3], self.bPS[4], self.bPS[5]
        b6 = self.bP6
        samp = u.sidx is not None
        if not samp:
            C, n, Cb, nb, m, bC = self.C[l][:, hs], self.n[l][:, hs], self.Cb[l][:, hs], self.nb[l][:, hs], self.m[l][hp], self.bC[l][hp]
        else:
            si = u.sidx
            C, n, Cb, nb, m, bC = self.Cs[:], self.ns[:], self.Csb[:], self.nsb[:], self.ms, self.bCs
            dma('sp', C, self.sC[l, si, hs].rearrange("h (dc p) e -> p h dc e", p=128), wr=[bC], chan=bC)
            self.op('sp', lambda e: e.dma_start(out=n, in_=self.sn[l, si, hs].rearrange("h (dc p) -> p h dc", p=128),
                                                allow_slow_non_contiguous=True), wr=[bC], chan=bC)
            dma('sp', m[:], self.sm[l, si, hs].rearrange("(h o) -> h o", o=1), wr=[bC], chan=bC)
            op('act', lambda e: e.activation(out=Cb, in_=C, func=AF.Copy), rd=[bC], wr=[bC])
            op('act', lambda e: e.activation(out=nb, in_=n, func=AF.Copy), rd=[bC], wr=[bC])
        bg = self.bgate[tt]
        sp_ = self.sp[0:L, tt, hs]
        op('pe', lambda e: e.matmul(P6[0:L, 0:2], lhsT=tri[0:L, 0:L], rhs=sp_, start=True, stop=True), rd=[bg, bcst], wr=[b6['b']])
        yield
        op('dve', lambda e: e.tensor_tensor(out=self.a_sb[0:L, :], in0=self.ig[0:L, tt, hs], in1=P6[0:L, 0:2], op=ALU.add),
           rd=[bg, b6['b']], wr=[self.ba])
        op('pe', lambda e: e.matmul(P6[0:2, 16:16 + L], lhsT=self.a_sb[0:L, :], rhs=idf[0:L, 0:L], start=True, stop=True),
           rd=[self.ba, bcst], wr=[b6['arow']])
        op('pe', lambda e: e.matmul(P6[0:2, 144:144 + L], lhsT=sp_, rhs=tri[0:L, 0:L], start=True, stop=True),
           rd=[bg, bcst], wr=[b6['brow']])
        op('dve', lambda e: e.tensor_copy(out=self.arow[:, 0:L], in_=P6[0:2, 16:16 + L]), rd=[b6['arow']], wr=[self.barow])
        yield
        op('dve', lambda e: e.tensor_tensor_scan(out=self.Mrow[:, 0:L], data0=self.arow[:, 0:L], data1=self.arow[:, 0:L],
                                                 initial=m[:], op0=ALU.max, op1=ALU.max), rd=[self.barow, bC], wr=[self.bMrow])
        op('dve', lambda e: e.tensor_tensor(out=self.mrow[:, 0:L], in0=self.Mrow[:, 0:L], in1=P6[0:2, 144:144 + L], op=ALU.subtract),
           rd=[self.bMrow, b6['brow']], wr=[self.bmrow])
        yield
        op('act', lambda e: e.activation(out=self.w0row[:, 0:L], in_=self.Mrow[:, 0:L], func=AF.Exp, scale=-1.0, bias=m[:]),
           rd=[self.bMrow, bC], wr=[self.bw0row])
        op('act', lambda e: e.activation(out=self.emrow[:, 0:L], in_=self.mrow[:, 0:L], func=AF.Exp, scale=-1.0),
           rd=[self.bmrow], wr=[self.bemrow])
        op('dve', lambda e: e.tensor_copy(out=m[:], in_=self.mrow[:, L - 1:L]), rd=[self.bmrow], wr=[bC])
        yield
        op('pe', lambda e: e.matmul(P6[0:L, 2:4], lhsT=self.emrow[0:2, 0:L], rhs=idf[0:2, 0:2], start=True, stop=True),
           rd=[self.bemrow, bcst], wr=[b6['tm']])

        def fn_bc(e):
            ins = None
            for hl in range(2):
                e.matmul(P3[0:L, hl * 128:hl * 128 + L], lhsT=sel(hl)[:, 0:L], rhs=self.Mrow[0:2, 0:L], start=True, stop=False)
                e.matmul(P3[0:L, hl * 128:hl * 128 + L], lhsT=idb[0:L, 0:L], rhs=bigb[0:L, 0:L], start=False, stop=True)
                ins = e.matmul(P3[:, 256 + hl * 128:256 + hl * 128 + L], lhsT=sel(hl), rhs=self.w0row[0:2, 0:L], start=True, stop=True)
            return ins
        op('pe', fn_bc, rd=[self.bMrow, self.bw0row, bcst], wr=[b3])
        yield
        for hl in range(2):
            op('act', lambda e, hl=hl: e.activation(out=self.wT[0:L, hl, 0:L], in_=P3[0:L, hl * 128:hl * 128 + L], func=AF.Exp,
                                                    scale=-1.0, bias=self.a_sb[0:L, hl:hl + 1]), rd=[b3, self.ba], wr=[self.bwT])

        yield
        def fn_s(e):
            ins = None
            for hl in range(2):
                for dc in range(2):
                    ins = e.matmul(P4[0:L, hl * 128:hl * 128 + L], lhsT=kT_l[:, hl, dc, cols], rhs=qT_l[:, hl, dc, cols],
                                   start=(dc == 0), stop=(dc == 1))
            return ins
        op('pe', fn_s, rd=bkT_l[0] + bkT_l[1] + bqT_l[0] + bqT_l[1], wr=[b4])
        for hl in range(2):
            op('dve', lambda e, hl=hl: e.tensor_tensor(out=self.swT[0:L, hl, 0:L], in0=P4[0:L, hl * 128:hl * 128 + L],
                                                       in1=self.wT[0:L, hl, 0:L], op=ALU.mult), rd=[b4, self.bwT], wr=[self.bswT])
        yield
        for hl in range(2):
            for dc in range(2):
                op('dve', lambda e, hl=hl, dc=dc: e.tensor_tensor(out=self.qs[:, hl, dc, 0:L], in0=qT_l[:, hl, dc, cols],
                                                                  in1=P3[:, 256 + hl * 128:256 + hl * 128 + L], op=ALU.mult),
                   rd=[b3, bqT_l[hl][dc]], wr=[self.bqs])
        op('dve', lambda e: e.tensor_copy(out=self.g0bc[:, :], in_=P3[:, 256:512].rearrange("p (a b) -> p a b", b=128)[:, :, L - 1]),
           rd=[b3], wr=[self.bg0bc])

        yield
        def fn_kt(e):
            ins = None
            for hl in range(2):
                for dc in range(2):
                    j = hl * 2 + dc
                    ins = e.transpose(self.PB[0:L, j * 128:(j + 1) * 128], kT_l[:, hl, dc, cols], idb)
            return ins
        op('pe', fn_kt, rd=bkT_l[0] + bkT_l[1] + [bcst], wr=[self.bPB[0]])
        op('act', lambda e: e.activation(out=self.ktok[0:L].rearrange("p a b -> p (a b)"), in_=self.PB[0:L, 0:512], func=AF.Copy),
           rd=[self.bPB[0]], wr=[self.bktok])

        yield
        def fn_num(e):
            ins = None
            for hl in range(2):
                e.matmul(P4[0:L, hl * 256:(hl + 1) * 256], lhsT=self.swT[0:L, hl, 0:L], rhs=vtok_l[0:L, tt, hl, :], start=True, stop=False)
                for dc in range(2):
                    e.matmul(P4[0:L, hl * 256:(hl + 1) * 256], lhsT=self.qs[:, hl, dc, 0:L], rhs=Cb[:, hl, dc, :],
                             start=False, stop=(dc == 1))
                e.matmul(P6[0:L, 4 + hl:5 + hl], lhsT=self.swT[0:L, hl, 0:L], rhs=onesb[0:L, :], start=True, stop=False)
                for dc in range(2):
                    ins = e.matmul(P6[0:L, 4 + hl:5 + hl], lhsT=self.qs[:, hl, dc, 0:L], rhs=nb[:, hl, dc:dc + 1],
                                   start=False, stop=(dc == 1))
            return ins
        op('pe', fn_num, rd=[self.bswT, self.bqs, bv_l[tt], bC, bcst], wr=[b4, b6['den']])
        yield
        s = self.sml
        bs = self.bsml
        op('act', lambda e: e.activation(out=s[0:L, 0:2], in_=P6[0:L, 4:6], func=AF.Abs), rd=[b6['den']], wr=[bs])
        op('dve', lambda e: e.tensor_tensor(out=s[0:L, 0:2], in0=s[0:L, 0:2], in1=P6[0:L, 2:4], op=ALU.max), rd=[bs, b6['tm']], wr=[bs])
        op('dve', lambda e: e.reciprocal(out=s[0:L, 2:4], in_=s[0:L, 0:2]), rd=[bs], wr=[bs])
        for hl in range(2):
            op('dve', lambda e, hl=hl: e.bn_stats(out=self.sm6[0:L, hl, :], in_=P4[0:L, hl * 256:(hl + 1) * 256]), rd=[b4], wr=[self.bsm6])
            op('dve', lambda e, hl=hl: e.bn_aggr(out=self.mv[0:L, hl, :], in_=self.sm6[0:L, hl, :]), rd=[self.bsm6], wr=[self.bmv])
        op('dve', lambda e: e.tensor_tensor(out=s[0:L, 4:6], in0=s[0:L, 2:4], in1=s[0:L, 2:4], op=ALU.mult), rd=[bs], wr=[bs])
        op('dve', lambda e: e.tensor_tensor(out=s[0:L, 4:6], in0=s[0:L, 4:6], in1=self.mv[0:L, :, 1], op=ALU.mult),
           rd=[bs, self.bmv], wr=[bs])
        self.rstd(s[0:L, 4:6], s[0:L, 4:6], L, bs)
        op('dve', lambda e: e.tensor_tensor(out=s[0:L, 6:8], in0=s[0:L, 4:6], in1=s[0:L, 2:4], op=ALU.mult), rd=[bs], wr=[bs])
        op('dve', lambda e: e.scalar_tensor_tensor(out=s[0:L, 8:10], in0=self.mv[0:L, :, 0], scalar=-1.0, in1=s[0:L, 6:8],
                                                   op0=ALU.mult, op1=ALU.mult), rd=[bs, self.bmv], wr=[bs])
        yield
        for hl in range(2):
            op('act', lambda e, hl=hl: e.activation(out=self.hn[0:L, hl, :], in_=P4[0:L, hl * 256:(hl + 1) * 256], func=AF.Identity,
                                                    scale=s[0:L, 6 + hl:7 + hl], bias=s[0:L, 8 + hl:9 + hl]), rd=[b4, bs], wr=[self.bhn])
        op('dve', lambda e: e.tensor_tensor(out=self.mtok[0:L, tt, hp * 512:(hp + 1) * 512], in0=self.hn[0:L].rearrange("p a b -> p (a b)"),
                                            in1=G_l[0:L, tt].rearrange("p a b -> p (a b)"), op=ALU.mult),
           rd=[self.bhn, bG_l[tt]], wr=[self.bmtok[tt][hp]])
        yield
        op('act', lambda e: e.activation(out=self.gb[0:L, :], in_=self.wT[0:L, :, L - 1], func=AF.Copy), rd=[self.bwT], wr=[self.bgb])
        for hl in range(2):
            op('dve', lambda e, hl=hl: e.tensor_scalar(out=self.gv[0:L, hl, :], in0=vtok_l[0:L, tt, hl, :],
                                                       scalar1=self.wT[0:L, hl, L - 1:L], scalar2=None, op0=ALU.mult),
               rd=[self.bwT, bv_l[tt]], wr=[self.bgv])
        for hl in range(2):
            yield

            def fn_dc(e, hl=hl):
                ins = None
                for dc in range(2):
                    e.matmul(P3[:, dc * 256:(dc + 1) * 256], lhsT=self.ktok[0:L, hl, dc * 128:(dc + 1) * 128], rhs=self.gv[0:L, hl, :],
                             start=True, stop=True)
                    ins = e.matmul(P6[:, 6 + 2 * hl + dc:7 + 2 * hl + dc], lhsT=self.ktok[0:L, hl, dc * 128:(dc + 1) * 128],
                                   rhs=self.gb[0:L, hl:hl + 1], start=True, stop=True)
                return ins
            op('pe', fn_dc, rd=[self.bktok, self.bgv, self.bgb], wr=[b3, b6['dn']])
            for dc in range(2):
                op('dve', lambda e, hl=hl, dc=dc: e.scalar_tensor_tensor(out=C[:, hl, dc, :], in0=C[:, hl, dc, :],
                                                                         scalar=self.g0bc[:, hl:hl + 1], in1=P3[:, dc * 256:(dc + 1) * 256],
                                                                         op0=ALU.mult, op1=ALU.add), rd=[b3, self.bg0bc, bC], wr=[bC])
            op('dve', lambda e, hl=hl: e.scalar_tensor_tensor(out=n[:, hl, :], in0=n[:, hl, :], scalar=self.g0bc[:, hl:hl + 1],
                                                              in1=P6[:, 6 + 2 * hl:8 + 2 * hl], op0=ALU.mult, op1=ALU.add),
               rd=[b6['dn'], self.bg0bc, bC], wr=[bC])
        op('act', lambda e: e.activation(out=Cb, in_=C, func=AF.Copy), rd=[bC], wr=[bC])
        op('act', lambda e: e.activation(out=nb, in_=n, func=AF.Copy), rd=[bC], wr=[bC])
        yield
        if samp:
            si = u.sidx
            dma('sp', self.oCs[l, si, hs].rearrange("h (dc p) e -> p h dc e", p=128), C, rd=[bC], chan=bC, final=True)
            self.final.append(bC) if bC not in self.final else None
            self.op('sp', lambda e: e.dma_start(out=self.ons[l, si, hs].rearrange("h (dc p) -> p h dc", p=128), in_=n,
                                                allow_slow_non_contiguous=True), rd=[bC], chan=bC)
            dma('sp', self.oms[l, si, hs].rearrange("(h o) -> h o", o=1), m[:], rd=[bC], chan=bC)
        elif last_blk and tt == NT - 1:
            dma('sp', self.oCp[l, hs].rearrange("h (dc p) e -> p h dc e", p=128), C, rd=[bC], chan=bC, final=True)
            self.op('sp', lambda e: e.dma_start(out=self.onp[l, hs].rearrange("h (dc p) -> p h dc", p=128), in_=n,
                                                allow_slow_non_contiguous=True), rd=[bC], chan=bC)
            dma('sp', self.omp[l, hs].rearrange("(h o) -> h o", o=1), m[:], rd=[bC], chan=bC)

    def conv_block(self, l, us):
        op = self.op
        cw, cb = self.cw, self.cb
        if us[0].sidx is None:
          for cc in range(4):
            op('dve', lambda e, cc=cc: e.tensor_scalar(out=self.yT[:, cc, 0:TB], in0=self.aT[:, cc, 0:TB], scalar1=cw[:, l, cc, 0:1],
                                                       scalar2=cb[:, l, cc:cc + 1], op0=ALU.mult, op1=ALU.add),
               rd=[self.baT, self.bpar], wr=[self.byT])
        for j in range(1, 31 if us[0].sidx is None else 1):
            yield
            for cc in range(4):
                op('dve', lambda e, cc=cc, j=j: e.scalar_tensor_tensor(out=self.yT[:, cc, 0:TB], in0=self.aT[:, cc, j:j + TB],
                                                                       scalar=cw[:, l, cc, j:j + 1], in1=self.yT[:, cc, 0:TB],
                                                                       op0=ALU.mult, op1=ALU.add),
                   rd=[self.baT, self.byT], wr=[self.byT])
        for u in us:
            if u.sidx is None:
                continue
            si = u.sidx
            self.dma('sp', self.aTs[:, :, si, 0:30], self.scv[l, si].rearrange("(cc c) j -> c cc j", c=128), wr=[self.baTs[si]],
                     chan=self.baTs[si])
        for u in us:
            if u.sidx is None:
                continue
            si = u.sidx
            c0_ = u.c0
            yield
            for cc in range(4):
                op('dve', lambda e, cc=cc, si=si: e.tensor_tensor(out=self.ctmp[:, :], in0=self.aTs[:, cc, si, :], in1=cw[:, l, cc, :],
                                                                  op=ALU.mult), rd=[self.baTs[si], self.bpar], wr=[self.bctmp])
                op('dve', lambda e, cc=cc, si=si, c0_=c0_: e.tensor_reduce(out=self.yT[:, cc, c0_:c0_ + 1], in_=self.ctmp[:, :],
                                                                  axis=AX.X, op=ALU.add), rd=[self.bctmp], wr=[self.byT])
                op('dve', lambda e, cc=cc, si=si, c0_=c0_: e.tensor_tensor(out=self.yT[:, cc, c0_:c0_ + 1],
                                                                  in0=self.yT[:, cc, c0_:c0_ + 1], in1=cb[:, l, cc:cc + 1],
                                                                  op=ALU.add), rd=[self.byT, self.bpar], wr=[self.byT])

    def conv_unit(self, l, u):
        op = self.op
        L, tt, c0 = u.L, u.tt, u.c0
        P5, b5 = self.PS[5], self.bPS[5]
        idf = self.cst_sb[:, C_ID:C_ID + 128]

        def fn(e):
            ins = None
            for cc in range(4):
                ins = e.transpose(P5[0:L, cc * 128:(cc + 1) * 128], self.yT[:, cc, c0:c0 + L], idf)
            return ins
        op('pe', fn, rd=[self.byT, self.bcst], wr=[b5])
        yield
        bs = self.bcsm
        op('dve', lambda e: e.bn_stats(out=self.cst6[0:L, :], in_=P5[0:L, :]), rd=[b5], wr=[bs])
        op('dve', lambda e: e.bn_aggr(out=self.cmv[0:L, :], in_=self.cst6[0:L, :]), rd=[bs], wr=[bs])
        self.rstd(self.csml[0:L, 0:1], self.cmv[0:L, 1:2], L, bs)
        op('dve', lambda e: e.scalar_tensor_tensor(out=self.csml[0:L, 1:2], in0=self.cmv[0:L, 0:1], scalar=-1.0, in1=self.csml[0:L, 0:1],
                                                   op0=ALU.mult, op1=ALU.mult), rd=[bs], wr=[bs])
        yield
        op('act', lambda e: e.activation(out=self.yn[0:L, :], in_=P5[0:L, :], func=AF.Identity, scale=self.csml[0:L, 0:1],
                                         bias=self.csml[0:L, 1:2]), rd=[b5, bs], wr=[self.byn])
        op('dve', lambda e: e.tensor_tensor(out=self.yn[0:L, :], in0=self.yn[0:L, :], in1=self.clg[0:L, :], op=ALU.mult),
           rd=[self.byn, self.bcl], wr=[self.byn])
        op('dve', lambda e: e.tensor_tensor(out=self.yn[0:L, :], in0=self.yn[0:L, :], in1=self.clb[0:L, :], op=ALU.add),
           rd=[self.byn, self.bcl], wr=[self.byn])
        op('act', lambda e: e.activation(out=self.yn[0:L, :], in_=self.yn[0:L, :], func=AF.Silu), rd=[self.byn], wr=[self.byn])
        op('dve', lambda e: e.tensor_tensor(out=self.mtok[0:L, tt, 1024:1536], in0=self.yn[0:L, :], in1=self.sz[0:L, tt, :], op=ALU.mult),
           rd=[self.byn, self.bsz[tt]], wr=[self.bmtok[tt][2]])

    def conv_state_out(self, l, src_of, bsrc, dst):
        P5, b5 = self.PS[5], self.bPS[5]
        idf = self.cst_sb[:, C_ID:C_ID + 128]

        def fn(e):
            ins = None
            for cc in range(4):
                ins = e.transpose(P5[0:30, cc * 128:(cc + 1) * 128], src_of(cc), idf)
            return ins
        self.op('pe', fn, rd=[bsrc, self.bcst], wr=[b5])
        self.op('act', lambda e: e.activation(out=self.cvo[:, :], in_=P5[0:30, :], func=AF.Copy), rd=[b5], wr=[self.bcvo])
        self.dma('sp', dst, self.cvo[:, :], rd=[self.bcvo], chan=self.bcvo, final=True)

    def conv_finish(self, l, us, last_blk):
        for u in us:
            if u.sidx is not None:
                si = u.sidx
                self.conv_state_out(l, lambda cc, si=si: self.aTs[:, cc, si, 1:31], self.baTs[si], self.ocvs[l, si])
        if us[0].sidx is not None:
            return
        if last_blk:
            self.conv_state_out(l, lambda cc: self.aT[:, cc, TB:TB + 30], self.baT, self.ocvp[l])
        else:
            self.op('act', lambda e: e.activation(out=self.hist[l][:, :, :], in_=self.aT[:, :, TB:TB + 30], func=AF.Copy),
                    rd=[self.baT], wr=[self.bhist[l]])

    def conv_start(self, l, blk):
        if blk < 0:
            return
        if blk == 0:
            self.op('dve', lambda e: e.memset(self.aT[:, :, 0:30], 0.0), wr=[self.baT])
        else:
            self.op('act', lambda e: e.activation(out=self.aT[:, :, 0:30], in_=self.hist[l][:, :, :], func=AF.Copy),
                    rd=[self.bhist[l]], wr=[self.baT])

    def attn_unit(self, l, u, ci, last_blk):
        op, dma = self.op, self.dma
        L, tt, c0 = u.L, u.tt, u.c0
        cols = slice(c0, c0 + L)
        idb = self.cstb[:, CB_ID:CB_ID + 128]
        idf = self.cst_sb[:, C_ID:C_ID + 128]
        nmp = self.cstb[:, CB_NMP:CB_NMP + 512].rearrange("p (a b) -> p a b", b=128)
        nmc = self.cstb[:, CB_NMC:CB_NMC + 512].rearrange("p (a b) -> p a b", b=128)
        P2, P3, P4, P5 = self.PS[2], self.PS[3], self.PS[4], self.PS[5]
        b2, b3, b4, b5 = self.bPS[2], self.bPS[3], self.bPS[4], self.bPS[5]
        samp = u.sidx is not None
        if samp:
            cur_kT, bcur_kT, cur_v, bcur_v = self.akT[l][:, 0, :], self.bakT[l][0], self.vaug[l][:, 0], self.bvaug[l][0]
        else:
            sl = ci % 2
            cur_kT, bcur_kT, cur_v, bcur_v = self.akT[l][:, sl, :], self.bakT[l][sl], self.vaug[l][:, sl], self.bvaug[l][sl]
        op('act', lambda e: e.activation(out=self.kvb[0:L, :], in_=self.kvf[0:L, tt, 0:128], func=AF.Copy), rd=[self.bkvf[tt]], wr=[self.bkvb])
        op('pe', lambda e: e.transpose(self.PB[:, 512:512 + L], self.kvb[0:L, :], idb[0:L, 0:L]), rd=[self.bkvb, self.bcst], wr=[self.bPB[1]])
        op('dve', lambda e: e.tensor_copy(out=cur_kT[:, 0:L], in_=self.PB[:, 512:512 + L]), rd=[self.bPB[1]], wr=[bcur_kT])
        op('dve', lambda e: e.tensor_copy(out=cur_v[0:L, :, 0:64], in_=self.kvf[0:L, tt, 128:256].rearrange("p (a b) -> p a b", b=64)),
           rd=[self.bkvf[tt]], wr=[bcur_v])
        yield
        blocks = []
        if samp:
            si = u.sidx
            bc = self.bcache
            dma('sp', self.ckf[:, :], self.ck[l, si], wr=[bc], chan=bc)
            dma('sp', self.cvf[:, :], self.cv[l, si], wr=[bc], chan=bc)
            op('act', lambda e: e.activation(out=self.kvb[:, :], in_=self.ckf[:, :], func=AF.Copy), rd=[bc], wr=[self.bkvb])
            op('pe', lambda e: e.transpose(self.PB[:, 512:640], self.kvb[:, :], idb), rd=[self.bkvb, self.bcst], wr=[self.bPB[1]])
            op('dve', lambda e: e.tensor_copy(out=self.akTs[:, :], in_=self.PB[:, 512:640]), rd=[self.bPB[1]], wr=[self.bakTs])
            op('dve', lambda e: e.tensor_copy(out=self.vaugs[:, :, 0:64], in_=self.cvf[:, :].rearrange("p (a b) -> p a b", b=64)),
               rd=[bc], wr=[self.bvaugs])
            blocks.append((self.akTs[:, :], self.vaugs[:, :, :], 128, nmp, [self.bakTs, self.bvaugs]))
            blocks.append((cur_kT, cur_v, 1, None, [bcur_kT, bcur_v]))
            bo = self.bout['oks']
            dma('sp', self.oks[l, si, 0:127, :], self.ck[l, si, 1:128, :], wr=[bo], chan=bo, final=True)
            dma('sp', self.ovs[l, si, 0:127, :], self.cv[l, si, 1:128, :], wr=[bo], chan=bo, final=True)
            dma('sp', self.oks[l, si, 127:128, :], self.kvf[0:1, tt, 0:128], rd=[self.bkvf[tt]], chan=self.bkvf[tt], final=True)
            dma('sp', self.ovs[l, si, 127:128, :], self.kvf[0:1, tt, 128:256], rd=[self.bkvf[tt]], chan=self.bkvf[tt], final=True)
        else:
            if ci > 0:
                ps_ = 1 - sl
                blocks.append((self.akT[l][:, ps_, :], self.vaug[l][:, ps_], 128, nmp, [self.bakT[l][ps_], self.bvaug[l][ps_]]))
            blocks.append((cur_kT, cur_v, 128, nmc, [bcur_kT, bcur_v]))
            if last_blk and tt == NT - 1:
                dma('sp', self.okp[l], self.kvf[:, tt, 0:128], rd=[self.bkvf[tt]], chan=self.bkvf[tt], final=True)
                dma('sp', self.ovp[l], self.kvf[:, tt, 128:256], rd=[self.bkvf[tt]], chan=self.bkvf[tt], final=True)
        for g in range(2):
            gp = slice(64 * g, 64 * g + 64)
            yield
            for bi, (kTa, va, Lk, mask, bufs) in enumerate(blocks):
                PSs, bPSs = P2, b2

                def fn(e, kTa=kTa, Lk=Lk, mask=mask, PSs=PSs, gp=gp):
                    out = PSs[0:Lk, :].rearrange("p (a b) -> p a b", b=128)[:, :, 0:L]
                    ins = e.matmul(out, lhsT=kTa[gp, 0:Lk], rhs=self.aqT[gp, :, cols], start=True, stop=(mask is None))
                    if mask is not None:
                        ins = e.matmul(out, lhsT=idb[0:Lk, 0:Lk], rhs=mask[0:Lk, :, 0:L], start=False, stop=True)
                    return ins
                op('pe', fn, rd=self.baq + [bufs[0], self.bcst], wr=[bPSs])
                op('act', lambda e, Lk=Lk, PSs=PSs, bi=bi: e.activation(
                    out=self.PT[0:Lk, bi, :, 0:L], in_=PSs[0:Lk, :].rearrange("p (a b) -> p a b", b=128)[:, :, 0:L], func=AF.Exp),
                    rd=[bPSs], wr=[self.bPT[bi]])
            Po, bPo = P5, b5
            yield

            def fn_pv(e, Po=Po, g=g):
                ins = None
                for i in range(4):
                    for bi, (kTa, va, Lk, mask, bufs) in enumerate(blocks):
                        ins = e.matmul(Po[0:L, i * 65:(i + 1) * 65], lhsT=self.PT[0:Lk, bi, i, 0:L], rhs=va[0:Lk, g, :],
                                       start=(bi == 0), stop=(bi == len(blocks) - 1))
                return ins
            op('pe', fn_pv, rd=[self.bPT[bi] for bi in range(len(blocks))] + [b[1] for b in [blk_[4] for blk_ in blocks]], wr=[bPo])
            po3 = Po[0:L, 0:260].rearrange("p (a b) -> p a b", b=65)
            yield
            bs = self.basml
            op('dve', lambda e, po3=po3, g=g: e.tensor_tensor(out=self.asml[0:L, 0:4], in0=po3[:, :, 64], in1=self.esk[0:L, l, 4 * g:4 * g + 4],
                                                              op=ALU.add), rd=[bPo, self.bpar], wr=[bs])
            op('dve', lambda e: e.reciprocal(out=self.asml[0:L, 4:8], in_=self.asml[0:L, 0:4]), rd=[bs], wr=[bs])
            for i in range(4):
                op('dve', lambda e, po3=po3, i=i: e.tensor_scalar(out=self.ao[0:L, i, :], in0=po3[:, i, 0:64], scalar1=self.asml[0:L, 4 + i:5 + i],
                                                                  scalar2=None, op0=ALU.mult), rd=[bPo, bs], wr=[self.bao])
            h0 = 1536 + 256 * g
            op('dve', lambda e, g=g, h0=h0: e.tensor_tensor(out=self.mtok[0:L, tt, h0:h0 + 256], in0=self.ao[0:L].rearrange("p a b -> p (a b)"),
                                                            in1=self.saz[0:L, tt, 256 * g:256 * g + 256], op=ALU.mult),
               rd=[self.bao, self.bsaz[tt]], wr=[self.bmtok[tt][3]])

    def merge_T(self, u, rounds=(0, 1, 2, 3)):
        L, tt = u.L, u.tt
        self.transpose_rows(lambda kc: self.mtok[0:L, tt, kc * 128:(kc + 1) * 128], lambda r: self.bmtok[tt][r], self.mT, self.bmT[tt], u,
                            k_evac=1, rounds=rounds)

    def outproj(self, l, us, part):
        op = self.op
        kcs = [0, 1, 2, 3] + list(range(8, 16)) if part == 'A' else [4, 5, 6, 7]
        for j in range(4):
            s = self.get_w()
            W, bW = self.W[s], self.bW[s]
            for u in us:
                L, tt, c0 = u.L, u.tt, u.c0
                k = self.pcount = getattr(self, 'pcount', 0) + 1
                ps, bps = self.PS[k % 2], self.bPS[k % 2]
                for g0 in range(0, len(kcs), 4):
                    self.pull(3)

                    def fn(e, ps=ps, L=L, c0=c0, W=W, g0=g0):
                        ins = None
                        for i_ in range(g0, min(g0 + 4, len(kcs))):
                            kc = kcs[i_]
                            ins = e.matmul(ps[0:L, :], lhsT=self.mT[:, kc, c0:c0 + L], rhs=W[:, kc, :], start=(i_ == 0),
                                           stop=(i_ == len(kcs) - 1))
                        return ins
                    op('pe', fn, rd=[bW, self.bmT[tt]], wr=[bps])
                xa, bx = self.Xap(u, slice(512 * j, 512 * (j + 1)))
                if part == 'A':
                    op('dve', lambda e, xa=xa, ps=ps, L=L: e.scalar_tensor_tensor(out=xa, in0=xa, scalar=ALPHA, in1=ps[0:L, :],
                                                                                  op0=ALU.mult, op1=ALU.add), rd=[bps, bx], wr=[bx])
                else:
                    op('dve', lambda e, xa=xa, ps=ps, L=L: e.tensor_tensor(out=xa, in0=xa, in1=ps[0:L, :], op=ALU.add), rd=[bps, bx], wr=[bx])

    def final_ln(self, l, u, blk):
        op = self.op
        L, tt = u.L, u.tt
        xa, bx = self.Xap(u)
        bs = self.blsm
        for j in range(4):
            xj, _ = self.Xap(u, slice(512 * j, 512 * (j + 1)))
            op('dve', lambda e, xj=xj, j=j: e.bn_stats(out=self.lst[0:L, j, :], in_=xj), rd=[bx], wr=[bs])
        op('dve', lambda e: e.bn_aggr(out=self.lmv[0:L, :], in_=self.lst[0:L].rearrange("p a b -> p (a b)")), rd=[bs], wr=[bs])
        self.rstd(self.lsm[0:L, 0:1], self.lmv[0:L, 1:2], L, bs)
        op('dve', lambda e: e.scalar_tensor_tensor(out=self.lsm[0:L, 1:2], in0=self.lmv[0:L, 0:1], scalar=-1.0, in1=self.lsm[0:L, 0:1],
                                                   op0=ALU.mult, op1=ALU.mult), rd=[bs], wr=[bs])
        op('act', lambda e: e.activation(out=xa, in_=xa, func=AF.Identity, scale=self.lsm[0:L, 0:1], bias=self.lsm[0:L, 1:2]),
           rd=[bx, bs], wr=[bx])

    def final_gain(self, l, us, blk):
        op = self.op
        for j in range(4):
            k = self.lncount = getattr(self, 'lncount', 0) + 1
            lp, bl = self.lnp[k % 2], self.bln[k % 2]
            cs = slice(512 * j, 512 * (j + 1))
            self.dma('sp', lp[:, 0, :], self.p_lng[l, cs].partition_broadcast(128), wr=[bl], chan=bl)
            self.dma('sp', lp[:, 1, :], self.p_lnb[l, cs].partition_broadcast(128), wr=[bl], chan=bl)
            for u in us:
                xj, bx = self.Xap(u, cs)
                L = u.L
                op('dve', lambda e, xj=xj, lp=lp, L=L: e.tensor_tensor(out=xj, in0=xj, in1=lp[0:L, 0, :], op=ALU.mult), rd=[bx, bl], wr=[bx])
                op('dve', lambda e, xj=xj, lp=lp, L=L: e.tensor_tensor(out=xj, in0=xj, in1=lp[0:L, 1, :], op=ALU.add), rd=[bx, bl], wr=[bx])
        for u in us:
            self.final_out(l, u, blk)
            if l < DEPTH - 1:
                self.make_xT(u)

    def final_out(self, l, u, blk):
        L, tt = u.L, u.tt
        xa, bx = self.Xap(u)
        if l == DEPTH - 1:
            if u.sidx is None:
                r0 = blk * TB + tt * 128
                self.dma('sp', self.yp[r0:r0 + 128, :], xa, rd=[bx], chan=bx, final=True)
            else:
                self.dma('sp', self.ys[u.sidx:u.sidx + 1, :], xa, rd=[bx], chan=bx, final=True)

    def seq(self, *gens):
        for g in gens:
            yield from g

    def layer_block(self, blk, l):
        us = self.units(blk)
        last_blk = (blk == self.nblk - 1)
        self.load_params(l)
        if l == 0:
            self.load_x(blk, us)
        self.chk()
        if l == 0:
            for u in us:
                self.make_xT(u)
        self.chk()
        self.conv_start(l, blk)
        T = lambda kind, i: self.inproj_tile(l, kind, i, us)
        T('cu', 0)
        T('cg', 0)
        g_cb = self.conv_block(l, us)
        self.bg.append(g_cb)
        T('kvg', 0)
        for k_, i_ in (('qk', 0), ('qk', 1), ('v', 0), ('o', 0), ('z', 0)):
            T(k_, i_)
        g_m0 = self.seq(*[self.mlstm_unit(l, 0, u, last_blk) for u in us])
        self.bg.append(g_m0)
        for k_, i_ in (('cz', 0), ('aq', 0), ('az', 0)):
            T(k_, i_)
        g_at = self.seq(*[self.attn_unit(l, u, (blk * NT + u.tt if u.sidx is None else None), last_blk) for u in us])
        self.bg.append(g_at)
        for k_, i_ in (('qk', 2), ('qk', 3), ('v', 1), ('o', 1), ('z', 1)):
            T(k_, i_)
        self.drain([g_cb, g_at])
        self.bg.append(self.seq(*[self.conv_unit(l, u) for u in us]))
        self.drain([g_m0])
        g_m1a = self.mlstm_unit(l, 1, us[0], last_blk)
        self.bg.append(g_m1a)
        self.drain()
        g_m1 = self.seq(*[self.mlstm_unit(l, 1, u, last_blk) for u in us[1:]])
        self.bg.append(g_m1)
        for u in us:
            self.merge_T(u, rounds=(0, 2, 3))
        self.outproj(l, us, 'A')
        self.drain()
        self.conv_finish(l, us, last_blk)
        self.chk()
        if self.debug and blk == 0 and l == 0:
            for u in us:
                self.dump("mtok%d" % u.tt, self.mtok[0:u.L, u.tt, :], list(self.bmtok[u.tt]), (u.L, D), BF16)
        for u in us:
            self.merge_T(u, rounds=(1,))
        self.chk()
        self.outproj(l, us, 'B')
        self.chk()
        for u in us:
            self.final_ln(l, u, blk)
        self.final_gain(l, us, blk)
        if self.debug and blk == 0 and l == 0:
            for u in us:
                self.dump("x1_%d" % u.tt, self.X[0:u.L, u.tt, :], self.bX[u.tt], (u.L, D))
        self.chk()

    def build(self):
        self.setup()
        seq = []
        blks = (list(range(-(NS // NSB), 0)) if self.with_sample else []) + list(range(self.nblk))
        for blk in blks:
            for l in range(DEPTH):
                seq += [(l, j) for j in range(NW)] + [(l, NW_IN + j, 'B') for j in range(4)]
        self.w_plan(seq)
        try:
            self.chk()
            for blk in blks:
                for l in range(DEPTH):
                    self.layer_block(blk, l)
        except StopIteration:
            pass
        self.P.emit(self.final)
        return self.nc


_CACHE = {}


def kernel(x_prompt, x_sample, state_C, state_n, state_m, state_conv, cache_k, cache_v,
           w_in, w_out, b_igate, b_fgate, m_norm_g, conv_w, conv_b, conv_ln_g, conv_ln_b,
           sinks, ln_g, ln_b):
    f = lambda a: np.ascontiguousarray(np.asarray(a, dtype=np.float32))
    x_prompt, x_sample = f(x_prompt), f(x_sample)
    if 'nc' not in _CACHE:
        _CACHE['nc'] = K().build()
    nc = _CACHE['nc']
    wt = host_weight_tiles(f(w_in), f(w_out))
    cst, cstB = host_consts()
    scv = np.ascontiguousarray(f(state_conv).transpose(0, 1, 3, 2))
    p_cw = np.ascontiguousarray(f(conv_w).transpose(0, 2, 1).reshape(DEPTH, 4, 128, 31).transpose(0, 2, 1, 3)).reshape(DEPTH, 128, 124)
    p_cb = np.ascontiguousarray(f(conv_b).reshape(DEPTH, 4, 128).transpose(0, 2, 1))
    sC, sn, sm = f(state_C), f(state_n), f(state_m)
    ck = f(cache_k).reshape(DEPTH, 32, 128, 128)
    cv = f(cache_v).reshape(DEPTH, 32, 128, 128)
    in_maps = []
    xp_dummy = np.zeros_like(x_prompt[0])
    for c in range(NCORE):
        b = c % 2
        ss = slice(NS * c, NS * (c + 1))
        in_maps.append({
            "xp": x_prompt[b] if c < 2 else xp_dummy, "xs": np.ascontiguousarray(x_sample[ss, 0, :]), "wt": wt,
            "sC": np.ascontiguousarray(sC[:, ss]), "sn": np.ascontiguousarray(sn[:, ss]), "sm": np.ascontiguousarray(sm[:, ss]),
            "scv": np.ascontiguousarray(scv[:, ss]), "ck": np.ascontiguousarray(ck[:, ss]), "cv": np.ascontiguousarray(cv[:, ss]),
            "cst": cst, "cstB": cstB, "p_bi": f(b_igate), "p_bf": f(b_fgate), "p_mng": f(m_norm_g), "p_cw": p_cw, "p_cb": p_cb,
            "p_clg": f(conv_ln_g), "p_clb": f(conv_ln_b), "p_sk": f(sinks), "p_lng": f(ln_g), "p_lnb": f(ln_b),
        })
    res = run_bass_kernel_spmd(nc, in_maps, core_ids=list(range(NCORE))).results
    cat = lambda k, ax: np.concatenate([r[k] for r in res], axis=ax)
    stack2 = lambda k: np.stack([res[0][k], res[1][k]], axis=1)
    y_prompt = np.stack([res[0]["yp"], res[1]["yp"]], axis=0)
    y_sample = cat("ys", 0).reshape(32, 1, D)
    new_C_p = stack2("oCp")
    new_n_p = stack2("onp")
    new_m_p = stack2("omp")
    new_conv_p = stack2("ocvp")
    new_k_p = stack2("okp").reshape(DEPTH, 2, 128, 2, 64)
    new_v_p = stack2("ovp").reshape(DEPTH, 2, 128, 2, 64)
    new_C_s = cat("oCs", 1)
    new_n_s = cat("ons", 1)
    new_m_s = cat("oms", 1)
    new_conv_s = cat("ocvs", 1)
    new_k_s = cat("oks", 1).reshape(DEPTH, 32, 128, 2, 64)
    new_v_s = cat("ovs", 1).reshape(DEPTH, 32, 128, 2, 64)
    return (y_prompt, y_sample, new_C_p, new_n_p, new_m_p, new_conv_p, new_k_p, new_v_p,
            new_C_s, new_n_s, new_m_s, new_conv_s, new_k_s, new_v_s)
```

```python
import numpy as np
import concourse.bass as bass
import concourse.mybir as mybir
from concourse.bass_utils import run_bass_kernel_spmd

F32 = mybir.dt.float32
BF16 = mybir.dt.bfloat16
ALU = mybir.AluOpType
AF = mybir.ActivationFunctionType
AX = mybir.AxisListType

D = 2048
SEQ = 4096
DEPTH = 2
NCORE = 8
NS = 4
TB = 256
NT = TB // 128
NBLK = SEQ // TB
NSB = 2
NTT = max(NT, NSB)
NCF = max(TB, NSB)
ALPHA = (2 * DEPTH) ** 0.25
EPS = 1e-5
BIG = 30000.0
NW_IN = 16
NW = 20
EPOCH = 3000

T_KVG, T_QK, T_V, T_O, T_Z, T_CU, T_CG, T_CZ, T_AQ, T_AZ = 'kvg', 'qk', 'v', 'o', 'z', 'cu', 'cg', 'cz', 'aq', 'az'
TILE_ORDER = [('cu', 0), ('cg', 0), ('kvg', 0), ('aq', 0), ('az', 0), ('qk', 0), ('qk', 1), ('v', 0), ('o', 0), ('z', 0),
              ('cz', 0), ('qk', 2), ('qk', 3), ('v', 1), ('o', 1), ('z', 1)]

C_ID, C_TRI, C_SEL, C_ONE, C_EPS = 0, 128, 256, 512, 513
NCST = 514
CB_ID, CB_BIGM, CB_NMP, CB_NMC, CB_ONE = 0, 128, 256, 768, 1280
NCSTB = 1281


def host_weight_tiles(w_in, w_out):
    mq, mk, mv, mo, mi, mf, mz = 0, 1024, 2048, 3072, 4096, 4100, 4104
    cu, cg, cz, aq, ak, av, az = 5128, 5640, 6152, 6664, 7176, 7304, 7432
    L = w_in.shape[0]
    out = np.zeros((L, NW, 2048, 512), np.float32)
    for l in range(L):
        W = w_in[l]
        for j, (kind, i) in enumerate(TILE_ORDER):
            t = out[l, j]
            if kind == 'kvg':
                t[:, 0:128] = W[:, ak:ak + 128]
                t[:, 128:256] = W[:, av:av + 128]
                t[:, 256:260] = W[:, mi:mi + 4]
                t[:, 260:264] = W[:, mf:mf + 4]
            elif kind == 'qk':
                t[:, 0:256] = W[:, mq + 256 * i: mq + 256 * (i + 1)]
                t[:, 256:512] = W[:, mk + 256 * i: mk + 256 * (i + 1)]
            elif kind == 'v':
                t[:] = W[:, mv + 512 * i: mv + 512 * (i + 1)]
            elif kind == 'o':
                t[:] = W[:, mo + 512 * i: mo + 512 * (i + 1)]
            elif kind == 'z':
                t[:] = W[:, mz + 512 * i: mz + 512 * (i + 1)]
            elif kind == 'cu':
                t[:] = W[:, cu:cu + 512]
            elif kind == 'cg':
                t[:] = W[:, cg:cg + 512]
            elif kind == 'cz':
                t[:] = W[:, cz:cz + 512]
            elif kind == 'aq':
                for c in range(4):
                    t[:, c * 128: c * 128 + 64] = W[:, aq + 64 * c: aq + 64 * (c + 1)]
                    t[:, c * 128 + 64: c * 128 + 128] = W[:, aq + 64 * (4 + c): aq + 64 * (5 + c)]
            elif kind == 'az':
                t[:] = W[:, az:az + 512]
        for j in range(4):
            out[l, NW_IN + j] = w_out[l][:, 512 * j: 512 * (j + 1)]
    out = out.reshape(L, NW, 16, 128, 512).transpose(0, 1, 3, 2, 4)
    return np.ascontiguousarray(out).reshape(L, NW, 128, 16 * 512)


def host_consts():
    c = np.zeros((128, NCST), np.float32)
    cb = np.zeros((128, NCSTB), np.float32)
    s = np.arange(128)[:, None]
    t = np.arange(128)[None, :]
    c[:, C_ID:C_ID + 128] = (s == t)
    c[:, C_TRI:C_TRI + 128] = (s <= t)
    c[0, C_SEL:C_SEL + 128] = 1.0
    c[1, C_SEL + 128:C_SEL + 256] = 1.0
    c[:, C_ONE] = 1.0
    c[:, C_EPS] = EPS
    cb[:, CB_ID:CB_ID + 128] = (s == t)
    cb[:, CB_BIGM:CB_BIGM + 128] = np.where(s > t, BIG, 0.0)
    nmp = np.where(s < t, -BIG, 0.0)
    nmc = np.where(s > t, -BIG, 0.0)
    for h in range(4):
        cb[:, CB_NMP + 128 * h: CB_NMP + 128 * (h + 1)] = nmp
        cb[:, CB_NMC + 128 * h: CB_NMC + 128 * (h + 1)] = nmc
    cb[:, CB_ONE] = 1.0
    return c, cb


class Buf:
    __slots__ = ('name', 'w', 'r', 'sem', 'cnt')

    def __init__(self, name):
        self.name = name
        self.w = None
        self.r = []
        self.sem = None
        self.cnt = 0


class Op:
    __slots__ = ('eng', 'fn', 'deps', 'dma', 'chan', 'val', 'sig', 'signo', 'idx', 'dmaw')


class Prog:
    ENGS = ('pe', 'act', 'dve', 'pool', 'sp')

    def __init__(self, nc):
        self.nc = nc
        self.ops = []
        self.nbuf = 0

    def buf(self, name=None):
        self.nbuf += 1
        return Buf(name or "b%d" % self.nbuf)

    def op(self, eng, fn, rd=(), wr=(), chan=None):
        o = Op()
        o.eng, o.fn, o.idx = eng, fn, len(self.ops)
        deps = set()
        for b in rd:
            if b.w is not None:
                deps.add(b.w)
        for b in wr:
            if b.w is not None:
                deps.add(b.w)
            deps.update(b.r)
        o.deps = deps
        o.dmaw = {}
        for d in deps:
            p = self.ops[d]
            if p.dma:
                o.dmaw[id(p.chan)] = (p.chan, 16 * p.chan.cnt)
        o.dma = chan is not None
        o.chan = chan
        o.sig = False
        o.signo = 0
        o.val = 0
        if chan is not None:
            chan.cnt += 1
            o.val = 16 * chan.cnt
        for b in rd:
            b.r.append(o.idx)
        for b in wr:
            b.w = o.idx
            b.r = []
        self.ops.append(o)
        return o

    def emit(self, final_chans):
        nc = self.nc
        ops = self.ops
        for o in ops:
            keep = {}
            for d in o.deps:
                p = ops[d]
                if p.dma:
                    continue
                if p.eng == 'pe' and o.eng == 'pe' and not o.dma:
                    continue
                k = ('c', p.eng)
                if k not in keep or keep[k] < d:
                    keep[k] = d
            o.deps = sorted(keep.values())
            for d in o.deps:
                ops[d].sig = True
        cnt = {e: 0 for e in self.ENGS}
        for o in ops:
            if o.sig and not o.dma:
                cnt[o.eng] += 1
                o.signo = cnt[o.eng]
        esems = {e: [nc.alloc_semaphore(name="s_%s_%d" % (e, i)) for i in range((cnt[e] + EPOCH - 1) // EPOCH + 1)]
                 for e in self.ENGS}
        chans = {}
        for o in ops:
            if o.dma and o.chan.sem is None:
                o.chan.sem = nc.alloc_semaphore(name="d_%s_%d" % (o.chan.name, len(chans)))
                chans[id(o.chan)] = o.chan

        def target(p):
            if p.dma:
                return p.chan.sem, p.val
            n = p.signo - 1
            return esems[p.eng][n // EPOCH], n % EPOCH + 1

        by_eng = {e: [o for o in ops if o.eng == e] for e in self.ENGS}

        def run(ename, eng):
            waited = {}
            for o in by_eng[ename]:
                for ch, val in o.dmaw.values():
                    k = id(ch.sem)
                    if waited.get(k, 0) >= val:
                        continue
                    eng.wait_ge(ch.sem, val)
                    waited[k] = val
                for d in o.deps:
                    sem, val = target(ops[d])
                    k = id(sem)
                    if waited.get(k, 0) >= val:
                        continue
                    eng.wait_ge(sem, val)
                    waited[k] = val
                ins = o.fn(eng)
                if o.dma:
                    ins.then_inc(o.chan.sem, 16)
                elif o.sig:
                    sem, _ = target(o)
                    ins.then_inc(sem, 1)
            if ename == 'sp':
                for ch in chans.values():
                    eng.wait_ge(ch.sem, 16 * ch.cnt)

        with nc.Block() as block:
            @block.tensor
            def _(e):
                run('pe', e)

            @block.scalar
            def _(e):
                run('act', e)

            @block.vector
            def _(e):
                run('dve', e)

            @block.gpsimd
            def _(e):
                run('pool', e)

            @block.sync
            def _(e):
                run('sp', e)


class Unit:
    def __init__(self, L, tt, c0, sidx=None):
        self.L, self.tt, self.c0, self.sidx = L, tt, c0, sidx


class K:
    def __init__(self, nblk=NBLK, with_sample=True, debug=False, stage=None):
        self.stage = stage
        self.stage_n = 0
        self.debug = debug
        self.bg = []
        self.nblk = nblk
        self.with_sample = with_sample
        self.debug = debug
        self.nc = nc = bass.Bass("TRN2", target_bir_lowering=False)
        self.P = Prog(nc)
        self.final = []
        self.dbg_outs = []
        di = lambda n, s: nc.dram_tensor(n, list(s), F32, kind="ExternalInput").ap()
        do = lambda n, s: nc.dram_tensor(n, list(s), F32, kind="ExternalOutput").ap()
        self.xp = di("xp", (SEQ, D))
        self.xs = di("xs", (NS, D))
        self.wt = di("wt", (DEPTH, NW, 128, 16 * 512))
        self.wb = nc.dram_tensor("wb", [DEPTH, NW, 128, 16 * 512], BF16, kind="Internal").ap()
        self.bwb = [[self.P.buf("wb%d_%d" % (l, j)) for j in range(NW)] for l in range(DEPTH)]
        self.sC = di("sC", (DEPTH, NS, 4, 256, 256))
        self.sn = di("sn", (DEPTH, NS, 4, 256))
        self.sm = di("sm", (DEPTH, NS, 4))
        self.scv = di("scv", (DEPTH, NS, 512, 30))
        self.ck = di("ck", (DEPTH, NS, 128, 128))
        self.cv = di("cv", (DEPTH, NS, 128, 128))
        self.cst = di("cst", (128, NCST))
        self.cstB = di("cstB", (128, NCSTB))
        self.p_bi = di("p_bi", (DEPTH, 4))
        self.p_bf = di("p_bf", (DEPTH, 4))
        self.p_mng = di("p_mng", (DEPTH, 1024))
        self.p_cw = di("p_cw", (DEPTH, 128, 4 * 31))
        self.p_cb = di("p_cb", (DEPTH, 128, 4))
        self.p_clg = di("p_clg", (DEPTH, 512))
        self.p_clb = di("p_clb", (DEPTH, 512))
        self.p_sk = di("p_sk", (DEPTH, 8))
        self.p_lng = di("p_lng", (DEPTH, D))
        self.p_lnb = di("p_lnb", (DEPTH, D))
        self.yp = do("yp", (SEQ, D))
        self.ys = do("ys", (NS, D))
        self.oCp = do("oCp", (DEPTH, 4, 256, 256))
        self.onp = do("onp", (DEPTH, 4, 256))
        self.omp = do("omp", (DEPTH, 4))
        self.ocvp = do("ocvp", (DEPTH, 30, 512))
        self.okp = do("okp", (DEPTH, 128, 128))
        self.ovp = do("ovp", (DEPTH, 128, 128))
        self.oCs = do("oCs", (DEPTH, NS, 4, 256, 256))
        self.ons = do("ons", (DEPTH, NS, 4, 256))
        self.oms = do("oms", (DEPTH, NS, 4))
        self.ocvs = do("ocvs", (DEPTH, NS, 30, 512))
        self.oks = do("oks", (DEPTH, NS, 128, 128))
        self.ovs = do("ovs", (DEPTH, NS, 128, 128))
        self.alloc()

    def chk(self):
        self.stage_n += 1
        if self.stage is not None and self.stage_n >= self.stage:
            raise StopIteration

    def dump(self, name, ap, buf, shape, dt=F32):
        o = self.nc.dram_tensor("dbg_" + name, list(shape), dt, kind="ExternalOutput").ap()
        bufs = buf if isinstance(buf, list) else [buf]
        self.dma('sp', o, ap, rd=bufs, chan=bufs[0], final=True)

    def sb(self, name, shape, dt=F32):
        return self.nc.alloc_sbuf_tensor(name, list(shape), dt)

    def B(self, name=None):
        return self.P.buf(name)

    def op(self, eng, fn, rd=(), wr=(), chan=None):
        return self.P.op(eng, fn, rd, wr, chan)

    def rstd(self, out, in_, L, b):
        epsc = self.cst_sb[0:L, C_EPS:C_EPS + 1]
        self.op('act', lambda e: e.activation(out=out, in_=in_, func=AF.Ln, bias=epsc), rd=[b, self.bcst], wr=[b])
        self.op('act', lambda e: e.activation(out=out, in_=out, func=AF.Exp, scale=-0.5), rd=[b], wr=[b])

    def pull(self, n=1):
        for _ in range(n):
            for g in list(self.bg):
                try:
                    next(g)
                except StopIteration:
                    self.bg.remove(g)

    def drain(self, gens=None):
        while True:
            act = [g for g in self.bg if gens is None or g in gens]
            if not act:
                return
            self.pull()

    def dma(self, q, out, in_, rd=(), wr=(), chan=None, final=False):
        if final and chan not in self.final:
            self.final.append(chan)
        return self.op(q, lambda e, o=out, i=in_: e.dma_start(out=o, in_=i), rd=rd, wr=wr, chan=chan)

    def alloc(self):
        nc = self.nc
        sb, B = self.sb, self.B
        self.PS = [nc.alloc_psum_tensor("ps%d" % i, [128, 512], F32) for i in range(7)]
        self.PB = nc.alloc_psum_tensor("psb", [128, 1024], BF16)
        self.bPS = [B("ps%d" % i) for i in range(7)]
        _b = B("psb")
        self.bPB = [_b, _b]
        self.bP6 = {k: self.bPS[6] for k in ('b', 'tm', 'den', 'dn', 'arow', 'brow')}
        self.cst_sb = sb("cst_sb", [128, NCST])
        self.cstb = sb("cstb", [128, NCSTB], BF16)
        self.bcst = B("cst")
        self.bcstb = B("cstb")
        self.NSLOT = 3
        self.W = [sb("w%d" % i, [128, 16, 512], BF16) for i in range(self.NSLOT)]
        self.bW = [B("w%d" % i) for i in range(self.NSLOT)]
        self.wcount = 0
        self.X = sb("X", [128, NTT, D])
        self.bX = [B("X%d" % i) for i in range(NTT)]
        self.xT = sb("xT", [128, 16, NCF], BF16)
        self.bxT = [B("xT%d" % i) for i in range(NTT)]
        self.mT = self.xT
        self.bmT = self.bxT
        self.mtok = sb("mtok", [128, NTT, D], BF16)
        self.bmtok = [[B("mtok%d_%d" % (i, j)) for j in range(4)] for i in range(NTT)]
        self.xb = self.mtok[:, 0, :]
        self.bxb = B("xb")
        self.qT_ = [sb("qT%d" % p, [128, 2, 2, NCF], BF16) for p in range(2)]
        self.kT_ = [sb("kT%d" % p, [128, 2, 2, NCF], BF16) for p in range(2)]
        self.bqT_ = [[[B() for _ in range(2)] for _ in range(2)] for p in range(2)]
        self.bkT_ = [[[B() for _ in range(2)] for _ in range(2)] for p in range(2)]
        self.vtok_ = [sb("vtok%d" % p, [128, NTT, 2, 256], BF16) for p in range(2)]
        self.bv_ = [[B() for _ in range(NTT)] for p in range(2)]
        self.G_ = [sb("G%d" % p, [128, NTT, 2, 256]) for p in range(2)]
        self.bG_ = [[B() for _ in range(NTT)] for p in range(2)]
        self.gtmp = sb("gtmp", [128, 512])
        self.bgtmp = B()
        self.graw = sb("graw", [128, NTT, 8])
        self.ig = sb("ig", [128, NTT, 4])
        self.sp = sb("spl", [128, NTT, 4])
        self.bgate = [B() for _ in range(NTT)]
        self.a_sb = sb("a_sb", [128, 2]); self.ba = B()
        self.arow = sb("arow", [2, 128]); self.barow = B()
        self.Mrow = sb("Mrow", [2, 128]); self.bMrow = B()
        self.mrow = sb("mrow", [2, 128]); self.bmrow = B()
        self.w0row = sb("w0row", [2, 128]); self.bw0row = B()
        self.emrow = sb("emrow", [2, 128]); self.bemrow = B()
        self.wT = sb("wT", [128, 2, 128]); self.bwT = B()
        self.swT = sb("swT", [128, 2, 128], BF16); self.bswT = B()
        self.qs = sb("qs", [128, 2, 2, 128], BF16); self.bqs = B()
        self.g0bc = sb("g0bc", [128, 2]); self.bg0bc = B()
        self.ktok = sb("ktok", [128, 2, 256], BF16); self.bktok = B()
        self.gv = sb("gv", [128, 2, 256], BF16); self.bgv = B()
        self.gb = sb("gb", [128, 2], BF16); self.bgb = B()
        self.sm6 = sb("sm6", [128, 2, 6]); self.bsm6 = B()
        self.mv = sb("mv", [128, 2, 2]); self.bmv = B()
        self.sml = sb("sml", [128, 16]); self.bsml = B()
        self.hn = sb("hn", [128, 2, 256]); self.bhn = B()
        self.C = [sb("C%d" % l, [128, 4, 2, 256]) for l in range(DEPTH)]
        self.n = [sb("n%d" % l, [128, 4, 2]) for l in range(DEPTH)]
        self.Cb = [sb("Cb%d" % l, [128, 4, 2, 256], BF16) for l in range(DEPTH)]
        self.nb = [sb("nb%d" % l, [128, 4, 2], BF16) for l in range(DEPTH)]
        self.m = [[sb("m%d_%d" % (l, hp), [2, 1]) for hp in range(2)] for l in range(DEPTH)]
        self.bC = [[B() for _ in range(2)] for _ in range(DEPTH)]
        self.Cs = sb("Cs", [128, 2, 2, 256]); self.ns = sb("ns", [128, 2, 2])
        self.Csb = sb("Csb", [128, 2, 2, 256], BF16); self.nsb = sb("nsb", [128, 2, 2], BF16)
        self.ms = sb("ms", [2, 1]); self.bCs = B("Cs")
        self.aT = sb("aT", [128, 4, 30 + TB])
        self.baT = B("aT")
        self.aTs = sb("aTs", [128, 4, NS, 31])
        self.baTs = [B() for _ in range(NS)]
        self.bcu = [B() for _ in range(4)]
        self.cuF = sb("cuF", [128, 4, NCF])
        self.sg = sb("sg", [128, NCF]); self.bsg = B()
        self.yT = sb("yT", [128, 4, NCF]); self.byT = B("yT")
        self.ctmp = sb("ctmp", [128, 31]); self.bctmp = B()
        self.sz = sb("sz", [128, NTT, 512], BF16); self.bsz = [B() for _ in range(NTT)]
        self.yn = sb("yn", [128, 512]); self.byn = B()
        self.cst6 = sb("cst6", [128, 6]); self.cmv = sb("cmv", [128, 2]); self.csml = sb("csml", [128, 4]); self.bcsm = B()
        self.cvo = sb("cvo", [30, 512]); self.bcvo = B("cvo")
        self.aqT = sb("aqT", [128, 4, NCF], BF16); self.baq = [B() for _ in range(4)]
        self.kvf = sb("kvf", [128, NTT, 256]); self.bkvf = [B() for _ in range(NTT)]
        self.kvb = sb("kvb", [128, 128], BF16); self.bkvb = B()
        self.akT = [sb("akT%d" % l, [128, 2, 128], BF16) for l in range(DEPTH)]
        self.bakT = [[B(), B()] for l in range(DEPTH)]
        self.vaug = [sb("vaug%d" % l, [128, 2, 2, 65], BF16) for l in range(DEPTH)]
        self.bvaug = [[B(), B()] for l in range(DEPTH)]
        self.hist = [sb("hist%d" % l, [128, 4, 30]) for l in range(DEPTH)]
        self.bhist = [B() for l in range(DEPTH)]
        self.akTs = sb("akTs", [128, 128], BF16); self.vaugs = sb("vaugs", [128, 2, 65], BF16)
        self.ckf = sb("ckf", [128, 128]); self.cvf = sb("cvf", [128, 128]); self.bcache = B("cache")
        self.bakTs = B(); self.bvaugs = B()
        self.saz = sb("saz", [128, NTT, 512], BF16); self.bsaz = [B() for _ in range(NTT)]
        self.PT = sb("PT", [128, 2, 4, 128], BF16); self.bPT = [B(), B()]
        self.asml = sb("asml", [128, 8]); self.basml = B()
        self.ao = sb("ao", [128, 4, 64]); self.bao = B()
        self.bi_bc = sb("bi_bc", [128, DEPTH, 4]); self.bf_bc = sb("bf_bc", [128, DEPTH, 4])
        self.esk = sb("esk", [128, DEPTH, 8])
        self.mng = sb("mng", [128, 1024])
        self.cw = sb("cw", [128, DEPTH, 4, 31]); self.cb = sb("cb", [128, DEPTH, 4])
        self.clg = sb("clg", [128, 512]); self.clb = sb("clb", [128, 512])
        self.lnp = [sb("lnp%d" % i, [128, 2, 512]) for i in range(2)]
        self.bpar = B("par"); self.bmng = B("mng"); self.bcl = B("cl"); self.bln = [B("ln0"), B("ln1")]
        self.lst = sb("lst", [128, 4, 6]); self.lmv = sb("lmv", [128, 2]); self.lsm = sb("lsm", [128, 4]); self.blsm = B()
        self.bout = {k: B(k) for k in ('oCp', 'onp', 'omp', 'okp', 'ovp', 'oms', 'oks', 'ovs')}

    def setup(self):
        op, dma = self.op, self.dma
        dma('sp', self.cst_sb[:], self.cst, wr=[self.bcst], chan=self.bcst)
        dma('pool', self.cstb[:], self.cstB, wr=[self.bcstb], chan=self.bcstb)
        op('dve', lambda e: e.tensor_copy(out=self.cst_sb[:, C_ONE:C_ONE + 1], in_=self.cst_sb[:, C_ONE:C_ONE + 1]),
           rd=[self.bcst, self.bcstb], wr=[self.bcst])
        bp = self.bpar

        def bc(dst, src):
            dma('sp', dst, src.partition_broadcast(128), wr=[bp], chan=bp)
        bc(self.bi_bc[:].rearrange("p l f -> p (l f)"), self.p_bi.rearrange("l f -> (l f)"))
        bc(self.bf_bc[:].rearrange("p l f -> p (l f)"), self.p_bf.rearrange("l f -> (l f)"))
        bc(self.esk[:].rearrange("p l f -> p (l f)"), self.p_sk.rearrange("l f -> (l f)"))
        for l in range(DEPTH):
            dma('sp', self.cw[:, l].rearrange("p a b -> p (a b)"), self.p_cw[l], wr=[bp], chan=bp)
            dma('sp', self.cb[:, l], self.p_cb[l], wr=[bp], chan=bp)
        op('act', lambda e: e.activation(out=self.esk[:], in_=self.esk[:], func=AF.Exp), rd=[bp], wr=[bp])
        for l in range(DEPTH):
            for hp in range(2):
                hs = slice(2 * hp, 2 * hp + 2)
                b = self.bC[l][hp]
                op('dve', lambda e, l=l, hs=hs: e.memset(self.C[l][:, hs], 0.0), wr=[b])
                op('dve', lambda e, l=l, hs=hs: e.memset(self.n[l][:, hs], 0.0), wr=[b])
                op('dve', lambda e, l=l, hs=hs: e.memset(self.Cb[l][:, hs], 0.0), wr=[b])
                op('dve', lambda e, l=l, hs=hs: e.memset(self.nb[l][:, hs], 0.0), wr=[b])
                op('dve', lambda e, l=l, hp=hp: e.memset(self.m[l][hp][:], 0.0), wr=[b])
        for l in range(DEPTH):
            for r in range(2):
                op('dve', lambda e, l=l, r=r: e.memset(self.vaug[l][:, r, :, 64:65], 1.0), wr=[self.bvaug[l][r]])
        op('dve', lambda e: e.memset(self.vaugs[:, :, 64:65], 1.0), wr=[self.bvaugs])

    def load_params(self, l):
        dma = self.dma
        dma('sp', self.mng[:, :], self.p_mng[l].partition_broadcast(128), wr=[self.bmng], chan=self.bmng)
        dma('sp', self.clg[:, :], self.p_clg[l].partition_broadcast(128), wr=[self.bcl], chan=self.bcl)
        dma('sp', self.clb[:, :], self.p_clb[l].partition_broadcast(128), wr=[self.bcl], chan=self.bcl)

    def w_plan(self, seq):
        seen = set()
        for ent in seq:
            l, j = ent[0], ent[1]
            if (l, j) not in seen:
                seen.add((l, j))
                self.dma('pool', self.wb[l, j], self.wt[l, j], wr=[self.bwb[l][j]], chan=self.bwb[l][j])
        self.wseq = seq
        self.w_issued = 0
        self.w_used = 0

    def get_w(self):
        while self.w_issued < len(self.wseq) and self.w_issued < self.w_used + self.NSLOT:
            ent = self.wseq[self.w_issued]
            l, j = ent[0], ent[1]
            s = self.w_issued % self.NSLOT
            if len(ent) > 2 and ent[2] == 'B':
                self.dma('pool', self.W[s][:, 4:8, :], self.wb[l, j].rearrange("p (a b) -> p a b", b=512)[:, 4:8, :],
                         rd=[self.bwb[l][j]], wr=[self.bW[s]], chan=self.bW[s])
            else:
                self.dma('pool', self.W[s][:].rearrange("p a b -> p (a b)"), self.wb[l, j], rd=[self.bwb[l][j]], wr=[self.bW[s]],
                         chan=self.bW[s])
            self.w_issued += 1
        s = self.w_used % self.NSLOT
        self.w_used += 1
        return s

    def units(self, blk):
        if blk < 0:
            s0 = NSB * (blk + NS // NSB)
            return [Unit(1, j, j, sidx=s0 + j) for j in range(NSB)]
        return [Unit(128, tt, tt * 128) for tt in range(NT)]

    def Xap(self, u, cols=slice(0, D)):
        return self.X[0:u.L, u.tt, cols], self.bX[u.tt]

    def load_x(self, blk, us):
        for u in us:
            xa, bx = self.Xap(u)
            if u.sidx is None:
                r0 = blk * TB + u.tt * 128
                self.dma('sp', xa, self.xp[r0:r0 + 128, :], wr=[bx], chan=bx)
            else:
                self.dma('sp', xa, self.xs[u.sidx:u.sidx + 1, :], wr=[bx], chan=bx)

    def transpose_rows(self, src_of, bsrc, dst, bdst, u, k_evac=0, rounds=(0, 1, 2, 3)):
        L, c0 = u.L, u.c0
        idb = self.cstb[:, CB_ID:CB_ID + 128]
        for r in rounds:
            h = r % 2
            pb = self.PB[:, h * 512:(h + 1) * 512]

            def fn(e, r=r, pb=pb):
                ins = None
                for j in range(4):
                    ins = e.transpose(pb[:, j * 128:j * 128 + L], src_of(4 * r + j), idb[0:L, 0:L])
                return ins
            bs_r = bsrc(r) if callable(bsrc) else bsrc
            self.op('pe', fn, rd=[bs_r, self.bcst] if not isinstance(bs_r, list) else bs_r + [self.bcst], wr=[self.bPB[h]])
            src = pb.rearrange("p (a b) -> p a b", b=128)[:, :, 0:L]
            out = dst[:, 4 * r:4 * r + 4, c0:c0 + L]
            if (r + k_evac) % 2 == 0:
                self.op('act', lambda e, o=out, s=src: e.activation(out=o, in_=s, func=AF.Copy), rd=[self.bPB[h]], wr=[bdst])
            else:
                self.op('dve', lambda e, o=out, s=src: e.tensor_copy(out=o, in_=s), rd=[self.bPB[h]], wr=[bdst])

    def make_xT(self, u):
        L = u.L
        xa, bx = self.Xap(u)
        bxb = list(self.bmtok[0])
        self.op('act', lambda e: e.activation(out=self.xb[0:L, :], in_=xa, func=AF.Copy), rd=[bx], wr=bxb)
        self.transpose_rows(lambda kc: self.xb[0:L, kc * 128:(kc + 1) * 128], bxb, self.xT, self.bxT[u.tt], u)

    def inproj_tile(self, l, kind, i, us):
        s = self.get_w()
        W, bW = self.W[s], self.bW[s]
        op = self.op
        pp_ = (i // 2) if kind == 'qk' else (i if kind in ('v', 'o', 'z') else 0)
        self.qT, self.kT, self.bqT, self.bkT = self.qT_[pp_], self.kT_[pp_], self.bqT_[pp_], self.bkT_[pp_]
        self.vtok, self.bv, self.G, self.bG = self.vtok_[pp_], self.bv_[pp_], self.G_[pp_], self.bG_[pp_]
        hs_samp = us[0].sidx is not None
        ncols = NSB if hs_samp else TB
        assert ncols <= 512
        bxT_all = [self.bxT[u.tt] for u in us]
        one = self.cst_sb[:, C_ONE:C_ONE + 1]
        if kind in ('qk', 'cu', 'cg', 'aq'):
            for ec in range(4):
                self.pull()
                k = self.pcount = getattr(self, 'pcount', 0) + 1
                ps, bps = self.PS[k % 2], self.bPS[k % 2]

                def fn(e, ec=ec, ps=ps):
                    ins = None
                    for kc in range(16):
                        ins = e.matmul(ps[:, 0:ncols], lhsT=W[:, kc, ec * 128:(ec + 1) * 128], rhs=self.xT[:, kc, 0:ncols],
                                       start=(kc == 0), stop=(kc == 15))
                    return ins
                op('pe', fn, rd=[bW] + bxT_all, wr=[bps])
                src = ps[:, 0:ncols]
                if kind == 'qk':
                    hl = i % 2
                    if ec < 2:
                        op('act', lambda e, s_=src, o=self.qT[:, hl, ec, 0:ncols]: e.activation(out=o, in_=s_, func=AF.Copy),
                           rd=[bps], wr=[self.bqT[hl][ec]])
                    else:
                        op('act', lambda e, s_=src, o=self.kT[:, hl, ec - 2, 0:ncols]: e.activation(out=o, in_=s_, func=AF.Copy, scale=1.0 / 16.0),
                           rd=[bps], wr=[self.bkT[hl][ec - 2]])
                elif kind == 'cu':
                    op('act', lambda e, s_=src, o=self.cuF[:, ec, 0:ncols]: e.activation(out=o, in_=s_, func=AF.Copy),
                       rd=[bps], wr=[self.bcu[ec]])
                elif kind == 'cg':
                    op('act', lambda e, s_=src: e.activation(out=self.sg[:, 0:ncols], in_=s_, func=AF.Sigmoid),
                       rd=[bps], wr=[self.bsg])
                    if not hs_samp:
                        op('dve', lambda e, ec=ec: e.tensor_tensor(out=self.aT[:, ec, 30:30 + TB], in0=self.cuF[:, ec, 0:TB],
                                                                   in1=self.sg[:, 0:TB], op=ALU.mult),
                           rd=[self.bsg, self.bcu[ec]], wr=[self.baT])
                    else:
                        s0 = us[0].sidx
                        op('dve', lambda e, ec=ec, s0=s0: e.tensor_tensor(out=self.aTs[:, ec, s0:s0 + NSB, 30], in0=self.cuF[:, ec, 0:NSB],
                                                                          in1=self.sg[:, 0:NSB], op=ALU.mult),
                           rd=[self.bsg, self.bcu[ec]], wr=[self.baTs[u_.sidx] for u_ in us])
                elif kind == 'aq':
                    op('act', lambda e, s_=src, o=self.aqT[:, ec, 0:ncols]: e.activation(out=o, in_=s_, func=AF.Copy, scale=0.125),
                       rd=[bps], wr=[self.baq[ec]])
            return
        for u in us:
            L, tt, c0 = u.L, u.tt, u.c0
            self.pull()
            k = self.pcount = getattr(self, 'pcount', 0) + 1
            ps, bps = self.PS[k % 2], self.bPS[k % 2]

            def fn(e, ps=ps, L=L, c0=c0):
                ins = None
                for kc in range(16):
                    ins = e.matmul(ps[0:L, :], lhsT=self.xT[:, kc, c0:c0 + L], rhs=W[:, kc, :], start=(kc == 0), stop=(kc == 15))
                return ins
            op('pe', fn, rd=[bW, self.bxT[tt]], wr=[bps])
            src = ps[0:L, :]
            if kind == 'v':
                op('act', lambda e, s_=src, o=self.vtok[0:L, tt].rearrange("p a b -> p (a b)"): e.activation(out=o, in_=s_, func=AF.Copy),
                   rd=[bps], wr=[self.bv[tt]])
            elif kind == 'o':
                g = self.G[0:L, tt].rearrange("p a b -> p (a b)")
                op('act', lambda e, s_=src, g=g: e.activation(out=g, in_=s_, func=AF.Sigmoid), rd=[bps], wr=[self.bG[tt]])
                op('dve', lambda e, g=g, L=L: e.tensor_tensor(out=g, in0=g, in1=self.mng[0:L, i * 512:(i + 1) * 512], op=ALU.mult),
                   rd=[self.bG[tt], self.bmng], wr=[self.bG[tt]])
            elif kind == 'z':
                g = self.G[0:L, tt].rearrange("p a b -> p (a b)")
                op('act', lambda e, s_=src, L=L: e.activation(out=self.gtmp[0:L, :], in_=s_, func=AF.Silu), rd=[bps], wr=[self.bgtmp])
                op('dve', lambda e, g=g, L=L: e.tensor_tensor(out=g, in0=g, in1=self.gtmp[0:L, :], op=ALU.mult),
                   rd=[self.bG[tt], self.bgtmp], wr=[self.bG[tt]])
            elif kind == 'cz':
                op('act', lambda e, s_=src, o=self.sz[0:L, tt, :]: e.activation(out=o, in_=s_, func=AF.Silu), rd=[bps], wr=[self.bsz[tt]])
            elif kind == 'az':
                op('act', lambda e, s_=src, o=self.saz[0:L, tt, :]: e.activation(out=o, in_=s_, func=AF.Silu), rd=[bps], wr=[self.bsaz[tt]])
            elif kind == 'kvg':
                bg = self.bgate[tt]
                op('act', lambda e, ps=ps, o=self.kvf[0:L, tt, :], L=L: e.activation(out=o, in_=ps[0:L, 0:256], func=AF.Copy),
                   rd=[bps], wr=[self.bkvf[tt]])
                op('dve', lambda e, ps=ps, L=L, tt=tt: e.tensor_tensor(out=self.ig[0:L, tt, :], in0=ps[0:L, 256:260],
                                                                      in1=self.bi_bc[0:L, l, :], op=ALU.add),
                   rd=[bps, self.bpar], wr=[bg])
                op('dve', lambda e, ps=ps, L=L, tt=tt: e.tensor_tensor(out=self.sp[0:L, tt, :], in0=ps[0:L, 260:264],
                                                                      in1=self.bf_bc[0:L, l, :], op=ALU.add),
                   rd=[bps, self.bpar], wr=[bg])
                op('act', lambda e, L=L, tt=tt: e.activation(out=self.sp[0:L, tt, :], in_=self.sp[0:L, tt, :], func=AF.Exp, scale=-1.0),
                   rd=[bg], wr=[bg])
                op('act', lambda e, L=L, tt=tt: e.activation(out=self.sp[0:L, tt, :], in_=self.sp[0:L, tt, :], func=AF.Ln,
                                                             bias=one[0:L, :]), rd=[bg, self.bcst], wr=[bg])

    def mlstm_unit(self, l, hp, u, last_blk):
        op, dma = self.op, self.dma
        qT_l, kT_l, vtok_l, G_l = self.qT_[hp], self.kT_[hp], self.vtok_[hp], self.G_[hp]
        bqT_l, bkT_l, bv_l, bG_l = self.bqT_[hp], self.bkT_[hp], self.bv_[hp], self.bG_[hp]
        L, tt, c0 = u.L, u.tt, u.c0
        cols = slice(c0, c0 + L)
        hs = slice(2 * hp, 2 * hp + 2)
        idf = self.cst_sb[:, C_ID:C_ID + 128]
        tri = self.cst_sb[:, C_TRI:C_TRI + 128]
        idb = self.cstb[:, CB_ID:CB_ID + 128]
        bigb = self.cstb[:, CB_BIGM:CB_BIGM + 128]
        onesb = self.cstb[:, CB_ONE:CB_ONE + 1]
        sel = lambda hl: self.cst_sb[0:2, C_SEL + 128 * hl:C_SEL + 128 * (hl + 1)]
        bcst = self.bcst
        P2, P3, P4, P5, P6 = self.PS[2], self.PS[3], self.PS[4], self.PS[5], self.PS[6]
        b2, b3, b4, b5 = self.bPS[2], self.bPS[3], self.bPS[4], self.bPS[5]
        b6 = self.bP6
        samp = u.sidx is not None
        if not samp:
            C, n, Cb, nb, m, bC = self.C[l][:, hs], self.n[l][:, hs], self.Cb[l][:, hs], self.nb[l][:, hs], self.m[l][hp], self.bC[l][hp]
        else:
            si = u.sidx
            C, n, Cb, nb, m, bC = self.Cs[:], self.ns[:], self.Csb[:], self.nsb[:], self.ms, self.bCs
            dma('sp', C, self.sC[l, si, hs].rearrange("h (dc p) e -> p h dc e", p=128), wr=[bC], chan=bC)
            self.op('sp', lambda e: e.dma_start(out=n, in_=self.sn[l, si, hs].rearrange("h (dc p) -> p h dc", p=128),
                                                allow_slow_non_contiguous=True), wr=[bC], chan=bC)
            dma('sp', m[:], self.sm[l, si, hs].rearrange("(h o) -> h o", o=1), wr=[bC], chan=bC)
            op('act', lambda e: e.activation(out=Cb, in_=C, func=AF.Copy), rd=[bC], wr=[bC])
            op('act', lambda e: e.activation(out=nb, in_=n, func=AF.Copy), rd=[bC], wr=[bC])
        bg = self.bgate[tt]
        sp_ = self.sp[0:L, tt, hs]
        op('pe', lambda e: e.matmul(P6[0:L, 0:2], lhsT=tri[0:L, 0:L], rhs=sp_, start=True, stop=True), rd=[bg, bcst], wr=[b6['b']])
        yield
        op('dve', lambda e: e.tensor_tensor(out=self.a_sb[0:L, :], in0=self.ig[0:L, tt, hs], in1=P6[0:L, 0:2], op=ALU.add),
           rd=[bg, b6['b']], wr=[self.ba])
        op('pe', lambda e: e.matmul(P6[0:2, 16:16 + L], lhsT=self.a_sb[0:L, :], rhs=idf[0:L, 0:L], start=True, stop=True),
           rd=[self.ba, bcst], wr=[b6['arow']])
        op('pe', lambda e: e.matmul(P6[0:2, 144:144 + L], lhsT=sp_, rhs=tri[0:L, 0:L], start=True, stop=True),
           rd=[bg, bcst], wr=[b6['brow']])
        op('dve', lambda e: e.tensor_copy(out=self.arow[:, 0:L], in_=P6[0:2, 16:16 + L]), rd=[b6['arow']], wr=[self.barow])
        yield
        op('dve', lambda e: e.tensor_tensor_scan(out=self.Mrow[:, 0:L], data0=self.arow[:, 0:L], data1=self.arow[:, 0:L],
                                                 initial=m[:], op0=ALU.max, op1=ALU.max), rd=[self.barow, bC], wr=[self.bMrow])
        op('dve', lambda e: e.tensor_tensor(out=self.mrow[:, 0:L], in0=self.Mrow[:, 0:L], in1=P6[0:2, 144:144 + L], op=ALU.subtract),
           rd=[self.bMrow, b6['brow']], wr=[self.bmrow])
        yield
        op('act', lambda e: e.activation(out=self.w0row[:, 0:L], in_=self.Mrow[:, 0:L], func=AF.Exp, scale=-1.0, bias=m[:]),
           rd=[self.bMrow, bC], wr=[self.bw0row])
        op('act', lambda e: e.activation(out=self.emrow[:, 0:L], in_=self.mrow[:, 0:L], func=AF.Exp, scale=-1.0),
           rd=[self.bmrow], wr=[self.bemrow])
        op('dve', lambda e: e.tensor_copy(out=m[:], in_=self.mrow[:, L - 1:L]), rd=[self.bmrow], wr=[bC])
        yield
        op('pe', lambda e: e.matmul(P6[0:L, 2:4], lhsT=self.emrow[0:2, 0:L], rhs=idf[0:2, 0:2], start=True, stop=True),
           rd=[self.bemrow, bcst], wr=[b6['tm']])

        def fn_bc(e):
            ins = None
            for hl in range(2):
                e.matmul(P3[0:L, hl * 128:hl * 128 + L], lhsT=sel(hl)[:, 0:L], rhs=self.Mrow[0:2, 0:L], start=True, stop=False)
                e.matmul(P3[0:L, hl * 128:hl * 128 + L], lhsT=idb[0:L, 0:L], rhs=bigb[0:L, 0:L], start=False, stop=True)
                ins = e.matmul(P3[:, 256 + hl * 128:256 + hl * 128 + L], lhsT=sel(hl), rhs=self.w0row[0:2, 0:L], start=True, stop=True)
            return ins
        op('pe', fn_bc, rd=[self.bMrow, self.bw0row, bcst], wr=[b3])
        yield
        for hl in range(2):
            op('act', lambda e, hl=hl: e.activation(out=self.wT[0:L, hl, 0:L], in_=P3[0:L, hl * 128:hl * 128 + L], func=AF.Exp,
                                                    scale=-1.0, bias=self.a_sb[0:L, hl:hl + 1]), rd=[b3, self.ba], wr=[self.bwT])

        yield
        def fn_s(e):
            ins = None
            for hl in range(2):
                for dc in range(2):
                    ins = e.matmul(P4[0:L, hl * 128:hl * 128 + L], lhsT=kT_l[:, hl, dc, cols], rhs=qT_l[:, hl, dc, cols],
                                   start=(dc == 0), stop=(dc == 1))
            return ins
        op('pe', fn_s, rd=bkT_l[0] + bkT_l[1] + bqT_l[0] + bqT_l[1], wr=[b4])
        for hl in range(2):
            op('dve', lambda e, hl=hl: e.tensor_tensor(out=self.swT[0:L, hl, 0:L], in0=P4[0:L, hl * 128:hl * 128 + L],
                                                       in1=self.wT[0:L, hl, 0:L], op=ALU.mult), rd=[b4, self.bwT], wr=[self.bswT])
        yield
        for hl in range(2):
            for dc in range(2):
                op('dve', lambda e, hl=hl, dc=dc: e.tensor_tensor(out=self.qs[:, hl, dc, 0:L], in0=qT_l[:, hl, dc, cols],
                                                                  in1=P3[:, 256 + hl * 128:256 + hl * 128 + L], op=ALU.mult),
                   rd=[b3, bqT_l[hl][dc]], wr=[self.bqs])
        op('dve', lambda e: e.tensor_copy(out=self.g0bc[:, :], in_=P3[:, 256:512].rearrange("p (a b) -> p a b", b=128)[:, :, L - 1]),
           rd=[b3], wr=[self.bg0bc])

        yield
        def fn_kt(e):
            ins = None
            for hl in range(2):
                for dc in range(2):
                    j = hl * 2 + dc
                    ins = e.transpose(self.PB[0:L, j * 128:(j + 1) * 128], kT_l[:, hl, dc, cols], idb)
            return ins
        op('pe', fn_kt, rd=bkT_l[0] + bkT_l[1] + [bcst], wr=[self.bPB[0]])
        op('act', lambda e: e.activation(out=self.ktok[0:L].rearrange("p a b -> p (a b)"), in_=self.PB[0:L, 0:512], func=AF.Copy),
           rd=[self.bPB[0]], wr=[self.bktok])

        yield
        def fn_num(e):
            ins = None
            for hl in range(2):
                e.matmul(P4[0:L, hl * 256:(hl + 1) * 256], lhsT=self.swT[0:L, hl, 0:L], rhs=vtok_l[0:L, tt, hl, :], start=True, stop=False)
                for dc in range(2):
                    e.matmul(P4[0:L, hl * 256:(hl + 1) * 256], lhsT=self.qs[:, hl, dc, 0:L], rhs=Cb[:, hl, dc, :],
                             start=False, stop=(dc == 1))
                e.matmul(P6[0:L, 4 + hl:5 + hl], lhsT=self.swT[0:L, hl, 0:L], rhs=onesb[0:L, :], start=True, stop=False)
                for dc in range(2):
                    ins = e.matmul(P6[0:L, 4 + hl:5 + hl], lhsT=self.qs[:, hl, dc, 0:L], rhs=nb[:, hl, dc:dc + 1],
                                   start=False, stop=(dc == 1))
            return ins
        op('pe', fn_num, rd=[self.bswT, self.bqs, bv_l[tt], bC, bcst], wr=[b4, b6['den']])
        yield
        s = self.sml
        bs = self.bsml
        op('act', lambda e: e.activation(out=s[0:L, 0:2], in_=P6[0:L, 4:6], func=AF.Abs), rd=[b6['den']], wr=[bs])
        op('dve', lambda e: e.tensor_tensor(out=s[0:L, 0:2], in0=s[0:L, 0:2], in1=P6[0:L, 2:4], op=ALU.max), rd=[bs, b6['tm']], wr=[bs])
        op('dve', lambda e: e.reciprocal(out=s[0:L, 2:4], in_=s[0:L, 0:2]), rd=[bs], wr=[bs])
        for hl in range(2):
            op('dve', lambda e, hl=hl: e.bn_stats(out=self.sm6[0:L, hl, :], in_=P4[0:L, hl * 256:(hl + 1) * 256]), rd=[b4], wr=[self.bsm6])
            op('dve', lambda e, hl=hl: e.bn_aggr(out=self.mv[0:L, hl, :], in_=self.sm6[0:L, hl, :]), rd=[self.bsm6], wr=[self.bmv])
        op('dve', lambda e: e.tensor_tensor(out=s[0:L, 4:6], in0=s[0:L, 2:4], in1=s[0:L, 2:4], op=ALU.mult), rd=[bs], wr=[bs])
        op('dve', lambda e: e.tensor_tensor(out=s[0:L, 4:6], in0=s[0:L, 4:6], in1=self.mv[0:L, :, 1], op=ALU.mult),
           rd=[bs, self.bmv], wr=[bs])
        self.rstd(s[0:L, 4:6], s[0:L, 4:6], L, bs)
        op('dve', lambda e: e.tensor_tensor(out=s[0:L, 6:8], in0=s[0:L, 4:6], in1=s[0:L, 2:4], op=ALU.mult), rd=[bs], wr=[bs])
        op('dve', lambda e: e.scalar_tensor_tensor(out=s[0:L, 8:10], in0=self.mv[0:L, :, 0], scalar=-1.0, in1=s[0:L, 6:8],
                                                   op0=ALU.mult, op1=ALU.mult), rd=[bs, self.bmv], wr=[bs])
        yield
        for hl in range(2):
            op('act', lambda e, hl=hl: e.activation(out=self.hn[0:L, hl, :], in_=P4[0:L, hl * 256:(hl + 1) * 256], func=AF.Identity,
                                                    scale=s[0:L, 6 + hl:7 + hl], bias=s[0:L, 8 + hl:9 + hl]), rd=[b4, bs], wr=[self.bhn])
        op('dve', lambda e: e.tensor_tensor(out=self.mtok[0:L, tt, hp * 512:(hp + 1) * 512], in0=self.hn[0:L].rearrange("p a b -> p (a b)"),
                                            in1=G_l[0:L, tt].rearrange("p a b -> p (a b)"), op=ALU.mult),
           rd=[self.bhn, bG_l[tt]], wr=[self.bmtok[tt][hp]])
        yield
        op('act', lambda e: e.activation(out=self.gb[0:L, :], in_=self.wT[0:L, :, L - 1], func=AF.Copy), rd=[self.bwT], wr=[self.bgb])
        for hl in range(2):
            op('dve', lambda e, hl=hl: e.tensor_scalar(out=self.gv[0:L, hl, :], in0=vtok_l[0:L, tt, hl, :],
                                                       scalar1=self.wT[0:L, hl, L - 1:L], scalar2=None, op0=ALU.mult),
               rd=[self.bwT, bv_l[tt]], wr=[self.bgv])
        for hl in range(2):
            yield

            def fn_dc(e, hl=hl):
                ins = None
                for dc in range(2):
                    e.matmul(P3[:, dc * 256:(dc + 1) * 256], lhsT=self.ktok[0:L, hl, dc * 128:(dc + 1) * 128], rhs=self.gv[0:L, hl, :],
                             start=True, stop=True)
                    ins = e.matmul(P6[:, 6 + 2 * hl + dc:7 + 2 * hl + dc], lhsT=self.ktok[0:L, hl, dc * 128:(dc + 1) * 128],
                                   rhs=self.gb[0:L, hl:hl + 1], start=True, stop=True)
                return ins
            op('pe', fn_dc, rd=[self.bktok, self.bgv, self.bgb], wr=[b3, b6['dn']])
            for dc in range(2):
                op('dve', lambda e, hl=hl, dc=dc: e.scalar_tensor_tensor(out=C[:, hl, dc, :], in0=C[:, hl, dc, :],
                                                                         scalar=self.g0bc[:, hl:hl + 1], in1=P3[:, dc * 256:(dc + 1) * 256],
                                                                         op0=ALU.mult, op1=ALU.add), rd=[b3, self.bg0bc, bC], wr=[bC])
            op('dve', lambda e, hl=hl: e.scalar_tensor_tensor(out=n[:, hl, :], in0=n[:, hl, :], scalar=self.g0bc[:, hl:hl + 1],
                                                              in1=P6[:, 6 + 2 * hl:8 + 2 * hl], op0=ALU.mult, op1=ALU.add),
               rd=[b6['dn'], self.bg0bc, bC], wr=[bC])
        op('act', lambda e: e.activation(out=Cb, in_=C, func=AF.Copy), rd=[bC], wr=[bC])
        op('act', lambda e: e.activation(out=nb, in_=n, func=AF.Copy), rd=[bC], wr=[bC])
        yield
        if samp:
            si = u.sidx
            dma('sp', self.oCs[l, si, hs].rearrange("h (dc p) e -> p h dc e", p=128), C, rd=[bC], chan=bC, final=True)
            self.final.append(bC) if bC not in self.final else None
            self.op('sp', lambda e: e.dma_start(out=self.ons[l, si, hs].rearrange("h (dc p) -> p h dc", p=128), in_=n,
                                                allow_slow_non_contiguous=True), rd=[bC], chan=bC)
            dma('sp', self.oms[l, si, hs].rearrange("(h o) -> h o", o=1), m[:], rd=[bC], chan=bC)
        elif last_blk and tt == NT - 1:
            dma('sp', self.oCp[l, hs].rearrange("h (dc p) e -> p h dc e", p=128), C, rd=[bC], chan=bC, final=True)
            self.op('sp', lambda e: e.dma_start(out=self.onp[l, hs].rearrange("h (dc p) -> p h dc", p=128), in_=n,
                                                allow_slow_non_contiguous=True), rd=[bC], chan=bC)
            dma('sp', self.omp[l, hs].rearrange("(h o) -> h o", o=1), m[:], rd=[bC], chan=bC)

    def conv_block(self, l, us):
        op = self.op
        cw, cb = self.cw, self.cb
        if us[0].sidx is None:
          for cc in range(4):
            op('dve', lambda e, cc=cc: e.tensor_scalar(out=self.yT[:, cc, 0:TB], in0=self.aT[:, cc, 0:TB], scalar1=cw[:, l, cc, 0:1],
                                                       scalar2=cb[:, l, cc:cc + 1], op0=ALU.mult, op1=ALU.add),
               rd=[self.baT, self.bpar], wr=[self.byT])
        for j in range(1, 31 if us[0].sidx is None else 1):
            yield
            for cc in range(4):
                op('dve', lambda e, cc=cc, j=j: e.scalar_tensor_tensor(out=self.yT[:, cc, 0:TB], in0=self.aT[:, cc, j:j + TB],
                                                                       scalar=cw[:, l, cc, j:j + 1], in1=self.yT[:, cc, 0:TB],
                                                                       op0=ALU.mult, op1=ALU.add),
                   rd=[self.baT, self.byT], wr=[self.byT])
        for u in us:
            if u.sidx is None:
                continue
            si = u.sidx
            self.dma('sp', self.aTs[:, :, si, 0:30], self.scv[l, si].rearrange("(cc c) j -> c cc j", c=128), wr=[self.baTs[si]],
                     chan=self.baTs[si])
        for u in us:
            if u.sidx is None:
                continue
            si = u.sidx
            c0_ = u.c0
            yield
            for cc in range(4):
                op('dve', lambda e, cc=cc, si=si: e.tensor_tensor(out=self.ctmp[:, :], in0=self.aTs[:, cc, si, :], in1=cw[:, l, cc, :],
                                                                  op=ALU.mult), rd=[self.baTs[si], self.bpar], wr=[self.bctmp])
                op('dve', lambda e, cc=cc, si=si, c0_=c0_: e.tensor_reduce(out=self.yT[:, cc, c0_:c0_ + 1], in_=self.ctmp[:, :],
                                                                  axis=AX.X, op=ALU.add), rd=[self.bctmp], wr=[self.byT])
                op('dve', lambda e, cc=cc, si=si, c0_=c0_: e.tensor_tensor(out=self.yT[:, cc, c0_:c0_ + 1],
                                                                  in0=self.yT[:, cc, c0_:c0_ + 1], in1=cb[:, l, cc:cc + 1],
                                                                  op=ALU.add), rd=[self.byT, self.bpar], wr=[self.byT])

    def conv_unit(self, l, u):
        op = self.op
        L, tt, c0 = u.L, u.tt, u.c0
        P5, b5 = self.PS[5], self.bPS[5]
        idf = self.cst_sb[:, C_ID:C_ID + 128]

        def fn(e):
            ins = None
            for cc in range(4):
                ins = e.transpose(P5[0:L, cc * 128:(cc + 1) * 128], self.yT[:, cc, c0:c0 + L], idf)
            return ins
        op('pe', fn, rd=[self.byT, self.bcst], wr=[b5])
        yield
        bs = self.bcsm
        op('dve', lambda e: e.bn_stats(out=self.cst6[0:L, :], in_=P5[0:L, :]), rd=[b5], wr=[bs])
        op('dve', lambda e: e.bn_aggr(out=self.cmv[0:L, :], in_=self.cst6[0:L, :]), rd=[bs], wr=[bs])
        self.rstd(self.csml[0:L, 0:1], self.cmv[0:L, 1:2], L, bs)
        op('dve', lambda e: e.scalar_tensor_tensor(out=self.csml[0:L, 1:2], in0=self.cmv[0:L, 0:1], scalar=-1.0, in1=self.csml[0:L, 0:1],
                                                   op0=ALU.mult, op1=ALU.mult), rd=[bs], wr=[bs])
        yield
        op('act', lambda e: e.activation(out=self.yn[0:L, :], in_=P5[0:L, :], func=AF.Identity, scale=self.csml[0:L, 0:1],
                                         bias=self.csml[0:L, 1:2]), rd=[b5, bs], wr=[self.byn])
        op('dve', lambda e: e.tensor_tensor(out=self.yn[0:L, :], in0=self.yn[0:L, :], in1=self.clg[0:L, :], op=ALU.mult),
           rd=[self.byn, self.bcl], wr=[self.byn])
        op('dve', lambda e: e.tensor_tensor(out=self.yn[0:L, :], in0=self.yn[0:L, :], in1=self.clb[0:L, :], op=ALU.add),
           rd=[self.byn, self.bcl], wr=[self.byn])
        op('act', lambda e: e.activation(out=self.yn[0:L, :], in_=self.yn[0:L, :], func=AF.Silu), rd=[self.byn], wr=[self.byn])
        op('dve', lambda e: e.tensor_tensor(out=self.mtok[0:L, tt, 1024:1536], in0=self.yn[0:L, :], in1=self.sz[0:L, tt, :], op=ALU.mult),
           rd=[self.byn, self.bsz[tt]], wr=[self.bmtok[tt][2]])

    def conv_state_out(self, l, src_of, bsrc, dst):
        P5, b5 = self.PS[5], self.bPS[5]
        idf = self.cst_sb[:, C_ID:C_ID + 128]

        def fn(e):
            ins = None
            for cc in range(4):
                ins = e.transpose(P5[0:30, cc * 128:(cc + 1) * 128], src_of(cc), idf)
            return ins
        self.op('pe', fn, rd=[bsrc, self.bcst], wr=[b5])
        self.op('act', lambda e: e.activation(out=self.cvo[:, :], in_=P5[0:30, :], func=AF.Copy), rd=[b5], wr=[self.bcvo])
        self.dma('sp', dst, self.cvo[:, :], rd=[self.bcvo], chan=self.bcvo, final=True)

    def conv_finish(self, l, us, last_blk):
        for u in us:
            if u.sidx is not None:
                si = u.sidx
                self.conv_state_out(l, lambda cc, si=si: self.aTs[:, cc, si, 1:31], self.baTs[si], self.ocvs[l, si])
        if us[0].sidx is not None:
            return
        if last_blk:
            self.conv_state_out(l, lambda cc: self.aT[:, cc, TB:TB + 30], self.baT, self.ocvp[l])
        else:
            self.op('act', lambda e: e.activation(out=self.hist[l][:, :, :], in_=self.aT[:, :, TB:TB + 30], func=AF.Copy),
                    rd=[self.baT], wr=[self.bhist[l]])

    def conv_start(self, l, blk):
        if blk < 0:
            return
        if blk == 0:
            self.op('dve', lambda e: e.memset(self.aT[:, :, 0:30], 0.0), wr=[self.baT])
        else:
            self.op('act', lambda e: e.activation(out=self.aT[:, :, 0:30], in_=self.hist[l][:, :, :], func=AF.Copy),
                    rd=[self.bhist[l]], wr=[self.baT])

    def attn_unit(self, l, u, ci, last_blk):
        op, dma = self.op, self.dma
        L, tt, c0 = u.L, u.tt, u.c0
        cols = slice(c0, c0 + L)
        idb = self.cstb[:, CB_ID:CB_ID + 128]
        idf = self.cst_sb[:, C_ID:C_ID + 128]
        nmp = self.cstb[:, CB_NMP:CB_NMP + 512].rearrange("p (a b) -> p a b", b=128)
        nmc = self.cstb[:, CB_NMC:CB_NMC + 512].rearrange("p (a b) -> p a b", b=128)
        P2, P3, P4, P5 = self.PS[2], self.PS[3], self.PS[4], self.PS[5]
        b2, b3, b4, b5 = self.bPS[2], self.bPS[3], self.bPS[4], self.bPS[5]
        samp = u.sidx is not None
        if samp:
            cur_kT, bcur_kT, cur_v, bcur_v = self.akT[l][:, 0, :], self.bakT[l][0], self.vaug[l][:, 0], self.bvaug[l][0]
        else:
            sl = ci % 2
            cur_kT, bcur_kT, cur_v, bcur_v = self.akT[l][:, sl, :], self.bakT[l][sl], self.vaug[l][:, sl], self.bvaug[l][sl]
        op('act', lambda e: e.activation(out=self.kvb[0:L, :], in_=self.kvf[0:L, tt, 0:128], func=AF.Copy), rd=[self.bkvf[tt]], wr=[self.bkvb])
        op('pe', lambda e: e.transpose(self.PB[:, 512:512 + L], self.kvb[0:L, :], idb[0:L, 0:L]), rd=[self.bkvb, self.bcst], wr=[self.bPB[1]])
        op('dve', lambda e: e.tensor_copy(out=cur_kT[:, 0:L], in_=self.PB[:, 512:512 + L]), rd=[self.bPB[1]], wr=[bcur_kT])
        op('dve', lambda e: e.tensor_copy(out=cur_v[0:L, :, 0:64], in_=self.kvf[0:L, tt, 128:256].rearrange("p (a b) -> p a b", b=64)),
           rd=[self.bkvf[tt]], wr=[bcur_v])
        yield
        blocks = []
        if samp:
            si = u.sidx
            bc = self.bcache
            dma('sp', self.ckf[:, :], self.ck[l, si], wr=[bc], chan=bc)
            dma('sp', self.cvf[:, :], self.cv[l, si], wr=[bc], chan=bc)
            op('act', lambda e: e.activation(out=self.kvb[:, :], in_=self.ckf[:, :], func=AF.Copy), rd=[bc], wr=[self.bkvb])
            op('pe', lambda e: e.transpose(self.PB[:, 512:640], self.kvb[:, :], idb), rd=[self.bkvb, self.bcst], wr=[self.bPB[1]])
            op('dve', lambda e: e.tensor_copy(out=self.akTs[:, :], in_=self.PB[:, 512:640]), rd=[self.bPB[1]], wr=[self.bakTs])
            op('dve', lambda e: e.tensor_copy(out=self.vaugs[:, :, 0:64], in_=self.cvf[:, :].rearrange("p (a b) -> p a b", b=64)),
               rd=[bc], wr=[self.bvaugs])
            blocks.append((self.akTs[:, :], self.vaugs[:, :, :], 128, nmp, [self.bakTs, self.bvaugs]))
            blocks.append((cur_kT, cur_v, 1, None, [bcur_kT, bcur_v]))
            bo = self.bout['oks']
            dma('sp', self.oks[l, si, 0:127, :], self.ck[l, si, 1:128, :], wr=[bo], chan=bo, final=True)
            dma('sp', self.ovs[l, si, 0:127, :], self.cv[l, si, 1:128, :], wr=[bo], chan=bo, final=True)
            dma('sp', self.oks[l, si, 127:128, :], self.kvf[0:1, tt, 0:128], rd=[self.bkvf[tt]], chan=self.bkvf[tt], final=True)
            dma('sp', self.ovs[l, si, 127:128, :], self.kvf[0:1, tt, 128:256], rd=[self.bkvf[tt]], chan=self.bkvf[tt], final=True)
        else:
            if ci > 0:
                ps_ = 1 - sl
                blocks.append((self.akT[l][:, ps_, :], self.vaug[l][:, ps_], 128, nmp, [self.bakT[l][ps_], self.bvaug[l][ps_]]))
            blocks.append((cur_kT, cur_v, 128, nmc, [bcur_kT, bcur_v]))
            if last_blk and tt == NT - 1:
                dma('sp', self.okp[l], self.kvf[:, tt, 0:128], rd=[self.bkvf[tt]], chan=self.bkvf[tt], final=True)
                dma('sp', self.ovp[l], self.kvf[:, tt, 128:256], rd=[self.bkvf[tt]], chan=self.bkvf[tt], final=True)
        for g in range(2):
            gp = slice(64 * g, 64 * g + 64)
            yield
            for bi, (kTa, va, Lk, mask, bufs) in enumerate(blocks):
                PSs, bPSs = P2, b2

                def fn(e, kTa=kTa, Lk=Lk, mask=mask, PSs=PSs, gp=gp):
                    out = PSs[0:Lk, :].rearrange("p (a b) -> p a b", b=128)[:, :, 0:L]
                    ins = e.matmul(out, lhsT=kTa[gp, 0:Lk], rhs=self.aqT[gp, :, cols], start=True, stop=(mask is None))
                    if mask is not None:
                        ins = e.matmul(out, lhsT=idb[0:Lk, 0:Lk], rhs=mask[0:Lk, :, 0:L], start=False, stop=True)
                    return ins
                op('pe', fn, rd=self.baq + [bufs[0], self.bcst], wr=[bPSs])
                op('act', lambda e, Lk=Lk, PSs=PSs, bi=bi: e.activation(
                    out=self.PT[0:Lk, bi, :, 0:L], in_=PSs[0:Lk, :].rearrange("p (a b) -> p a b", b=128)[:, :, 0:L], func=AF.Exp),
                    rd=[bPSs], wr=[self.bPT[bi]])
            Po, bPo = P5, b5
            yield

            def fn_pv(e, Po=Po, g=g):
                ins = None
                for i in range(4):
                    for bi, (kTa, va, Lk, mask, bufs) in enumerate(blocks):
                        ins = e.matmul(Po[0:L, i * 65:(i + 1) * 65], lhsT=self.PT[0:Lk, bi, i, 0:L], rhs=va[0:Lk, g, :],
                                       start=(bi == 0), stop=(bi == len(blocks) - 1))
                return ins
            op('pe', fn_pv, rd=[self.bPT[bi] for bi in range(len(blocks))] + [b[1] for b in [blk_[4] for blk_ in blocks]], wr=[bPo])
            po3 = Po[0:L, 0:260].rearrange("p (a b) -> p a b", b=65)
            yield
            bs = self.basml
            op('dve', lambda e, po3=po3, g=g: e.tensor_tensor(out=self.asml[0:L, 0:4], in0=po3[:, :, 64], in1=self.esk[0:L, l, 4 * g:4 * g + 4],
                                                              op=ALU.add), rd=[bPo, self.bpar], wr=[bs])
            op('dve', lambda e: e.reciprocal(out=self.asml[0:L, 4:8], in_=self.asml[0:L, 0:4]), rd=[bs], wr=[bs])
            for i in range(4):
                op('dve', lambda e, po3=po3, i=i: e.tensor_scalar(out=self.ao[0:L, i, :], in0=po3[:, i, 0:64], scalar1=self.asml[0:L, 4 + i:5 + i],
                                                                  scalar2=None, op0=ALU.mult), rd=[bPo, bs], wr=[self.bao])
            h0 = 1536 + 256 * g
            op('dve', lambda e, g=g, h0=h0: e.tensor_tensor(out=self.mtok[0:L, tt, h0:h0 + 256], in0=self.ao[0:L].rearrange("p a b -> p (a b)"),
                                                            in1=self.saz[0:L, tt, 256 * g:256 * g + 256], op=ALU.mult),
               rd=[self.bao, self.bsaz[tt]], wr=[self.bmtok[tt][3]])

    def merge_T(self, u, rounds=(0, 1, 2, 3)):
        L, tt = u.L, u.tt
        self.transpose_rows(lambda kc: self.mtok[0:L, tt, kc * 128:(kc + 1) * 128], lambda r: self.bmtok[tt][r], self.mT, self.bmT[tt], u,
                            k_evac=1, rounds=rounds)

    def outproj(self, l, us, part):
        op = self.op
        kcs = [0, 1, 2, 3] + list(range(8, 16)) if part == 'A' else [4, 5, 6, 7]
        for j in range(4):
            s = self.get_w()
            W, bW = self.W[s], self.bW[s]
            for u in us:
                L, tt, c0 = u.L, u.tt, u.c0
                k = self.pcount = getattr(self, 'pcount', 0) + 1
                ps, bps = self.PS[k % 2], self.bPS[k % 2]
                for g0 in range(0, len(kcs), 4):
                    self.pull(3)

                    def fn(e, ps=ps, L=L, c0=c0, W=W, g0=g0):
                        ins = None
                        for i_ in range(g0, min(g0 + 4, len(kcs))):
                            kc = kcs[i_]
                            ins = e.matmul(ps[0:L, :], lhsT=self.mT[:, kc, c0:c0 + L], rhs=W[:, kc, :], start=(i_ == 0),
                                           stop=(i_ == len(kcs) - 1))
                        return ins
                    op('pe', fn, rd=[bW, self.bmT[tt]], wr=[bps])
                xa, bx = self.Xap(u, slice(512 * j, 512 * (j + 1)))
                if part == 'A':
                    op('dve', lambda e, xa=xa, ps=ps, L=L: e.scalar_tensor_tensor(out=xa, in0=xa, scalar=ALPHA, in1=ps[0:L, :],
                                                                                  op0=ALU.mult, op1=ALU.add), rd=[bps, bx], wr=[bx])
                else:
                    op('dve', lambda e, xa=xa, ps=ps, L=L: e.tensor_tensor(out=xa, in0=xa, in1=ps[0:L, :], op=ALU.add), rd=[bps, bx], wr=[bx])

    def final_ln(self, l, u, blk):
        op = self.op
        L, tt = u.L, u.tt
        xa, bx = self.Xap(u)
        bs = self.blsm
        for j in range(4):
            xj, _ = self.Xap(u, slice(512 * j, 512 * (j + 1)))
            op('dve', lambda e, xj=xj, j=j: e.bn_stats(out=self.lst[0:L, j, :], in_=xj), rd=[bx], wr=[bs])
        op('dve', lambda e: e.bn_aggr(out=self.lmv[0:L, :], in_=self.lst[0:L].rearrange("p a b -> p (a b)")), rd=[bs], wr=[bs])
        self.rstd(self.lsm[0:L, 0:1], self.lmv[0:L, 1:2], L, bs)
        op('dve', lambda e: e.scalar_tensor_tensor(out=self.lsm[0:L, 1:2], in0=self.lmv[0:L, 0:1], scalar=-1.0, in1=self.lsm[0:L, 0:1],
                                                   op0=ALU.mult, op1=ALU.mult), rd=[bs], wr=[bs])
        op('act', lambda e: e.activation(out=xa, in_=xa, func=AF.Identity, scale=self.lsm[0:L, 0:1], bias=self.lsm[0:L, 1:2]),
           rd=[bx, bs], wr=[bx])

    def final_gain(self, l, us, blk):
        op = self.op
        for j in range(4):
            k = self.lncount = getattr(self, 'lncount', 0) + 1
            lp, bl = self.lnp[k % 2], self.bln[k % 2]
            cs = slice(512 * j, 512 * (j + 1))
            self.dma('sp', lp[:, 0, :], self.p_lng[l, cs].partition_broadcast(128), wr=[bl], chan=bl)
            self.dma('sp', lp[:, 1, :], self.p_lnb[l, cs].partition_broadcast(128), wr=[bl], chan=bl)
            for u in us:
                xj, bx = self.Xap(u, cs)
                L = u.L
                op('dve', lambda e, xj=xj, lp=lp, L=L: e.tensor_tensor(out=xj, in0=xj, in1=lp[0:L, 0, :], op=ALU.mult), rd=[bx, bl], wr=[bx])
                op('dve', lambda e, xj=xj, lp=lp, L=L: e.tensor_tensor(out=xj, in0=xj, in1=lp[0:L, 1, :], op=ALU.add), rd=[bx, bl], wr=[bx])
        for u in us:
            self.final_out(l, u, blk)
            if l < DEPTH - 1:
                self.make_xT(u)

    def final_out(self, l, u, blk):
        L, tt = u.L, u.tt
        xa, bx = self.Xap(u)
        if l == DEPTH - 1:
            if u.sidx is None:
                r0 = blk * TB + tt * 128
                self.dma('sp', self.yp[r0:r0 + 128, :], xa, rd=[bx], chan=bx, final=True)
            else:
                self.dma('sp', self.ys[u.sidx:u.sidx + 1, :], xa, rd=[bx], chan=bx, final=True)

    def seq(self, *gens):
        for g in gens:
            yield from g

    def layer_block(self, blk, l):
        us = self.units(blk)
        last_blk = (blk == self.nblk - 1)
        self.load_params(l)
        if l == 0:
            self.load_x(blk, us)
        self.chk()
        if l == 0:
            for u in us:
                self.make_xT(u)
        self.chk()
        self.conv_start(l, blk)
        T = lambda kind, i: self.inproj_tile(l, kind, i, us)
        T('cu', 0)
        T('cg', 0)
        g_cb = self.conv_block(l, us)
        self.bg.append(g_cb)
        T('kvg', 0)
        T('aq', 0)
        T('az', 0)
        g_at = self.seq(*[self.attn_unit(l, u, (blk * NT + u.tt if u.sidx is None else None), last_blk) for u in us])
        self.bg.append(g_at)
        for k_, i_ in (('qk', 0), ('qk', 1), ('v', 0), ('o', 0), ('z', 0)):
            T(k_, i_)
        g_m0 = self.seq(*[self.mlstm_unit(l, 0, u, last_blk) for u in us])
        self.bg.append(g_m0)
        started_cu = False
        for k_, i_ in (('cz', 0), ('qk', 2), ('qk', 3), ('v', 1), ('o', 1), ('z', 1)):
            T(k_, i_)
            if not started_cu and k_ != 'cz' and g_cb not in self.bg and g_at not in self.bg:
                self.bg.append(self.seq(*[self.conv_unit(l, u) for u in us]))
                started_cu = True
        if not started_cu:
            self.drain([g_cb, g_at])
            self.bg.append(self.seq(*[self.conv_unit(l, u) for u in us]))
        self.drain([g_m0])
        g_m1 = self.seq(*[self.mlstm_unit(l, 1, u, last_blk) for u in us])
        self.bg.append(g_m1)
        self.drain([g for g in self.bg if g is not g_m1])
        for u in us:
            self.merge_T(u, rounds=(0, 2, 3))
        self.outproj(l, us, 'A')
        self.drain()
        self.conv_finish(l, us, last_blk)
        self.chk()
        if self.debug and blk == 0 and l == 0:
            for u in us:
                self.dump("mtok%d" % u.tt, self.mtok[0:u.L, u.tt, :], list(self.bmtok[u.tt]), (u.L, D), BF16)
        for u in us:
            self.merge_T(u, rounds=(1,))
        self.chk()
        self.outproj(l, us, 'B')
        self.chk()
        for u in us:
            self.final_ln(l, u, blk)
        self.final_gain(l, us, blk)
        if self.debug and blk == 0 and l == 0:
            for u in us:
                self.dump("x1_%d" % u.tt, self.X[0:u.L, u.tt, :], self.bX[u.tt], (u.L, D))
        self.chk()

    def build(self):
        self.setup()
        seq = []
        blks = (list(range(-(NS // NSB), 0)) if self.with_sample else []) + list(range(self.nblk))
        for blk in blks:
            for l in range(DEPTH):
                seq += [(l, j) for j in range(NW)] + [(l, NW_IN + j, 'B') for j in range(4)]
        self.w_plan(seq)
        try:
            self.chk()
            for blk in blks:
                for l in range(DEPTH):
                    self.layer_block(blk, l)
        except StopIteration:
            pass
        self.P.emit(self.final)
        return self.nc


_CACHE = {}


def kernel(x_prompt, x_sample, state_C, state_n, state_m, state_conv, cache_k, cache_v,
           w_in, w_out, b_igate, b_fgate, m_norm_g, conv_w, conv_b, conv_ln_g, conv_ln_b,
           sinks, ln_g, ln_b):
    f = lambda a: np.ascontiguousarray(np.asarray(a, dtype=np.float32))
    x_prompt, x_sample = f(x_prompt), f(x_sample)
    if 'nc' not in _CACHE:
        _CACHE['nc'] = K().build()
    nc = _CACHE['nc']
    wt = host_weight_tiles(f(w_in), f(w_out))
    cst, cstB = host_consts()
    scv = np.ascontiguousarray(f(state_conv).transpose(0, 1, 3, 2))
    p_cw = np.ascontiguousarray(f(conv_w).transpose(0, 2, 1).reshape(DEPTH, 4, 128, 31).transpose(0, 2, 1, 3)).reshape(DEPTH, 128, 124)
    p_cb = np.ascontiguousarray(f(conv_b).reshape(DEPTH, 4, 128).transpose(0, 2, 1))
    sC, sn, sm = f(state_C), f(state_n), f(state_m)
    ck = f(cache_k).reshape(DEPTH, 32, 128, 128)
    cv = f(cache_v).reshape(DEPTH, 32, 128, 128)
    in_maps = []
    xp_dummy = np.zeros_like(x_prompt[0])
    for c in range(NCORE):
        b = c % 2
        ss = slice(NS * c, NS * (c + 1))
        in_maps.append({
            "xp": x_prompt[b] if c < 2 else xp_dummy, "xs": np.ascontiguousarray(x_sample[ss, 0, :]), "wt": wt,
            "sC": np.ascontiguousarray(sC[:, ss]), "sn": np.ascontiguousarray(sn[:, ss]), "sm": np.ascontiguousarray(sm[:, ss]),
            "scv": np.ascontiguousarray(scv[:, ss]), "ck": np.ascontiguousarray(ck[:, ss]), "cv": np.ascontiguousarray(cv[:, ss]),
            "cst": cst, "cstB": cstB, "p_bi": f(b_igate), "p_bf": f(b_fgate), "p_mng": f(m_norm_g), "p_cw": p_cw, "p_cb": p_cb,
            "p_clg": f(conv_ln_g), "p_clb": f(conv_ln_b), "p_sk": f(sinks), "p_lng": f(ln_g), "p_lnb": f(ln_b),
        })
    res = run_bass_kernel_spmd(nc, in_maps, core_ids=list(range(NCORE))).results
    cat = lambda k, ax: np.concatenate([r[k] for r in res], axis=ax)
    stack2 = lambda k: np.stack([res[0][k], res[1][k]], axis=1)
    y_prompt = np.stack([res[0]["yp"], res[1]["yp"]], axis=0)
    y_sample = cat("ys", 0).reshape(32, 1, D)
    new_C_p = stack2("oCp")
    new_n_p = stack2("onp")
    new_m_p = stack2("omp")
    new_conv_p = stack2("ocvp")
    new_k_p = stack2("okp").reshape(DEPTH, 2, 128, 2, 64)
    new_v_p = stack2("ovp").reshape(DEPTH, 2, 128, 2, 64)
    new_C_s = cat("oCs", 1)
    new_n_s = cat("ons", 1)
    new_m_s = cat("oms", 1)
    new_conv_s = cat("ocvs", 1)
    new_k_s = cat("oks", 1).reshape(DEPTH, 32, 128, 2, 64)
    new_v_s = cat("ovs", 1).reshape(DEPTH, 32, 128, 2, 64)
    return (y_prompt, y_sample, new_C_p, new_n_p, new_m_p, new_conv_p, new_k_p, new_v_p,
            new_C_s, new_n_s, new_m_s, new_conv_s, new_k_s, new_v_s)
```

```python
import numpy as np
import concourse.bass as bass
import concourse.mybir as mybir
from concourse.bass_utils import run_bass_kernel_spmd

F32 = mybir.dt.float32
BF16 = mybir.dt.bfloat16
ALU = mybir.AluOpType
AF = mybir.ActivationFunctionType
AX = mybir.AxisListType

D = 2048
SEQ = 4096
DEPTH = 2
NCORE = 8
NS = 4
TB = 256
NT = TB // 128
NBLK = SEQ // TB
NSB = 2
NTT = max(NT, NSB)
NCF = max(TB, NSB)
ALPHA = (2 * DEPTH) ** 0.25
EPS = 1e-5
BIG = 30000.0
NW_IN = 16
NW = 20
EPOCH = 3000

T_KVG, T_QK, T_V, T_O, T_Z, T_CU, T_CG, T_CZ, T_AQ, T_AZ = 'kvg', 'qk', 'v', 'o', 'z', 'cu', 'cg', 'cz', 'aq', 'az'
TILE_ORDER = [('cu', 0), ('cg', 0), ('kvg', 0), ('qk', 0), ('qk', 1), ('v', 0), ('o', 0), ('z', 0),
              ('cz', 0), ('aq', 0), ('az', 0),
              ('qk', 2), ('qk', 3), ('v', 1), ('o', 1), ('z', 1)]

C_ID, C_TRI, C_SEL, C_ONE, C_EPS = 0, 128, 256, 512, 513
NCST = 514
CB_ID, CB_BIGM, CB_NMP, CB_NMC, CB_ONE = 0, 128, 256, 768, 1280
NCSTB = 1281


def host_weight_tiles(w_in, w_out):
    mq, mk, mv, mo, mi, mf, mz = 0, 1024, 2048, 3072, 4096, 4100, 4104
    cu, cg, cz, aq, ak, av, az = 5128, 5640, 6152, 6664, 7176, 7304, 7432
    L = w_in.shape[0]
    out = np.zeros((L, NW, 2048, 512), np.float32)
    for l in range(L):
        W = w_in[l]
        for j, (kind, i) in enumerate(TILE_ORDER):
            t = out[l, j]
            if kind == 'kvg':
                t[:, 0:128] = W[:, ak:ak + 128]
                t[:, 128:256] = W[:, av:av + 128]
                t[:, 256:260] = W[:, mi:mi + 4]
                t[:, 260:264] = W[:, mf:mf + 4]
            elif kind == 'qk':
                t[:, 0:256] = W[:, mq + 256 * i: mq + 256 * (i + 1)]
                t[:, 256:512] = W[:, mk + 256 * i: mk + 256 * (i + 1)]
            elif kind == 'v':
                t[:] = W[:, mv + 512 * i: mv + 512 * (i + 1)]
            elif kind == 'o':
                t[:] = W[:, mo + 512 * i: mo + 512 * (i + 1)]
            elif kind == 'z':
                t[:] = W[:, mz + 512 * i: mz + 512 * (i + 1)]
            elif kind == 'cu':
                t[:] = W[:, cu:cu + 512]
            elif kind == 'cg':
                t[:] = W[:, cg:cg + 512]
            elif kind == 'cz':
                t[:] = W[:, cz:cz + 512]
            elif kind == 'aq':
                for c in range(4):
                    t[:, c * 128: c * 128 + 64] = W[:, aq + 64 * c: aq + 64 * (c + 1)]
                    t[:, c * 128 + 64: c * 128 + 128] = W[:, aq + 64 * (4 + c): aq + 64 * (5 + c)]
            elif kind == 'az':
                t[:] = W[:, az:az + 512]
        for j in range(4):
            out[l, NW_IN + j] = w_out[l][:, 512 * j: 512 * (j + 1)]
    out = out.reshape(L, NW, 16, 128, 512).transpose(0, 1, 3, 2, 4)
    return np.ascontiguousarray(out).reshape(L, NW, 128, 16 * 512)


def host_consts():
    c = np.zeros((128, NCST), np.float32)
    cb = np.zeros((128, NCSTB), np.float32)
    s = np.arange(128)[:, None]
    t = np.arange(128)[None, :]
    c[:, C_ID:C_ID + 128] = (s == t)
    c[:, C_TRI:C_TRI + 128] = (s <= t)
    c[0, C_SEL:C_SEL + 128] = 1.0
    c[1, C_SEL + 128:C_SEL + 256] = 1.0
    c[:, C_ONE] = 1.0
    c[:, C_EPS] = EPS
    cb[:, CB_ID:CB_ID + 128] = (s == t)
    cb[:, CB_BIGM:CB_BIGM + 128] = np.where(s > t, BIG, 0.0)
    nmp = np.where(s < t, -BIG, 0.0)
    nmc = np.where(s > t, -BIG, 0.0)
    for h in range(4):
        cb[:, CB_NMP + 128 * h: CB_NMP + 128 * (h + 1)] = nmp
        cb[:, CB_NMC + 128 * h: CB_NMC + 128 * (h + 1)] = nmc
    cb[:, CB_ONE] = 1.0
    return c, cb


class Buf:
    __slots__ = ('name', 'w', 'r', 'sem', 'cnt')

    def __init__(self, name):
        self.name = name
        self.w = None
        self.r = []
        self.sem = None
        self.cnt = 0


class Op:
    __slots__ = ('eng', 'fn', 'deps', 'dma', 'chan', 'val', 'sig', 'signo', 'idx', 'dmaw')


class Prog:
    ENGS = ('pe', 'act', 'dve', 'pool', 'sp')

    def __init__(self, nc):
        self.nc = nc
        self.ops = []
        self.nbuf = 0

    def buf(self, name=None):
        self.nbuf += 1
        return Buf(name or "b%d" % self.nbuf)

    def op(self, eng, fn, rd=(), wr=(), chan=None):
        o = Op()
        o.eng, o.fn, o.idx = eng, fn, len(self.ops)
        deps = set()
        for b in rd:
            if b.w is not None:
                deps.add(b.w)
        for b in wr:
            if b.w is not None:
                deps.add(b.w)
            deps.update(b.r)
        o.deps = deps
        o.dmaw = {}
        for d in deps:
            p = self.ops[d]
            if p.dma:
                o.dmaw[id(p.chan)] = (p.chan, 16 * p.chan.cnt)
        o.dma = chan is not None
        o.chan = chan
        o.sig = False
        o.signo = 0
        o.val = 0
        if chan is not None:
            chan.cnt += 1
            o.val = 16 * chan.cnt
        for b in rd:
            b.r.append(o.idx)
        for b in wr:
            b.w = o.idx
            b.r = []
        self.ops.append(o)
        return o

    def emit(self, final_chans):
        nc = self.nc
        ops = self.ops
        for o in ops:
            keep = {}
            for d in o.deps:
                p = ops[d]
                if p.dma:
                    continue
                if p.eng == 'pe' and o.eng == 'pe' and not o.dma:
                    continue
                k = ('c', p.eng)
                if k not in keep or keep[k] < d:
                    keep[k] = d
            o.deps = sorted(keep.values())
            for d in o.deps:
                ops[d].sig = True
        cnt = {e: 0 for e in self.ENGS}
        for o in ops:
            if o.sig and not o.dma:
                cnt[o.eng] += 1
                o.signo = cnt[o.eng]
        esems = {e: [nc.alloc_semaphore(name="s_%s_%d" % (e, i)) for i in range((cnt[e] + EPOCH - 1) // EPOCH + 1)]
                 for e in self.ENGS}
        chans = {}
        for o in ops:
            if o.dma and o.chan.sem is None:
                o.chan.sem = nc.alloc_semaphore(name="d_%s_%d" % (o.chan.name, len(chans)))
                chans[id(o.chan)] = o.chan

        def target(p):
            if p.dma:
                return p.chan.sem, p.val
            n = p.signo - 1
            return esems[p.eng][n // EPOCH], n % EPOCH + 1

        by_eng = {e: [o for o in ops if o.eng == e] for e in self.ENGS}

        def run(ename, eng):
            waited = {}
            for o in by_eng[ename]:
                for ch, val in o.dmaw.values():
                    k = id(ch.sem)
                    if waited.get(k, 0) >= val:
                        continue
                    eng.wait_ge(ch.sem, val)
                    waited[k] = val
                for d in o.deps:
                    sem, val = target(ops[d])
                    k = id(sem)
                    if waited.get(k, 0) >= val:
                        continue
                    eng.wait_ge(sem, val)
                    waited[k] = val
                ins = o.fn(eng)
                if o.dma:
                    ins.then_inc(o.chan.sem, 16)
                elif o.sig:
                    sem, _ = target(o)
                    ins.then_inc(sem, 1)
            if ename == 'sp':
                for ch in chans.values():
                    eng.wait_ge(ch.sem, 16 * ch.cnt)

        with nc.Block() as block:
            @block.tensor
            def _(e):
                run('pe', e)

            @block.scalar
            def _(e):
                run('act', e)

            @block.vector
            def _(e):
                run('dve', e)

            @block.gpsimd
            def _(e):
                run('pool', e)

            @block.sync
            def _(e):
                run('sp', e)


class Unit:
    def __init__(self, L, tt, c0, sidx=None):
        self.L, self.tt, self.c0, self.sidx = L, tt, c0, sidx


class K:
    def __init__(self, nblk=NBLK, with_sample=True, debug=False, stage=None):
        self.stage = stage
        self.stage_n = 0
        self.debug = debug
        self.bg = []
        self.nblk = nblk
        self.with_sample = with_sample
        self.debug = debug
        self.nc = nc = bass.Bass("TRN2", target_bir_lowering=False)
        self.P = Prog(nc)
        self.final = []
        self.dbg_outs = []
        di = lambda n, s: nc.dram_tensor(n, list(s), F32, kind="ExternalInput").ap()
        do = lambda n, s: nc.dram_tensor(n, list(s), F32, kind="ExternalOutput").ap()
        self.xp = di("xp", (SEQ, D))
        self.xs = di("xs", (NS, D))
        self.wt = di("wt", (DEPTH, NW, 128, 16 * 512))
        self.wb = nc.dram_tensor("wb", [DEPTH, NW, 128, 16 * 512], BF16, kind="Internal").ap()
        self.bwb = [[self.P.buf("wb%d_%d" % (l, j)) for j in range(NW)] for l in range(DEPTH)]
        self.sC = di("sC", (DEPTH, NS, 4, 256, 256))
        self.sn = di("sn", (DEPTH, NS, 4, 256))
        self.sm = di("sm", (DEPTH, NS, 4))
        self.scv = di("scv", (DEPTH, NS, 512, 30))
        self.ck = di("ck", (DEPTH, NS, 128, 128))
        self.cv = di("cv", (DEPTH, NS, 128, 128))
        self.cst = di("cst", (128, NCST))
        self.cstB = di("cstB", (128, NCSTB))
        self.p_bi = di("p_bi", (DEPTH, 4))
        self.p_bf = di("p_bf", (DEPTH, 4))
        self.p_mng = di("p_mng", (DEPTH, 1024))
        self.p_cw = di("p_cw", (DEPTH, 128, 4 * 31))
        self.p_cb = di("p_cb", (DEPTH, 128, 4))
        self.p_clg = di("p_clg", (DEPTH, 512))
        self.p_clb = di("p_clb", (DEPTH, 512))
        self.p_sk = di("p_sk", (DEPTH, 8))
        self.p_lng = di("p_lng", (DEPTH, D))
        self.p_lnb = di("p_lnb", (DEPTH, D))
        self.yp = do("yp", (SEQ, D))
        self.ys = do("ys", (NS, D))
        self.oCp = do("oCp", (DEPTH, 4, 256, 256))
        self.onp = do("onp", (DEPTH, 4, 256))
        self.omp = do("omp", (DEPTH, 4))
        self.ocvp = do("ocvp", (DEPTH, 30, 512))
        self.okp = do("okp", (DEPTH, 128, 128))
        self.ovp = do("ovp", (DEPTH, 128, 128))
        self.oCs = do("oCs", (DEPTH, NS, 4, 256, 256))
        self.ons = do("ons", (DEPTH, NS, 4, 256))
        self.oms = do("oms", (DEPTH, NS, 4))
        self.ocvs = do("ocvs", (DEPTH, NS, 30, 512))
        self.oks = do("oks", (DEPTH, NS, 128, 128))
        self.ovs = do("ovs", (DEPTH, NS, 128, 128))
        self.alloc()

    def chk(self):
        self.stage_n += 1
        if self.stage is not None and self.stage_n >= self.stage:
            raise StopIteration

    def dump(self, name, ap, buf, shape, dt=F32):
        o = self.nc.dram_tensor("dbg_" + name, list(shape), dt, kind="ExternalOutput").ap()
        bufs = buf if isinstance(buf, list) else [buf]
        self.dma('sp', o, ap, rd=bufs, chan=bufs[0], final=True)

    def sb(self, name, shape, dt=F32):
        return self.nc.alloc_sbuf_tensor(name, list(shape), dt)

    def B(self, name=None):
        return self.P.buf(name)

    def op(self, eng, fn, rd=(), wr=(), chan=None):
        return self.P.op(eng, fn, rd, wr, chan)

    def rstd(self, out, in_, L, b):
        epsc = self.cst_sb[0:L, C_EPS:C_EPS + 1]
        self.op('act', lambda e: e.activation(out=out, in_=in_, func=AF.Ln, bias=epsc), rd=[b, self.bcst], wr=[b])
        self.op('act', lambda e: e.activation(out=out, in_=out, func=AF.Exp, scale=-0.5), rd=[b], wr=[b])

    def pull(self, n=1):
        for _ in range(n):
            for g in list(self.bg):
                try:
                    next(g)
                except StopIteration:
                    self.bg.remove(g)

    def drain(self, gens=None):
        while True:
            act = [g for g in self.bg if gens is None or g in gens]
            if not act:
                return
            self.pull()

    def dma(self, q, out, in_, rd=(), wr=(), chan=None, final=False):
        if final and chan not in self.final:
            self.final.append(chan)
        return self.op(q, lambda e, o=out, i=in_: e.dma_start(out=o, in_=i), rd=rd, wr=wr, chan=chan)

    def alloc(self):
        nc = self.nc
        sb, B = self.sb, self.B
        self.PS = [nc.alloc_psum_tensor("ps%d" % i, [128, 512], F32) for i in range(7)]
        self.PB = nc.alloc_psum_tensor("psb", [128, 1024], BF16)
        self.bPS = [B("ps%d" % i) for i in range(7)]
        _b = B("psb")
        self.bPB = [_b, _b]
        self.bP6 = {k: self.bPS[6] for k in ('b', 'tm', 'den', 'dn', 'arow', 'brow')}
        self.cst_sb = sb("cst_sb", [128, NCST])
        self.cstb = sb("cstb", [128, NCSTB], BF16)
        self.bcst = B("cst")
        self.bcstb = B("cstb")
        self.NSLOT = 3
        self.W = [sb("w%d" % i, [128, 16, 512], BF16) for i in range(self.NSLOT)]
        self.bW = [B("w%d" % i) for i in range(self.NSLOT)]
        self.wcount = 0
        self.X = sb("X", [128, NTT, D])
        self.bX = [B("X%d" % i) for i in range(NTT)]
        self.xT = sb("xT", [128, 16, NCF], BF16)
        self.bxT = [B("xT%d" % i) for i in range(NTT)]
        self.mT = self.xT
        self.bmT = self.bxT
        self.mtok = sb("mtok", [128, NTT, D], BF16)
        self.bmtok = [[B("mtok%d_%d" % (i, j)) for j in range(4)] for i in range(NTT)]
        self.xb = self.mtok[:, 0, :]
        self.bxb = B("xb")
        self.xpre = sb("xpre", [128, NT, D], BF16)
        self.bxpre = [B("xpre%d" % i) for i in range(NT)]
        self.xT_prefetched = False
        self.qT_ = [sb("qT%d" % p, [128, 2, 2, NCF], BF16) for p in range(2)]
        self.kT_ = [sb("kT%d" % p, [128, 2, 2, NCF], BF16) for p in range(2)]
        self.bqT_ = [[[B() for _ in range(2)] for _ in range(2)] for p in range(2)]
        self.bkT_ = [[[B() for _ in range(2)] for _ in range(2)] for p in range(2)]
        self.vtok_ = [sb("vtok%d" % p, [128, NTT, 2, 256], BF16) for p in range(2)]
        self.bv_ = [[B() for _ in range(NTT)] for p in range(2)]
        self.G_ = [sb("G%d" % p, [128, NTT, 2, 256]) for p in range(2)]
        self.bG_ = [[B() for _ in range(NTT)] for p in range(2)]
        self.gtmp = sb("gtmp", [128, 512])
        self.bgtmp = B()
        self.graw = sb("graw", [128, NTT, 8])
        self.ig = sb("ig", [128, NTT, 4])
        self.sp = sb("spl", [128, NTT, 4])
        self.bgate = [B() for _ in range(NTT)]
        self.a_sb = sb("a_sb", [128, 2]); self.ba = B()
        self.arow = sb("arow", [2, 128]); self.barow = B()
        self.Mrow = sb("Mrow", [2, 128]); self.bMrow = B()
        self.mrow = sb("mrow", [2, 128]); self.bmrow = B()
        self.w0row = sb("w0row", [2, 128]); self.bw0row = B()
        self.emrow = sb("emrow", [2, 128]); self.bemrow = B()
        self.wT = sb("wT", [128, 2, 128]); self.bwT = B()
        self.swT = sb("swT", [128, 2, 128], BF16); self.bswT = B()
        self.qs = sb("qs", [128, 2, 2, 128], BF16); self.bqs = B()
        self.g0bc = sb("g0bc", [128, 2]); self.bg0bc = B()
        self.ktok = sb("ktok", [128, 2, 256], BF16); self.bktok = B()
        self.gv = sb("gv", [128, 2, 256], BF16); self.bgv = B()
        self.gb = sb("gb", [128, 2], BF16); self.bgb = B()
        self.sm6 = sb("sm6", [128, 2, 6]); self.bsm6 = B()
        self.mv = sb("mv", [128, 2, 2]); self.bmv = B()
        self.sml = sb("sml", [128, 16]); self.bsml = B()
        self.hn = sb("hn", [128, 2, 256]); self.bhn = B()
        self.C = [sb("C%d" % l, [128, 4, 2, 256]) for l in range(DEPTH)]
        self.n = [sb("n%d" % l, [128, 4, 2]) for l in range(DEPTH)]
        self.Cb = [sb("Cb%d" % l, [128, 4, 2, 256], BF16) for l in range(DEPTH)]
        self.nb = [sb("nb%d" % l, [128, 4, 2], BF16) for l in range(DEPTH)]
        self.m = [[sb("m%d_%d" % (l, hp), [2, 1]) for hp in range(2)] for l in range(DEPTH)]
        self.bC = [[B() for _ in range(2)] for _ in range(DEPTH)]
        self.Cs = sb("Cs", [128, 2, 2, 256]); self.ns = sb("ns", [128, 2, 2])
        self.Csb = sb("Csb", [128, 2, 2, 256], BF16); self.nsb = sb("nsb", [128, 2, 2], BF16)
        self.ms = sb("ms", [2, 1]); self.bCs = B("Cs")
        self.aT = sb("aT", [128, 4, 30 + TB])
        self.baT = B("aT")
        self.aTs = sb("aTs", [128, 4, NS, 31])
        self.baTs = [B() for _ in range(NS)]
        self.bcu = [B() for _ in range(4)]
        self.cuF = sb("cuF", [128, 4, NCF])
        self.sg = sb("sg", [128, NCF]); self.bsg = B()
        self.yT = sb("yT", [128, 4, NCF]); self.byT = B("yT")
        self.ctmp = sb("ctmp", [128, 31]); self.bctmp = B()
        self.sz = sb("sz", [128, NTT, 512], BF16); self.bsz = [B() for _ in range(NTT)]
        self.yn = sb("yn", [128, 512]); self.byn = B()
        self.cst6 = sb("cst6", [128, 6]); self.cmv = sb("cmv", [128, 2]); self.csml = sb("csml", [128, 4]); self.bcsm = B()
        self.cvo = sb("cvo", [30, 512]); self.bcvo = B("cvo")
        self.aqT = sb("aqT", [128, 4, NCF], BF16); self.baq = [B() for _ in range(4)]
        self.kvf = sb("kvf", [128, NTT, 256]); self.bkvf = [B() for _ in range(NTT)]
        self.kvb = sb("kvb", [128, 128], BF16); self.bkvb = B()
        self.akT = [sb("akT%d" % l, [128, 2, 128], BF16) for l in range(DEPTH)]
        self.bakT = [[B(), B()] for l in range(DEPTH)]
        self.vaug = [sb("vaug%d" % l, [128, 2, 2, 65], BF16) for l in range(DEPTH)]
        self.bvaug = [[B(), B()] for l in range(DEPTH)]
        self.hist = [sb("hist%d" % l, [128, 4, 30]) for l in range(DEPTH)]
        self.bhist = [B() for l in range(DEPTH)]
        self.akTs = sb("akTs", [128, 128], BF16); self.vaugs = sb("vaugs", [128, 2, 65], BF16)
        self.ckf = sb("ckf", [128, 128]); self.cvf = sb("cvf", [128, 128]); self.bcache = B("cache")
        self.bakTs = B(); self.bvaugs = B()
        self.saz = sb("saz", [128, NTT, 512], BF16); self.bsaz = [B() for _ in range(NTT)]
        self.PT = sb("PT", [128, 2, 4, 128], BF16); self.bPT = [B(), B()]
        self.asml = sb("asml", [128, 8]); self.basml = B()
        self.ao = sb("ao", [128, 4, 64]); self.bao = B()
        self.bi_bc = sb("bi_bc", [128, DEPTH, 4]); self.bf_bc = sb("bf_bc", [128, DEPTH, 4])
        self.esk = sb("esk", [128, DEPTH, 8])
        self.mng = sb("mng", [128, 1024])
        self.cw = sb("cw", [128, DEPTH, 4, 31]); self.cb = sb("cb", [128, DEPTH, 4])
        self.clg = sb("clg", [128, 512]); self.clb = sb("clb", [128, 512])
        self.lnp = [sb("lnp%d" % i, [128, 2, 512]) for i in range(2)]
        self.bpar = B("par"); self.bmng = B("mng"); self.bcl = B("cl"); self.bln = [B("ln0"), B("ln1")]
        self.lst = sb("lst", [128, 4, 6]); self.lmv = sb("lmv", [128, 2]); self.lsm = sb("lsm", [128, 4]); self.blsm = B()
        self.bout = {k: B(k) for k in ('oCp', 'onp', 'omp', 'okp', 'ovp', 'oms', 'oks', 'ovs')}

    def setup(self):
        op, dma = self.op, self.dma
        dma('sp', self.cst_sb[:], self.cst, wr=[self.bcst], chan=self.bcst)
        dma('pool', self.cstb[:], self.cstB, wr=[self.bcstb], chan=self.bcstb)
        op('dve', lambda e: e.tensor_copy(out=self.cst_sb[:, C_ONE:C_ONE + 1], in_=self.cst_sb[:, C_ONE:C_ONE + 1]),
           rd=[self.bcst, self.bcstb], wr=[self.bcst])
        bp = self.bpar

        def bc(dst, src):
            dma('sp', dst, src.partition_broadcast(128), wr=[bp], chan=bp)
        bc(self.bi_bc[:].rearrange("p l f -> p (l f)"), self.p_bi.rearrange("l f -> (l f)"))
        bc(self.bf_bc[:].rearrange("p l f -> p (l f)"), self.p_bf.rearrange("l f -> (l f)"))
        bc(self.esk[:].rearrange("p l f -> p (l f)"), self.p_sk.rearrange("l f -> (l f)"))
        for l in range(DEPTH):
            dma('sp', self.cw[:, l].rearrange("p a b -> p (a b)"), self.p_cw[l], wr=[bp], chan=bp)
            dma('sp', self.cb[:, l], self.p_cb[l], wr=[bp], chan=bp)
        op('act', lambda e: e.activation(out=self.esk[:], in_=self.esk[:], func=AF.Exp), rd=[bp], wr=[bp])
        for l in range(DEPTH):
            for hp in range(2):
                hs = slice(2 * hp, 2 * hp + 2)
                b = self.bC[l][hp]
                op('dve', lambda e, l=l, hs=hs: e.memset(self.C[l][:, hs], 0.0), wr=[b])
                op('dve', lambda e, l=l, hs=hs: e.memset(self.n[l][:, hs], 0.0), wr=[b])
                op('dve', lambda e, l=l, hs=hs: e.memset(self.Cb[l][:, hs], 0.0), wr=[b])
                op('dve', lambda e, l=l, hs=hs: e.memset(self.nb[l][:, hs], 0.0), wr=[b])
                op('dve', lambda e, l=l, hp=hp: e.memset(self.m[l][hp][:], 0.0), wr=[b])
        for l in range(DEPTH):
            for r in range(2):
                op('dve', lambda e, l=l, r=r: e.memset(self.vaug[l][:, r, :, 64:65], 1.0), wr=[self.bvaug[l][r]])
        op('dve', lambda e: e.memset(self.vaugs[:, :, 64:65], 1.0), wr=[self.bvaugs])

    def load_params(self, l):
        dma = self.dma
        dma('sp', self.mng[:, :], self.p_mng[l].partition_broadcast(128), wr=[self.bmng], chan=self.bmng)
        dma('sp', self.clg[:, :], self.p_clg[l].partition_broadcast(128), wr=[self.bcl], chan=self.bcl)
        dma('sp', self.clb[:, :], self.p_clb[l].partition_broadcast(128), wr=[self.bcl], chan=self.bcl)

    def w_plan(self, seq):
        seen = set()
        for ent in seq:
            l, j = ent[0], ent[1]
            if (l, j) not in seen:
                seen.add((l, j))
                self.dma('pool', self.wb[l, j], self.wt[l, j], wr=[self.bwb[l][j]], chan=self.bwb[l][j])
        self.wseq = seq
        self.w_issued = 0
        self.w_used = 0

    def get_w(self):
        while self.w_issued < len(self.wseq) and self.w_issued < self.w_used + self.NSLOT:
            ent = self.wseq[self.w_issued]
            l, j = ent[0], ent[1]
            s = self.w_issued % self.NSLOT
            if len(ent) > 2 and ent[2] == 'B':
                self.dma('pool', self.W[s][:, 4:8, :], self.wb[l, j].rearrange("p (a b) -> p a b", b=512)[:, 4:8, :],
                         rd=[self.bwb[l][j]], wr=[self.bW[s]], chan=self.bW[s])
            else:
                self.dma('pool', self.W[s][:].rearrange("p a b -> p (a b)"), self.wb[l, j], rd=[self.bwb[l][j]], wr=[self.bW[s]],
                         chan=self.bW[s])
            self.w_issued += 1
        s = self.w_used % self.NSLOT
        self.w_used += 1
        return s

    def units(self, blk):
        if blk < 0:
            s0 = NSB * (blk + NS // NSB)
            return [Unit(1, j, j, sidx=s0 + j) for j in range(NSB)]
        return [Unit(128, tt, tt * 128) for tt in range(NT)]

    def Xap(self, u, cols=slice(0, D)):
        return self.X[0:u.L, u.tt, cols], self.bX[u.tt]

    def load_x(self, blk, us):
        for u in us:
            xa, bx = self.Xap(u)
            if u.sidx is None:
                r0 = blk * TB + u.tt * 128
                self.dma('sp', xa, self.xp[r0:r0 + 128, :], wr=[bx], chan=bx)
            else:
                self.dma('sp', xa, self.xs[u.sidx:u.sidx + 1, :], wr=[bx], chan=bx)

    def transpose_rows(self, src_of, bsrc, dst, bdst, u, k_evac=0, rounds=(0, 1, 2, 3), act_only=False):
        L, c0 = u.L, u.c0
        idb = self.cstb[:, CB_ID:CB_ID + 128]
        for r in rounds:
            h = r % 2
            pb = self.PB[:, h * 512:(h + 1) * 512]

            def fn(e, r=r, pb=pb):
                ins = None
                for j in range(4):
                    ins = e.transpose(pb[:, j * 128:j * 128 + L], src_of(4 * r + j), idb[0:L, 0:L])
                return ins
            bs_r = bsrc(r) if callable(bsrc) else bsrc
            self.op('pe', fn, rd=[bs_r, self.bcst] if not isinstance(bs_r, list) else bs_r + [self.bcst], wr=[self.bPB[h]])
            src = pb.rearrange("p (a b) -> p a b", b=128)[:, :, 0:L]
            out = dst[:, 4 * r:4 * r + 4, c0:c0 + L]
            if act_only or (r + k_evac) % 2 == 0:
                self.op('act', lambda e, o=out, s=src: e.activation(out=o, in_=s, func=AF.Copy), rd=[self.bPB[h]], wr=[bdst])
            else:
                self.op('dve', lambda e, o=out, s=src: e.tensor_copy(out=o, in_=s), rd=[self.bPB[h]], wr=[bdst])

    def prefetch_x(self, blk):
        for tt in range(NT):
            r0 = blk * TB + tt * 128
            self.dma('pool', self.xpre[:, tt, :], self.xp[r0:r0 + 128, :], wr=[self.bxpre[tt]], chan=self.bxpre[tt])

    def build_xT_prefetched(self):
        for tt in range(NT):
            u = Unit(128, tt, tt * 128)
            self.transpose_rows(lambda kc, tt=tt: self.xpre[0:128, tt, kc * 128:(kc + 1) * 128], self.bxpre[tt], self.xT, self.bxT[tt], u,
                                act_only=True)
        self.xT_prefetched = True

    def make_xT(self, u):
        L = u.L
        xa, bx = self.Xap(u)
        bxb = list(self.bmtok[0])
        self.op('act', lambda e: e.activation(out=self.xb[0:L, :], in_=xa, func=AF.Copy), rd=[bx], wr=bxb)
        self.transpose_rows(lambda kc: self.xb[0:L, kc * 128:(kc + 1) * 128], bxb, self.xT, self.bxT[u.tt], u)

    def inproj_tile(self, l, kind, i, us):
        s = self.get_w()
        W, bW = self.W[s], self.bW[s]
        op = self.op
        pp_ = (i // 2) if kind == 'qk' else (i if kind in ('v', 'o', 'z') else 0)
        self.qT, self.kT, self.bqT, self.bkT = self.qT_[pp_], self.kT_[pp_], self.bqT_[pp_], self.bkT_[pp_]
        self.vtok, self.bv, self.G, self.bG = self.vtok_[pp_], self.bv_[pp_], self.G_[pp_], self.bG_[pp_]
        hs_samp = us[0].sidx is not None
        ncols = NSB if hs_samp else TB
        assert ncols <= 512
        bxT_all = [self.bxT[u.tt] for u in us]
        one = self.cst_sb[:, C_ONE:C_ONE + 1]
        if kind in ('qk', 'cu', 'cg', 'aq'):
            for ec in range(4):
                self.pull()
                k = self.pcount = getattr(self, 'pcount', 0) + 1
                ps, bps = self.PS[k % 2], self.bPS[k % 2]

                def fn(e, ec=ec, ps=ps):
                    ins = None
                    for kc in range(16):
                        ins = e.matmul(ps[:, 0:ncols], lhsT=W[:, kc, ec * 128:(ec + 1) * 128], rhs=self.xT[:, kc, 0:ncols],
                                       start=(kc == 0), stop=(kc == 15))
                    return ins
                op('pe', fn, rd=[bW] + bxT_all, wr=[bps])
                src = ps[:, 0:ncols]
                if kind == 'qk':
                    hl = i % 2
                    if ec < 2:
                        op('act', lambda e, s_=src, o=self.qT[:, hl, ec, 0:ncols]: e.activation(out=o, in_=s_, func=AF.Copy),
                           rd=[bps], wr=[self.bqT[hl][ec]])
                    else:
                        op('act', lambda e, s_=src, o=self.kT[:, hl, ec - 2, 0:ncols]: e.activation(out=o, in_=s_, func=AF.Copy, scale=1.0 / 16.0),
                           rd=[bps], wr=[self.bkT[hl][ec - 2]])
                elif kind == 'cu':
                    op('act', lambda e, s_=src, o=self.cuF[:, ec, 0:ncols]: e.activation(out=o, in_=s_, func=AF.Copy),
                       rd=[bps], wr=[self.bcu[ec]])
                elif kind == 'cg':
                    op('act', lambda e, s_=src: e.activation(out=self.sg[:, 0:ncols], in_=s_, func=AF.Sigmoid),
                       rd=[bps], wr=[self.bsg])
                    if not hs_samp:
                        op('dve', lambda e, ec=ec: e.tensor_tensor(out=self.aT[:, ec, 30:30 + TB], in0=self.cuF[:, ec, 0:TB],
                                                                   in1=self.sg[:, 0:TB], op=ALU.mult),
                           rd=[self.bsg, self.bcu[ec]], wr=[self.baT])
                    else:
                        s0 = us[0].sidx
                        op('dve', lambda e, ec=ec, s0=s0: e.tensor_tensor(out=self.aTs[:, ec, s0:s0 + NSB, 30], in0=self.cuF[:, ec, 0:NSB],
                                                                          in1=self.sg[:, 0:NSB], op=ALU.mult),
                           rd=[self.bsg, self.bcu[ec]], wr=[self.baTs[u_.sidx] for u_ in us])
                elif kind == 'aq':
                    op('act', lambda e, s_=src, o=self.aqT[:, ec, 0:ncols]: e.activation(out=o, in_=s_, func=AF.Copy, scale=0.125),
                       rd=[bps], wr=[self.baq[ec]])
            return
        for u in us:
            L, tt, c0 = u.L, u.tt, u.c0
            self.pull()
            k = self.pcount = getattr(self, 'pcount', 0) + 1
            ps, bps = self.PS[k % 2], self.bPS[k % 2]

            def fn(e, ps=ps, L=L, c0=c0):
                ins = None
                for kc in range(16):
                    ins = e.matmul(ps[0:L, :], lhsT=self.xT[:, kc, c0:c0 + L], rhs=W[:, kc, :], start=(kc == 0), stop=(kc == 15))
                return ins
            op('pe', fn, rd=[bW, self.bxT[tt]], wr=[bps])
            src = ps[0:L, :]
            if kind == 'v':
                op('act', lambda e, s_=src, o=self.vtok[0:L, tt].rearrange("p a b -> p (a b)"): e.activation(out=o, in_=s_, func=AF.Copy),
                   rd=[bps], wr=[self.bv[tt]])
            elif kind == 'o':
                g = self.G[0:L, tt].rearrange("p a b -> p (a b)")
                op('act', lambda e, s_=src, g=g: e.activation(out=g, in_=s_, func=AF.Sigmoid), rd=[bps], wr=[self.bG[tt]])
                op('dve', lambda e, g=g, L=L: e.tensor_tensor(out=g, in0=g, in1=self.mng[0:L, i * 512:(i + 1) * 512], op=ALU.mult),
                   rd=[self.bG[tt], self.bmng], wr=[self.bG[tt]])
            elif kind == 'z':
                g = self.G[0:L, tt].rearrange("p a b -> p (a b)")
                op('act', lambda e, s_=src, L=L: e.activation(out=self.gtmp[0:L, :], in_=s_, func=AF.Silu), rd=[bps], wr=[self.bgtmp])
                op('dve', lambda e, g=g, L=L: e.tensor_tensor(out=g, in0=g, in1=self.gtmp[0:L, :], op=ALU.mult),
                   rd=[self.bG[tt], self.bgtmp], wr=[self.bG[tt]])
            elif kind == 'cz':
                op('act', lambda e, s_=src, o=self.sz[0:L, tt, :]: e.activation(out=o, in_=s_, func=AF.Silu), rd=[bps], wr=[self.bsz[tt]])
            elif kind == 'az':
                op('act', lambda e, s_=src, o=self.saz[0:L, tt, :]: e.activation(out=o, in_=s_, func=AF.Silu), rd=[bps], wr=[self.bsaz[tt]])
            elif kind == 'kvg':
                bg = self.bgate[tt]
                op('act', lambda e, ps=ps, o=self.kvf[0:L, tt, :], L=L: e.activation(out=o, in_=ps[0:L, 0:256], func=AF.Copy),
                   rd=[bps], wr=[self.bkvf[tt]])
                op('dve', lambda e, ps=ps, L=L, tt=tt: e.tensor_tensor(out=self.ig[0:L, tt, :], in0=ps[0:L, 256:260],
                                                                      in1=self.bi_bc[0:L, l, :], op=ALU.add),
                   rd=[bps, self.bpar], wr=[bg])
                op('dve', lambda e, ps=ps, L=L, tt=tt: e.tensor_tensor(out=self.sp[0:L, tt, :], in0=ps[0:L, 260:264],
                                                                      in1=self.bf_bc[0:L, l, :], op=ALU.add),
                   rd=[bps, self.bpar], wr=[bg])
                op('act', lambda e, L=L, tt=tt: e.activation(out=self.sp[0:L, tt, :], in_=self.sp[0:L, tt, :], func=AF.Exp, scale=-1.0),
                   rd=[bg], wr=[bg])
                op('act', lambda e, L=L, tt=tt: e.activation(out=self.sp[0:L, tt, :], in_=self.sp[0:L, tt, :], func=AF.Ln,
                                                             bias=one[0:L, :]), rd=[bg, self.bcst], wr=[bg])

    def mlstm_unit(self, l, hp, u, last_blk):
        op, dma = self.op, self.dma
        qT_l, kT_l, vtok_l, G_l = self.qT_[hp], self.kT_[hp], self.vtok_[hp], self.G_[hp]
        bqT_l, bkT_l, bv_l, bG_l = self.bqT_[hp], self.bkT_[hp], self.bv_[hp], self.bG_[hp]
        L, tt, c0 = u.L, u.tt, u.c0
        cols = slice(c0, c0 + L)
        hs = slice(2 * hp, 2 * hp + 2)
        idf = self.cst_sb[:, C_ID:C_ID + 128]
        tri = self.cst_sb[:, C_TRI:C_TRI + 128]
        idb = self.cstb[:, CB_ID:CB_ID + 128]
        bigb = self.cstb[:, CB_BIGM:CB_BIGM + 128]
        onesb = self.cstb[:, CB_ONE:CB_ONE + 1]
        sel = lambda hl: self.cst_sb[0:2, C_SEL + 128 * hl:C_SEL + 128 * (hl + 1)]
        bcst = self.bcst
        P2, P3, P4, P5, P6 = self.PS[2], self.PS[3], self.PS[4], self.PS[5], self.PS[6]
        b2, b3, b4, b5 = self.bPS[2], self.bPS[3], self.bPS[4], self.bPS[5]
        b6 = self.bP6
        samp = u.sidx is not None
        if not samp:
            C, n, Cb, nb, m, bC = self.C[l][:, hs], self.n[l][:, hs], self.Cb[l][:, hs], self.nb[l][:, hs], self.m[l][hp], self.bC[l][hp]
        else:
            si = u.sidx
            C, n, Cb, nb, m, bC = self.Cs[:], self.ns[:], self.Csb[:], self.nsb[:], self.ms, self.bCs
            dma('sp', C, self.sC[l, si, hs].rearrange("h (dc p) e -> p h dc e", p=128), wr=[bC], chan=bC)
            self.op('sp', lambda e: e.dma_start(out=n, in_=self.sn[l, si, hs].rearrange("h (dc p) -> p h dc", p=128),
                                                allow_slow_non_contiguous=True), wr=[bC], chan=bC)
            dma('sp', m[:], self.sm[l, si, hs].rearrange("(h o) -> h o", o=1), wr=[bC], chan=bC)
            op('act', lambda e: e.activation(out=Cb, in_=C, func=AF.Copy), rd=[bC], wr=[bC])
            op('act', lambda e: e.activation(out=nb, in_=n, func=AF.Copy), rd=[bC], wr=[bC])
        bg = self.bgate[tt]
        sp_ = self.sp[0:L, tt, hs]
        op('pe', lambda e: e.matmul(P6[0:L, 0:2], lhsT=tri[0:L, 0:L], rhs=sp_, start=True, stop=True), rd=[bg, bcst], wr=[b6['b']])
        yield
        op('dve', lambda e: e.tensor_tensor(out=self.a_sb[0:L, :], in0=self.ig[0:L, tt, hs], in1=P6[0:L, 0:2], op=ALU.add),
           rd=[bg, b6['b']], wr=[self.ba])
        op('pe', lambda e: e.matmul(P6[0:2, 16:16 + L], lhsT=self.a_sb[0:L, :], rhs=idf[0:L, 0:L], start=True, stop=True),
           rd=[self.ba, bcst], wr=[b6['arow']])
        op('pe', lambda e: e.matmul(P6[0:2, 144:144 + L], lhsT=sp_, rhs=tri[0:L, 0:L], start=True, stop=True),
           rd=[bg, bcst], wr=[b6['brow']])
        op('dve', lambda e: e.tensor_copy(out=self.arow[:, 0:L], in_=P6[0:2, 16:16 + L]), rd=[b6['arow']], wr=[self.barow])
        yield
        op('dve', lambda e: e.tensor_tensor_scan(out=self.Mrow[:, 0:L], data0=self.arow[:, 0:L], data1=self.arow[:, 0:L],
                                                 initial=m[:], op0=ALU.max, op1=ALU.max), rd=[self.barow, bC], wr=[self.bMrow])
        op('dve', lambda e: e.tensor_tensor(out=self.mrow[:, 0:L], in0=self.Mrow[:, 0:L], in1=P6[0:2, 144:144 + L], op=ALU.subtract),
           rd=[self.bMrow, b6['brow']], wr=[self.bmrow])
        yield
        op('act', lambda e: e.activation(out=self.w0row[:, 0:L], in_=self.Mrow[:, 0:L], func=AF.Exp, scale=-1.0, bias=m[:]),
           rd=[self.bMrow, bC], wr=[self.bw0row])
        op('act', lambda e: e.activation(out=self.emrow[:, 0:L], in_=self.mrow[:, 0:L], func=AF.Exp, scale=-1.0),
           rd=[self.bmrow], wr=[self.bemrow])
        op('dve', lambda e: e.tensor_copy(out=m[:], in_=self.mrow[:, L - 1:L]), rd=[self.bmrow], wr=[bC])
        yield
        op('pe', lambda e: e.matmul(P6[0:L, 2:4], lhsT=self.emrow[0:2, 0:L], rhs=idf[0:2, 0:2], start=True, stop=True),
           rd=[self.bemrow, bcst], wr=[b6['tm']])

        def fn_bc(e):
            ins = None
            for hl in range(2):
                e.matmul(P3[0:L, hl * 128:hl * 128 + L], lhsT=sel(hl)[:, 0:L], rhs=self.Mrow[0:2, 0:L], start=True, stop=False)
                e.matmul(P3[0:L, hl * 128:hl * 128 + L], lhsT=idb[0:L, 0:L], rhs=bigb[0:L, 0:L], start=False, stop=True)
                ins = e.matmul(P3[:, 256 + hl * 128:256 + hl * 128 + L], lhsT=sel(hl), rhs=self.w0row[0:2, 0:L], start=True, stop=True)
            return ins
        op('pe', fn_bc, rd=[self.bMrow, self.bw0row, bcst], wr=[b3])
        yield
        for hl in range(2):
            op('act', lambda e, hl=hl: e.activation(out=self.wT[0:L, hl, 0:L], in_=P3[0:L, hl * 128:hl * 128 + L], func=AF.Exp,
                                                    scale=-1.0, bias=self.a_sb[0:L, hl:hl + 1]), rd=[b3, self.ba], wr=[self.bwT])

        yield
        def fn_s(e):
            ins = None
            for hl in range(2):
                for dc in range(2):
                    ins = e.matmul(P4[0:L, hl * 128:hl * 128 + L], lhsT=kT_l[:, hl, dc, cols], rhs=qT_l[:, hl, dc, cols],
                                   start=(dc == 0), stop=(dc == 1))
            return ins
        op('pe', fn_s, rd=bkT_l[0] + bkT_l[1] + bqT_l[0] + bqT_l[1], wr=[b4])
        for hl in range(2):
            op('dve', lambda e, hl=hl: e.tensor_tensor(out=self.swT[0:L, hl, 0:L], in0=P4[0:L, hl * 128:hl * 128 + L],
                                                       in1=self.wT[0:L, hl, 0:L], op=ALU.mult), rd=[b4, self.bwT], wr=[self.bswT])
        yield
        for hl in range(2):
            for dc in range(2):
                op('dve', lambda e, hl=hl, dc=dc: e.tensor_tensor(out=self.qs[:, hl, dc, 0:L], in0=qT_l[:, hl, dc, cols],
                                                                  in1=P3[:, 256 + hl * 128:256 + hl * 128 + L], op=ALU.mult),
                   rd=[b3, bqT_l[hl][dc]], wr=[self.bqs])
        op('dve', lambda e: e.tensor_copy(out=self.g0bc[:, :], in_=P3[:, 256:512].rearrange("p (a b) -> p a b", b=128)[:, :, L - 1]),
           rd=[b3], wr=[self.bg0bc])

        yield
        def fn_kt(e):
            ins = None
            for hl in range(2):
                for dc in range(2):
                    j = hl * 2 + dc
                    ins = e.transpose(self.PB[0:L, j * 128:(j + 1) * 128], kT_l[:, hl, dc, cols], idb)
            return ins
        op('pe', fn_kt, rd=bkT_l[0] + bkT_l[1] + [bcst], wr=[self.bPB[0]])
        op('act', lambda e: e.activation(out=self.ktok[0:L].rearrange("p a b -> p (a b)"), in_=self.PB[0:L, 0:512], func=AF.Copy),
           rd=[self.bPB[0]], wr=[self.bktok])

        yield
        def fn_num(e):
            ins = None
            for hl in range(2):
                e.matmul(P4[0:L, hl * 256:(hl + 1) * 256], lhsT=self.swT[0:L, hl, 0:L], rhs=vtok_l[0:L, tt, hl, :], start=True, stop=False)
                for dc in range(2):
                    e.matmul(P4[0:L, hl * 256:(hl + 1) * 256], lhsT=self.qs[:, hl, dc, 0:L], rhs=Cb[:, hl, dc, :],
                             start=False, stop=(dc == 1))
                e.matmul(P6[0:L, 4 + hl:5 + hl], lhsT=self.swT[0:L, hl, 0:L], rhs=onesb[0:L, :], start=True, stop=False)
                for dc in range(2):
                    ins = e.matmul(P6[0:L, 4 + hl:5 + hl], lhsT=self.qs[:, hl, dc, 0:L], rhs=nb[:, hl, dc:dc + 1],
                                   start=False, stop=(dc == 1))
            return ins
        op('pe', fn_num, rd=[self.bswT, self.bqs, bv_l[tt], bC, bcst], wr=[b4, b6['den']])
        yield
        s = self.sml
        bs = self.bsml
        op('act', lambda e: e.activation(out=s[0:L, 0:2], in_=P6[0:L, 4:6], func=AF.Abs), rd=[b6['den']], wr=[bs])
        op('dve', lambda e: e.tensor_tensor(out=s[0:L, 0:2], in0=s[0:L, 0:2], in1=P6[0:L, 2:4], op=ALU.max), rd=[bs, b6['tm']], wr=[bs])
        op('dve', lambda e: e.reciprocal(out=s[0:L, 2:4], in_=s[0:L, 0:2]), rd=[bs], wr=[bs])
        for hl in range(2):
            op('dve', lambda e, hl=hl: e.bn_stats(out=self.sm6[0:L, hl, :], in_=P4[0:L, hl * 256:(hl + 1) * 256]), rd=[b4], wr=[self.bsm6])
            op('dve', lambda e, hl=hl: e.bn_aggr(out=self.mv[0:L, hl, :], in_=self.sm6[0:L, hl, :]), rd=[self.bsm6], wr=[self.bmv])
        op('dve', lambda e: e.tensor_tensor(out=s[0:L, 4:6], in0=s[0:L, 2:4], in1=s[0:L, 2:4], op=ALU.mult), rd=[bs], wr=[bs])
        op('dve', lambda e: e.tensor_tensor(out=s[0:L, 4:6], in0=s[0:L, 4:6], in1=self.mv[0:L, :, 1], op=ALU.mult),
           rd=[bs, self.bmv], wr=[bs])
        self.rstd(s[0:L, 4:6], s[0:L, 4:6], L, bs)
        op('dve', lambda e: e.tensor_tensor(out=s[0:L, 6:8], in0=s[0:L, 4:6], in1=s[0:L, 2:4], op=ALU.mult), rd=[bs], wr=[bs])
        op('dve', lambda e: e.scalar_tensor_tensor(out=s[0:L, 8:10], in0=self.mv[0:L, :, 0], scalar=-1.0, in1=s[0:L, 6:8],
                                                   op0=ALU.mult, op1=ALU.mult), rd=[bs, self.bmv], wr=[bs])
        yield
        for hl in range(2):
            op('act', lambda e, hl=hl: e.activation(out=self.hn[0:L, hl, :], in_=P4[0:L, hl * 256:(hl + 1) * 256], func=AF.Identity,
                                                    scale=s[0:L, 6 + hl:7 + hl], bias=s[0:L, 8 + hl:9 + hl]), rd=[b4, bs], wr=[self.bhn])
        op('dve', lambda e: e.tensor_tensor(out=self.mtok[0:L, tt, hp * 512:(hp + 1) * 512], in0=self.hn[0:L].rearrange("p a b -> p (a b)"),
                                            in1=G_l[0:L, tt].rearrange("p a b -> p (a b)"), op=ALU.mult),
           rd=[self.bhn, bG_l[tt]], wr=[self.bmtok[tt][hp]])
        yield
        op('act', lambda e: e.activation(out=self.gb[0:L, :], in_=self.wT[0:L, :, L - 1], func=AF.Copy), rd=[self.bwT], wr=[self.bgb])
        for hl in range(2):
            op('dve', lambda e, hl=hl: e.tensor_scalar(out=self.gv[0:L, hl, :], in0=vtok_l[0:L, tt, hl, :],
                                                       scalar1=self.wT[0:L, hl, L - 1:L], scalar2=None, op0=ALU.mult),
               rd=[self.bwT, bv_l[tt]], wr=[self.bgv])
        for hl in range(2):
            yield

            def fn_dc(e, hl=hl):
                ins = None
                for dc in range(2):
                    e.matmul(P3[:, dc * 256:(dc + 1) * 256], lhsT=self.ktok[0:L, hl, dc * 128:(dc + 1) * 128], rhs=self.gv[0:L, hl, :],
                             start=True, stop=True)
                    ins = e.matmul(P6[:, 6 + 2 * hl + dc:7 + 2 * hl + dc], lhsT=self.ktok[0:L, hl, dc * 128:(dc + 1) * 128],
                                   rhs=self.gb[0:L, hl:hl + 1], start=True, stop=True)
                return ins
            op('pe', fn_dc, rd=[self.bktok, self.bgv, self.bgb], wr=[b3, b6['dn']])
            for dc in range(2):
                op('dve', lambda e, hl=hl, dc=dc: e.scalar_tensor_tensor(out=C[:, hl, dc, :], in0=C[:, hl, dc, :],
                                                                         scalar=self.g0bc[:, hl:hl + 1], in1=P3[:, dc * 256:(dc + 1) * 256],
                                                                         op0=ALU.mult, op1=ALU.add), rd=[b3, self.bg0bc, bC], wr=[bC])
            op('dve', lambda e, hl=hl: e.scalar_tensor_tensor(out=n[:, hl, :], in0=n[:, hl, :], scalar=self.g0bc[:, hl:hl + 1],
                                                              in1=P6[:, 6 + 2 * hl:8 + 2 * hl], op0=ALU.mult, op1=ALU.add),
               rd=[b6['dn'], self.bg0bc, bC], wr=[bC])
        op('act', lambda e: e.activation(out=Cb, in_=C, func=AF.Copy), rd=[bC], wr=[bC])
        op('act', lambda e: e.activation(out=nb, in_=n, func=AF.Copy), rd=[bC], wr=[bC])
        yield
        if samp:
            si = u.sidx
            dma('sp', self.oCs[l, si, hs].rearrange("h (dc p) e -> p h dc e", p=128), C, rd=[bC], chan=bC, final=True)
            self.final.append(bC) if bC not in self.final else None
            self.op('sp', lambda e: e.dma_start(out=self.ons[l, si, hs].rearrange("h (dc p) -> p h dc", p=128), in_=n,
                                                allow_slow_non_contiguous=True), rd=[bC], chan=bC)
            dma('sp', self.oms[l, si, hs].rearrange("(h o) -> h o", o=1), m[:], rd=[bC], chan=bC)
        elif last_blk and tt == NT - 1:
            dma('sp', self.oCp[l, hs].rearrange("h (dc p) e -> p h dc e", p=128), C, rd=[bC], chan=bC, final=True)
            self.op('sp', lambda e: e.dma_start(out=self.onp[l, hs].rearrange("h (dc p) -> p h dc", p=128), in_=n,
                                                allow_slow_non_contiguous=True), rd=[bC], chan=bC)
            dma('sp', self.omp[l, hs].rearrange("(h o) -> h o", o=1), m[:], rd=[bC], chan=bC)

    def conv_block(self, l, us):
        op = self.op
        cw, cb = self.cw, self.cb
        if us[0].sidx is None:
          for cc in range(4):
            op('dve', lambda e, cc=cc: e.tensor_scalar(out=self.yT[:, cc, 0:TB], in0=self.aT[:, cc, 0:TB], scalar1=cw[:, l, cc, 0:1],
                                                       scalar2=cb[:, l, cc:cc + 1], op0=ALU.mult, op1=ALU.add),
               rd=[self.baT, self.bpar], wr=[self.byT])
        for j in range(1, 31 if us[0].sidx is None else 1):
            yield
            for cc in range(4):
                op('dve', lambda e, cc=cc, j=j: e.scalar_tensor_tensor(out=self.yT[:, cc, 0:TB], in0=self.aT[:, cc, j:j + TB],
                                                                       scalar=cw[:, l, cc, j:j + 1], in1=self.yT[:, cc, 0:TB],
                                                                       op0=ALU.mult, op1=ALU.add),
                   rd=[self.baT, self.byT], wr=[self.byT])
        for u in us:
            if u.sidx is None:
                continue
            si = u.sidx
            self.dma('sp', self.aTs[:, :, si, 0:30], self.scv[l, si].rearrange("(cc c) j -> c cc j", c=128), wr=[self.baTs[si]],
                     chan=self.baTs[si])
        for u in us:
            if u.sidx is None:
                continue
            si = u.sidx
            c0_ = u.c0
            yield
            for cc in range(4):
                op('dve', lambda e, cc=cc, si=si: e.tensor_tensor(out=self.ctmp[:, :], in0=self.aTs[:, cc, si, :], in1=cw[:, l, cc, :],
                                                                  op=ALU.mult), rd=[self.baTs[si], self.bpar], wr=[self.bctmp])
                op('dve', lambda e, cc=cc, si=si, c0_=c0_: e.tensor_reduce(out=self.yT[:, cc, c0_:c0_ + 1], in_=self.ctmp[:, :],
                                                                  axis=AX.X, op=ALU.add), rd=[self.bctmp], wr=[self.byT])
                op('dve', lambda e, cc=cc, si=si, c0_=c0_: e.tensor_tensor(out=self.yT[:, cc, c0_:c0_ + 1],
                                                                  in0=self.yT[:, cc, c0_:c0_ + 1], in1=cb[:, l, cc:cc + 1],
                                                                  op=ALU.add), rd=[self.byT, self.bpar], wr=[self.byT])

    def conv_unit(self, l, u):
        op = self.op
        L, tt, c0 = u.L, u.tt, u.c0
        P5, b5 = self.PS[5], self.bPS[5]
        idf = self.cst_sb[:, C_ID:C_ID + 128]

        def fn(e):
            ins = None
            for cc in range(4):
                ins = e.transpose(P5[0:L, cc * 128:(cc + 1) * 128], self.yT[:, cc, c0:c0 + L], idf)
            return ins
        op('pe', fn, rd=[self.byT, self.bcst], wr=[b5])
        yield
        bs = self.bcsm
        op('dve', lambda e: e.bn_stats(out=self.cst6[0:L, :], in_=P5[0:L, :]), rd=[b5], wr=[bs])
        op('dve', lambda e: e.bn_aggr(out=self.cmv[0:L, :], in_=self.cst6[0:L, :]), rd=[bs], wr=[bs])
        self.rstd(self.csml[0:L, 0:1], self.cmv[0:L, 1:2], L, bs)
        op('dve', lambda e: e.scalar_tensor_tensor(out=self.csml[0:L, 1:2], in0=self.cmv[0:L, 0:1], scalar=-1.0, in1=self.csml[0:L, 0:1],
                                                   op0=ALU.mult, op1=ALU.mult), rd=[bs], wr=[bs])
        yield
        op('act', lambda e: e.activation(out=self.yn[0:L, :], in_=P5[0:L, :], func=AF.Identity, scale=self.csml[0:L, 0:1],
                                         bias=self.csml[0:L, 1:2]), rd=[b5, bs], wr=[self.byn])
        op('dve', lambda e: e.tensor_tensor(out=self.yn[0:L, :], in0=self.yn[0:L, :], in1=self.clg[0:L, :], op=ALU.mult),
           rd=[self.byn, self.bcl], wr=[self.byn])
        op('dve', lambda e: e.tensor_tensor(out=self.yn[0:L, :], in0=self.yn[0:L, :], in1=self.clb[0:L, :], op=ALU.add),
           rd=[self.byn, self.bcl], wr=[self.byn])
        op('act', lambda e: e.activation(out=self.yn[0:L, :], in_=self.yn[0:L, :], func=AF.Silu), rd=[self.byn], wr=[self.byn])
        op('dve', lambda e: e.tensor_tensor(out=self.mtok[0:L, tt, 1024:1536], in0=self.yn[0:L, :], in1=self.sz[0:L, tt, :], op=ALU.mult),
           rd=[self.byn, self.bsz[tt]], wr=[self.bmtok[tt][2]])

    def conv_state_out(self, l, src_of, bsrc, dst):
        P5, b5 = self.PS[5], self.bPS[5]
        idf = self.cst_sb[:, C_ID:C_ID + 128]

        def fn(e):
            ins = None
            for cc in range(4):
                ins = e.transpose(P5[0:30, cc * 128:(cc + 1) * 128], src_of(cc), idf)
            return ins
        self.op('pe', fn, rd=[bsrc, self.bcst], wr=[b5])
        self.op('act', lambda e: e.activation(out=self.cvo[:, :], in_=P5[0:30, :], func=AF.Copy), rd=[b5], wr=[self.bcvo])
        self.dma('sp', dst, self.cvo[:, :], rd=[self.bcvo], chan=self.bcvo, final=True)

    def conv_finish(self, l, us, last_blk):
        for u in us:
            if u.sidx is not None:
                si = u.sidx
                self.conv_state_out(l, lambda cc, si=si: self.aTs[:, cc, si, 1:31], self.baTs[si], self.ocvs[l, si])
        if us[0].sidx is not None:
            return
        if last_blk:
            self.conv_state_out(l, lambda cc: self.aT[:, cc, TB:TB + 30], self.baT, self.ocvp[l])
        else:
            self.op('act', lambda e: e.activation(out=self.hist[l][:, :, :], in_=self.aT[:, :, TB:TB + 30], func=AF.Copy),
                    rd=[self.baT], wr=[self.bhist[l]])

    def conv_start(self, l, blk):
        if blk < 0:
            return
        if blk == 0:
            self.op('dve', lambda e: e.memset(self.aT[:, :, 0:30], 0.0), wr=[self.baT])
        else:
            self.op('act', lambda e: e.activation(out=self.aT[:, :, 0:30], in_=self.hist[l][:, :, :], func=AF.Copy),
                    rd=[self.bhist[l]], wr=[self.baT])

    def attn_unit(self, l, u, ci, last_blk):
        op, dma = self.op, self.dma
        L, tt, c0 = u.L, u.tt, u.c0
        cols = slice(c0, c0 + L)
        idb = self.cstb[:, CB_ID:CB_ID + 128]
        idf = self.cst_sb[:, C_ID:C_ID + 128]
        nmp = self.cstb[:, CB_NMP:CB_NMP + 512].rearrange("p (a b) -> p a b", b=128)
        nmc = self.cstb[:, CB_NMC:CB_NMC + 512].rearrange("p (a b) -> p a b", b=128)
        P2, P3, P4, P5 = self.PS[2], self.PS[3], self.PS[4], self.PS[5]
        b2, b3, b4, b5 = self.bPS[2], self.bPS[3], self.bPS[4], self.bPS[5]
        samp = u.sidx is not None
        if samp:
            cur_kT, bcur_kT, cur_v, bcur_v = self.akT[l][:, 0, :], self.bakT[l][0], self.vaug[l][:, 0], self.bvaug[l][0]
        else:
            sl = ci % 2
            cur_kT, bcur_kT, cur_v, bcur_v = self.akT[l][:, sl, :], self.bakT[l][sl], self.vaug[l][:, sl], self.bvaug[l][sl]
        op('act', lambda e: e.activation(out=self.kvb[0:L, :], in_=self.kvf[0:L, tt, 0:128], func=AF.Copy), rd=[self.bkvf[tt]], wr=[self.bkvb])
        op('pe', lambda e: e.transpose(self.PB[:, 512:512 + L], self.kvb[0:L, :], idb[0:L, 0:L]), rd=[self.bkvb, self.bcst], wr=[self.bPB[1]])
        op('dve', lambda e: e.tensor_copy(out=cur_kT[:, 0:L], in_=self.PB[:, 512:512 + L]), rd=[self.bPB[1]], wr=[bcur_kT])
        op('dve', lambda e: e.tensor_copy(out=cur_v[0:L, :, 0:64], in_=self.kvf[0:L, tt, 128:256].rearrange("p (a b) -> p a b", b=64)),
           rd=[self.bkvf[tt]], wr=[bcur_v])
        yield
        blocks = []
        if samp:
            si = u.sidx
            bc = self.bcache
            dma('sp', self.ckf[:, :], self.ck[l, si], wr=[bc], chan=bc)
            dma('sp', self.cvf[:, :], self.cv[l, si], wr=[bc], chan=bc)
            op('act', lambda e: e.activation(out=self.kvb[:, :], in_=self.ckf[:, :], func=AF.Copy), rd=[bc], wr=[self.bkvb])
            op('pe', lambda e: e.transpose(self.PB[:, 512:640], self.kvb[:, :], idb), rd=[self.bkvb, self.bcst], wr=[self.bPB[1]])
            op('dve', lambda e: e.tensor_copy(out=self.akTs[:, :], in_=self.PB[:, 512:640]), rd=[self.bPB[1]], wr=[self.bakTs])
            op('dve', lambda e: e.tensor_copy(out=self.vaugs[:, :, 0:64], in_=self.cvf[:, :].rearrange("p (a b) -> p a b", b=64)),
               rd=[bc], wr=[self.bvaugs])
            blocks.append((self.akTs[:, :], self.vaugs[:, :, :], 128, nmp, [self.bakTs, self.bvaugs]))
            blocks.append((cur_kT, cur_v, 1, None, [bcur_kT, bcur_v]))
            bo = self.bout['oks']
            dma('sp', self.oks[l, si, 0:127, :], self.ck[l, si, 1:128, :], wr=[bo], chan=bo, final=True)
            dma('sp', self.ovs[l, si, 0:127, :], self.cv[l, si, 1:128, :], wr=[bo], chan=bo, final=True)
            dma('sp', self.oks[l, si, 127:128, :], self.kvf[0:1, tt, 0:128], rd=[self.bkvf[tt]], chan=self.bkvf[tt], final=True)
            dma('sp', self.ovs[l, si, 127:128, :], self.kvf[0:1, tt, 128:256], rd=[self.bkvf[tt]], chan=self.bkvf[tt], final=True)
        else:
            if ci > 0:
                ps_ = 1 - sl
                blocks.append((self.akT[l][:, ps_, :], self.vaug[l][:, ps_], 128, nmp, [self.bakT[l][ps_], self.bvaug[l][ps_]]))
            blocks.append((cur_kT, cur_v, 128, nmc, [bcur_kT, bcur_v]))
            if last_blk and tt == NT - 1:
                dma('sp', self.okp[l], self.kvf[:, tt, 0:128], rd=[self.bkvf[tt]], chan=self.bkvf[tt], final=True)
                dma('sp', self.ovp[l], self.kvf[:, tt, 128:256], rd=[self.bkvf[tt]], chan=self.bkvf[tt], final=True)
        for g in range(2):
            gp = slice(64 * g, 64 * g + 64)
            yield
            for bi, (kTa, va, Lk, mask, bufs) in enumerate(blocks):
                PSs, bPSs = P2, b2

                def fn(e, kTa=kTa, Lk=Lk, mask=mask, PSs=PSs, gp=gp):
                    out = PSs[0:Lk, :].rearrange("p (a b) -> p a b", b=128)[:, :, 0:L]
                    ins = e.matmul(out, lhsT=kTa[gp, 0:Lk], rhs=self.aqT[gp, :, cols], start=True, stop=(mask is None))
                    if mask is not None:
                        ins = e.matmul(out, lhsT=idb[0:Lk, 0:Lk], rhs=mask[0:Lk, :, 0:L], start=False, stop=True)
                    return ins
                op('pe', fn, rd=self.baq + [bufs[0], self.bcst], wr=[bPSs])
                op('act', lambda e, Lk=Lk, PSs=PSs, bi=bi: e.activation(
                    out=self.PT[0:Lk, bi, :, 0:L], in_=PSs[0:Lk, :].rearrange("p (a b) -> p a b", b=128)[:, :, 0:L], func=AF.Exp),
                    rd=[bPSs], wr=[self.bPT[bi]])
            Po, bPo = P5, b5
            yield

            def fn_pv(e, Po=Po, g=g):
                ins = None
                for i in range(4):
                    for bi, (kTa, va, Lk, mask, bufs) in enumerate(blocks):
                        ins = e.matmul(Po[0:L, i * 65:(i + 1) * 65], lhsT=self.PT[0:Lk, bi, i, 0:L], rhs=va[0:Lk, g, :],
                                       start=(bi == 0), stop=(bi == len(blocks) - 1))
                return ins
            op('pe', fn_pv, rd=[self.bPT[bi] for bi in range(len(blocks))] + [b[1] for b in [blk_[4] for blk_ in blocks]], wr=[bPo])
            po3 = Po[0:L, 0:260].rearrange("p (a b) -> p a b", b=65)
            yield
            bs = self.basml
            op('dve', lambda e, po3=po3, g=g: e.tensor_tensor(out=self.asml[0:L, 0:4], in0=po3[:, :, 64], in1=self.esk[0:L, l, 4 * g:4 * g + 4],
                                                              op=ALU.add), rd=[bPo, self.bpar], wr=[bs])
            op('dve', lambda e: e.reciprocal(out=self.asml[0:L, 4:8], in_=self.asml[0:L, 0:4]), rd=[bs], wr=[bs])
            for i in range(4):
                op('dve', lambda e, po3=po3, i=i: e.tensor_scalar(out=self.ao[0:L, i, :], in0=po3[:, i, 0:64], scalar1=self.asml[0:L, 4 + i:5 + i],
                                                                  scalar2=None, op0=ALU.mult), rd=[bPo, bs], wr=[self.bao])
            h0 = 1536 + 256 * g
            op('dve', lambda e, g=g, h0=h0: e.tensor_tensor(out=self.mtok[0:L, tt, h0:h0 + 256], in0=self.ao[0:L].rearrange("p a b -> p (a b)"),
                                                            in1=self.saz[0:L, tt, 256 * g:256 * g + 256], op=ALU.mult),
               rd=[self.bao, self.bsaz[tt]], wr=[self.bmtok[tt][3]])

    def merge_T(self, u, rounds=(0, 1, 2, 3)):
        L, tt = u.L, u.tt
        self.transpose_rows(lambda kc: self.mtok[0:L, tt, kc * 128:(kc + 1) * 128], lambda r: self.bmtok[tt][r], self.mT, self.bmT[tt], u,
                            k_evac=1, rounds=rounds)

    def outproj(self, l, us, part):
        op = self.op
        kcs = [0, 1, 2, 3] + list(range(8, 16)) if part == 'A' else [4, 5, 6, 7]
        for j in range(4):
            s = self.get_w()
            W, bW = self.W[s], self.bW[s]
            for u in us:
                L, tt, c0 = u.L, u.tt, u.c0
                k = self.pcount = getattr(self, 'pcount', 0) + 1
                ps, bps = self.PS[k % 2], self.bPS[k % 2]
                for g0 in range(0, len(kcs), 4):
                    self.pull(3)

                    def fn(e, ps=ps, L=L, c0=c0, W=W, g0=g0):
                        ins = None
                        for i_ in range(g0, min(g0 + 4, len(kcs))):
                            kc = kcs[i_]
                            ins = e.matmul(ps[0:L, :], lhsT=self.mT[:, kc, c0:c0 + L], rhs=W[:, kc, :], start=(i_ == 0),
                                           stop=(i_ == len(kcs) - 1))
                        return ins
                    op('pe', fn, rd=[bW, self.bmT[tt]], wr=[bps])
                xa, bx = self.Xap(u, slice(512 * j, 512 * (j + 1)))
                if part == 'A':
                    op('dve', lambda e, xa=xa, ps=ps, L=L: e.scalar_tensor_tensor(out=xa, in0=xa, scalar=ALPHA, in1=ps[0:L, :],
                                                                                  op0=ALU.mult, op1=ALU.add), rd=[bps, bx], wr=[bx])
                else:
                    op('dve', lambda e, xa=xa, ps=ps, L=L: e.tensor_tensor(out=xa, in0=xa, in1=ps[0:L, :], op=ALU.add), rd=[bps, bx], wr=[bx])

    def final_ln(self, l, u, blk):
        op = self.op
        L, tt = u.L, u.tt
        xa, bx = self.Xap(u)
        bs = self.blsm
        for j in range(4):
            xj, _ = self.Xap(u, slice(512 * j, 512 * (j + 1)))
            op('dve', lambda e, xj=xj, j=j: e.bn_stats(out=self.lst[0:L, j, :], in_=xj), rd=[bx], wr=[bs])
        op('dve', lambda e: e.bn_aggr(out=self.lmv[0:L, :], in_=self.lst[0:L].rearrange("p a b -> p (a b)")), rd=[bs], wr=[bs])
        self.rstd(self.lsm[0:L, 0:1], self.lmv[0:L, 1:2], L, bs)
        op('dve', lambda e: e.scalar_tensor_tensor(out=self.lsm[0:L, 1:2], in0=self.lmv[0:L, 0:1], scalar=-1.0, in1=self.lsm[0:L, 0:1],
                                                   op0=ALU.mult, op1=ALU.mult), rd=[bs], wr=[bs])
        op('act', lambda e: e.activation(out=xa, in_=xa, func=AF.Identity, scale=self.lsm[0:L, 0:1], bias=self.lsm[0:L, 1:2]),
           rd=[bx, bs], wr=[bx])

    def final_gain(self, l, us, blk):
        op = self.op
        for j in range(4):
            k = self.lncount = getattr(self, 'lncount', 0) + 1
            lp, bl = self.lnp[k % 2], self.bln[k % 2]
            cs = slice(512 * j, 512 * (j + 1))
            self.dma('sp', lp[:, 0, :], self.p_lng[l, cs].partition_broadcast(128), wr=[bl], chan=bl)
            self.dma('sp', lp[:, 1, :], self.p_lnb[l, cs].partition_broadcast(128), wr=[bl], chan=bl)
            for u in us:
                xj, bx = self.Xap(u, cs)
                L = u.L
                op('dve', lambda e, xj=xj, lp=lp, L=L: e.tensor_tensor(out=xj, in0=xj, in1=lp[0:L, 0, :], op=ALU.mult), rd=[bx, bl], wr=[bx])
                op('dve', lambda e, xj=xj, lp=lp, L=L: e.tensor_tensor(out=xj, in0=xj, in1=lp[0:L, 1, :], op=ALU.add), rd=[bx, bl], wr=[bx])
        for u in us:
            self.final_out(l, u, blk)
            if l < DEPTH - 1:
                self.make_xT(u)

    def final_out(self, l, u, blk):
        L, tt = u.L, u.tt
        xa, bx = self.Xap(u)
        if l == DEPTH - 1:
            if u.sidx is None:
                r0 = blk * TB + tt * 128
                self.dma('sp', self.yp[r0:r0 + 128, :], xa, rd=[bx], chan=bx, final=True)
            else:
                self.dma('sp', self.ys[u.sidx:u.sidx + 1, :], xa, rd=[bx], chan=bx, final=True)

    def seq(self, *gens):
        for g in gens:
            yield from g

    def layer_block(self, blk, l):
        us = self.units(blk)
        last_blk = (blk == self.nblk - 1)
        self.load_params(l)
        if l == 0:
            self.load_x(blk, us)
        self.chk()
        nxt_prompt = (l == DEPTH - 1) and (0 <= blk + 1 < self.nblk)
        if nxt_prompt:
            self.prefetch_x(blk + 1)
        if l == 0:
            if self.xT_prefetched:
                self.xT_prefetched = False
            else:
                for u in us:
                    self.make_xT(u)
        self.chk()
        self.conv_start(l, blk)
        T = lambda kind, i: self.inproj_tile(l, kind, i, us)
        T('cu', 0)
        T('cg', 0)
        g_cb = self.conv_block(l, us)
        self.bg.append(g_cb)
        T('kvg', 0)
        for k_, i_ in (('qk', 0), ('qk', 1), ('v', 0), ('o', 0), ('z', 0)):
            T(k_, i_)
        g_m0 = self.seq(*[self.mlstm_unit(l, 0, u, last_blk) for u in us])
        self.bg.append(g_m0)
        for k_, i_ in (('cz', 0), ('aq', 0), ('az', 0)):
            T(k_, i_)
        g_at = self.seq(*[self.attn_unit(l, u, (blk * NT + u.tt if u.sidx is None else None), last_blk) for u in us])
        self.bg.append(g_at)
        for k_, i_ in (('qk', 2), ('qk', 3), ('v', 1), ('o', 1), ('z', 1)):
            T(k_, i_)
        self.drain([g_cb, g_at])
        self.bg.append(self.seq(*[self.conv_unit(l, u) for u in us]))
        self.drain([g_m0])
        g_m1 = self.seq(*[self.mlstm_unit(l, 1, u, last_blk) for u in us])
        self.bg.append(g_m1)
        self.drain([g for g in self.bg if g is not g_m1])
        for u in us:
            self.merge_T(u, rounds=(0, 2, 3))
        self.outproj(l, us, 'A')
        self.drain()
        self.conv_finish(l, us, last_blk)
        self.chk()
        if self.debug and blk == 0 and l == 0:
            for u in us:
                self.dump("mtok%d" % u.tt, self.mtok[0:u.L, u.tt, :], list(self.bmtok[u.tt]), (u.L, D), BF16)
        for u in us:
            self.merge_T(u, rounds=(1,))
        self.chk()
        self.outproj(l, us, 'B')
        if nxt_prompt:
            self.build_xT_prefetched()
        self.chk()
        for u in us:
            self.final_ln(l, u, blk)
        self.final_gain(l, us, blk)
        if self.debug and blk == 0 and l == 0:
            for u in us:
                self.dump("x1_%d" % u.tt, self.X[0:u.L, u.tt, :], self.bX[u.tt], (u.L, D))
        self.chk()

    def build(self):
        self.setup()
        seq = []
        blks = (list(range(-(NS // NSB), 0)) if self.with_sample else []) + list(range(self.nblk))
        for blk in blks:
            for l in range(DEPTH):
                seq += [(l, j) for j in range(NW)] + [(l, NW_IN + j, 'B') for j in range(4)]
        self.w_plan(seq)
        try:
            self.chk()
            for blk in blks:
                for l in range(DEPTH):
                    self.layer_block(blk, l)
        except StopIteration:
            pass
        self.P.emit(self.final)
        return self.nc


_CACHE = {}


def kernel(x_prompt, x_sample, state_C, state_n, state_m, state_conv, cache_k, cache_v,
           w_in, w_out, b_igate, b_fgate, m_norm_g, conv_w, conv_b, conv_ln_g, conv_ln_b,
           sinks, ln_g, ln_b):
    f = lambda a: np.ascontiguousarray(np.asarray(a, dtype=np.float32))
    x_prompt, x_sample = f(x_prompt), f(x_sample)
    if 'nc' not in _CACHE:
        _CACHE['nc'] = K().build()
    nc = _CACHE['nc']
    wt = host_weight_tiles(f(w_in), f(w_out))
    cst, cstB = host_consts()
    scv = np.ascontiguousarray(f(state_conv).transpose(0, 1, 3, 2))
    p_cw = np.ascontiguousarray(f(conv_w).transpose(0, 2, 1).reshape(DEPTH, 4, 128, 31).transpose(0, 2, 1, 3)).reshape(DEPTH, 128, 124)
    p_cb = np.ascontiguousarray(f(conv_b).reshape(DEPTH, 4, 128).transpose(0, 2, 1))
    sC, sn, sm = f(state_C), f(state_n), f(state_m)
    ck = f(cache_k).reshape(DEPTH, 32, 128, 128)
    cv = f(cache_v).reshape(DEPTH, 32, 128, 128)
    in_maps = []
    xp_dummy = np.zeros_like(x_prompt[0])
    for c in range(NCORE):
        b = c % 2
        ss = slice(NS * c, NS * (c + 1))
        in_maps.append({
            "xp": x_prompt[b] if c < 2 else xp_dummy, "xs": np.ascontiguousarray(x_sample[ss, 0, :]), "wt": wt,
            "sC": np.ascontiguousarray(sC[:, ss]), "sn": np.ascontiguousarray(sn[:, ss]), "sm": np.ascontiguousarray(sm[:, ss]),
            "scv": np.ascontiguousarray(scv[:, ss]), "ck": np.ascontiguousarray(ck[:, ss]), "cv": np.ascontiguousarray(cv[:, ss]),
            "cst": cst, "cstB": cstB, "p_bi": f(b_igate), "p_bf": f(b_fgate), "p_mng": f(m_norm_g), "p_cw": p_cw, "p_cb": p_cb,
            "p_clg": f(conv_ln_g), "p_clb": f(conv_ln_b), "p_sk": f(sinks), "p_lng": f(ln_g), "p_lnb": f(ln_b),
        })
    res = run_bass_kernel_spmd(nc, in_maps, core_ids=list(range(NCORE))).results
    cat = lambda k, ax: np.concatenate([r[k] for r in res], axis=ax)
    stack2 = lambda k: np.stack([res[0][k], res[1][k]], axis=1)
    y_prompt = np.stack([res[0]["yp"], res[1]["yp"]], axis=0)
    y_sample = cat("ys", 0).reshape(32, 1, D)
    new_C_p = stack2("oCp")
    new_n_p = stack2("onp")
    new_m_p = stack2("omp")
    new_conv_p = stack2("ocvp")
    new_k_p = stack2("okp").reshape(DEPTH, 2, 128, 2, 64)
    new_v_p = stack2("ovp").reshape(DEPTH, 2, 128, 2, 64)
    new_C_s = cat("oCs", 1)
    new_n_s = cat("ons", 1)
    new_m_s = cat("oms", 1)
    new_conv_s = cat("ocvs", 1)
    new_k_s = cat("oks", 1).reshape(DEPTH, 32, 128, 2, 64)
    new_v_s = cat("ovs", 1).reshape(DEPTH, 32, 128, 2, 64)
    return (y_prompt, y_sample, new_C_p, new_n_p, new_m_p, new_conv_p, new_k_p, new_v_p,
            new_C_s, new_n_s, new_m_s, new_conv_s, new_k_s, new_v_s)
```

```python
import numpy as np
import concourse.bass as bass
import concourse.mybir as mybir
from concourse.bass_utils import run_bass_kernel_spmd

F32 = mybir.dt.float32
BF16 = mybir.dt.bfloat16
ALU = mybir.AluOpType
AF = mybir.ActivationFunctionType
AX = mybir.AxisListType

D = 2048
SEQ = 4096
DEPTH = 2
NCORE = 8
NS = 4
TB = 256
NT = TB // 128
NBLK = SEQ // TB
NSB = 2
NTT = max(NT, NSB)
NCF = max(TB, NSB)
ALPHA = (2 * DEPTH) ** 0.25
EPS = 1e-5
BIG = 30000.0
NW_IN = 16
NW = 20
EPOCH = 3000

T_KVG, T_QK, T_V, T_O, T_Z, T_CU, T_CG, T_CZ, T_AQ, T_AZ = 'kvg', 'qk', 'v', 'o', 'z', 'cu', 'cg', 'cz', 'aq', 'az'
TILE_ORDER = [('cu', 0), ('cg', 0), ('kvg', 0), ('qk', 0), ('qk', 1), ('v', 0), ('o', 0), ('z', 0),
              ('cz', 0), ('aq', 0), ('az', 0),
              ('qk', 2), ('qk', 3), ('v', 1), ('o', 1), ('z', 1)]

C_ID, C_TRI, C_SEL, C_ONE, C_EPS = 0, 128, 256, 512, 513
NCST = 514
CB_ID, CB_BIGM, CB_NMP, CB_NMC, CB_ONE = 0, 128, 256, 768, 1280
NCSTB = 1281


def host_weight_tiles(w_in, w_out):
    mq, mk, mv, mo, mi, mf, mz = 0, 1024, 2048, 3072, 4096, 4100, 4104
    cu, cg, cz, aq, ak, av, az = 5128, 5640, 6152, 6664, 7176, 7304, 7432
    L = w_in.shape[0]
    out = np.zeros((L, NW, 2048, 512), np.float32)
    for l in range(L):
        W = w_in[l]
        for j, (kind, i) in enumerate(TILE_ORDER):
            t = out[l, j]
            if kind == 'kvg':
                t[:, 0:128] = W[:, ak:ak + 128]
                t[:, 128:256] = W[:, av:av + 128]
                t[:, 256:260] = W[:, mi:mi + 4]
                t[:, 260:264] = W[:, mf:mf + 4]
            elif kind == 'qk':
                t[:, 0:256] = W[:, mq + 256 * i: mq + 256 * (i + 1)]
                t[:, 256:512] = W[:, mk + 256 * i: mk + 256 * (i + 1)]
            elif kind == 'v':
                t[:] = W[:, mv + 512 * i: mv + 512 * (i + 1)]
            elif kind == 'o':
                t[:] = W[:, mo + 512 * i: mo + 512 * (i + 1)]
            elif kind == 'z':
                t[:] = W[:, mz + 512 * i: mz + 512 * (i + 1)]
            elif kind == 'cu':
                t[:] = W[:, cu:cu + 512]
            elif kind == 'cg':
                t[:] = W[:, cg:cg + 512]
            elif kind == 'cz':
                t[:] = W[:, cz:cz + 512]
            elif kind == 'aq':
                for c in range(4):
                    t[:, c * 128: c * 128 + 64] = W[:, aq + 64 * c: aq + 64 * (c + 1)]
                    t[:, c * 128 + 64: c * 128 + 128] = W[:, aq + 64 * (4 + c): aq + 64 * (5 + c)]
            elif kind == 'az':
                t[:] = W[:, az:az + 512]
        for j in range(4):
            out[l, NW_IN + j] = w_out[l][:, 512 * j: 512 * (j + 1)]
    out = out.reshape(L, NW, 16, 128, 512).transpose(0, 1, 3, 2, 4)
    return np.ascontiguousarray(out).reshape(L, NW, 128, 16 * 512)


def host_consts():
    c = np.zeros((128, NCST), np.float32)
    cb = np.zeros((128, NCSTB), np.float32)
    s = np.arange(128)[:, None]
    t = np.arange(128)[None, :]
    c[:, C_ID:C_ID + 128] = (s == t)
    c[:, C_TRI:C_TRI + 128] = (s <= t)
    c[0, C_SEL:C_SEL + 128] = 1.0
    c[1, C_SEL + 128:C_SEL + 256] = 1.0
    c[:, C_ONE] = 1.0
    c[:, C_EPS] = EPS
    cb[:, CB_ID:CB_ID + 128] = (s == t)
    cb[:, CB_BIGM:CB_BIGM + 128] = np.where(s > t, BIG, 0.0)
    nmp = np.where(s < t, -BIG, 0.0)
    nmc = np.where(s > t, -BIG, 0.0)
    for h in range(4):
        cb[:, CB_NMP + 128 * h: CB_NMP + 128 * (h + 1)] = nmp
        cb[:, CB_NMC + 128 * h: CB_NMC + 128 * (h + 1)] = nmc
    cb[:, CB_ONE] = 1.0
    return c, cb


class Buf:
    __slots__ = ('name', 'w', 'r', 'sem', 'cnt')

    def __init__(self, name):
        self.name = name
        self.w = None
        self.r = []
        self.sem = None
        self.cnt = 0


class Op:
    __slots__ = ('eng', 'fn', 'deps', 'dma', 'chan', 'val', 'sig', 'signo', 'idx', 'dmaw')


class Prog:
    ENGS = ('pe', 'act', 'dve', 'pool', 'sp')

    def __init__(self, nc):
        self.nc = nc
        self.ops = []
        self.nbuf = 0

    def buf(self, name=None):
        self.nbuf += 1
        return Buf(name or "b%d" % self.nbuf)

    def op(self, eng, fn, rd=(), wr=(), chan=None):
        o = Op()
        o.eng, o.fn, o.idx = eng, fn, len(self.ops)
        deps = set()
        for b in rd:
            if b.w is not None:
                deps.add(b.w)
        for b in wr:
            if b.w is not None:
                deps.add(b.w)
            deps.update(b.r)
        o.deps = deps
        o.dmaw = {}
        for d in deps:
            p = self.ops[d]
            if p.dma:
                o.dmaw[id(p.chan)] = (p.chan, 16 * p.chan.cnt)
        o.dma = chan is not None
        o.chan = chan
        o.sig = False
        o.signo = 0
        o.val = 0
        if chan is not None:
            chan.cnt += 1
            o.val = 16 * chan.cnt
        for b in rd:
            b.r.append(o.idx)
        for b in wr:
            b.w = o.idx
            b.r = []
        self.ops.append(o)
        return o

    def emit(self, final_chans):
        nc = self.nc
        ops = self.ops
        for o in ops:
            keep = {}
            for d in o.deps:
                p = ops[d]
                if p.dma:
                    continue
                if p.eng == 'pe' and o.eng == 'pe' and not o.dma:
                    continue
                k = ('c', p.eng)
                if k not in keep or keep[k] < d:
                    keep[k] = d
            o.deps = sorted(keep.values())
            for d in o.deps:
                ops[d].sig = True
        cnt = {e: 0 for e in self.ENGS}
        for o in ops:
            if o.sig and not o.dma:
                cnt[o.eng] += 1
                o.signo = cnt[o.eng]
        esems = {e: [nc.alloc_semaphore(name="s_%s_%d" % (e, i)) for i in range((cnt[e] + EPOCH - 1) // EPOCH + 1)]
                 for e in self.ENGS}
        chans = {}
        for o in ops:
            if o.dma and o.chan.sem is None:
                o.chan.sem = nc.alloc_semaphore(name="d_%s_%d" % (o.chan.name, len(chans)))
                chans[id(o.chan)] = o.chan

        def target(p):
            if p.dma:
                return p.chan.sem, p.val
            n = p.signo - 1
            return esems[p.eng][n // EPOCH], n % EPOCH + 1

        by_eng = {e: [o for o in ops if o.eng == e] for e in self.ENGS}

        def run(ename, eng):
            waited = {}
            for o in by_eng[ename]:
                for ch, val in o.dmaw.values():
                    k = id(ch.sem)
                    if waited.get(k, 0) >= val:
                        continue
                    eng.wait_ge(ch.sem, val)
                    waited[k] = val
                for d in o.deps:
                    sem, val = target(ops[d])
                    k = id(sem)
                    if waited.get(k, 0) >= val:
                        continue
                    eng.wait_ge(sem, val)
                    waited[k] = val
                ins = o.fn(eng)
                if o.dma:
                    ins.then_inc(o.chan.sem, 16)
                elif o.sig:
                    sem, _ = target(o)
                    ins.then_inc(sem, 1)
            if ename == 'sp':
                for ch in chans.values():
                    eng.wait_ge(ch.sem, 16 * ch.cnt)

        with nc.Block() as block:
            @block.tensor
            def _(e):
                run('pe', e)

            @block.scalar
            def _(e):
                run('act', e)

            @block.vector
            def _(e):
                run('dve', e)

            @block.gpsimd
            def _(e):
                run('pool', e)

            @block.sync
            def _(e):
                run('sp', e)


class Unit:
    def __init__(self, L, tt, c0, sidx=None):
        self.L, self.tt, self.c0, self.sidx = L, tt, c0, sidx


class K:
    def __init__(self, nblk=NBLK, with_sample=True, debug=False, stage=None):
        self.stage = stage
        self.stage_n = 0
        self.debug = debug
        self.bg = []
        self.nblk = nblk
        self.with_sample = with_sample
        self.debug = debug
        self.nc = nc = bass.Bass("TRN2", target_bir_lowering=False)
        self.P = Prog(nc)
        self.final = []
        self.dbg_outs = []
        di = lambda n, s: nc.dram_tensor(n, list(s), F32, kind="ExternalInput").ap()
        do = lambda n, s: nc.dram_tensor(n, list(s), F32, kind="ExternalOutput").ap()
        self.xp = di("xp", (SEQ, D))
        self.xs = di("xs", (NS, D))
        self.wt = di("wt", (DEPTH, NW, 128, 16 * 512))
        self.wb = nc.dram_tensor("wb", [DEPTH, NW, 128, 16 * 512], BF16, kind="Internal").ap()
        self.bwb = [[self.P.buf("wb%d_%d" % (l, j)) for j in range(NW)] for l in range(DEPTH)]
        self.sC = di("sC", (DEPTH, NS, 4, 256, 256))
        self.sn = di("sn", (DEPTH, NS, 4, 256))
        self.sm = di("sm", (DEPTH, NS, 4))
        self.scv = di("scv", (DEPTH, NS, 512, 30))
        self.ck = di("ck", (DEPTH, NS, 128, 128))
        self.cv = di("cv", (DEPTH, NS, 128, 128))
        self.cst = di("cst", (128, NCST))
        self.cstB = di("cstB", (128, NCSTB))
        self.p_bi = di("p_bi", (DEPTH, 4))
        self.p_bf = di("p_bf", (DEPTH, 4))
        self.p_mng = di("p_mng", (DEPTH, 1024))
        self.p_cw = di("p_cw", (DEPTH, 128, 4 * 31))
        self.p_cb = di("p_cb", (DEPTH, 128, 4))
        self.p_clg = di("p_clg", (DEPTH, 512))
        self.p_clb = di("p_clb", (DEPTH, 512))
        self.p_sk = di("p_sk", (DEPTH, 8))
        self.p_lng = di("p_lng", (DEPTH, D))
        self.p_lnb = di("p_lnb", (DEPTH, D))
        self.yp = do("yp", (SEQ, D))
        self.ys = do("ys", (NS, D))
        self.oCp = do("oCp", (DEPTH, 4, 256, 256))
        self.onp = do("onp", (DEPTH, 4, 256))
        self.omp = do("omp", (DEPTH, 4))
        self.ocvp = do("ocvp", (DEPTH, 30, 512))
        self.okp = do("okp", (DEPTH, 128, 128))
        self.ovp = do("ovp", (DEPTH, 128, 128))
        self.oCs = do("oCs", (DEPTH, NS, 4, 256, 256))
        self.ons = do("ons", (DEPTH, NS, 4, 256))
        self.oms = do("oms", (DEPTH, NS, 4))
        self.ocvs = do("ocvs", (DEPTH, NS, 30, 512))
        self.oks = do("oks", (DEPTH, NS, 128, 128))
        self.ovs = do("ovs", (DEPTH, NS, 128, 128))
        self.alloc()

    def chk(self):
        self.stage_n += 1
        if self.stage is not None and self.stage_n >= self.stage:
            raise StopIteration

    def dump(self, name, ap, buf, shape, dt=F32):
        o = self.nc.dram_tensor("dbg_" + name, list(shape), dt, kind="ExternalOutput").ap()
        bufs = buf if isinstance(buf, list) else [buf]
        self.dma('sp', o, ap, rd=bufs, chan=bufs[0], final=True)

    def sb(self, name, shape, dt=F32):
        return self.nc.alloc_sbuf_tensor(name, list(shape), dt)

    def B(self, name=None):
        return self.P.buf(name)

    def op(self, eng, fn, rd=(), wr=(), chan=None):
        return self.P.op(eng, fn, rd, wr, chan)

    def rstd(self, out, in_, L, b):
        epsc = self.cst_sb[0:L, C_EPS:C_EPS + 1]
        self.op('act', lambda e: e.activation(out=out, in_=in_, func=AF.Ln, bias=epsc), rd=[b, self.bcst], wr=[b])
        self.op('act', lambda e: e.activation(out=out, in_=out, func=AF.Exp, scale=-0.5), rd=[b], wr=[b])

    def pull(self, n=1):
        for _ in range(n):
            for g in list(self.bg):
                try:
                    next(g)
                except StopIteration:
                    self.bg.remove(g)

    def drain(self, gens=None):
        while True:
            act = [g for g in self.bg if gens is None or g in gens]
            if not act:
                return
            self.pull()

    def dma(self, q, out, in_, rd=(), wr=(), chan=None, final=False):
        if final and chan not in self.final:
            self.final.append(chan)
        return self.op(q, lambda e, o=out, i=in_: e.dma_start(out=o, in_=i), rd=rd, wr=wr, chan=chan)

    def alloc(self):
        nc = self.nc
        sb, B = self.sb, self.B
        self.PS = [nc.alloc_psum_tensor("ps%d" % i, [128, 512], F32) for i in range(7)]
        self.PB = nc.alloc_psum_tensor("psb", [128, 1024], BF16)
        self.bPS = [B("ps%d" % i) for i in range(7)]
        _b = B("psb")
        self.bPB = [_b, _b]
        self.bP6 = {k: self.bPS[6] for k in ('b', 'tm', 'den', 'dn', 'arow', 'brow')}
        self.cst_sb = sb("cst_sb", [128, NCST])
        self.cstb = sb("cstb", [128, NCSTB], BF16)
        self.bcst = B("cst")
        self.bcstb = B("cstb")
        self.NSLOT = 3
        self.W = [sb("w%d" % i, [128, 16, 512], BF16) for i in range(self.NSLOT)]
        self.bW = [B("w%d" % i) for i in range(self.NSLOT)]
        self.wcount = 0
        self.X = sb("X", [128, NTT, D])
        self.bX = [B("X%d" % i) for i in range(NTT)]
        self.xT = sb("xT", [128, 16, NCF], BF16)
        self.bxT = [B("xT%d" % i) for i in range(NTT)]
        self.mT = self.xT
        self.bmT = self.bxT
        self.mtok = sb("mtok", [128, NTT, D], BF16)
        self.bmtok = [[B("mtok%d_%d" % (i, j)) for j in range(4)] for i in range(NTT)]
        self.xb = self.mtok[:, 0, :]
        self.bxb = B("xb")
        self.xpre = sb("xpre", [128, NT, D], BF16)
        self.bxpre = [B("xpre%d" % i) for i in range(NT)]
        self.xT_prefetched = False
        self.qT_ = [sb("qT%d" % p, [128, 2, 2, NCF], BF16) for p in range(2)]
        self.kT_ = [sb("kT%d" % p, [128, 2, 2, NCF], BF16) for p in range(2)]
        self.bqT_ = [[[B() for _ in range(2)] for _ in range(2)] for p in range(2)]
        self.bkT_ = [[[B() for _ in range(2)] for _ in range(2)] for p in range(2)]
        self.vtok_ = [sb("vtok%d" % p, [128, NTT, 2, 256], BF16) for p in range(2)]
        self.bv_ = [[B() for _ in range(NTT)] for p in range(2)]
        self.G_ = [sb("G%d" % p, [128, NTT, 2, 256]) for p in range(2)]
        self.bG_ = [[B() for _ in range(NTT)] for p in range(2)]
        self.gtmp = sb("gtmp", [128, 512])
        self.bgtmp = B()
        self.graw = sb("graw", [128, NTT, 8])
        self.ig = sb("ig", [128, NTT, 4])
        self.sp = sb("spl", [128, NTT, 4])
        self.bgate = [B() for _ in range(NTT)]
        self.a_sb = sb("a_sb", [128, 2]); self.ba = B()
        self.arow = sb("arow", [2, 128]); self.barow = B()
        self.Mrow = sb("Mrow", [2, 128]); self.bMrow = B()
        self.mrow = sb("mrow", [2, 128]); self.bmrow = B()
        self.w0row = sb("w0row", [2, 128]); self.bw0row = B()
        self.emrow = sb("emrow", [2, 128]); self.bemrow = B()
        self.wT = sb("wT", [128, 2, 128]); self.bwT = B()
        self.swT = sb("swT", [128, 2, 128], BF16); self.bswT = B()
        self.qs = sb("qs", [128, 2, 2, 128], BF16); self.bqs = B()
        self.g0bc = sb("g0bc", [128, 2]); self.bg0bc = B()
        self.ktok = sb("ktok", [128, 2, 256], BF16); self.bktok = B()
        self.gv = sb("gv", [128, 2, 256], BF16); self.bgv = B()
        self.gb = sb("gb", [128, 2], BF16); self.bgb = B()
        self.sm6 = sb("sm6", [128, 2, 6]); self.bsm6 = B()
        self.mv = sb("mv", [128, 2, 2]); self.bmv = B()
        self.sml = sb("sml", [128, 16]); self.bsml = B()
        self.hn = sb("hn", [128, 2, 256]); self.bhn = B()
        self.C = [sb("C%d" % l, [128, 4, 2, 256]) for l in range(DEPTH)]
        self.n = [sb("n%d" % l, [128, 4, 2]) for l in range(DEPTH)]
        self.Cb = [sb("Cb%d" % l, [128, 4, 2, 256], BF16) for l in range(DEPTH)]
        self.nb = [sb("nb%d" % l, [128, 4, 2], BF16) for l in range(DEPTH)]
        self.m = [[sb("m%d_%d" % (l, hp), [2, 1]) for hp in range(2)] for l in range(DEPTH)]
        self.bC = [[B() for _ in range(2)] for _ in range(DEPTH)]
        self.Cs = sb("Cs", [128, 2, 2, 256]); self.ns = sb("ns", [128, 2, 2])
        self.Csb = sb("Csb", [128, 2, 2, 256], BF16); self.nsb = sb("nsb", [128, 2, 2], BF16)
        self.ms = sb("ms", [2, 1]); self.bCs = B("Cs")
        self.aT = sb("aT", [128, 4, 30 + TB])
        self.baT = B("aT")
        self.aTs = sb("aTs", [128, 4, NS, 31])
        self.baTs = [B() for _ in range(NS)]
        self.bcu = [B() for _ in range(4)]
        self.cuF = sb("cuF", [128, 4, NCF])
        self.sg = sb("sg", [128, NCF]); self.bsg = B()
        self.yT = sb("yT", [128, 4, NCF]); self.byT = B("yT")
        self.ctmp = sb("ctmp", [128, 31]); self.bctmp = B()
        self.sz = sb("sz", [128, NTT, 512], BF16); self.bsz = [B() for _ in range(NTT)]
        self.yn = sb("yn", [128, 512]); self.byn = B()
        self.cst6 = sb("cst6", [128, 6]); self.cmv = sb("cmv", [128, 2]); self.csml = sb("csml", [128, 4]); self.bcsm = B()
        self.cvo = sb("cvo", [30, 512]); self.bcvo = B("cvo")
        self.aqT = sb("aqT", [128, 4, NCF], BF16); self.baq = [B() for _ in range(4)]
        self.kvf = sb("kvf", [128, NTT, 256]); self.bkvf = [B() for _ in range(NTT)]
        self.kvb = sb("kvb", [128, 128], BF16); self.bkvb = B()
        self.akT = [sb("akT%d" % l, [128, 2, 128], BF16) for l in range(DEPTH)]
        self.bakT = [[B(), B()] for l in range(DEPTH)]
        self.vaug = [sb("vaug%d" % l, [128, 2, 2, 65], BF16) for l in range(DEPTH)]
        self.bvaug = [[B(), B()] for l in range(DEPTH)]
        self.hist = [sb("hist%d" % l, [128, 4, 30]) for l in range(DEPTH)]
        self.bhist = [B() for l in range(DEPTH)]
        self.akTs = sb("akTs", [128, 128], BF16); self.vaugs = sb("vaugs", [128, 2, 65], BF16)
        self.ckf = sb("ckf", [128, 128]); self.cvf = sb("cvf", [128, 128]); self.bcache = B("cache")
        self.bakTs = B(); self.bvaugs = B()
        self.saz = sb("saz", [128, NTT, 512], BF16); self.bsaz = [B() for _ in range(NTT)]
        self.PT = sb("PT", [128, 2, 4, 128], BF16); self.bPT = [B(), B()]
        self.asml = sb("asml", [128, 8]); self.basml = B()
        self.ao = sb("ao", [128, 4, 64]); self.bao = B()
        self.bi_bc = sb("bi_bc", [128, DEPTH, 4]); self.bf_bc = sb("bf_bc", [128, DEPTH, 4])
        self.esk = sb("esk", [128, DEPTH, 8])
        self.mng = sb("mng", [128, 1024])
        self.cw = sb("cw", [128, DEPTH, 4, 31]); self.cb = sb("cb", [128, DEPTH, 4])
        self.clg = sb("clg", [128, 512]); self.clb = sb("clb", [128, 512])
        self.lnp = [sb("lnp%d" % i, [128, 2, 512]) for i in range(2)]
        self.bpar = B("par"); self.bmng = B("mng"); self.bcl = B("cl"); self.bln = [B("ln0"), B("ln1")]
        self.lst = sb("lst", [128, 4, 6]); self.lmv = sb("lmv", [128, 2]); self.lsm = sb("lsm", [128, 4]); self.blsm = B()
        self.bout = {k: B(k) for k in ('oCp', 'onp', 'omp', 'okp', 'ovp', 'oms', 'oks', 'ovs')}

    def setup(self):
        op, dma = self.op, self.dma
        dma('sp', self.cst_sb[:], self.cst, wr=[self.bcst], chan=self.bcst)
        dma('pool', self.cstb[:], self.cstB, wr=[self.bcstb], chan=self.bcstb)
        op('dve', lambda e: e.tensor_copy(out=self.cst_sb[:, C_ONE:C_ONE + 1], in_=self.cst_sb[:, C_ONE:C_ONE + 1]),
           rd=[self.bcst, self.bcstb], wr=[self.bcst])
        bp = self.bpar

        def bc(dst, src):
            dma('sp', dst, src.partition_broadcast(128), wr=[bp], chan=bp)
        bc(self.bi_bc[:].rearrange("p l f -> p (l f)"), self.p_bi.rearrange("l f -> (l f)"))
        bc(self.bf_bc[:].rearrange("p l f -> p (l f)"), self.p_bf.rearrange("l f -> (l f)"))
        bc(self.esk[:].rearrange("p l f -> p (l f)"), self.p_sk.rearrange("l f -> (l f)"))
        for l in range(DEPTH):
            dma('sp', self.cw[:, l].rearrange("p a b -> p (a b)"), self.p_cw[l], wr=[bp], chan=bp)
            dma('sp', self.cb[:, l], self.p_cb[l], wr=[bp], chan=bp)
        op('act', lambda e: e.activation(out=self.esk[:], in_=self.esk[:], func=AF.Exp), rd=[bp], wr=[bp])
        for l in range(DEPTH):
            for hp in range(2):
                hs = slice(2 * hp, 2 * hp + 2)
                b = self.bC[l][hp]
                op('dve', lambda e, l=l, hs=hs: e.memset(self.C[l][:, hs], 0.0), wr=[b])
                op('dve', lambda e, l=l, hs=hs: e.memset(self.n[l][:, hs], 0.0), wr=[b])
                op('dve', lambda e, l=l, hs=hs: e.memset(self.Cb[l][:, hs], 0.0), wr=[b])
                op('dve', lambda e, l=l, hs=hs: e.memset(self.nb[l][:, hs], 0.0), wr=[b])
                op('dve', lambda e, l=l, hp=hp: e.memset(self.m[l][hp][:], 0.0), wr=[b])
        for l in range(DEPTH):
            for r in range(2):
                op('dve', lambda e, l=l, r=r: e.memset(self.vaug[l][:, r, :, 64:65], 1.0), wr=[self.bvaug[l][r]])
        op('dve', lambda e: e.memset(self.vaugs[:, :, 64:65], 1.0), wr=[self.bvaugs])

    def load_params(self, l):
        dma = self.dma
        dma('sp', self.mng[:, :], self.p_mng[l].partition_broadcast(128), wr=[self.bmng], chan=self.bmng)
        dma('sp', self.clg[:, :], self.p_clg[l].partition_broadcast(128), wr=[self.bcl], chan=self.bcl)
        dma('sp', self.clb[:, :], self.p_clb[l].partition_broadcast(128), wr=[self.bcl], chan=self.bcl)

    def w_plan(self, seq):
        seen = set()
        for ent in seq:
            l, j = ent[0], ent[1]
            if (l, j) not in seen:
                seen.add((l, j))
                self.dma('pool', self.wb[l, j], self.wt[l, j], wr=[self.bwb[l][j]], chan=self.bwb[l][j])
        self.wseq = seq
        self.w_issued = 0
        self.w_used = 0

    def get_w(self):
        while self.w_issued < len(self.wseq) and self.w_issued < self.w_used + self.NSLOT:
            ent = self.wseq[self.w_issued]
            l, j = ent[0], ent[1]
            s = self.w_issued % self.NSLOT
            if len(ent) > 2 and ent[2] == 'B':
                self.dma('pool', self.W[s][:, 4:8, :], self.wb[l, j].rearrange("p (a b) -> p a b", b=512)[:, 4:8, :],
                         rd=[self.bwb[l][j]], wr=[self.bW[s]], chan=self.bW[s])
            else:
                self.dma('pool', self.W[s][:].rearrange("p a b -> p (a b)"), self.wb[l, j], rd=[self.bwb[l][j]], wr=[self.bW[s]],
                         chan=self.bW[s])
            self.w_issued += 1
        s = self.w_used % self.NSLOT
        self.w_used += 1
        return s

    def units(self, blk):
        if blk < 0:
            s0 = NSB * (blk + NS // NSB)
            return [Unit(1, j, j, sidx=s0 + j) for j in range(NSB)]
        return [Unit(128, tt, tt * 128) for tt in range(NT)]

    def Xap(self, u, cols=slice(0, D)):
        return self.X[0:u.L, u.tt, cols], self.bX[u.tt]

    def load_x(self, blk, us):
        for u in us:
            xa, bx = self.Xap(u)
            if u.sidx is None:
                r0 = blk * TB + u.tt * 128
                self.dma('sp', xa, self.xp[r0:r0 + 128, :], wr=[bx], chan=bx)
            else:
                self.dma('sp', xa, self.xs[u.sidx:u.sidx + 1, :], wr=[bx], chan=bx)

    def transpose_rows(self, src_of, bsrc, dst, bdst, u, k_evac=0, rounds=(0, 1, 2, 3), act_only=False):
        L, c0 = u.L, u.c0
        idb = self.cstb[:, CB_ID:CB_ID + 128]
        for r in rounds:
            h = r % 2
            pb = self.PB[:, h * 512:(h + 1) * 512]

            def fn(e, r=r, pb=pb):
                ins = None
                for j in range(4):
                    ins = e.transpose(pb[:, j * 128:j * 128 + L], src_of(4 * r + j), idb[0:L, 0:L])
                return ins
            bs_r = bsrc(r) if callable(bsrc) else bsrc
            self.op('pe', fn, rd=[bs_r, self.bcst] if not isinstance(bs_r, list) else bs_r + [self.bcst], wr=[self.bPB[h]])
            src = pb.rearrange("p (a b) -> p a b", b=128)[:, :, 0:L]
            out = dst[:, 4 * r:4 * r + 4, c0:c0 + L]
            if act_only or (r + k_evac) % 2 == 0:
                self.op('act', lambda e, o=out, s=src: e.activation(out=o, in_=s, func=AF.Copy), rd=[self.bPB[h]], wr=[bdst])
            else:
                self.op('dve', lambda e, o=out, s=src: e.tensor_copy(out=o, in_=s), rd=[self.bPB[h]], wr=[bdst])

    def prefetch_x(self, blk):
        for tt in range(NT):
            r0 = blk * TB + tt * 128
            self.dma('pool', self.xpre[:, tt, :], self.xp[r0:r0 + 128, :], wr=[self.bxpre[tt]], chan=self.bxpre[tt])

    def build_xT_prefetched(self):
        for tt in range(NT):
            u = Unit(128, tt, tt * 128)
            self.transpose_rows(lambda kc, tt=tt: self.xpre[0:128, tt, kc * 128:(kc + 1) * 128], self.bxpre[tt], self.xT, self.bxT[tt], u,
                                act_only=True)
        self.xT_prefetched = True

    def make_xT(self, u):
        L = u.L
        xa, bx = self.Xap(u)
        bxb = list(self.bmtok[0])
        self.op('act', lambda e: e.activation(out=self.xb[0:L, :], in_=xa, func=AF.Copy), rd=[bx], wr=bxb)
        self.transpose_rows(lambda kc: self.xb[0:L, kc * 128:(kc + 1) * 128], bxb, self.xT, self.bxT[u.tt], u)

    def inproj_tile(self, l, kind, i, us):
        s = self.get_w()
        W, bW = self.W[s], self.bW[s]
        op = self.op
        pp_ = (i // 2) if kind == 'qk' else (i if kind in ('v', 'o', 'z') else 0)
        self.qT, self.kT, self.bqT, self.bkT = self.qT_[pp_], self.kT_[pp_], self.bqT_[pp_], self.bkT_[pp_]
        self.vtok, self.bv, self.G, self.bG = self.vtok_[pp_], self.bv_[pp_], self.G_[pp_], self.bG_[pp_]
        hs_samp = us[0].sidx is not None
        ncols = NSB if hs_samp else TB
        assert ncols <= 512
        bxT_all = [self.bxT[u.tt] for u in us]
        one = self.cst_sb[:, C_ONE:C_ONE + 1]
        if kind in ('qk', 'cu', 'cg', 'aq'):
            for ec in range(4):
                self.pull()
                k = self.pcount = getattr(self, 'pcount', 0) + 1
                ps, bps = self.PS[k % 2], self.bPS[k % 2]

                def fn(e, ec=ec, ps=ps):
                    ins = None
                    for kc in range(16):
                        ins = e.matmul(ps[:, 0:ncols], lhsT=W[:, kc, ec * 128:(ec + 1) * 128], rhs=self.xT[:, kc, 0:ncols],
                                       start=(kc == 0), stop=(kc == 15))
                    return ins
                op('pe', fn, rd=[bW] + bxT_all, wr=[bps])
                src = ps[:, 0:ncols]
                if kind == 'qk':
                    hl = i % 2
                    if ec < 2:
                        op('act', lambda e, s_=src, o=self.qT[:, hl, ec, 0:ncols]: e.activation(out=o, in_=s_, func=AF.Copy),
                           rd=[bps], wr=[self.bqT[hl][ec]])
                    else:
                        op('act', lambda e, s_=src, o=self.kT[:, hl, ec - 2, 0:ncols]: e.activation(out=o, in_=s_, func=AF.Copy, scale=1.0 / 16.0),
                           rd=[bps], wr=[self.bkT[hl][ec - 2]])
                elif kind == 'cu':
                    op('act', lambda e, s_=src, o=self.cuF[:, ec, 0:ncols]: e.activation(out=o, in_=s_, func=AF.Copy),
                       rd=[bps], wr=[self.bcu[ec]])
                elif kind == 'cg':
                    op('act', lambda e, s_=src: e.activation(out=self.sg[:, 0:ncols], in_=s_, func=AF.Sigmoid),
                       rd=[bps], wr=[self.bsg])
                    if not hs_samp:
                        op('dve', lambda e, ec=ec: e.tensor_tensor(out=self.aT[:, ec, 30:30 + TB], in0=self.cuF[:, ec, 0:TB],
                                                                   in1=self.sg[:, 0:TB], op=ALU.mult),
                           rd=[self.bsg, self.bcu[ec]], wr=[self.baT])
                    else:
                        s0 = us[0].sidx
                        op('dve', lambda e, ec=ec, s0=s0: e.tensor_tensor(out=self.aTs[:, ec, s0:s0 + NSB, 30], in0=self.cuF[:, ec, 0:NSB],
                                                                          in1=self.sg[:, 0:NSB], op=ALU.mult),
                           rd=[self.bsg, self.bcu[ec]], wr=[self.baTs[u_.sidx] for u_ in us])
                elif kind == 'aq':
                    op('act', lambda e, s_=src, o=self.aqT[:, ec, 0:ncols]: e.activation(out=o, in_=s_, func=AF.Copy, scale=0.125),
                       rd=[bps], wr=[self.baq[ec]])
            return
        for u in us:
            L, tt, c0 = u.L, u.tt, u.c0
            self.pull()
            k = self.pcount = getattr(self, 'pcount', 0) + 1
            ps, bps = self.PS[k % 2], self.bPS[k % 2]

            def fn(e, ps=ps, L=L, c0=c0):
                ins = None
                for kc in range(16):
                    ins = e.matmul(ps[0:L, :], lhsT=self.xT[:, kc, c0:c0 + L], rhs=W[:, kc, :], start=(kc == 0), stop=(kc == 15))
                return ins
            op('pe', fn, rd=[bW, self.bxT[tt]], wr=[bps])
            src = ps[0:L, :]
            if kind == 'v':
                op('act', lambda e, s_=src, o=self.vtok[0:L, tt].rearrange("p a b -> p (a b)"): e.activation(out=o, in_=s_, func=AF.Copy),
                   rd=[bps], wr=[self.bv[tt]])
            elif kind == 'o':
                g = self.G[0:L, tt].rearrange("p a b -> p (a b)")
                op('act', lambda e, s_=src, g=g: e.activation(out=g, in_=s_, func=AF.Sigmoid), rd=[bps], wr=[self.bG[tt]])
                op('dve', lambda e, g=g, L=L: e.tensor_tensor(out=g, in0=g, in1=self.mng[0:L, i * 512:(i + 1) * 512], op=ALU.mult),
                   rd=[self.bG[tt], self.bmng], wr=[self.bG[tt]])
            elif kind == 'z':
                g = self.G[0:L, tt].rearrange("p a b -> p (a b)")
                op('act', lambda e, s_=src, L=L: e.activation(out=self.gtmp[0:L, :], in_=s_, func=AF.Silu), rd=[bps], wr=[self.bgtmp])
                op('dve', lambda e, g=g, L=L: e.tensor_tensor(out=g, in0=g, in1=self.gtmp[0:L, :], op=ALU.mult),
                   rd=[self.bG[tt], self.bgtmp], wr=[self.bG[tt]])
            elif kind == 'cz':
                op('act', lambda e, s_=src, o=self.sz[0:L, tt, :]: e.activation(out=o, in_=s_, func=AF.Silu), rd=[bps], wr=[self.bsz[tt]])
            elif kind == 'az':
                op('act', lambda e, s_=src, o=self.saz[0:L, tt, :]: e.activation(out=o, in_=s_, func=AF.Silu), rd=[bps], wr=[self.bsaz[tt]])
            elif kind == 'kvg':
                bg = self.bgate[tt]
                op('act', lambda e, ps=ps, o=self.kvf[0:L, tt, :], L=L: e.activation(out=o, in_=ps[0:L, 0:256], func=AF.Copy),
                   rd=[bps], wr=[self.bkvf[tt]])
                op('dve', lambda e, ps=ps, L=L, tt=tt: e.tensor_tensor(out=self.ig[0:L, tt, :], in0=ps[0:L, 256:260],
                                                                      in1=self.bi_bc[0:L, l, :], op=ALU.add),
                   rd=[bps, self.bpar], wr=[bg])
                op('dve', lambda e, ps=ps, L=L, tt=tt: e.tensor_tensor(out=self.sp[0:L, tt, :], in0=ps[0:L, 260:264],
                                                                      in1=self.bf_bc[0:L, l, :], op=ALU.add),
                   rd=[bps, self.bpar], wr=[bg])
                op('act', lambda e, L=L, tt=tt: e.activation(out=self.sp[0:L, tt, :], in_=self.sp[0:L, tt, :], func=AF.Exp, scale=-1.0),
                   rd=[bg], wr=[bg])
                op('act', lambda e, L=L, tt=tt: e.activation(out=self.sp[0:L, tt, :], in_=self.sp[0:L, tt, :], func=AF.Ln,
                                                             bias=one[0:L, :]), rd=[bg, self.bcst], wr=[bg])

    def mlstm_unit(self, l, hp, u, last_blk):
        op, dma = self.op, self.dma
        qT_l, kT_l, vtok_l, G_l = self.qT_[hp], self.kT_[hp], self.vtok_[hp], self.G_[hp]
        bqT_l, bkT_l, bv_l, bG_l = self.bqT_[hp], self.bkT_[hp], self.bv_[hp], self.bG_[hp]
        L, tt, c0 = u.L, u.tt, u.c0
        cols = slice(c0, c0 + L)
        hs = slice(2 * hp, 2 * hp + 2)
        idf = self.cst_sb[:, C_ID:C_ID + 128]
        tri = self.cst_sb[:, C_TRI:C_TRI + 128]
        idb = self.cstb[:, CB_ID:CB_ID + 128]
        bigb = self.cstb[:, CB_BIGM:CB_BIGM + 128]
        onesb = self.cstb[:, CB_ONE:CB_ONE + 1]
        sel = lambda hl: self.cst_sb[0:2, C_SEL + 128 * hl:C_SEL + 128 * (hl + 1)]
        bcst = self.bcst
        P2, P3, P4, P5, P6 = self.PS[2], self.PS[3], self.PS[4], self.PS[5], self.PS[6]
        b2, b3, b4, b5 = self.bPS[2], self.bPS[3], self.bPS[4], self.bPS[5]
        b6 = self.bP6
        samp = u.sidx is not None
        if not samp:
            C, n, Cb, nb, m, bC = self.C[l][:, hs], self.n[l][:, hs], self.Cb[l][:, hs], self.nb[l][:, hs], self.m[l][hp], self.bC[l][hp]
        else:
            si = u.sidx
            C, n, Cb, nb, m, bC = self.Cs[:], self.ns[:], self.Csb[:], self.nsb[:], self.ms, self.bCs
            dma('sp', C, self.sC[l, si, hs].rearrange("h (dc p) e -> p h dc e", p=128), wr=[bC], chan=bC)
            self.op('sp', lambda e: e.dma_start(out=n, in_=self.sn[l, si, hs].rearrange("h (dc p) -> p h dc", p=128),
                                                allow_slow_non_contiguous=True), wr=[bC], chan=bC)
            dma('sp', m[:], self.sm[l, si, hs].rearrange("(h o) -> h o", o=1), wr=[bC], chan=bC)
            op('act', lambda e: e.activation(out=Cb, in_=C, func=AF.Copy), rd=[bC], wr=[bC])
            op('act', lambda e: e.activation(out=nb, in_=n, func=AF.Copy), rd=[bC], wr=[bC])
        bg = self.bgate[tt]
        sp_ = self.sp[0:L, tt, hs]
        op('pe', lambda e: e.matmul(P6[0:L, 0:2], lhsT=tri[0:L, 0:L], rhs=sp_, start=True, stop=True), rd=[bg, bcst], wr=[b6['b']])
        yield
        op('dve', lambda e: e.tensor_tensor(out=self.a_sb[0:L, :], in0=self.ig[0:L, tt, hs], in1=P6[0:L, 0:2], op=ALU.add),
           rd=[bg, b6['b']], wr=[self.ba])
        op('pe', lambda e: e.matmul(P6[0:2, 16:16 + L], lhsT=self.a_sb[0:L, :], rhs=idf[0:L, 0:L], start=True, stop=True),
           rd=[self.ba, bcst], wr=[b6['arow']])
        op('pe', lambda e: e.matmul(P6[0:2, 144:144 + L], lhsT=sp_, rhs=tri[0:L, 0:L], start=True, stop=True),
           rd=[bg, bcst], wr=[b6['brow']])
        op('dve', lambda e: e.tensor_copy(out=self.arow[:, 0:L], in_=P6[0:2, 16:16 + L]), rd=[b6['arow']], wr=[self.barow])
        yield
        op('dve', lambda e: e.tensor_tensor_scan(out=self.Mrow[:, 0:L], data0=self.arow[:, 0:L], data1=self.arow[:, 0:L],
                                                 initial=m[:], op0=ALU.max, op1=ALU.max), rd=[self.barow, bC], wr=[self.bMrow])
        op('dve', lambda e: e.tensor_tensor(out=self.mrow[:, 0:L], in0=self.Mrow[:, 0:L], in1=P6[0:2, 144:144 + L], op=ALU.subtract),
           rd=[self.bMrow, b6['brow']], wr=[self.bmrow])
        yield
        op('act', lambda e: e.activation(out=self.w0row[:, 0:L], in_=self.Mrow[:, 0:L], func=AF.Exp, scale=-1.0, bias=m[:]),
           rd=[self.bMrow, bC], wr=[self.bw0row])
        op('act', lambda e: e.activation(out=self.emrow[:, 0:L], in_=self.mrow[:, 0:L], func=AF.Exp, scale=-1.0),
           rd=[self.bmrow], wr=[self.bemrow])
        op('dve', lambda e: e.tensor_copy(out=m[:], in_=self.mrow[:, L - 1:L]), rd=[self.bmrow], wr=[bC])
        yield
        op('pe', lambda e: e.matmul(P6[0:L, 2:4], lhsT=self.emrow[0:2, 0:L], rhs=idf[0:2, 0:2], start=True, stop=True),
           rd=[self.bemrow, bcst], wr=[b6['tm']])

        def fn_bc(e):
            ins = None
            for hl in range(2):
                e.matmul(P3[0:L, hl * 128:hl * 128 + L], lhsT=sel(hl)[:, 0:L], rhs=self.Mrow[0:2, 0:L], start=True, stop=False)
                e.matmul(P3[0:L, hl * 128:hl * 128 + L], lhsT=idb[0:L, 0:L], rhs=bigb[0:L, 0:L], start=False, stop=True)
                ins = e.matmul(P3[:, 256 + hl * 128:256 + hl * 128 + L], lhsT=sel(hl), rhs=self.w0row[0:2, 0:L], start=True, stop=True)
            return ins
        op('pe', fn_bc, rd=[self.bMrow, self.bw0row, bcst], wr=[b3])
        yield
        for hl in range(2):
            op('act', lambda e, hl=hl: e.activation(out=self.wT[0:L, hl, 0:L], in_=P3[0:L, hl * 128:hl * 128 + L], func=AF.Exp,
                                                    scale=-1.0, bias=self.a_sb[0:L, hl:hl + 1]), rd=[b3, self.ba], wr=[self.bwT])

        yield
        def fn_s(e):
            ins = None
            for hl in range(2):
                for dc in range(2):
                    ins = e.matmul(P4[0:L, hl * 128:hl * 128 + L], lhsT=kT_l[:, hl, dc, cols], rhs=qT_l[:, hl, dc, cols],
                                   start=(dc == 0), stop=(dc == 1))
            return ins
        op('pe', fn_s, rd=bkT_l[0] + bkT_l[1] + bqT_l[0] + bqT_l[1], wr=[b4])
        for hl in range(2):
            op('dve', lambda e, hl=hl: e.tensor_tensor(out=self.swT[0:L, hl, 0:L], in0=P4[0:L, hl * 128:hl * 128 + L],
                                                       in1=self.wT[0:L, hl, 0:L], op=ALU.mult), rd=[b4, self.bwT], wr=[self.bswT])
        yield
        for hl in range(2):
            for dc in range(2):
                op('dve', lambda e, hl=hl, dc=dc: e.tensor_tensor(out=self.qs[:, hl, dc, 0:L], in0=qT_l[:, hl, dc, cols],
                                                                  in1=P3[:, 256 + hl * 128:256 + hl * 128 + L], op=ALU.mult),
                   rd=[b3, bqT_l[hl][dc]], wr=[self.bqs])
        op('dve', lambda e: e.tensor_copy(out=self.g0bc[:, :], in_=P3[:, 256:512].rearrange("p (a b) -> p a b", b=128)[:, :, L - 1]),
           rd=[b3], wr=[self.bg0bc])

        yield
        def fn_kt(e):
            ins = None
            for hl in range(2):
                for dc in range(2):
                    j = hl * 2 + dc
                    ins = e.transpose(self.PB[0:L, j * 128:(j + 1) * 128], kT_l[:, hl, dc, cols], idb)
            return ins
        op('pe', fn_kt, rd=bkT_l[0] + bkT_l[1] + [bcst], wr=[self.bPB[0]])
        op('act', lambda e: e.activation(out=self.ktok[0:L].rearrange("p a b -> p (a b)"), in_=self.PB[0:L, 0:512], func=AF.Copy),
           rd=[self.bPB[0]], wr=[self.bktok])

        yield
        def fn_num(e):
            ins = None
            for hl in range(2):
                e.matmul(P4[0:L, hl * 256:(hl + 1) * 256], lhsT=self.swT[0:L, hl, 0:L], rhs=vtok_l[0:L, tt, hl, :], start=True, stop=False)
                for dc in range(2):
                    e.matmul(P4[0:L, hl * 256:(hl + 1) * 256], lhsT=self.qs[:, hl, dc, 0:L], rhs=Cb[:, hl, dc, :],
                             start=False, stop=(dc == 1))
                e.matmul(P6[0:L, 4 + hl:5 + hl], lhsT=self.swT[0:L, hl, 0:L], rhs=onesb[0:L, :], start=True, stop=False)
                for dc in range(2):
                    ins = e.matmul(P6[0:L, 4 + hl:5 + hl], lhsT=self.qs[:, hl, dc, 0:L], rhs=nb[:, hl, dc:dc + 1],
                                   start=False, stop=(dc == 1))
            return ins
        op('pe', fn_num, rd=[self.bswT, self.bqs, bv_l[tt], bC, bcst], wr=[b4, b6['den']])
        yield
        s = self.sml
        bs = self.bsml
        op('act', lambda e: e.activation(out=s[0:L, 0:2], in_=P6[0:L, 4:6], func=AF.Abs), rd=[b6['den']], wr=[bs])
        op('dve', lambda e: e.tensor_tensor(out=s[0:L, 0:2], in0=s[0:L, 0:2], in1=P6[0:L, 2:4], op=ALU.max), rd=[bs, b6['tm']], wr=[bs])
        op('dve', lambda e: e.reciprocal(out=s[0:L, 2:4], in_=s[0:L, 0:2]), rd=[bs], wr=[bs])
        for hl in range(2):
            op('dve', lambda e, hl=hl: e.bn_stats(out=self.sm6[0:L, hl, :], in_=P4[0:L, hl * 256:(hl + 1) * 256]), rd=[b4], wr=[self.bsm6])
            op('dve', lambda e, hl=hl: e.bn_aggr(out=self.mv[0:L, hl, :], in_=self.sm6[0:L, hl, :]), rd=[self.bsm6], wr=[self.bmv])
        op('dve', lambda e: e.tensor_tensor(out=s[0:L, 4:6], in0=s[0:L, 2:4], in1=s[0:L, 2:4], op=ALU.mult), rd=[bs], wr=[bs])
        op('dve', lambda e: e.tensor_tensor(out=s[0:L, 4:6], in0=s[0:L, 4:6], in1=self.mv[0:L, :, 1], op=ALU.mult),
           rd=[bs, self.bmv], wr=[bs])
        self.rstd(s[0:L, 4:6], s[0:L, 4:6], L, bs)
        op('dve', lambda e: e.tensor_tensor(out=s[0:L, 6:8], in0=s[0:L, 4:6], in1=s[0:L, 2:4], op=ALU.mult), rd=[bs], wr=[bs])
        op('dve', lambda e: e.scalar_tensor_tensor(out=s[0:L, 8:10], in0=self.mv[0:L, :, 0], scalar=-1.0, in1=s[0:L, 6:8],
                                                   op0=ALU.mult, op1=ALU.mult), rd=[bs, self.bmv], wr=[bs])
        yield
        for hl in range(2):
            op('act', lambda e, hl=hl: e.activation(out=self.hn[0:L, hl, :], in_=P4[0:L, hl * 256:(hl + 1) * 256], func=AF.Identity,
                                                    scale=s[0:L, 6 + hl:7 + hl], bias=s[0:L, 8 + hl:9 + hl]), rd=[b4, bs], wr=[self.bhn])
        op('dve', lambda e: e.tensor_tensor(out=self.mtok[0:L, tt, hp * 512:(hp + 1) * 512], in0=self.hn[0:L].rearrange("p a b -> p (a b)"),
                                            in1=G_l[0:L, tt].rearrange("p a b -> p (a b)"), op=ALU.mult),
           rd=[self.bhn, bG_l[tt]], wr=[self.bmtok[tt][hp]])
        yield
        op('act', lambda e: e.activation(out=self.gb[0:L, :], in_=self.wT[0:L, :, L - 1], func=AF.Copy), rd=[self.bwT], wr=[self.bgb])
        for hl in range(2):
            op('dve', lambda e, hl=hl: e.tensor_scalar(out=self.gv[0:L, hl, :], in0=vtok_l[0:L, tt, hl, :],
                                                       scalar1=self.wT[0:L, hl, L - 1:L], scalar2=None, op0=ALU.mult),
               rd=[self.bwT, bv_l[tt]], wr=[self.bgv])
        for hl in range(2):
            yield

            def fn_dc(e, hl=hl):
                ins = None
                for dc in range(2):
                    e.matmul(P3[:, dc * 256:(dc + 1) * 256], lhsT=self.ktok[0:L, hl, dc * 128:(dc + 1) * 128], rhs=self.gv[0:L, hl, :],
                             start=True, stop=True)
                    ins = e.matmul(P6[:, 6 + 2 * hl + dc:7 + 2 * hl + dc], lhsT=self.ktok[0:L, hl, dc * 128:(dc + 1) * 128],
                                   rhs=self.gb[0:L, hl:hl + 1], start=True, stop=True)
                return ins
            op('pe', fn_dc, rd=[self.bktok, self.bgv, self.bgb], wr=[b3, b6['dn']])
            for dc in range(2):
                op('dve', lambda e, hl=hl, dc=dc: e.scalar_tensor_tensor(out=C[:, hl, dc, :], in0=C[:, hl, dc, :],
                                                                         scalar=self.g0bc[:, hl:hl + 1], in1=P3[:, dc * 256:(dc + 1) * 256],
                                                                         op0=ALU.mult, op1=ALU.add), rd=[b3, self.bg0bc, bC], wr=[bC])
            op('dve', lambda e, hl=hl: e.scalar_tensor_tensor(out=n[:, hl, :], in0=n[:, hl, :], scalar=self.g0bc[:, hl:hl + 1],
                                                              in1=P6[:, 6 + 2 * hl:8 + 2 * hl], op0=ALU.mult, op1=ALU.add),
               rd=[b6['dn'], self.bg0bc, bC], wr=[bC])
        op('act', lambda e: e.activation(out=Cb, in_=C, func=AF.Copy), rd=[bC], wr=[bC])
        op('act', lambda e: e.activation(out=nb, in_=n, func=AF.Copy), rd=[bC], wr=[bC])
        yield
        if samp:
            si = u.sidx
            dma('sp', self.oCs[l, si, hs].rearrange("h (dc p) e -> p h dc e", p=128), C, rd=[bC], chan=bC, final=True)
            self.final.append(bC) if bC not in self.final else None
            self.op('sp', lambda e: e.dma_start(out=self.ons[l, si, hs].rearrange("h (dc p) -> p h dc", p=128), in_=n,
                                                allow_slow_non_contiguous=True), rd=[bC], chan=bC)
            dma('sp', self.oms[l, si, hs].rearrange("(h o) -> h o", o=1), m[:], rd=[bC], chan=bC)
        elif last_blk and tt == NT - 1:
            dma('sp', self.oCp[l, hs].rearrange("h (dc p) e -> p h dc e", p=128), C, rd=[bC], chan=bC, final=True)
            self.op('sp', lambda e: e.dma_start(out=self.onp[l, hs].rearrange("h (dc p) -> p h dc", p=128), in_=n,
                                                allow_slow_non_contiguous=True), rd=[bC], chan=bC)
            dma('sp', self.omp[l, hs].rearrange("(h o) -> h o", o=1), m[:], rd=[bC], chan=bC)

    def conv_block(self, l, us):
        op = self.op
        cw, cb = self.cw, self.cb
        if us[0].sidx is None:
          for cc in range(4):
            op('dve', lambda e, cc=cc: e.tensor_scalar(out=self.yT[:, cc, 0:TB], in0=self.aT[:, cc, 0:TB], scalar1=cw[:, l, cc, 0:1],
                                                       scalar2=cb[:, l, cc:cc + 1], op0=ALU.mult, op1=ALU.add),
               rd=[self.baT, self.bpar], wr=[self.byT])
        for j in range(1, 31 if us[0].sidx is None else 1):
            yield
            for cc in range(4):
                op('dve', lambda e, cc=cc, j=j: e.scalar_tensor_tensor(out=self.yT[:, cc, 0:TB], in0=self.aT[:, cc, j:j + TB],
                                                                       scalar=cw[:, l, cc, j:j + 1], in1=self.yT[:, cc, 0:TB],
                                                                       op0=ALU.mult, op1=ALU.add),
                   rd=[self.baT, self.byT], wr=[self.byT])
        for u in us:
            if u.sidx is None:
                continue
            si = u.sidx
            self.dma('sp', self.aTs[:, :, si, 0:30], self.scv[l, si].rearrange("(cc c) j -> c cc j", c=128), wr=[self.baTs[si]],
                     chan=self.baTs[si])
        for u in us:
            if u.sidx is None:
                continue
            si = u.sidx
            c0_ = u.c0
            yield
            for cc in range(4):
                op('dve', lambda e, cc=cc, si=si: e.tensor_tensor(out=self.ctmp[:, :], in0=self.aTs[:, cc, si, :], in1=cw[:, l, cc, :],
                                                                  op=ALU.mult), rd=[self.baTs[si], self.bpar], wr=[self.bctmp])
                op('dve', lambda e, cc=cc, si=si, c0_=c0_: e.tensor_reduce(out=self.yT[:, cc, c0_:c0_ + 1], in_=self.ctmp[:, :],
                                                                  axis=AX.X, op=ALU.add), rd=[self.bctmp], wr=[self.byT])
                op('dve', lambda e, cc=cc, si=si, c0_=c0_: e.tensor_tensor(out=self.yT[:, cc, c0_:c0_ + 1],
                                                                  in0=self.yT[:, cc, c0_:c0_ + 1], in1=cb[:, l, cc:cc + 1],
                                                                  op=ALU.add), rd=[self.byT, self.bpar], wr=[self.byT])

    def conv_unit(self, l, u):
        op = self.op
        L, tt, c0 = u.L, u.tt, u.c0
        P5, b5 = self.PS[5], self.bPS[5]
        idf = self.cst_sb[:, C_ID:C_ID + 128]

        def fn(e):
            ins = None
            for cc in range(4):
                ins = e.transpose(P5[0:L, cc * 128:(cc + 1) * 128], self.yT[:, cc, c0:c0 + L], idf)
            return ins
        op('pe', fn, rd=[self.byT, self.bcst], wr=[b5])
        yield
        bs = self.bcsm
        op('dve', lambda e: e.bn_stats(out=self.cst6[0:L, :], in_=P5[0:L, :]), rd=[b5], wr=[bs])
        op('dve', lambda e: e.bn_aggr(out=self.cmv[0:L, :], in_=self.cst6[0:L, :]), rd=[bs], wr=[bs])
        self.rstd(self.csml[0:L, 0:1], self.cmv[0:L, 1:2], L, bs)
        op('dve', lambda e: e.scalar_tensor_tensor(out=self.csml[0:L, 1:2], in0=self.cmv[0:L, 0:1], scalar=-1.0, in1=self.csml[0:L, 0:1],
                                                   op0=ALU.mult, op1=ALU.mult), rd=[bs], wr=[bs])
        yield
        op('act', lambda e: e.activation(out=self.yn[0:L, :], in_=P5[0:L, :], func=AF.Identity, scale=self.csml[0:L, 0:1],
                                         bias=self.csml[0:L, 1:2]), rd=[b5, bs], wr=[self.byn])
        op('dve', lambda e: e.tensor_tensor(out=self.yn[0:L, :], in0=self.yn[0:L, :], in1=self.clg[0:L, :], op=ALU.mult),
           rd=[self.byn, self.bcl], wr=[self.byn])
        op('dve', lambda e: e.tensor_tensor(out=self.yn[0:L, :], in0=self.yn[0:L, :], in1=self.clb[0:L, :], op=ALU.add),
           rd=[self.byn, self.bcl], wr=[self.byn])
        op('act', lambda e: e.activation(out=self.yn[0:L, :], in_=self.yn[0:L, :], func=AF.Silu), rd=[self.byn], wr=[self.byn])
        op('dve', lambda e: e.tensor_tensor(out=self.mtok[0:L, tt, 1024:1536], in0=self.yn[0:L, :], in1=self.sz[0:L, tt, :], op=ALU.mult),
           rd=[self.byn, self.bsz[tt]], wr=[self.bmtok[tt][2]])

    def conv_state_out(self, l, src_of, bsrc, dst):
        P5, b5 = self.PS[5], self.bPS[5]
        idf = self.cst_sb[:, C_ID:C_ID + 128]

        def fn(e):
            ins = None
            for cc in range(4):
                ins = e.transpose(P5[0:30, cc * 128:(cc + 1) * 128], src_of(cc), idf)
            return ins
        self.op('pe', fn, rd=[bsrc, self.bcst], wr=[b5])
        self.op('act', lambda e: e.activation(out=self.cvo[:, :], in_=P5[0:30, :], func=AF.Copy), rd=[b5], wr=[self.bcvo])
        self.dma('sp', dst, self.cvo[:, :], rd=[self.bcvo], chan=self.bcvo, final=True)

    def conv_finish(self, l, us, last_blk):
        for u in us:
            if u.sidx is not None:
                si = u.sidx
                self.conv_state_out(l, lambda cc, si=si: self.aTs[:, cc, si, 1:31], self.baTs[si], self.ocvs[l, si])
        if us[0].sidx is not None:
            return
        if last_blk:
            self.conv_state_out(l, lambda cc: self.aT[:, cc, TB:TB + 30], self.baT, self.ocvp[l])
        else:
            self.op('act', lambda e: e.activation(out=self.hist[l][:, :, :], in_=self.aT[:, :, TB:TB + 30], func=AF.Copy),
                    rd=[self.baT], wr=[self.bhist[l]])

    def conv_start(self, l, blk):
        if blk < 0:
            return
        if blk == 0:
            self.op('dve', lambda e: e.memset(self.aT[:, :, 0:30], 0.0), wr=[self.baT])
        else:
            self.op('act', lambda e: e.activation(out=self.aT[:, :, 0:30], in_=self.hist[l][:, :, :], func=AF.Copy),
                    rd=[self.bhist[l]], wr=[self.baT])

    def attn_unit(self, l, u, ci, last_blk):
        op, dma = self.op, self.dma
        L, tt, c0 = u.L, u.tt, u.c0
        cols = slice(c0, c0 + L)
        idb = self.cstb[:, CB_ID:CB_ID + 128]
        idf = self.cst_sb[:, C_ID:C_ID + 128]
        nmp = self.cstb[:, CB_NMP:CB_NMP + 512].rearrange("p (a b) -> p a b", b=128)
        nmc = self.cstb[:, CB_NMC:CB_NMC + 512].rearrange("p (a b) -> p a b", b=128)
        P2, P3, P4, P5 = self.PS[2], self.PS[3], self.PS[4], self.PS[5]
        b2, b3, b4, b5 = self.bPS[2], self.bPS[3], self.bPS[4], self.bPS[5]
        samp = u.sidx is not None
        if samp:
            cur_kT, bcur_kT, cur_v, bcur_v = self.akT[l][:, 0, :], self.bakT[l][0], self.vaug[l][:, 0], self.bvaug[l][0]
        else:
            sl = ci % 2
            cur_kT, bcur_kT, cur_v, bcur_v = self.akT[l][:, sl, :], self.bakT[l][sl], self.vaug[l][:, sl], self.bvaug[l][sl]
        op('act', lambda e: e.activation(out=self.kvb[0:L, :], in_=self.kvf[0:L, tt, 0:128], func=AF.Copy), rd=[self.bkvf[tt]], wr=[self.bkvb])
        op('pe', lambda e: e.transpose(self.PB[:, 512:512 + L], self.kvb[0:L, :], idb[0:L, 0:L]), rd=[self.bkvb, self.bcst], wr=[self.bPB[1]])
        op('dve', lambda e: e.tensor_copy(out=cur_kT[:, 0:L], in_=self.PB[:, 512:512 + L]), rd=[self.bPB[1]], wr=[bcur_kT])
        op('dve', lambda e: e.tensor_copy(out=cur_v[0:L, :, 0:64], in_=self.kvf[0:L, tt, 128:256].rearrange("p (a b) -> p a b", b=64)),
           rd=[self.bkvf[tt]], wr=[bcur_v])
        yield
        blocks = []
        if samp:
            si = u.sidx
            bc = self.bcache
            dma('sp', self.ckf[:, :], self.ck[l, si], wr=[bc], chan=bc)
            dma('sp', self.cvf[:, :], self.cv[l, si], wr=[bc], chan=bc)
            op('act', lambda e: e.activation(out=self.kvb[:, :], in_=self.ckf[:, :], func=AF.Copy), rd=[bc], wr=[self.bkvb])
            op('pe', lambda e: e.transpose(self.PB[:, 512:640], self.kvb[:, :], idb), rd=[self.bkvb, self.bcst], wr=[self.bPB[1]])
            op('dve', lambda e: e.tensor_copy(out=self.akTs[:, :], in_=self.PB[:, 512:640]), rd=[self.bPB[1]], wr=[self.bakTs])
            op('dve', lambda e: e.tensor_copy(out=self.vaugs[:, :, 0:64], in_=self.cvf[:, :].rearrange("p (a b) -> p a b", b=64)),
               rd=[bc], wr=[self.bvaugs])
            blocks.append((self.akTs[:, :], self.vaugs[:, :, :], 128, nmp, [self.bakTs, self.bvaugs]))
            blocks.append((cur_kT, cur_v, 1, None, [bcur_kT, bcur_v]))
            bo = self.bout['oks']
            dma('sp', self.oks[l, si, 0:127, :], self.ck[l, si, 1:128, :], wr=[bo], chan=bo, final=True)
            dma('sp', self.ovs[l, si, 0:127, :], self.cv[l, si, 1:128, :], wr=[bo], chan=bo, final=True)
            dma('sp', self.oks[l, si, 127:128, :], self.kvf[0:1, tt, 0:128], rd=[self.bkvf[tt]], chan=self.bkvf[tt], final=True)
            dma('sp', self.ovs[l, si, 127:128, :], self.kvf[0:1, tt, 128:256], rd=[self.bkvf[tt]], chan=self.bkvf[tt], final=True)
        else:
            if ci > 0:
                ps_ = 1 - sl
                blocks.append((self.akT[l][:, ps_, :], self.vaug[l][:, ps_], 128, nmp, [self.bakT[l][ps_], self.bvaug[l][ps_]]))
            blocks.append((cur_kT, cur_v, 128, nmc, [bcur_kT, bcur_v]))
            if last_blk and tt == NT - 1:
                dma('sp', self.okp[l], self.kvf[:, tt, 0:128], rd=[self.bkvf[tt]], chan=self.bkvf[tt], final=True)
                dma('sp', self.ovp[l], self.kvf[:, tt, 128:256], rd=[self.bkvf[tt]], chan=self.bkvf[tt], final=True)
        for g in range(2):
            gp = slice(64 * g, 64 * g + 64)
            yield
            for bi, (kTa, va, Lk, mask, bufs) in enumerate(blocks):
                PSs, bPSs = P2, b2

                def fn(e, kTa=kTa, Lk=Lk, mask=mask, PSs=PSs, gp=gp):
                    out = PSs[0:Lk, :].rearrange("p (a b) -> p a b", b=128)[:, :, 0:L]
                    ins = e.matmul(out, lhsT=kTa[gp, 0:Lk], rhs=self.aqT[gp, :, cols], start=True, stop=(mask is None))
                    if mask is not None:
                        ins = e.matmul(out, lhsT=idb[0:Lk, 0:Lk], rhs=mask[0:Lk, :, 0:L], start=False, stop=True)
                    return ins
                op('pe', fn, rd=self.baq + [bufs[0], self.bcst], wr=[bPSs])
                op('act', lambda e, Lk=Lk, PSs=PSs, bi=bi: e.activation(
                    out=self.PT[0:Lk, bi, :, 0:L], in_=PSs[0:Lk, :].rearrange("p (a b) -> p a b", b=128)[:, :, 0:L], func=AF.Exp),
                    rd=[bPSs], wr=[self.bPT[bi]])
            Po, bPo = P5, b5
            yield

            def fn_pv(e, Po=Po, g=g):
                ins = None
                for i in range(4):
                    for bi, (kTa, va, Lk, mask, bufs) in enumerate(blocks):
                        ins = e.matmul(Po[0:L, i * 65:(i + 1) * 65], lhsT=self.PT[0:Lk, bi, i, 0:L], rhs=va[0:Lk, g, :],
                                       start=(bi == 0), stop=(bi == len(blocks) - 1))
                return ins
            op('pe', fn_pv, rd=[self.bPT[bi] for bi in range(len(blocks))] + [b[1] for b in [blk_[4] for blk_ in blocks]], wr=[bPo])
            po3 = Po[0:L, 0:260].rearrange("p (a b) -> p a b", b=65)
            yield
            bs = self.basml
            op('dve', lambda e, po3=po3, g=g: e.tensor_tensor(out=self.asml[0:L, 0:4], in0=po3[:, :, 64], in1=self.esk[0:L, l, 4 * g:4 * g + 4],
                                                              op=ALU.add), rd=[bPo, self.bpar], wr=[bs])
            op('dve', lambda e: e.reciprocal(out=self.asml[0:L, 4:8], in_=self.asml[0:L, 0:4]), rd=[bs], wr=[bs])
            for i in range(4):
                op('dve', lambda e, po3=po3, i=i: e.tensor_scalar(out=self.ao[0:L, i, :], in0=po3[:, i, 0:64], scalar1=self.asml[0:L, 4 + i:5 + i],
                                                                  scalar2=None, op0=ALU.mult), rd=[bPo, bs], wr=[self.bao])
            h0 = 1536 + 256 * g
            op('dve', lambda e, g=g, h0=h0: e.tensor_tensor(out=self.mtok[0:L, tt, h0:h0 + 256], in0=self.ao[0:L].rearrange("p a b -> p (a b)"),
                                                            in1=self.saz[0:L, tt, 256 * g:256 * g + 256], op=ALU.mult),
               rd=[self.bao, self.bsaz[tt]], wr=[self.bmtok[tt][3]])

    def merge_T(self, u, rounds=(0, 1, 2, 3)):
        L, tt = u.L, u.tt
        self.transpose_rows(lambda kc: self.mtok[0:L, tt, kc * 128:(kc + 1) * 128], lambda r: self.bmtok[tt][r], self.mT, self.bmT[tt], u,
                            k_evac=1, rounds=rounds)

    def outproj(self, l, us, part):
        op = self.op
        kcs = [0, 1, 2, 3] + list(range(8, 16)) if part == 'A' else [4, 5, 6, 7]
        for j in range(4):
            s = self.get_w()
            W, bW = self.W[s], self.bW[s]
            for u in us:
                L, tt, c0 = u.L, u.tt, u.c0
                k = self.pcount = getattr(self, 'pcount', 0) + 1
                ps, bps = self.PS[k % 2], self.bPS[k % 2]
                for g0 in range(0, len(kcs), 4):
                    self.pull(3)

                    def fn(e, ps=ps, L=L, c0=c0, W=W, g0=g0):
                        ins = None
                        for i_ in range(g0, min(g0 + 4, len(kcs))):
                            kc = kcs[i_]
                            ins = e.matmul(ps[0:L, :], lhsT=self.mT[:, kc, c0:c0 + L], rhs=W[:, kc, :], start=(i_ == 0),
                                           stop=(i_ == len(kcs) - 1))
                        return ins
                    op('pe', fn, rd=[bW, self.bmT[tt]], wr=[bps])
                xa, bx = self.Xap(u, slice(512 * j, 512 * (j + 1)))
                if part == 'A':
                    op('dve', lambda e, xa=xa, ps=ps, L=L: e.scalar_tensor_tensor(out=xa, in0=xa, scalar=ALPHA, in1=ps[0:L, :],
                                                                                  op0=ALU.mult, op1=ALU.add), rd=[bps, bx], wr=[bx])
                else:
                    op('dve', lambda e, xa=xa, ps=ps, L=L: e.tensor_tensor(out=xa, in0=xa, in1=ps[0:L, :], op=ALU.add), rd=[bps, bx], wr=[bx])

    def final_ln(self, l, u, blk):
        op = self.op
        L, tt = u.L, u.tt
        xa, bx = self.Xap(u)
        bs = self.blsm
        for j in range(4):
            xj, _ = self.Xap(u, slice(512 * j, 512 * (j + 1)))
            op('dve', lambda e, xj=xj, j=j: e.bn_stats(out=self.lst[0:L, j, :], in_=xj), rd=[bx], wr=[bs])
        op('dve', lambda e: e.bn_aggr(out=self.lmv[0:L, :], in_=self.lst[0:L].rearrange("p a b -> p (a b)")), rd=[bs], wr=[bs])
        self.rstd(self.lsm[0:L, 0:1], self.lmv[0:L, 1:2], L, bs)
        op('dve', lambda e: e.scalar_tensor_tensor(out=self.lsm[0:L, 1:2], in0=self.lmv[0:L, 0:1], scalar=-1.0, in1=self.lsm[0:L, 0:1],
                                                   op0=ALU.mult, op1=ALU.mult), rd=[bs], wr=[bs])
        op('act', lambda e: e.activation(out=xa, in_=xa, func=AF.Identity, scale=self.lsm[0:L, 0:1], bias=self.lsm[0:L, 1:2]),
           rd=[bx, bs], wr=[bx])

    def final_gain(self, l, us, blk):
        op = self.op
        for j in range(4):
            k = self.lncount = getattr(self, 'lncount', 0) + 1
            lp, bl = self.lnp[k % 2], self.bln[k % 2]
            cs = slice(512 * j, 512 * (j + 1))
            self.dma('sp', lp[:, 0, :], self.p_lng[l, cs].partition_broadcast(128), wr=[bl], chan=bl)
            self.dma('sp', lp[:, 1, :], self.p_lnb[l, cs].partition_broadcast(128), wr=[bl], chan=bl)
            for u in us:
                xj, bx = self.Xap(u, cs)
                L = u.L
                op('dve', lambda e, xj=xj, lp=lp, L=L: e.tensor_tensor(out=xj, in0=xj, in1=lp[0:L, 0, :], op=ALU.mult), rd=[bx, bl], wr=[bx])
                op('dve', lambda e, xj=xj, lp=lp, L=L: e.tensor_tensor(out=xj, in0=xj, in1=lp[0:L, 1, :], op=ALU.add), rd=[bx, bl], wr=[bx])
                if l < DEPTH - 1:
                    tt = u.tt
                    stg = self.mtok[0:L, tt, cs]
                    bst = self.bmtok[tt][j]
                    op('act', lambda e, stg=stg, xj=xj: e.activation(out=stg, in_=xj, func=AF.Copy), rd=[bx], wr=[bst])
                    self.transpose_rows(lambda kc, tt=tt, L=L: self.mtok[0:L, tt, kc * 128:(kc + 1) * 128], lambda r, tt=tt: self.bmtok[tt][r],
                                        self.xT, self.bxT[tt], u, rounds=(j,), act_only=True)
        for u in us:
            self.final_out(l, u, blk)

    def final_out(self, l, u, blk):
        L, tt = u.L, u.tt
        xa, bx = self.Xap(u)
        if l == DEPTH - 1:
            if u.sidx is None:
                r0 = blk * TB + tt * 128
                self.dma('sp', self.yp[r0:r0 + 128, :], xa, rd=[bx], chan=bx, final=True)
            else:
                self.dma('sp', self.ys[u.sidx:u.sidx + 1, :], xa, rd=[bx], chan=bx, final=True)

    def seq(self, *gens):
        for g in gens:
            yield from g

    def layer_block(self, blk, l):
        us = self.units(blk)
        last_blk = (blk == self.nblk - 1)
        self.load_params(l)
        if l == 0:
            self.load_x(blk, us)
        self.chk()
        nxt_prompt = (l == DEPTH - 1) and (0 <= blk + 1 < self.nblk)
        if nxt_prompt:
            self.prefetch_x(blk + 1)
        if l == 0:
            if self.xT_prefetched:
                self.xT_prefetched = False
            else:
                for u in us:
                    self.make_xT(u)
        self.chk()
        self.conv_start(l, blk)
        T = lambda kind, i: self.inproj_tile(l, kind, i, us)
        T('cu', 0)
        T('cg', 0)
        g_cb = self.conv_block(l, us)
        self.bg.append(g_cb)
        T('kvg', 0)
        for k_, i_ in (('qk', 0), ('qk', 1), ('v', 0), ('o', 0), ('z', 0)):
            T(k_, i_)
        g_m0 = self.seq(*[self.mlstm_unit(l, 0, u, last_blk) for u in us])
        self.bg.append(g_m0)
        for k_, i_ in (('cz', 0), ('aq', 0), ('az', 0)):
            T(k_, i_)
        g_at = self.seq(*[self.attn_unit(l, u, (blk * NT + u.tt if u.sidx is None else None), last_blk) for u in us])
        self.bg.append(g_at)
        for k_, i_ in (('qk', 2), ('qk', 3), ('v', 1), ('o', 1), ('z', 1)):
            T(k_, i_)
        self.drain([g_cb, g_at])
        self.bg.append(self.seq(*[self.conv_unit(l, u) for u in us]))
        self.drain([g_m0])
        g_m1 = self.seq(*[self.mlstm_unit(l, 1, u, last_blk) for u in us])
        self.bg.append(g_m1)
        self.drain([g for g in self.bg if g is not g_m1])
        for u in us:
            self.merge_T(u, rounds=(0, 2, 3))
        self.outproj(l, us, 'A')
        self.drain()
        self.conv_finish(l, us, last_blk)
        self.chk()
        if self.debug and blk == 0 and l == 0:
            for u in us:
                self.dump("mtok%d" % u.tt, self.mtok[0:u.L, u.tt, :], list(self.bmtok[u.tt]), (u.L, D), BF16)
        for u in us:
            self.merge_T(u, rounds=(1,))
        self.chk()
        self.outproj(l, us, 'B')
        if nxt_prompt:
            self.build_xT_prefetched()
        self.chk()
        for u in us:
            self.final_ln(l, u, blk)
        self.final_gain(l, us, blk)
        if self.debug and blk == 0 and l == 0:
            for u in us:
                self.dump("x1_%d" % u.tt, self.X[0:u.L, u.tt, :], self.bX[u.tt], (u.L, D))
        self.chk()

    def build(self):
        self.setup()
        seq = []
        blks = (list(range(-(NS // NSB), 0)) if self.with_sample else []) + list(range(self.nblk))
        for blk in blks:
            for l in range(DEPTH):
                seq += [(l, j) for j in range(NW)] + [(l, NW_IN + j, 'B') for j in range(4)]
        self.w_plan(seq)
        try:
            self.chk()
            for blk in blks:
                for l in range(DEPTH):
                    self.layer_block(blk, l)
        except StopIteration:
            pass
        self.P.emit(self.final)
        return self.nc


_CACHE = {}


def kernel(x_prompt, x_sample, state_C, state_n, state_m, state_conv, cache_k, cache_v,
           w_in, w_out, b_igate, b_fgate, m_norm_g, conv_w, conv_b, conv_ln_g, conv_ln_b,
           sinks, ln_g, ln_b):
    f = lambda a: np.ascontiguousarray(np.asarray(a, dtype=np.float32))
    x_prompt, x_sample = f(x_prompt), f(x_sample)
    if 'nc' not in _CACHE:
        _CACHE['nc'] = K().build()
    nc = _CACHE['nc']
    wt = host_weight_tiles(f(w_in), f(w_out))
    cst, cstB = host_consts()
    scv = np.ascontiguousarray(f(state_conv).transpose(0, 1, 3, 2))
    p_cw = np.ascontiguousarray(f(conv_w).transpose(0, 2, 1).reshape(DEPTH, 4, 128, 31).transpose(0, 2, 1, 3)).reshape(DEPTH, 128, 124)
    p_cb = np.ascontiguousarray(f(conv_b).reshape(DEPTH, 4, 128).transpose(0, 2, 1))
    sC, sn, sm = f(state_C), f(state_n), f(state_m)
    ck = f(cache_k).reshape(DEPTH, 32, 128, 128)
    cv = f(cache_v).reshape(DEPTH, 32, 128, 128)
    in_maps = []
    xp_dummy = np.zeros_like(x_prompt[0])
    for c in range(NCORE):
        b = c % 2
        ss = slice(NS * c, NS * (c + 1))
        in_maps.append({
            "xp": x_prompt[b] if c < 2 else xp_dummy, "xs": np.ascontiguousarray(x_sample[ss, 0, :]), "wt": wt,
            "sC": np.ascontiguousarray(sC[:, ss]), "sn": np.ascontiguousarray(sn[:, ss]), "sm": np.ascontiguousarray(sm[:, ss]),
            "scv": np.ascontiguousarray(scv[:, ss]), "ck": np.ascontiguousarray(ck[:, ss]), "cv": np.ascontiguousarray(cv[:, ss]),
            "cst": cst, "cstB": cstB, "p_bi": f(b_igate), "p_bf": f(b_fgate), "p_mng": f(m_norm_g), "p_cw": p_cw, "p_cb": p_cb,
            "p_clg": f(conv_ln_g), "p_clb": f(conv_ln_b), "p_sk": f(sinks), "p_lng": f(ln_g), "p_lnb": f(ln_b),
        })
    res = run_bass_kernel_spmd(nc, in_maps, core_ids=list(range(NCORE))).results
    cat = lambda k, ax: np.concatenate([r[k] for r in res], axis=ax)
    stack2 = lambda k: np.stack([res[0][k], res[1][k]], axis=1)
    y_prompt = np.stack([res[0]["yp"], res[1]["yp"]], axis=0)
    y_sample = cat("ys", 0).reshape(32, 1, D)
    new_C_p = stack2("oCp")
    new_n_p = stack2("onp")
    new_m_p = stack2("omp")
    new_conv_p = stack2("ocvp")
    new_k_p = stack2("okp").reshape(DEPTH, 2, 128, 2, 64)
    new_v_p = stack2("ovp").reshape(DEPTH, 2, 128, 2, 64)
    new_C_s = cat("oCs", 1)
    new_n_s = cat("ons", 1)
    new_m_s = cat("oms", 1)
    new_conv_s = cat("ocvs", 1)
    new_k_s = cat("oks", 1).reshape(DEPTH, 32, 128, 2, 64)
    new_v_s = cat("ovs", 1).reshape(DEPTH, 32, 128, 2, 64)
    return (y_prompt, y_sample, new_C_p, new_n_p, new_m_p, new_conv_p, new_k_p, new_v_p,
            new_C_s, new_n_s, new_m_s, new_conv_s, new_k_s, new_v_s)
```

```python
import numpy as np
import concourse.bass as bass
import concourse.mybir as mybir
from concourse.bass_utils import run_bass_kernel_spmd

F32 = mybir.dt.float32
BF16 = mybir.dt.bfloat16
ALU = mybir.AluOpType
AF = mybir.ActivationFunctionType
AX = mybir.AxisListType

D = 2048
SEQ = 4096
DEPTH = 2
NCORE = 8
NS = 4
TB = 256
NT = TB // 128
NBLK = SEQ // TB
NSB = 2
NTT = max(NT, NSB)
NCF = max(TB, NSB)
ALPHA = (2 * DEPTH) ** 0.25
EPS = 1e-5
BIG = 30000.0
NW_IN = 16
NW = 20
EPOCH = 3000

T_KVG, T_QK, T_V, T_O, T_Z, T_CU, T_CG, T_CZ, T_AQ, T_AZ = 'kvg', 'qk', 'v', 'o', 'z', 'cu', 'cg', 'cz', 'aq', 'az'
TILE_ORDER = [('cu', 0), ('cg', 0), ('kvg', 0), ('qk', 0), ('qk', 1), ('v', 0), ('o', 0), ('z', 0),
              ('cz', 0), ('aq', 0), ('az', 0),
              ('qk', 2), ('qk', 3), ('v', 1), ('o', 1), ('z', 1)]

C_ID, C_TRI, C_SEL, C_ONE, C_EPS = 0, 128, 256, 512, 513
NCST = 514
CB_ID, CB_BIGM, CB_NMP, CB_NMC, CB_ONE = 0, 128, 256, 768, 1280
NCSTB = 1281


def host_weight_tiles(w_in, w_out):
    mq, mk, mv, mo, mi, mf, mz = 0, 1024, 2048, 3072, 4096, 4100, 4104
    cu, cg, cz, aq, ak, av, az = 5128, 5640, 6152, 6664, 7176, 7304, 7432
    L = w_in.shape[0]
    out = np.zeros((L, NW, 2048, 512), np.float32)
    for l in range(L):
        W = w_in[l]
        for j, (kind, i) in enumerate(TILE_ORDER):
            t = out[l, j]
            if kind == 'kvg':
                t[:, 0:128] = W[:, ak:ak + 128]
                t[:, 128:256] = W[:, av:av + 128]
                t[:, 256:260] = W[:, mi:mi + 4]
                t[:, 260:264] = W[:, mf:mf + 4]
            elif kind == 'qk':
                t[:, 0:256] = W[:, mq + 256 * i: mq + 256 * (i + 1)]
                t[:, 256:512] = W[:, mk + 256 * i: mk + 256 * (i + 1)]
            elif kind == 'v':
                t[:] = W[:, mv + 512 * i: mv + 512 * (i + 1)]
            elif kind == 'o':
                t[:] = W[:, mo + 512 * i: mo + 512 * (i + 1)]
            elif kind == 'z':
                t[:] = W[:, mz + 512 * i: mz + 512 * (i + 1)]
            elif kind == 'cu':
                t[:] = W[:, cu:cu + 512]
            elif kind == 'cg':
                t[:] = W[:, cg:cg + 512]
            elif kind == 'cz':
                t[:] = W[:, cz:cz + 512]
            elif kind == 'aq':
                for c in range(4):
                    t[:, c * 128: c * 128 + 64] = W[:, aq + 64 * c: aq + 64 * (c + 1)]
                    t[:, c * 128 + 64: c * 128 + 128] = W[:, aq + 64 * (4 + c): aq + 64 * (5 + c)]
            elif kind == 'az':
                t[:] = W[:, az:az + 512]
        for j in range(4):
            out[l, NW_IN + j] = w_out[l][:, 512 * j: 512 * (j + 1)]
    out = out.reshape(L, NW, 16, 128, 512).transpose(0, 1, 3, 2, 4)
    return np.ascontiguousarray(out).reshape(L, NW, 128, 16 * 512)


def host_consts():
    c = np.zeros((128, NCST), np.float32)
    cb = np.zeros((128, NCSTB), np.float32)
    s = np.arange(128)[:, None]
    t = np.arange(128)[None, :]
    c[:, C_ID:C_ID + 128] = (s == t)
    c[:, C_TRI:C_TRI + 128] = (s <= t)
    c[0, C_SEL:C_SEL + 128] = 1.0
    c[1, C_SEL + 128:C_SEL + 256] = 1.0
    c[:, C_ONE] = 1.0
    c[:, C_EPS] = EPS
    cb[:, CB_ID:CB_ID + 128] = (s == t)
    cb[:, CB_BIGM:CB_BIGM + 128] = np.where(s > t, BIG, 0.0)
    nmp = np.where(s < t, -BIG, 0.0)
    nmc = np.where(s > t, -BIG, 0.0)
    for h in range(4):
        cb[:, CB_NMP + 128 * h: CB_NMP + 128 * (h + 1)] = nmp
        cb[:, CB_NMC + 128 * h: CB_NMC + 128 * (h + 1)] = nmc
    cb[:, CB_ONE] = 1.0
    return c, cb


class Buf:
    __slots__ = ('name', 'w', 'r', 'sem', 'cnt')

    def __init__(self, name):
        self.name = name
        self.w = None
        self.r = []
        self.sem = None
        self.cnt = 0


class Op:
    __slots__ = ('eng', 'fn', 'deps', 'dma', 'chan', 'val', 'sig', 'signo', 'idx', 'dmaw')


class Prog:
    ENGS = ('pe', 'act', 'dve', 'pool', 'sp')

    def __init__(self, nc):
        self.nc = nc
        self.ops = []
        self.nbuf = 0

    def buf(self, name=None):
        self.nbuf += 1
        return Buf(name or "b%d" % self.nbuf)

    def op(self, eng, fn, rd=(), wr=(), chan=None):
        o = Op()
        o.eng, o.fn, o.idx = eng, fn, len(self.ops)
        deps = set()
        for b in rd:
            if b.w is not None:
                deps.add(b.w)
        for b in wr:
            if b.w is not None:
                deps.add(b.w)
            deps.update(b.r)
        o.deps = deps
        o.dmaw = {}
        for d in deps:
            p = self.ops[d]
            if p.dma:
                o.dmaw[id(p.chan)] = (p.chan, 16 * p.chan.cnt)
        o.dma = chan is not None
        o.chan = chan
        o.sig = False
        o.signo = 0
        o.val = 0
        if chan is not None:
            chan.cnt += 1
            o.val = 16 * chan.cnt
        for b in rd:
            b.r.append(o.idx)
        for b in wr:
            b.w = o.idx
            b.r = []
        self.ops.append(o)
        return o

    def emit(self, final_chans):
        nc = self.nc
        ops = self.ops
        for o in ops:
            keep = {}
            for d in o.deps:
                p = ops[d]
                if p.dma:
                    continue
                if p.eng == 'pe' and o.eng == 'pe' and not o.dma:
                    continue
                k = ('c', p.eng)
                if k not in keep or keep[k] < d:
                    keep[k] = d
            o.deps = sorted(keep.values())
            for d in o.deps:
                ops[d].sig = True
        cnt = {e: 0 for e in self.ENGS}
        for o in ops:
            if o.sig and not o.dma:
                cnt[o.eng] += 1
                o.signo = cnt[o.eng]
        esems = {e: [nc.alloc_semaphore(name="s_%s_%d" % (e, i)) for i in range((cnt[e] + EPOCH - 1) // EPOCH + 1)]
                 for e in self.ENGS}
        chans = {}
        for o in ops:
            if o.dma and o.chan.sem is None:
                o.chan.sem = nc.alloc_semaphore(name="d_%s_%d" % (o.chan.name, len(chans)))
                chans[id(o.chan)] = o.chan

        def target(p):
            if p.dma:
                return p.chan.sem, p.val
            n = p.signo - 1
            return esems[p.eng][n // EPOCH], n % EPOCH + 1

        by_eng = {e: [o for o in ops if o.eng == e] for e in self.ENGS}

        def run(ename, eng):
            waited = {}
            for o in by_eng[ename]:
                for ch, val in o.dmaw.values():
                    k = id(ch.sem)
                    if waited.get(k, 0) >= val:
                        continue
                    eng.wait_ge(ch.sem, val)
                    waited[k] = val
                for d in o.deps:
                    sem, val = target(ops[d])
                    k = id(sem)
                    if waited.get(k, 0) >= val:
                        continue
                    eng.wait_ge(sem, val)
                    waited[k] = val
                ins = o.fn(eng)
                if o.dma:
                    ins.then_inc(o.chan.sem, 16)
                elif o.sig:
                    sem, _ = target(o)
                    ins.then_inc(sem, 1)
            if ename == 'sp':
                for ch in chans.values():
                    eng.wait_ge(ch.sem, 16 * ch.cnt)

        with nc.Block() as block:
            @block.tensor
            def _(e):
                run('pe', e)

            @block.scalar
            def _(e):
                run('act', e)

            @block.vector
            def _(e):
                run('dve', e)

            @block.gpsimd
            def _(e):
                run('pool', e)

            @block.sync
            def _(e):
                run('sp', e)


class Unit:
    def __init__(self, L, tt, c0, sidx=None):
        self.L, self.tt, self.c0, self.sidx = L, tt, c0, sidx


class K:
    def __init__(self, nblk=NBLK, with_sample=True, debug=False, stage=None):
        self.stage = stage
        self.stage_n = 0
        self.debug = debug
        self.bg = []
        self.nblk = nblk
        self.with_sample = with_sample
        self.debug = debug
        self.nc = nc = bass.Bass("TRN2", target_bir_lowering=False)
        self.P = Prog(nc)
        self.final = []
        self.dbg_outs = []
        di = lambda n, s: nc.dram_tensor(n, list(s), F32, kind="ExternalInput").ap()
        do = lambda n, s: nc.dram_tensor(n, list(s), F32, kind="ExternalOutput").ap()
        self.xp = di("xp", (SEQ, D))
        self.xs = di("xs", (NS, D))
        self.wt = di("wt", (DEPTH, NW, 128, 16 * 512))
        self.wb = nc.dram_tensor("wb", [DEPTH, NW, 128, 16 * 512], BF16, kind="Internal").ap()
        self.bwb = [[self.P.buf("wb%d_%d" % (l, j)) for j in range(NW)] for l in range(DEPTH)]
        self.sC = di("sC", (DEPTH, NS, 4, 256, 256))
        self.sn = di("sn", (DEPTH, NS, 4, 256))
        self.sm = di("sm", (DEPTH, NS, 4))
        self.scv = di("scv", (DEPTH, NS, 512, 30))
        self.ck = di("ck", (DEPTH, NS, 128, 128))
        self.cv = di("cv", (DEPTH, NS, 128, 128))
        self.cst = di("cst", (128, NCST))
        self.cstB = di("cstB", (128, NCSTB))
        self.p_bi = di("p_bi", (DEPTH, 4))
        self.p_bf = di("p_bf", (DEPTH, 4))
        self.p_mng = di("p_mng", (DEPTH, 1024))
        self.p_cw = di("p_cw", (DEPTH, 128, 4 * 31))
        self.p_cb = di("p_cb", (DEPTH, 128, 4))
        self.p_clg = di("p_clg", (DEPTH, 512))
        self.p_clb = di("p_clb", (DEPTH, 512))
        self.p_sk = di("p_sk", (DEPTH, 8))
        self.p_lng = di("p_lng", (DEPTH, D))
        self.p_lnb = di("p_lnb", (DEPTH, D))
        self.yp = do("yp", (SEQ, D))
        self.ys = do("ys", (NS, D))
        self.oCp = do("oCp", (DEPTH, 4, 256, 256))
        self.onp = do("onp", (DEPTH, 4, 256))
        self.omp = do("omp", (DEPTH, 4))
        self.ocvp = do("ocvp", (DEPTH, 30, 512))
        self.okp = do("okp", (DEPTH, 128, 128))
        self.ovp = do("ovp", (DEPTH, 128, 128))
        self.oCs = do("oCs", (DEPTH, NS, 4, 256, 256))
        self.ons = do("ons", (DEPTH, NS, 4, 256))
        self.oms = do("oms", (DEPTH, NS, 4))
        self.ocvs = do("ocvs", (DEPTH, NS, 30, 512))
        self.oks = do("oks", (DEPTH, NS, 128, 128))
        self.ovs = do("ovs", (DEPTH, NS, 128, 128))
        self.alloc()

    def chk(self):
        self.stage_n += 1
        if self.stage is not None and self.stage_n >= self.stage:
            raise StopIteration

    def dump(self, name, ap, buf, shape, dt=F32):
        o = self.nc.dram_tensor("dbg_" + name, list(shape), dt, kind="ExternalOutput").ap()
        bufs = buf if isinstance(buf, list) else [buf]
        self.dma('sp', o, ap, rd=bufs, chan=bufs[0], final=True)

    def sb(self, name, shape, dt=F32):
        return self.nc.alloc_sbuf_tensor(name, list(shape), dt)

    def B(self, name=None):
        return self.P.buf(name)

    def op(self, eng, fn, rd=(), wr=(), chan=None):
        return self.P.op(eng, fn, rd, wr, chan)

    def rstd(self, out, in_, L, b):
        epsc = self.cst_sb[0:L, C_EPS:C_EPS + 1]
        self.op('act', lambda e: e.activation(out=out, in_=in_, func=AF.Ln, bias=epsc), rd=[b, self.bcst], wr=[b])
        self.op('act', lambda e: e.activation(out=out, in_=out, func=AF.Exp, scale=-0.5), rd=[b], wr=[b])

    def pull(self, n=1):
        for _ in range(n):
            for g in list(self.bg):
                try:
                    next(g)
                except StopIteration:
                    self.bg.remove(g)

    def drain(self, gens=None):
        while True:
            act = [g for g in self.bg if gens is None or g in gens]
            if not act:
                return
            self.pull()

    def dma(self, q, out, in_, rd=(), wr=(), chan=None, final=False):
        if final and chan not in self.final:
            self.final.append(chan)
        return self.op(q, lambda e, o=out, i=in_: e.dma_start(out=o, in_=i), rd=rd, wr=wr, chan=chan)

    def alloc(self):
        nc = self.nc
        sb, B = self.sb, self.B
        self.PS = [nc.alloc_psum_tensor("ps%d" % i, [128, 512], F32) for i in range(7)]
        self.PB = nc.alloc_psum_tensor("psb", [128, 1024], BF16)
        self.bPS = [B("ps%d" % i) for i in range(7)]
        _b = B("psb")
        self.bPB = [_b, _b]
        self.bP6 = {k: self.bPS[6] for k in ('b', 'tm', 'den', 'dn', 'arow', 'brow')}
        self.cst_sb = sb("cst_sb", [128, NCST])
        self.cstb = sb("cstb", [128, NCSTB], BF16)
        self.bcst = B("cst")
        self.bcstb = B("cstb")
        self.NSLOT = 3
        self.W = [sb("w%d" % i, [128, 16, 512], BF16) for i in range(self.NSLOT)]
        self.bW = [B("w%d" % i) for i in range(self.NSLOT)]
        self.wcount = 0
        self.X = sb("X", [128, NTT, D])
        self.bX = [B("X%d" % i) for i in range(NTT)]
        self.xT = sb("xT", [128, 16, NCF], BF16)
        self.bxT = [B("xT%d" % i) for i in range(NTT)]
        self.mT = self.xT
        self.bmT = self.bxT
        self.mtok = sb("mtok", [128, NTT, D], BF16)
        self.bmtok = [[B("mtok%d_%d" % (i, j)) for j in range(4)] for i in range(NTT)]
        self.xb = self.mtok[:, 0, :]
        self.bxb = B("xb")
        self.xpre = sb("xpre", [128, NT, D], BF16)
        self.bxpre = [B("xpre%d" % i) for i in range(NT)]
        self.xT_prefetched = False
        self.qT_ = [sb("qT%d" % p, [128, 2, 2, NCF], BF16) for p in range(2)]
        self.kT_ = [sb("kT%d" % p, [128, 2, 2, NCF], BF16) for p in range(2)]
        self.bqT_ = [[[B() for _ in range(2)] for _ in range(2)] for p in range(2)]
        self.bkT_ = [[[B() for _ in range(2)] for _ in range(2)] for p in range(2)]
        self.vtok_ = [sb("vtok%d" % p, [128, NTT, 2, 256], BF16) for p in range(2)]
        self.bv_ = [[B() for _ in range(NTT)] for p in range(2)]
        self.G_ = [sb("G%d" % p, [128, NTT, 2, 256]) for p in range(2)]
        self.bG_ = [[B() for _ in range(NTT)] for p in range(2)]
        self.gtmp = sb("gtmp", [128, 512])
        self.bgtmp = B()
        self.graw = sb("graw", [128, NTT, 8])
        self.ig = sb("ig", [128, NTT, 4])
        self.sp = sb("spl", [128, NTT, 4])
        self.bgate = [B() for _ in range(NTT)]
        self.a_sb = sb("a_sb", [128, 2]); self.ba = B()
        self.arow = sb("arow", [2, 128]); self.barow = B()
        self.Mrow = sb("Mrow", [2, 128]); self.bMrow = B()
        self.mrow = sb("mrow", [2, 128]); self.bmrow = B()
        self.w0row = sb("w0row", [2, 128]); self.bw0row = B()
        self.emrow = sb("emrow", [2, 128]); self.bemrow = B()
        self.wT = sb("wT", [128, 2, 128]); self.bwT = B()
        self.swT = sb("swT", [128, 2, 128], BF16); self.bswT = B()
        self.qs = sb("qs", [128, 2, 2, 128], BF16); self.bqs = B()
        self.g0bc = sb("g0bc", [128, 2]); self.bg0bc = B()
        self.ktok = sb("ktok", [128, 2, 256], BF16); self.bktok = B()
        self.gv = sb("gv", [128, 2, 256], BF16); self.bgv = B()
        self.gb = sb("gb", [128, 2], BF16); self.bgb = B()
        self.sm6 = sb("sm6", [128, 2, 6]); self.bsm6 = B()
        self.mv = sb("mv", [128, 2, 2]); self.bmv = B()
        self.sml = sb("sml", [128, 16]); self.bsml = B()
        self.hn = sb("hn", [128, 2, 256]); self.bhn = B()
        self.C = [sb("C%d" % l, [128, 4, 2, 256]) for l in range(DEPTH)]
        self.n = [sb("n%d" % l, [128, 4, 2]) for l in range(DEPTH)]
        self.Cb = [sb("Cb%d" % l, [128, 4, 2, 256], BF16) for l in range(DEPTH)]
        self.nb = [sb("nb%d" % l, [128, 4, 2], BF16) for l in range(DEPTH)]
        self.m = [[sb("m%d_%d" % (l, hp), [2, 1]) for hp in range(2)] for l in range(DEPTH)]
        self.bC = [[B() for _ in range(2)] for _ in range(DEPTH)]
        self.Cs = sb("Cs", [128, 2, 2, 256]); self.ns = sb("ns", [128, 2, 2])
        self.Csb = sb("Csb", [128, 2, 2, 256], BF16); self.nsb = sb("nsb", [128, 2, 2], BF16)
        self.ms = sb("ms", [2, 1]); self.bCs = B("Cs")
        self.aT = sb("aT", [128, 4, 30 + TB])
        self.baT = B("aT")
        self.aTs = sb("aTs", [128, 4, NS, 31])
        self.baTs = [B() for _ in range(NS)]
        self.bcu = [B() for _ in range(4)]
        self.cuF = sb("cuF", [128, 4, NCF])
        self.sg = sb("sg", [128, NCF]); self.bsg = B()
        self.yT = sb("yT", [128, 4, NCF]); self.byT = B("yT")
        self.ctmp = sb("ctmp", [128, 31]); self.bctmp = B()
        self.sz = sb("sz", [128, NTT, 512], BF16); self.bsz = [B() for _ in range(NTT)]
        self.yn = sb("yn", [128, 512]); self.byn = B()
        self.cst6 = sb("cst6", [128, 6]); self.cmv = sb("cmv", [128, 2]); self.csml = sb("csml", [128, 4]); self.bcsm = B()
        self.cvo = sb("cvo", [30, 512]); self.bcvo = B("cvo")
        self.aqT = sb("aqT", [128, 4, NCF], BF16); self.baq = [B() for _ in range(4)]
        self.kvf = sb("kvf", [128, NTT, 256]); self.bkvf = [B() for _ in range(NTT)]
        self.kvb = sb("kvb", [128, 128], BF16); self.bkvb = B()
        self.akT = [sb("akT%d" % l, [128, 2, 128], BF16) for l in range(DEPTH)]
        self.bakT = [[B(), B()] for l in range(DEPTH)]
        self.vaug = [sb("vaug%d" % l, [128, 2, 2, 65], BF16) for l in range(DEPTH)]
        self.bvaug = [[B(), B()] for l in range(DEPTH)]
        self.hist = [sb("hist%d" % l, [128, 4, 30]) for l in range(DEPTH)]
        self.bhist = [B() for l in range(DEPTH)]
        self.akTs = sb("akTs", [128, 128], BF16); self.vaugs = sb("vaugs", [128, 2, 65], BF16)
        self.ckf = sb("ckf", [128, 128]); self.cvf = sb("cvf", [128, 128]); self.bcache = B("cache")
        self.bakTs = B(); self.bvaugs = B()
        self.saz = sb("saz", [128, NTT, 512], BF16); self.bsaz = [B() for _ in range(NTT)]
        self.PT = sb("PT", [128, 2, 4, 128], BF16); self.bPT = [B(), B()]
        self.asml = sb("asml", [128, 8]); self.basml = B()
        self.ao = sb("ao", [128, 4, 64]); self.bao = B()
        self.bi_bc = sb("bi_bc", [128, DEPTH, 4]); self.bf_bc = sb("bf_bc", [128, DEPTH, 4])
        self.esk = sb("esk", [128, DEPTH, 8])
        self.mng = sb("mng", [128, 1024])
        self.cw = sb("cw", [128, DEPTH, 4, 31]); self.cb = sb("cb", [128, DEPTH, 4])
        self.clg = sb("clg", [128, 512]); self.clb = sb("clb", [128, 512])
        self.lnp = [sb("lnp%d" % i, [128, 2, 512]) for i in range(2)]
        self.bpar = B("par"); self.bmng = B("mng"); self.bcl = B("cl"); self.bln = [B("ln0"), B("ln1")]
        self.lst = sb("lst", [128, NTT, 4, 6]); self.lmv = sb("lmv", [128, 2]); self.lsm = sb("lsm", [128, 4]); self.blsm = B()
        self.bout = {k: B(k) for k in ('oCp', 'onp', 'omp', 'okp', 'ovp', 'oms', 'oks', 'ovs')}

    def setup(self):
        op, dma = self.op, self.dma
        dma('sp', self.cst_sb[:], self.cst, wr=[self.bcst], chan=self.bcst)
        dma('pool', self.cstb[:], self.cstB, wr=[self.bcstb], chan=self.bcstb)
        op('dve', lambda e: e.tensor_copy(out=self.cst_sb[:, C_ONE:C_ONE + 1], in_=self.cst_sb[:, C_ONE:C_ONE + 1]),
           rd=[self.bcst, self.bcstb], wr=[self.bcst])
        bp = self.bpar

        def bc(dst, src):
            dma('sp', dst, src.partition_broadcast(128), wr=[bp], chan=bp)
        bc(self.bi_bc[:].rearrange("p l f -> p (l f)"), self.p_bi.rearrange("l f -> (l f)"))
        bc(self.bf_bc[:].rearrange("p l f -> p (l f)"), self.p_bf.rearrange("l f -> (l f)"))
        bc(self.esk[:].rearrange("p l f -> p (l f)"), self.p_sk.rearrange("l f -> (l f)"))
        for l in range(DEPTH):
            dma('sp', self.cw[:, l].rearrange("p a b -> p (a b)"), self.p_cw[l], wr=[bp], chan=bp)
            dma('sp', self.cb[:, l], self.p_cb[l], wr=[bp], chan=bp)
        op('act', lambda e: e.activation(out=self.esk[:], in_=self.esk[:], func=AF.Exp), rd=[bp], wr=[bp])
        for l in range(DEPTH):
            for hp in range(2):
                hs = slice(2 * hp, 2 * hp + 2)
                b = self.bC[l][hp]
                op('dve', lambda e, l=l, hs=hs: e.memset(self.C[l][:, hs], 0.0), wr=[b])
                op('dve', lambda e, l=l, hs=hs: e.memset(self.n[l][:, hs], 0.0), wr=[b])
                op('dve', lambda e, l=l, hs=hs: e.memset(self.Cb[l][:, hs], 0.0), wr=[b])
                op('dve', lambda e, l=l, hs=hs: e.memset(self.nb[l][:, hs], 0.0), wr=[b])
                op('dve', lambda e, l=l, hp=hp: e.memset(self.m[l][hp][:], 0.0), wr=[b])
        for l in range(DEPTH):
            for r in range(2):
                op('dve', lambda e, l=l, r=r: e.memset(self.vaug[l][:, r, :, 64:65], 1.0), wr=[self.bvaug[l][r]])
        op('dve', lambda e: e.memset(self.vaugs[:, :, 64:65], 1.0), wr=[self.bvaugs])

    def load_params(self, l):
        dma = self.dma
        dma('sp', self.mng[:, :], self.p_mng[l].partition_broadcast(128), wr=[self.bmng], chan=self.bmng)
        dma('sp', self.clg[:, :], self.p_clg[l].partition_broadcast(128), wr=[self.bcl], chan=self.bcl)
        dma('sp', self.clb[:, :], self.p_clb[l].partition_broadcast(128), wr=[self.bcl], chan=self.bcl)

    def w_plan(self, seq):
        seen = set()
        for ent in seq:
            l, j = ent[0], ent[1]
            if (l, j) not in seen:
                seen.add((l, j))
                self.dma('pool', self.wb[l, j], self.wt[l, j], wr=[self.bwb[l][j]], chan=self.bwb[l][j])
        self.wseq = seq
        self.w_issued = 0
        self.w_used = 0

    def get_w(self):
        while self.w_issued < len(self.wseq) and self.w_issued < self.w_used + self.NSLOT:
            ent = self.wseq[self.w_issued]
            l, j = ent[0], ent[1]
            s = self.w_issued % self.NSLOT
            if len(ent) > 2 and ent[2] == 'B':
                self.dma('pool', self.W[s][:, 4:8, :], self.wb[l, j].rearrange("p (a b) -> p a b", b=512)[:, 4:8, :],
                         rd=[self.bwb[l][j]], wr=[self.bW[s]], chan=self.bW[s])
            else:
                self.dma('pool', self.W[s][:].rearrange("p a b -> p (a b)"), self.wb[l, j], rd=[self.bwb[l][j]], wr=[self.bW[s]],
                         chan=self.bW[s])
            self.w_issued += 1
        s = self.w_used % self.NSLOT
        self.w_used += 1
        return s

    def units(self, blk):
        if blk < 0:
            s0 = NSB * (blk + NS // NSB)
            return [Unit(1, j, j, sidx=s0 + j) for j in range(NSB)]
        return [Unit(128, tt, tt * 128) for tt in range(NT)]

    def Xap(self, u, cols=slice(0, D)):
        return self.X[0:u.L, u.tt, cols], self.bX[u.tt]

    def load_x(self, blk, us):
        for u in us:
            xa, bx = self.Xap(u)
            if u.sidx is None:
                r0 = blk * TB + u.tt * 128
                self.dma('sp', xa, self.xp[r0:r0 + 128, :], wr=[bx], chan=bx)
            else:
                self.dma('sp', xa, self.xs[u.sidx:u.sidx + 1, :], wr=[bx], chan=bx)

    def transpose_rows(self, src_of, bsrc, dst, bdst, u, k_evac=0, rounds=(0, 1, 2, 3), act_only=False):
        L, c0 = u.L, u.c0
        idb = self.cstb[:, CB_ID:CB_ID + 128]
        for r in rounds:
            h = r % 2
            pb = self.PB[:, h * 512:(h + 1) * 512]

            def fn(e, r=r, pb=pb):
                ins = None
                for j in range(4):
                    ins = e.transpose(pb[:, j * 128:j * 128 + L], src_of(4 * r + j), idb[0:L, 0:L])
                return ins
            bs_r = bsrc(r) if callable(bsrc) else bsrc
            self.op('pe', fn, rd=[bs_r, self.bcst] if not isinstance(bs_r, list) else bs_r + [self.bcst], wr=[self.bPB[h]])
            src = pb.rearrange("p (a b) -> p a b", b=128)[:, :, 0:L]
            out = dst[:, 4 * r:4 * r + 4, c0:c0 + L]
            if act_only or (r + k_evac) % 2 == 0:
                self.op('act', lambda e, o=out, s=src: e.activation(out=o, in_=s, func=AF.Copy), rd=[self.bPB[h]], wr=[bdst])
            else:
                self.op('dve', lambda e, o=out, s=src: e.tensor_copy(out=o, in_=s), rd=[self.bPB[h]], wr=[bdst])

    def prefetch_x(self, blk):
        for tt in range(NT):
            r0 = blk * TB + tt * 128
            self.dma('pool', self.xpre[:, tt, :], self.xp[r0:r0 + 128, :], wr=[self.bxpre[tt]], chan=self.bxpre[tt])

    def build_xT_prefetched(self):
        for tt in range(NT):
            u = Unit(128, tt, tt * 128)
            self.transpose_rows(lambda kc, tt=tt: self.xpre[0:128, tt, kc * 128:(kc + 1) * 128], self.bxpre[tt], self.xT, self.bxT[tt], u,
                                act_only=True)
        self.xT_prefetched = True

    def make_xT(self, u):
        L = u.L
        xa, bx = self.Xap(u)
        bxb = list(self.bmtok[0])
        self.op('act', lambda e: e.activation(out=self.xb[0:L, :], in_=xa, func=AF.Copy), rd=[bx], wr=bxb)
        self.transpose_rows(lambda kc: self.xb[0:L, kc * 128:(kc + 1) * 128], bxb, self.xT, self.bxT[u.tt], u)

    def inproj_tile(self, l, kind, i, us):
        s = self.get_w()
        W, bW = self.W[s], self.bW[s]
        op = self.op
        pp_ = (i // 2) if kind == 'qk' else (i if kind in ('v', 'o', 'z') else 0)
        self.qT, self.kT, self.bqT, self.bkT = self.qT_[pp_], self.kT_[pp_], self.bqT_[pp_], self.bkT_[pp_]
        self.vtok, self.bv, self.G, self.bG = self.vtok_[pp_], self.bv_[pp_], self.G_[pp_], self.bG_[pp_]
        hs_samp = us[0].sidx is not None
        ncols = NSB if hs_samp else TB
        assert ncols <= 512
        bxT_all = [self.bxT[u.tt] for u in us]
        one = self.cst_sb[:, C_ONE:C_ONE + 1]
        if kind in ('qk', 'cu', 'cg', 'aq'):
            for ec in range(4):
                self.pull()
                k = self.pcount = getattr(self, 'pcount', 0) + 1
                ps, bps = self.PS[k % 2], self.bPS[k % 2]

                def fn(e, ec=ec, ps=ps):
                    ins = None
                    for kc in range(16):
                        ins = e.matmul(ps[:, 0:ncols], lhsT=W[:, kc, ec * 128:(ec + 1) * 128], rhs=self.xT[:, kc, 0:ncols],
                                       start=(kc == 0), stop=(kc == 15))
                    return ins
                op('pe', fn, rd=[bW] + bxT_all, wr=[bps])
                src = ps[:, 0:ncols]
                if kind == 'qk':
                    hl = i % 2
                    if ec < 2:
                        op('act', lambda e, s_=src, o=self.qT[:, hl, ec, 0:ncols]: e.activation(out=o, in_=s_, func=AF.Copy),
                           rd=[bps], wr=[self.bqT[hl][ec]])
                    else:
                        op('act', lambda e, s_=src, o=self.kT[:, hl, ec - 2, 0:ncols]: e.activation(out=o, in_=s_, func=AF.Copy, scale=1.0 / 16.0),
                           rd=[bps], wr=[self.bkT[hl][ec - 2]])
                elif kind == 'cu':
                    op('act', lambda e, s_=src, o=self.cuF[:, ec, 0:ncols]: e.activation(out=o, in_=s_, func=AF.Copy),
                       rd=[bps], wr=[self.bcu[ec]])
                elif kind == 'cg':
                    op('act', lambda e, s_=src: e.activation(out=self.sg[:, 0:ncols], in_=s_, func=AF.Sigmoid),
                       rd=[bps], wr=[self.bsg])
                    if not hs_samp:
                        op('dve', lambda e, ec=ec: e.tensor_tensor(out=self.aT[:, ec, 30:30 + TB], in0=self.cuF[:, ec, 0:TB],
                                                                   in1=self.sg[:, 0:TB], op=ALU.mult),
                           rd=[self.bsg, self.bcu[ec]], wr=[self.baT])
                    else:
                        s0 = us[0].sidx
                        op('dve', lambda e, ec=ec, s0=s0: e.tensor_tensor(out=self.aTs[:, ec, s0:s0 + NSB, 30], in0=self.cuF[:, ec, 0:NSB],
                                                                          in1=self.sg[:, 0:NSB], op=ALU.mult),
                           rd=[self.bsg, self.bcu[ec]], wr=[self.baTs[u_.sidx] for u_ in us])
                elif kind == 'aq':
                    op('act', lambda e, s_=src, o=self.aqT[:, ec, 0:ncols]: e.activation(out=o, in_=s_, func=AF.Copy, scale=0.125),
                       rd=[bps], wr=[self.baq[ec]])
            return
        for u in us:
            L, tt, c0 = u.L, u.tt, u.c0
            self.pull()
            k = self.pcount = getattr(self, 'pcount', 0) + 1
            ps, bps = self.PS[k % 2], self.bPS[k % 2]

            def fn(e, ps=ps, L=L, c0=c0):
                ins = None
                for kc in range(16):
                    ins = e.matmul(ps[0:L, :], lhsT=self.xT[:, kc, c0:c0 + L], rhs=W[:, kc, :], start=(kc == 0), stop=(kc == 15))
                return ins
            op('pe', fn, rd=[bW, self.bxT[tt]], wr=[bps])
            src = ps[0:L, :]
            if kind == 'v':
                op('act', lambda e, s_=src, o=self.vtok[0:L, tt].rearrange("p a b -> p (a b)"): e.activation(out=o, in_=s_, func=AF.Copy),
                   rd=[bps], wr=[self.bv[tt]])
            elif kind == 'o':
                g = self.G[0:L, tt].rearrange("p a b -> p (a b)")
                op('act', lambda e, s_=src, g=g: e.activation(out=g, in_=s_, func=AF.Sigmoid), rd=[bps], wr=[self.bG[tt]])
                op('dve', lambda e, g=g, L=L: e.tensor_tensor(out=g, in0=g, in1=self.mng[0:L, i * 512:(i + 1) * 512], op=ALU.mult),
                   rd=[self.bG[tt], self.bmng], wr=[self.bG[tt]])
            elif kind == 'z':
                g = self.G[0:L, tt].rearrange("p a b -> p (a b)")
                op('act', lambda e, s_=src, L=L: e.activation(out=self.gtmp[0:L, :], in_=s_, func=AF.Silu), rd=[bps], wr=[self.bgtmp])
                op('dve', lambda e, g=g, L=L: e.tensor_tensor(out=g, in0=g, in1=self.gtmp[0:L, :], op=ALU.mult),
                   rd=[self.bG[tt], self.bgtmp], wr=[self.bG[tt]])
            elif kind == 'cz':
                op('act', lambda e, s_=src, o=self.sz[0:L, tt, :]: e.activation(out=o, in_=s_, func=AF.Silu), rd=[bps], wr=[self.bsz[tt]])
            elif kind == 'az':
                op('act', lambda e, s_=src, o=self.saz[0:L, tt, :]: e.activation(out=o, in_=s_, func=AF.Silu), rd=[bps], wr=[self.bsaz[tt]])
            elif kind == 'kvg':
                bg = self.bgate[tt]
                op('act', lambda e, ps=ps, o=self.kvf[0:L, tt, :], L=L: e.activation(out=o, in_=ps[0:L, 0:256], func=AF.Copy),
                   rd=[bps], wr=[self.bkvf[tt]])
                op('dve', lambda e, ps=ps, L=L, tt=tt: e.tensor_tensor(out=self.ig[0:L, tt, :], in0=ps[0:L, 256:260],
                                                                      in1=self.bi_bc[0:L, l, :], op=ALU.add),
                   rd=[bps, self.bpar], wr=[bg])
                op('dve', lambda e, ps=ps, L=L, tt=tt: e.tensor_tensor(out=self.sp[0:L, tt, :], in0=ps[0:L, 260:264],
                                                                      in1=self.bf_bc[0:L, l, :], op=ALU.add),
                   rd=[bps, self.bpar], wr=[bg])
                op('act', lambda e, L=L, tt=tt: e.activation(out=self.sp[0:L, tt, :], in_=self.sp[0:L, tt, :], func=AF.Exp, scale=-1.0),
                   rd=[bg], wr=[bg])
                op('act', lambda e, L=L, tt=tt: e.activation(out=self.sp[0:L, tt, :], in_=self.sp[0:L, tt, :], func=AF.Ln,
                                                             bias=one[0:L, :]), rd=[bg, self.bcst], wr=[bg])

    def mlstm_unit(self, l, hp, u, last_blk):
        op, dma = self.op, self.dma
        qT_l, kT_l, vtok_l, G_l = self.qT_[hp], self.kT_[hp], self.vtok_[hp], self.G_[hp]
        bqT_l, bkT_l, bv_l, bG_l = self.bqT_[hp], self.bkT_[hp], self.bv_[hp], self.bG_[hp]
        L, tt, c0 = u.L, u.tt, u.c0
        cols = slice(c0, c0 + L)
        hs = slice(2 * hp, 2 * hp + 2)
        idf = self.cst_sb[:, C_ID:C_ID + 128]
        tri = self.cst_sb[:, C_TRI:C_TRI + 128]
        idb = self.cstb[:, CB_ID:CB_ID + 128]
        bigb = self.cstb[:, CB_BIGM:CB_BIGM + 128]
        onesb = self.cstb[:, CB_ONE:CB_ONE + 1]
        sel = lambda hl: self.cst_sb[0:2, C_SEL + 128 * hl:C_SEL + 128 * (hl + 1)]
        bcst = self.bcst
        P2, P3, P4, P5, P6 = self.PS[2], self.PS[3], self.PS[4], self.PS[5], self.PS[6]
        b2, b3, b4, b5 = self.bPS[2], self.bPS[3], self.bPS[4], self.bPS[5]
        b6 = self.bP6
        samp = u.sidx is not None
        if not samp:
            C, n, Cb, nb, m, bC = self.C[l][:, hs], self.n[l][:, hs], self.Cb[l][:, hs], self.nb[l][:, hs], self.m[l][hp], self.bC[l][hp]
        else:
            si = u.sidx
            C, n, Cb, nb, m, bC = self.Cs[:], self.ns[:], self.Csb[:], self.nsb[:], self.ms, self.bCs
            dma('sp', C, self.sC[l, si, hs].rearrange("h (dc p) e -> p h dc e", p=128), wr=[bC], chan=bC)
            self.op('sp', lambda e: e.dma_start(out=n, in_=self.sn[l, si, hs].rearrange("h (dc p) -> p h dc", p=128),
                                                allow_slow_non_contiguous=True), wr=[bC], chan=bC)
            dma('sp', m[:], self.sm[l, si, hs].rearrange("(h o) -> h o", o=1), wr=[bC], chan=bC)
            op('act', lambda e: e.activation(out=Cb, in_=C, func=AF.Copy), rd=[bC], wr=[bC])
            op('act', lambda e: e.activation(out=nb, in_=n, func=AF.Copy), rd=[bC], wr=[bC])
        bg = self.bgate[tt]
        sp_ = self.sp[0:L, tt, hs]
        op('pe', lambda e: e.matmul(P6[0:L, 0:2], lhsT=tri[0:L, 0:L], rhs=sp_, start=True, stop=True), rd=[bg, bcst], wr=[b6['b']])
        yield
        op('dve', lambda e: e.tensor_tensor(out=self.a_sb[0:L, :], in0=self.ig[0:L, tt, hs], in1=P6[0:L, 0:2], op=ALU.add),
           rd=[bg, b6['b']], wr=[self.ba])
        op('pe', lambda e: e.matmul(P6[0:2, 16:16 + L], lhsT=self.a_sb[0:L, :], rhs=idf[0:L, 0:L], start=True, stop=True),
           rd=[self.ba, bcst], wr=[b6['arow']])
        op('pe', lambda e: e.matmul(P6[0:2, 144:144 + L], lhsT=sp_, rhs=tri[0:L, 0:L], start=True, stop=True),
           rd=[bg, bcst], wr=[b6['brow']])
        op('dve', lambda e: e.tensor_copy(out=self.arow[:, 0:L], in_=P6[0:2, 16:16 + L]), rd=[b6['arow']], wr=[self.barow])
        yield
        op('dve', lambda e: e.tensor_tensor_scan(out=self.Mrow[:, 0:L], data0=self.arow[:, 0:L], data1=self.arow[:, 0:L],
                                                 initial=m[:], op0=ALU.max, op1=ALU.max), rd=[self.barow, bC], wr=[self.bMrow])
        op('dve', lambda e: e.tensor_tensor(out=self.mrow[:, 0:L], in0=self.Mrow[:, 0:L], in1=P6[0:2, 144:144 + L], op=ALU.subtract),
           rd=[self.bMrow, b6['brow']], wr=[self.bmrow])
        yield
        op('act', lambda e: e.activation(out=self.w0row[:, 0:L], in_=self.Mrow[:, 0:L], func=AF.Exp, scale=-1.0, bias=m[:]),
           rd=[self.bMrow, bC], wr=[self.bw0row])
        op('act', lambda e: e.activation(out=self.emrow[:, 0:L], in_=self.mrow[:, 0:L], func=AF.Exp, scale=-1.0),
           rd=[self.bmrow], wr=[self.bemrow])
        op('dve', lambda e: e.tensor_copy(out=m[:], in_=self.mrow[:, L - 1:L]), rd=[self.bmrow], wr=[bC])
        yield
        op('pe', lambda e: e.matmul(P6[0:L, 2:4], lhsT=self.emrow[0:2, 0:L], rhs=idf[0:2, 0:2], start=True, stop=True),
           rd=[self.bemrow, bcst], wr=[b6['tm']])

        def fn_bc(e):
            ins = None
            for hl in range(2):
                e.matmul(P3[0:L, hl * 128:hl * 128 + L], lhsT=sel(hl)[:, 0:L], rhs=self.Mrow[0:2, 0:L], start=True, stop=False)
                e.matmul(P3[0:L, hl * 128:hl * 128 + L], lhsT=idb[0:L, 0:L], rhs=bigb[0:L, 0:L], start=False, stop=True)
                ins = e.matmul(P3[:, 256 + hl * 128:256 + hl * 128 + L], lhsT=sel(hl), rhs=self.w0row[0:2, 0:L], start=True, stop=True)
            return ins
        op('pe', fn_bc, rd=[self.bMrow, self.bw0row, bcst], wr=[b3])
        yield
        for hl in range(2):
            op('act', lambda e, hl=hl: e.activation(out=self.wT[0:L, hl, 0:L], in_=P3[0:L, hl * 128:hl * 128 + L], func=AF.Exp,
                                                    scale=-1.0, bias=self.a_sb[0:L, hl:hl + 1]), rd=[b3, self.ba], wr=[self.bwT])

        yield
        def fn_s(e):
            ins = None
            for hl in range(2):
                for dc in range(2):
                    ins = e.matmul(P4[0:L, hl * 128:hl * 128 + L], lhsT=kT_l[:, hl, dc, cols], rhs=qT_l[:, hl, dc, cols],
                                   start=(dc == 0), stop=(dc == 1))
            return ins
        op('pe', fn_s, rd=bkT_l[0] + bkT_l[1] + bqT_l[0] + bqT_l[1], wr=[b4])
        for hl in range(2):
            op('dve', lambda e, hl=hl: e.tensor_tensor(out=self.swT[0:L, hl, 0:L], in0=P4[0:L, hl * 128:hl * 128 + L],
                                                       in1=self.wT[0:L, hl, 0:L], op=ALU.mult), rd=[b4, self.bwT], wr=[self.bswT])
        yield
        for hl in range(2):
            for dc in range(2):
                op('dve', lambda e, hl=hl, dc=dc: e.tensor_tensor(out=self.qs[:, hl, dc, 0:L], in0=qT_l[:, hl, dc, cols],
                                                                  in1=P3[:, 256 + hl * 128:256 + hl * 128 + L], op=ALU.mult),
                   rd=[b3, bqT_l[hl][dc]], wr=[self.bqs])
        op('dve', lambda e: e.tensor_copy(out=self.g0bc[:, :], in_=P3[:, 256:512].rearrange("p (a b) -> p a b", b=128)[:, :, L - 1]),
           rd=[b3], wr=[self.bg0bc])

        yield
        def fn_kt(e):
            ins = None
            for hl in range(2):
                for dc in range(2):
                    j = hl * 2 + dc
                    ins = e.transpose(self.PB[0:L, j * 128:(j + 1) * 128], kT_l[:, hl, dc, cols], idb)
            return ins
        op('pe', fn_kt, rd=bkT_l[0] + bkT_l[1] + [bcst], wr=[self.bPB[0]])
        op('act', lambda e: e.activation(out=self.ktok[0:L].rearrange("p a b -> p (a b)"), in_=self.PB[0:L, 0:512], func=AF.Copy),
           rd=[self.bPB[0]], wr=[self.bktok])

        yield
        def fn_num(e):
            ins = None
            for hl in range(2):
                e.matmul(P4[0:L, hl * 256:(hl + 1) * 256], lhsT=self.swT[0:L, hl, 0:L], rhs=vtok_l[0:L, tt, hl, :], start=True, stop=False)
                for dc in range(2):
                    e.matmul(P4[0:L, hl * 256:(hl + 1) * 256], lhsT=self.qs[:, hl, dc, 0:L], rhs=Cb[:, hl, dc, :],
                             start=False, stop=(dc == 1))
                e.matmul(P6[0:L, 4 + hl:5 + hl], lhsT=self.swT[0:L, hl, 0:L], rhs=onesb[0:L, :], start=True, stop=False)
                for dc in range(2):
                    ins = e.matmul(P6[0:L, 4 + hl:5 + hl], lhsT=self.qs[:, hl, dc, 0:L], rhs=nb[:, hl, dc:dc + 1],
                                   start=False, stop=(dc == 1))
            return ins
        op('pe', fn_num, rd=[self.bswT, self.bqs, bv_l[tt], bC, bcst], wr=[b4, b6['den']])
        yield
        s = self.sml
        bs = self.bsml
        op('act', lambda e: e.activation(out=s[0:L, 0:2], in_=P6[0:L, 4:6], func=AF.Abs), rd=[b6['den']], wr=[bs])
        op('dve', lambda e: e.tensor_tensor(out=s[0:L, 0:2], in0=s[0:L, 0:2], in1=P6[0:L, 2:4], op=ALU.max), rd=[bs, b6['tm']], wr=[bs])
        op('dve', lambda e: e.reciprocal(out=s[0:L, 2:4], in_=s[0:L, 0:2]), rd=[bs], wr=[bs])
        for hl in range(2):
            op('dve', lambda e, hl=hl: e.bn_stats(out=self.sm6[0:L, hl, :], in_=P4[0:L, hl * 256:(hl + 1) * 256]), rd=[b4], wr=[self.bsm6])
            op('dve', lambda e, hl=hl: e.bn_aggr(out=self.mv[0:L, hl, :], in_=self.sm6[0:L, hl, :]), rd=[self.bsm6], wr=[self.bmv])
        op('dve', lambda e: e.tensor_tensor(out=s[0:L, 4:6], in0=s[0:L, 2:4], in1=s[0:L, 2:4], op=ALU.mult), rd=[bs], wr=[bs])
        op('dve', lambda e: e.tensor_tensor(out=s[0:L, 4:6], in0=s[0:L, 4:6], in1=self.mv[0:L, :, 1], op=ALU.mult),
           rd=[bs, self.bmv], wr=[bs])
        self.rstd(s[0:L, 4:6], s[0:L, 4:6], L, bs)
        op('dve', lambda e: e.tensor_tensor(out=s[0:L, 6:8], in0=s[0:L, 4:6], in1=s[0:L, 2:4], op=ALU.mult), rd=[bs], wr=[bs])
        op('dve', lambda e: e.scalar_tensor_tensor(out=s[0:L, 8:10], in0=self.mv[0:L, :, 0], scalar=-1.0, in1=s[0:L, 6:8],
                                                   op0=ALU.mult, op1=ALU.mult), rd=[bs, self.bmv], wr=[bs])
        yield
        for hl in range(2):
            op('act', lambda e, hl=hl: e.activation(out=self.hn[0:L, hl, :], in_=P4[0:L, hl * 256:(hl + 1) * 256], func=AF.Identity,
                                                    scale=s[0:L, 6 + hl:7 + hl], bias=s[0:L, 8 + hl:9 + hl]), rd=[b4, bs], wr=[self.bhn])
        op('dve', lambda e: e.tensor_tensor(out=self.mtok[0:L, tt, hp * 512:(hp + 1) * 512], in0=self.hn[0:L].rearrange("p a b -> p (a b)"),
                                            in1=G_l[0:L, tt].rearrange("p a b -> p (a b)"), op=ALU.mult),
           rd=[self.bhn, bG_l[tt]], wr=[self.bmtok[tt][hp]])
        yield
        op('act', lambda e: e.activation(out=self.gb[0:L, :], in_=self.wT[0:L, :, L - 1], func=AF.Copy), rd=[self.bwT], wr=[self.bgb])
        for hl in range(2):
            op('dve', lambda e, hl=hl: e.tensor_scalar(out=self.gv[0:L, hl, :], in0=vtok_l[0:L, tt, hl, :],
                                                       scalar1=self.wT[0:L, hl, L - 1:L], scalar2=None, op0=ALU.mult),
               rd=[self.bwT, bv_l[tt]], wr=[self.bgv])
        for hl in range(2):
            yield

            def fn_dc(e, hl=hl):
                ins = None
                for dc in range(2):
                    e.matmul(P3[:, dc * 256:(dc + 1) * 256], lhsT=self.ktok[0:L, hl, dc * 128:(dc + 1) * 128], rhs=self.gv[0:L, hl, :],
                             start=True, stop=True)
                    ins = e.matmul(P6[:, 6 + 2 * hl + dc:7 + 2 * hl + dc], lhsT=self.ktok[0:L, hl, dc * 128:(dc + 1) * 128],
                                   rhs=self.gb[0:L, hl:hl + 1], start=True, stop=True)
                return ins
            op('pe', fn_dc, rd=[self.bktok, self.bgv, self.bgb], wr=[b3, b6['dn']])
            for dc in range(2):
                op('dve', lambda e, hl=hl, dc=dc: e.scalar_tensor_tensor(out=C[:, hl, dc, :], in0=C[:, hl, dc, :],
                                                                         scalar=self.g0bc[:, hl:hl + 1], in1=P3[:, dc * 256:(dc + 1) * 256],
                                                                         op0=ALU.mult, op1=ALU.add), rd=[b3, self.bg0bc, bC], wr=[bC])
            op('dve', lambda e, hl=hl: e.scalar_tensor_tensor(out=n[:, hl, :], in0=n[:, hl, :], scalar=self.g0bc[:, hl:hl + 1],
                                                              in1=P6[:, 6 + 2 * hl:8 + 2 * hl], op0=ALU.mult, op1=ALU.add),
               rd=[b6['dn'], self.bg0bc, bC], wr=[bC])
        op('act', lambda e: e.activation(out=Cb, in_=C, func=AF.Copy), rd=[bC], wr=[bC])
        op('act', lambda e: e.activation(out=nb, in_=n, func=AF.Copy), rd=[bC], wr=[bC])
        yield
        if samp:
            si = u.sidx
            dma('sp', self.oCs[l, si, hs].rearrange("h (dc p) e -> p h dc e", p=128), C, rd=[bC], chan=bC, final=True)
            self.final.append(bC) if bC not in self.final else None
            self.op('sp', lambda e: e.dma_start(out=self.ons[l, si, hs].rearrange("h (dc p) -> p h dc", p=128), in_=n,
                                                allow_slow_non_contiguous=True), rd=[bC], chan=bC)
            dma('sp', self.oms[l, si, hs].rearrange("(h o) -> h o", o=1), m[:], rd=[bC], chan=bC)
        elif last_blk and tt == NT - 1:
            dma('sp', self.oCp[l, hs].rearrange("h (dc p) e -> p h dc e", p=128), C, rd=[bC], chan=bC, final=True)
            self.op('sp', lambda e: e.dma_start(out=self.onp[l, hs].rearrange("h (dc p) -> p h dc", p=128), in_=n,
                                                allow_slow_non_contiguous=True), rd=[bC], chan=bC)
            dma('sp', self.omp[l, hs].rearrange("(h o) -> h o", o=1), m[:], rd=[bC], chan=bC)

    def conv_block(self, l, us):
        op = self.op
        cw, cb = self.cw, self.cb
        if us[0].sidx is None:
          for cc in range(4):
            op('dve', lambda e, cc=cc: e.tensor_scalar(out=self.yT[:, cc, 0:TB], in0=self.aT[:, cc, 0:TB], scalar1=cw[:, l, cc, 0:1],
                                                       scalar2=cb[:, l, cc:cc + 1], op0=ALU.mult, op1=ALU.add),
               rd=[self.baT, self.bpar], wr=[self.byT])
        for j in range(1, 31 if us[0].sidx is None else 1):
            yield
            for cc in range(4):
                op('dve', lambda e, cc=cc, j=j: e.scalar_tensor_tensor(out=self.yT[:, cc, 0:TB], in0=self.aT[:, cc, j:j + TB],
                                                                       scalar=cw[:, l, cc, j:j + 1], in1=self.yT[:, cc, 0:TB],
                                                                       op0=ALU.mult, op1=ALU.add),
                   rd=[self.baT, self.byT], wr=[self.byT])
        for u in us:
            if u.sidx is None:
                continue
            si = u.sidx
            self.dma('sp', self.aTs[:, :, si, 0:30], self.scv[l, si].rearrange("(cc c) j -> c cc j", c=128), wr=[self.baTs[si]],
                     chan=self.baTs[si])
        for u in us:
            if u.sidx is None:
                continue
            si = u.sidx
            c0_ = u.c0
            yield
            for cc in range(4):
                op('dve', lambda e, cc=cc, si=si: e.tensor_tensor(out=self.ctmp[:, :], in0=self.aTs[:, cc, si, :], in1=cw[:, l, cc, :],
                                                                  op=ALU.mult), rd=[self.baTs[si], self.bpar], wr=[self.bctmp])
                op('dve', lambda e, cc=cc, si=si, c0_=c0_: e.tensor_reduce(out=self.yT[:, cc, c0_:c0_ + 1], in_=self.ctmp[:, :],
                                                                  axis=AX.X, op=ALU.add), rd=[self.bctmp], wr=[self.byT])
                op('dve', lambda e, cc=cc, si=si, c0_=c0_: e.tensor_tensor(out=self.yT[:, cc, c0_:c0_ + 1],
                                                                  in0=self.yT[:, cc, c0_:c0_ + 1], in1=cb[:, l, cc:cc + 1],
                                                                  op=ALU.add), rd=[self.byT, self.bpar], wr=[self.byT])

    def conv_unit(self, l, u):
        op = self.op
        L, tt, c0 = u.L, u.tt, u.c0
        P5, b5 = self.PS[5], self.bPS[5]
        idf = self.cst_sb[:, C_ID:C_ID + 128]

        def fn(e):
            ins = None
            for cc in range(4):
                ins = e.transpose(P5[0:L, cc * 128:(cc + 1) * 128], self.yT[:, cc, c0:c0 + L], idf)
            return ins
        op('pe', fn, rd=[self.byT, self.bcst], wr=[b5])
        yield
        bs = self.bcsm
        op('dve', lambda e: e.bn_stats(out=self.cst6[0:L, :], in_=P5[0:L, :]), rd=[b5], wr=[bs])
        op('dve', lambda e: e.bn_aggr(out=self.cmv[0:L, :], in_=self.cst6[0:L, :]), rd=[bs], wr=[bs])
        self.rstd(self.csml[0:L, 0:1], self.cmv[0:L, 1:2], L, bs)
        op('dve', lambda e: e.scalar_tensor_tensor(out=self.csml[0:L, 1:2], in0=self.cmv[0:L, 0:1], scalar=-1.0, in1=self.csml[0:L, 0:1],
                                                   op0=ALU.mult, op1=ALU.mult), rd=[bs], wr=[bs])
        yield
        op('act', lambda e: e.activation(out=self.yn[0:L, :], in_=P5[0:L, :], func=AF.Identity, scale=self.csml[0:L, 0:1],
                                         bias=self.csml[0:L, 1:2]), rd=[b5, bs], wr=[self.byn])
        op('dve', lambda e: e.tensor_tensor(out=self.yn[0:L, :], in0=self.yn[0:L, :], in1=self.clg[0:L, :], op=ALU.mult),
           rd=[self.byn, self.bcl], wr=[self.byn])
        op('dve', lambda e: e.tensor_tensor(out=self.yn[0:L, :], in0=self.yn[0:L, :], in1=self.clb[0:L, :], op=ALU.add),
           rd=[self.byn, self.bcl], wr=[self.byn])
        op('act', lambda e: e.activation(out=self.yn[0:L, :], in_=self.yn[0:L, :], func=AF.Silu), rd=[self.byn], wr=[self.byn])
        op('dve', lambda e: e.tensor_tensor(out=self.mtok[0:L, tt, 1024:1536], in0=self.yn[0:L, :], in1=self.sz[0:L, tt, :], op=ALU.mult),
           rd=[self.byn, self.bsz[tt]], wr=[self.bmtok[tt][2]])

    def conv_state_out(self, l, src_of, bsrc, dst):
        P5, b5 = self.PS[5], self.bPS[5]
        idf = self.cst_sb[:, C_ID:C_ID + 128]

        def fn(e):
            ins = None
            for cc in range(4):
                ins = e.transpose(P5[0:30, cc * 128:(cc + 1) * 128], src_of(cc), idf)
            return ins
        self.op('pe', fn, rd=[bsrc, self.bcst], wr=[b5])
        self.op('act', lambda e: e.activation(out=self.cvo[:, :], in_=P5[0:30, :], func=AF.Copy), rd=[b5], wr=[self.bcvo])
        self.dma('sp', dst, self.cvo[:, :], rd=[self.bcvo], chan=self.bcvo, final=True)

    def conv_finish(self, l, us, last_blk):
        for u in us:
            if u.sidx is not None:
                si = u.sidx
                self.conv_state_out(l, lambda cc, si=si: self.aTs[:, cc, si, 1:31], self.baTs[si], self.ocvs[l, si])
        if us[0].sidx is not None:
            return
        if last_blk:
            self.conv_state_out(l, lambda cc: self.aT[:, cc, TB:TB + 30], self.baT, self.ocvp[l])
        else:
            self.op('act', lambda e: e.activation(out=self.hist[l][:, :, :], in_=self.aT[:, :, TB:TB + 30], func=AF.Copy),
                    rd=[self.baT], wr=[self.bhist[l]])

    def conv_start(self, l, blk):
        if blk < 0:
            return
        if blk == 0:
            self.op('dve', lambda e: e.memset(self.aT[:, :, 0:30], 0.0), wr=[self.baT])
        else:
            self.op('act', lambda e: e.activation(out=self.aT[:, :, 0:30], in_=self.hist[l][:, :, :], func=AF.Copy),
                    rd=[self.bhist[l]], wr=[self.baT])

    def attn_unit(self, l, u, ci, last_blk):
        op, dma = self.op, self.dma
        L, tt, c0 = u.L, u.tt, u.c0
        cols = slice(c0, c0 + L)
        idb = self.cstb[:, CB_ID:CB_ID + 128]
        idf = self.cst_sb[:, C_ID:C_ID + 128]
        nmp = self.cstb[:, CB_NMP:CB_NMP + 512].rearrange("p (a b) -> p a b", b=128)
        nmc = self.cstb[:, CB_NMC:CB_NMC + 512].rearrange("p (a b) -> p a b", b=128)
        P2, P3, P4, P5 = self.PS[2], self.PS[3], self.PS[4], self.PS[5]
        b2, b3, b4, b5 = self.bPS[2], self.bPS[3], self.bPS[4], self.bPS[5]
        samp = u.sidx is not None
        if samp:
            cur_kT, bcur_kT, cur_v, bcur_v = self.akT[l][:, 0, :], self.bakT[l][0], self.vaug[l][:, 0], self.bvaug[l][0]
        else:
            sl = ci % 2
            cur_kT, bcur_kT, cur_v, bcur_v = self.akT[l][:, sl, :], self.bakT[l][sl], self.vaug[l][:, sl], self.bvaug[l][sl]
        op('act', lambda e: e.activation(out=self.kvb[0:L, :], in_=self.kvf[0:L, tt, 0:128], func=AF.Copy), rd=[self.bkvf[tt]], wr=[self.bkvb])
        op('pe', lambda e: e.transpose(self.PB[:, 512:512 + L], self.kvb[0:L, :], idb[0:L, 0:L]), rd=[self.bkvb, self.bcst], wr=[self.bPB[1]])
        op('dve', lambda e: e.tensor_copy(out=cur_kT[:, 0:L], in_=self.PB[:, 512:512 + L]), rd=[self.bPB[1]], wr=[bcur_kT])
        op('dve', lambda e: e.tensor_copy(out=cur_v[0:L, :, 0:64], in_=self.kvf[0:L, tt, 128:256].rearrange("p (a b) -> p a b", b=64)),
           rd=[self.bkvf[tt]], wr=[bcur_v])
        yield
        blocks = []
        if samp:
            si = u.sidx
            bc = self.bcache
            dma('sp', self.ckf[:, :], self.ck[l, si], wr=[bc], chan=bc)
            dma('sp', self.cvf[:, :], self.cv[l, si], wr=[bc], chan=bc)
            op('act', lambda e: e.activation(out=self.kvb[:, :], in_=self.ckf[:, :], func=AF.Copy), rd=[bc], wr=[self.bkvb])
            op('pe', lambda e: e.transpose(self.PB[:, 512:640], self.kvb[:, :], idb), rd=[self.bkvb, self.bcst], wr=[self.bPB[1]])
            op('dve', lambda e: e.tensor_copy(out=self.akTs[:, :], in_=self.PB[:, 512:640]), rd=[self.bPB[1]], wr=[self.bakTs])
            op('dve', lambda e: e.tensor_copy(out=self.vaugs[:, :, 0:64], in_=self.cvf[:, :].rearrange("p (a b) -> p a b", b=64)),
               rd=[bc], wr=[self.bvaugs])
            blocks.append((self.akTs[:, :], self.vaugs[:, :, :], 128, nmp, [self.bakTs, self.bvaugs]))
            blocks.append((cur_kT, cur_v, 1, None, [bcur_kT, bcur_v]))
            bo = self.bout['oks']
            dma('sp', self.oks[l, si, 0:127, :], self.ck[l, si, 1:128, :], wr=[bo], chan=bo, final=True)
            dma('sp', self.ovs[l, si, 0:127, :], self.cv[l, si, 1:128, :], wr=[bo], chan=bo, final=True)
            dma('sp', self.oks[l, si, 127:128, :], self.kvf[0:1, tt, 0:128], rd=[self.bkvf[tt]], chan=self.bkvf[tt], final=True)
            dma('sp', self.ovs[l, si, 127:128, :], self.kvf[0:1, tt, 128:256], rd=[self.bkvf[tt]], chan=self.bkvf[tt], final=True)
        else:
            if ci > 0:
                ps_ = 1 - sl
                blocks.append((self.akT[l][:, ps_, :], self.vaug[l][:, ps_], 128, nmp, [self.bakT[l][ps_], self.bvaug[l][ps_]]))
            blocks.append((cur_kT, cur_v, 128, nmc, [bcur_kT, bcur_v]))
            if last_blk and tt == NT - 1:
                dma('sp', self.okp[l], self.kvf[:, tt, 0:128], rd=[self.bkvf[tt]], chan=self.bkvf[tt], final=True)
                dma('sp', self.ovp[l], self.kvf[:, tt, 128:256], rd=[self.bkvf[tt]], chan=self.bkvf[tt], final=True)
        for g in range(2):
            gp = slice(64 * g, 64 * g + 64)
            yield
            for bi, (kTa, va, Lk, mask, bufs) in enumerate(blocks):
                PSs, bPSs = P2, b2

                def fn(e, kTa=kTa, Lk=Lk, mask=mask, PSs=PSs, gp=gp):
                    out = PSs[0:Lk, :].rearrange("p (a b) -> p a b", b=128)[:, :, 0:L]
                    ins = e.matmul(out, lhsT=kTa[gp, 0:Lk], rhs=self.aqT[gp, :, cols], start=True, stop=(mask is None))
                    if mask is not None:
                        ins = e.matmul(out, lhsT=idb[0:Lk, 0:Lk], rhs=mask[0:Lk, :, 0:L], start=False, stop=True)
                    return ins
                op('pe', fn, rd=self.baq + [bufs[0], self.bcst], wr=[bPSs])
                op('act', lambda e, Lk=Lk, PSs=PSs, bi=bi: e.activation(
                    out=self.PT[0:Lk, bi, :, 0:L], in_=PSs[0:Lk, :].rearrange("p (a b) -> p a b", b=128)[:, :, 0:L], func=AF.Exp),
                    rd=[bPSs], wr=[self.bPT[bi]])
            Po, bPo = P5, b5
            yield

            def fn_pv(e, Po=Po, g=g):
                ins = None
                for i in range(4):
                    for bi, (kTa, va, Lk, mask, bufs) in enumerate(blocks):
                        ins = e.matmul(Po[0:L, i * 65:(i + 1) * 65], lhsT=self.PT[0:Lk, bi, i, 0:L], rhs=va[0:Lk, g, :],
                                       start=(bi == 0), stop=(bi == len(blocks) - 1))
                return ins
            op('pe', fn_pv, rd=[self.bPT[bi] for bi in range(len(blocks))] + [b[1] for b in [blk_[4] for blk_ in blocks]], wr=[bPo])
            po3 = Po[0:L, 0:260].rearrange("p (a b) -> p a b", b=65)
            yield
            bs = self.basml
            op('dve', lambda e, po3=po3, g=g: e.tensor_tensor(out=self.asml[0:L, 0:4], in0=po3[:, :, 64], in1=self.esk[0:L, l, 4 * g:4 * g + 4],
                                                              op=ALU.add), rd=[bPo, self.bpar], wr=[bs])
            op('dve', lambda e: e.reciprocal(out=self.asml[0:L, 4:8], in_=self.asml[0:L, 0:4]), rd=[bs], wr=[bs])
            for i in range(4):
                op('dve', lambda e, po3=po3, i=i: e.tensor_scalar(out=self.ao[0:L, i, :], in0=po3[:, i, 0:64], scalar1=self.asml[0:L, 4 + i:5 + i],
                                                                  scalar2=None, op0=ALU.mult), rd=[bPo, bs], wr=[self.bao])
            h0 = 1536 + 256 * g
            op('dve', lambda e, g=g, h0=h0: e.tensor_tensor(out=self.mtok[0:L, tt, h0:h0 + 256], in0=self.ao[0:L].rearrange("p a b -> p (a b)"),
                                                            in1=self.saz[0:L, tt, 256 * g:256 * g + 256], op=ALU.mult),
               rd=[self.bao, self.bsaz[tt]], wr=[self.bmtok[tt][3]])

    def merge_T(self, u, rounds=(0, 1, 2, 3)):
        L, tt = u.L, u.tt
        self.transpose_rows(lambda kc: self.mtok[0:L, tt, kc * 128:(kc + 1) * 128], lambda r: self.bmtok[tt][r], self.mT, self.bmT[tt], u,
                            k_evac=1, rounds=rounds)

    def outproj(self, l, us, part):
        op = self.op
        kcs = [0, 1, 2, 3] + list(range(8, 16)) if part == 'A' else [4, 5, 6, 7]
        for j in range(4):
            s = self.get_w()
            W, bW = self.W[s], self.bW[s]
            for u in us:
                L, tt, c0 = u.L, u.tt, u.c0
                k = self.pcount = getattr(self, 'pcount', 0) + 1
                ps, bps = self.PS[k % 2], self.bPS[k % 2]
                for g0 in range(0, len(kcs), 4):
                    self.pull(3)

                    def fn(e, ps=ps, L=L, c0=c0, W=W, g0=g0):
                        ins = None
                        for i_ in range(g0, min(g0 + 4, len(kcs))):
                            kc = kcs[i_]
                            ins = e.matmul(ps[0:L, :], lhsT=self.mT[:, kc, c0:c0 + L], rhs=W[:, kc, :], start=(i_ == 0),
                                           stop=(i_ == len(kcs) - 1))
                        return ins
                    op('pe', fn, rd=[bW, self.bmT[tt]], wr=[bps])
                xa, bx = self.Xap(u, slice(512 * j, 512 * (j + 1)))
                if part == 'A':
                    op('dve', lambda e, xa=xa, ps=ps, L=L: e.scalar_tensor_tensor(out=xa, in0=xa, scalar=ALPHA, in1=ps[0:L, :],
                                                                                  op0=ALU.mult, op1=ALU.add), rd=[bps, bx], wr=[bx])
                else:
                    op('dve', lambda e, xa=xa, ps=ps, L=L: e.tensor_tensor(out=xa, in0=xa, in1=ps[0:L, :], op=ALU.add), rd=[bps, bx], wr=[bx])
                    op('dve', lambda e, xa=xa, L=L, tt=tt, j=j: e.bn_stats(out=self.lst[0:L, tt, j, :], in_=xa), rd=[bx], wr=[self.blsm])

    def final_ln(self, l, u, blk):
        op = self.op
        L, tt = u.L, u.tt
        xa, bx = self.Xap(u)
        bs = self.blsm
        op('dve', lambda e: e.bn_aggr(out=self.lmv[0:L, :], in_=self.lst[0:L, tt].rearrange("p a b -> p (a b)")), rd=[bs, bx], wr=[bs])
        self.rstd(self.lsm[0:L, 0:1], self.lmv[0:L, 1:2], L, bs)
        op('dve', lambda e: e.scalar_tensor_tensor(out=self.lsm[0:L, 1:2], in0=self.lmv[0:L, 0:1], scalar=-1.0, in1=self.lsm[0:L, 0:1],
                                                   op0=ALU.mult, op1=ALU.mult), rd=[bs], wr=[bs])
        op('act', lambda e: e.activation(out=xa, in_=xa, func=AF.Identity, scale=self.lsm[0:L, 0:1], bias=self.lsm[0:L, 1:2]),
           rd=[bx, bs], wr=[bx])

    def lnp_load(self, l, j):
        k = self.lncount = getattr(self, 'lncount', 0) + 1
        lp, bl = self.lnp[k % 2], self.bln[k % 2]
        cs = slice(512 * j, 512 * (j + 1))
        self.dma('sp', lp[:, 0, :], self.p_lng[l, cs].partition_broadcast(128), wr=[bl], chan=bl)
        self.dma('sp', lp[:, 1, :], self.p_lnb[l, cs].partition_broadcast(128), wr=[bl], chan=bl)
        return lp, bl

    def final_gain(self, l, us, blk):
        op = self.op
        pre = getattr(self, 'lnp_pre', {})
        self.lnp_pre = {}
        for j in range(4):
            lp, bl = pre[j] if j in pre else self.lnp_load(l, j)
            cs = slice(512 * j, 512 * (j + 1))
            for u in us:
                xj, bx = self.Xap(u, cs)
                L = u.L
                op('dve', lambda e, xj=xj, lp=lp, L=L: e.tensor_tensor(out=xj, in0=xj, in1=lp[0:L, 0, :], op=ALU.mult), rd=[bx, bl], wr=[bx])
                op('dve', lambda e, xj=xj, lp=lp, L=L: e.tensor_tensor(out=xj, in0=xj, in1=lp[0:L, 1, :], op=ALU.add), rd=[bx, bl], wr=[bx])
                if l < DEPTH - 1:
                    tt = u.tt
                    stg = self.mtok[0:L, tt, cs]
                    bst = self.bmtok[tt][j]
                    op('act', lambda e, stg=stg, xj=xj: e.activation(out=stg, in_=xj, func=AF.Copy), rd=[bx], wr=[bst])
                    self.transpose_rows(lambda kc, tt=tt, L=L: self.mtok[0:L, tt, kc * 128:(kc + 1) * 128], lambda r, tt=tt: self.bmtok[tt][r],
                                        self.xT, self.bxT[tt], u, rounds=(j,), act_only=True)
        for u in us:
            self.final_out(l, u, blk)

    def final_out(self, l, u, blk):
        L, tt = u.L, u.tt
        xa, bx = self.Xap(u)
        if l == DEPTH - 1:
            if u.sidx is None:
                r0 = blk * TB + tt * 128
                self.dma('sp', self.yp[r0:r0 + 128, :], xa, rd=[bx], chan=bx, final=True)
            else:
                self.dma('sp', self.ys[u.sidx:u.sidx + 1, :], xa, rd=[bx], chan=bx, final=True)

    def seq(self, *gens):
        for g in gens:
            yield from g

    def layer_block(self, blk, l):
        us = self.units(blk)
        last_blk = (blk == self.nblk - 1)
        self.load_params(l)
        if l == 0:
            self.load_x(blk, us)
        self.chk()
        nxt_prompt = (l == DEPTH - 1) and (0 <= blk + 1 < self.nblk)
        if nxt_prompt:
            self.prefetch_x(blk + 1)
        if l == 0:
            if self.xT_prefetched:
                self.xT_prefetched = False
            else:
                for u in us:
                    self.make_xT(u)
        self.chk()
        self.conv_start(l, blk)
        T = lambda kind, i: self.inproj_tile(l, kind, i, us)
        T('cu', 0)
        T('cg', 0)
        g_cb = self.conv_block(l, us)
        self.bg.append(g_cb)
        T('kvg', 0)
        for k_, i_ in (('qk', 0), ('qk', 1), ('v', 0), ('o', 0), ('z', 0)):
            T(k_, i_)
        g_m0 = self.seq(*[self.mlstm_unit(l, 0, u, last_blk) for u in us])
        self.bg.append(g_m0)
        for k_, i_ in (('cz', 0), ('aq', 0), ('az', 0)):
            T(k_, i_)
        g_at = self.seq(*[self.attn_unit(l, u, (blk * NT + u.tt if u.sidx is None else None), last_blk) for u in us])
        self.bg.append(g_at)
        for k_, i_ in (('qk', 2), ('qk', 3), ('v', 1), ('o', 1), ('z', 1)):
            T(k_, i_)
        self.drain([g_cb, g_at])
        self.bg.append(self.seq(*[self.conv_unit(l, u) for u in us]))
        self.drain([g_m0])
        g_m1 = self.seq(*[self.mlstm_unit(l, 1, u, last_blk) for u in us])
        self.bg.append(g_m1)
        self.drain([g for g in self.bg if g is not g_m1])
        for u in us:
            self.merge_T(u, rounds=(0, 2, 3))
        self.outproj(l, us, 'A')
        self.drain()
        self.conv_finish(l, us, last_blk)
        self.chk()
        if self.debug and blk == 0 and l == 0:
            for u in us:
                self.dump("mtok%d" % u.tt, self.mtok[0:u.L, u.tt, :], list(self.bmtok[u.tt]), (u.L, D), BF16)
        for u in us:
            self.merge_T(u, rounds=(1,))
        self.chk()
        self.lnp_pre = {0: self.lnp_load(l, 0), 1: self.lnp_load(l, 1)}
        self.outproj(l, us, 'B')
        if nxt_prompt:
            self.build_xT_prefetched()
        self.chk()
        for u in us:
            self.final_ln(l, u, blk)
        self.final_gain(l, us, blk)
        if self.debug and blk == 0 and l == 0:
            for u in us:
                self.dump("x1_%d" % u.tt, self.X[0:u.L, u.tt, :], self.bX[u.tt], (u.L, D))
        self.chk()

    def build(self):
        self.setup()
        seq = []
        blks = (list(range(-(NS // NSB), 0)) if self.with_sample else []) + list(range(self.nblk))
        for blk in blks:
            for l in range(DEPTH):
                seq += [(l, j) for j in range(NW)] + [(l, NW_IN + j, 'B') for j in range(4)]
        self.w_plan(seq)
        try:
            self.chk()
            for blk in blks:
                for l in range(DEPTH):
                    self.layer_block(blk, l)
        except StopIteration:
            pass
        self.P.emit(self.final)
        return self.nc


_CACHE = {}


def kernel(x_prompt, x_sample, state_C, state_n, state_m, state_conv, cache_k, cache_v,
           w_in, w_out, b_igate, b_fgate, m_norm_g, conv_w, conv_b, conv_ln_g, conv_ln_b,
           sinks, ln_g, ln_b):
    f = lambda a: np.ascontiguousarray(np.asarray(a, dtype=np.float32))
    x_prompt, x_sample = f(x_prompt), f(x_sample)
    if 'nc' not in _CACHE:
        _CACHE['nc'] = K().build()
    nc = _CACHE['nc']
    wt = host_weight_tiles(f(w_in), f(w_out))
    cst, cstB = host_consts()
    scv = np.ascontiguousarray(f(state_conv).transpose(0, 1, 3, 2))
    p_cw = np.ascontiguousarray(f(conv_w).transpose(0, 2, 1).reshape(DEPTH, 4, 128, 31).transpose(0, 2, 1, 3)).reshape(DEPTH, 128, 124)
    p_cb = np.ascontiguousarray(f(conv_b).reshape(DEPTH, 4, 128).transpose(0, 2, 1))
    sC, sn, sm = f(state_C), f(state_n), f(state_m)
    ck = f(cache_k).reshape(DEPTH, 32, 128, 128)
    cv = f(cache_v).reshape(DEPTH, 32, 128, 128)
    in_maps = []
    xp_dummy = np.zeros_like(x_prompt[0])
    for c in range(NCORE):
        b = c % 2
        ss = slice(NS * c, NS * (c + 1))
        in_maps.append({
            "xp": x_prompt[b] if c < 2 else xp_dummy, "xs": np.ascontiguousarray(x_sample[ss, 0, :]), "wt": wt,
            "sC": np.ascontiguousarray(sC[:, ss]), "sn": np.ascontiguousarray(sn[:, ss]), "sm": np.ascontiguousarray(sm[:, ss]),
            "scv": np.ascontiguousarray(scv[:, ss]), "ck": np.ascontiguousarray(ck[:, ss]), "cv": np.ascontiguousarray(cv[:, ss]),
            "cst": cst, "cstB": cstB, "p_bi": f(b_igate), "p_bf": f(b_fgate), "p_mng": f(m_norm_g), "p_cw": p_cw, "p_cb": p_cb,
            "p_clg": f(conv_ln_g), "p_clb": f(conv_ln_b), "p_sk": f(sinks), "p_lng": f(ln_g), "p_lnb": f(ln_b),
        })
    res = run_bass_kernel_spmd(nc, in_maps, core_ids=list(range(NCORE))).results
    cat = lambda k, ax: np.concatenate([r[k] for r in res], axis=ax)
    stack2 = lambda k: np.stack([res[0][k], res[1][k]], axis=1)
    y_prompt = np.stack([res[0]["yp"], res[1]["yp"]], axis=0)
    y_sample = cat("ys", 0).reshape(32, 1, D)
    new_C_p = stack2("oCp")
    new_n_p = stack2("onp")
    new_m_p = stack2("omp")
    new_conv_p = stack2("ocvp")
    new_k_p = stack2("okp").reshape(DEPTH, 2, 128, 2, 64)
    new_v_p = stack2("ovp").reshape(DEPTH, 2, 128, 2, 64)
    new_C_s = cat("oCs", 1)
    new_n_s = cat("ons", 1)
    new_m_s = cat("oms", 1)
    new_conv_s = cat("ocvs", 1)
    new_k_s = cat("oks", 1).reshape(DEPTH, 32, 128, 2, 64)
    new_v_s = cat("ovs", 1).reshape(DEPTH, 32, 128, 2, 64)
    return (y_prompt, y_sample, new_C_p, new_n_p, new_m_p, new_conv_p, new_k_p, new_v_p,
            new_C_s, new_n_s, new_m_s, new_conv_s, new_k_s, new_v_s)
```
